# Optimizing a Trainium2 kernel written in Bass

```python
import jax
import jax.numpy as jnp
from jax import lax
import numpy as np

D_MODEL = 2048
BATCH = 2
SEQ = 8192
DEPTH = 1
DEC_BATCH = 16
DEC_SEQ = 64
PAST_LEN = 4096

CHUNK = 64
N_PREV_CHUNKS = 8
ATT_REACH = N_PREV_CHUNKS * CHUNK
REL_CLIP = 256
N_REL = 2 * REL_CLIP + 1
ATT_HEAD_DIM = 128
ATT_WIDTH = D_MODEL // 2
ATT_HEADS = ATT_WIDTH // ATT_HEAD_DIM
MLSTM_WIDTH = D_MODEL - ATT_WIDTH
MLSTM_HEADS = 4
MLSTM_HEAD_DIM = MLSTM_WIDTH // MLSTM_HEADS
MIX_WIDTH = ATT_WIDTH + MLSTM_WIDTH
IN_COLS = 3 * ATT_WIDTH + 4 * MLSTM_WIDTH + 2 * MLSTM_HEADS
IN_SPLITS = (ATT_WIDTH, 2 * ATT_WIDTH, 3 * ATT_WIDTH,
             3 * ATT_WIDTH + MLSTM_WIDTH, 3 * ATT_WIDTH + 2 * MLSTM_WIDTH,
             3 * ATT_WIDTH + 3 * MLSTM_WIDTH, 3 * ATT_WIDTH + 4 * MLSTM_WIDTH,
             3 * ATT_WIDTH + 4 * MLSTM_WIDTH + MLSTM_HEADS)
D_FF = 5632
FFN_CONV = 3
EPS = 1e-6

kernel_name = "hybrid_stream_band_attn_mlstm_convffn_step"


def rms_norm(x, g):
    xf = x.astype(jnp.float32)
    y = xf * lax.rsqrt(jnp.mean(xf * xf, axis=-1, keepdims=True) + EPS)
    return (y * g.astype(jnp.float32)).astype(x.dtype)


def prompt_band(t, n_chunks):
    bn = t.shape[0]
    tc = t.reshape(bn, n_chunks, CHUNK, ATT_HEADS, ATT_HEAD_DIM)
    tc = jnp.pad(tc, ((0, 0), (N_PREV_CHUNKS, 0), (0, 0), (0, 0), (0, 0)))
    idx = jnp.arange(n_chunks)[:, None] + jnp.arange(N_PREV_CHUNKS + 1)[None, :]
    band = tc[:, idx]
    return band.reshape(bn, n_chunks, (N_PREV_CHUNKS + 1) * CHUNK, ATT_HEADS, ATT_HEAD_DIM)


def band_attention(q, k, v, rel_bias, past_rows, valid):
    lq, lk = q.shape[2], k.shape[2]
    dist = past_rows + jnp.arange(lq)[:, None] - jnp.arange(lk)[None, :]
    bias = rel_bias[:, jnp.clip(dist, -REL_CLIP, REL_CLIP) + REL_CLIP].astype(jnp.float32)
    s = jnp.einsum("bnqhd,bnkhd->bnhqk", q, k).astype(jnp.float32) * (ATT_HEAD_DIM ** -0.5) + bias
    s = jnp.where(valid[None, :, None, None, :], s, -jnp.inf)
    p = jax.nn.softmax(s, axis=-1).astype(v.dtype)
    return jnp.einsum("bnhqk,bnkhd->bnqhd", p, v)


def mlstm_chunk(state, inp):
    c_prev, n_prev, m_prev = state
    q, k, v, ig, lf = inp
    L = q.shape[2]
    b = jnp.cumsum(lf, axis=-1)
    causal = jnp.tril(jnp.ones((L, L), dtype=bool))
    dmat = jnp.where(causal, b[..., :, None] - b[..., None, :] + ig[..., None, :], -jnp.inf)
    inter = b + m_prev[..., None]
    m_row = jnp.maximum(jnp.max(dmat, axis=-1), inter)
    w = jnp.exp(dmat - m_row[..., None])
    w_inter = jnp.exp(inter - m_row)
    qk = jnp.einsum("bhtd,bhsd->bhts", q, k) * w
    num = jnp.einsum("bhts,bhse->bhte", qk, v) + w_inter[..., None] * jnp.einsum("bhtd,bhde->bhte", q, c_prev)
    den = jnp.sum(qk, axis=-1) + w_inter * jnp.einsum("bhtd,bhd->bht", q, n_prev)
    h = num / jnp.maximum(jnp.abs(den), jnp.exp(-m_row))[..., None]
    b_last = b[..., -1]
    dec_in = b_last[..., None] - b + ig
    m_new = jnp.maximum(b_last + m_prev, jnp.max(dec_in, axis=-1))
    w_s = jnp.exp(dec_in - m_new[..., None])
    w_prev = jnp.exp(b_last + m_prev - m_new)
    c_new = w_prev[..., None, None] * c_prev + jnp.einsum("bhs,bhsd,bhse->bhde", w_s, k, v)
    n_new = w_prev[..., None] * n_prev + jnp.einsum("bhs,bhsd->bhd", w_s, k)
    return (c_new, n_new, m_new), h


def hybrid_layer(x, c, prm, att_past, m_state, conv_prev):
    f32 = jnp.float32
    bn, T, _ = x.shape
    mod = (jax.nn.silu(c) @ prm["ada_w"] + prm["ada_b"])[:, None, :]
    sh1, sc1, gt1, sh2, sc2, gt2 = jnp.split(mod, 6, axis=-1)

    hmod = rms_norm(x, prm["norm_pre_mix"]) * (1 + sc1) + sh1
    proj = hmod @ prm["w_in"]
    qa, ka, va, qm, km, vm, om, ig, fg = jnp.split(proj, IN_SPLITS, axis=-1)

    qa = qa.reshape(bn, T, ATT_HEADS, ATT_HEAD_DIM)
    ka = ka.reshape(bn, T, ATT_HEADS, ATT_HEAD_DIM)
    va = va.reshape(bn, T, ATT_HEADS, ATT_HEAD_DIM)
    if att_past is None:
        nc = T // CHUNK
        chunk_ok = (jnp.arange(nc)[:, None] + jnp.arange(N_PREV_CHUNKS + 1)[None, :]) >= N_PREV_CHUNKS
        valid = jnp.repeat(chunk_ok, CHUNK, axis=1)
        att = band_attention(qa.reshape(bn, nc, CHUNK, ATT_HEADS, ATT_HEAD_DIM),
                             prompt_band(ka, nc), prompt_band(va, nc),
                             prm["att_rel_bias"], ATT_REACH, valid)
        keep = min(ATT_REACH, T)
        new_k, new_v = ka[:, T - keep:], va[:, T - keep:]
    else:
        k_cache, v_cache = att_past
        rows = k_cache.shape[1]
        kb = jnp.concatenate([k_cache.astype(ka.dtype), ka], axis=1)[:, None]
        vb = jnp.concatenate([v_cache.astype(va.dtype), va], axis=1)[:, None]
        valid = jnp.ones((1, rows + T), dtype=bool)
        att = band_attention(qa[:, None], kb, vb, prm["att_rel_bias"], rows, valid)
        new_k, new_v = ka, va
    att = att.reshape(bn, T, ATT_WIDTH)

    L = min(T, CHUNK)
    nc_m = T // L

    def heads_to_chunks(t):
        return t.reshape(bn, nc_m, L, MLSTM_HEADS, MLSTM_HEAD_DIM).transpose(1, 0, 3, 2, 4)

    def gates_to_chunks(t):
        return t.reshape(bn, nc_m, L, MLSTM_HEADS).transpose(1, 0, 3, 2)

    qm = qm.reshape(bn, T, MLSTM_HEADS, MLSTM_HEAD_DIM).astype(f32)
    km = km.reshape(bn, T, MLSTM_HEADS, MLSTM_HEAD_DIM).astype(f32) * (MLSTM_HEAD_DIM ** -0.5)
    vm = vm.reshape(bn, T, MLSTM_HEADS, MLSTM_HEAD_DIM).astype(f32)
    ig = (ig + prm["b_igate"]).astype(f32)
    lf = jax.nn.log_sigmoid((fg + prm["b_fgate"]).astype(f32))
    init = tuple(s.astype(f32) for s in m_state)
    (c_new, n_new, m_new), hm = lax.scan(
        mlstm_chunk, init,
        (heads_to_chunks(qm), heads_to_chunks(km), heads_to_chunks(vm),
         gates_to_chunks(ig), gates_to_chunks(lf)))
    hm = hm.transpose(1, 0, 3, 2, 4).reshape(bn, T, MLSTM_HEADS, MLSTM_HEAD_DIM)
    hm = hm * lax.rsqrt(jnp.mean(hm * hm, axis=-1, keepdims=True) + EPS)
    hm = (hm.reshape(bn, T, MLSTM_WIDTH) * prm["mlstm_norm"].astype(f32)).astype(x.dtype)
    hm = hm * jax.nn.sigmoid(om)

    mix = jnp.concatenate([att.astype(x.dtype), hm], axis=-1) @ prm["w_out"]
    x = x + gt1 * rms_norm(mix, prm["norm_post_mix"])

    h2 = rms_norm(x, prm["norm_pre_ffn"]) * (1 + sc2) + sh2
    ug = h2 @ prm["w_ffn_gate"]
    uv = h2 @ prm["w_ffn_up"]
    if conv_prev is None:
        conv_prev = jnp.zeros((bn, FFN_CONV - 1, D_FF), dtype=ug.dtype)
    up = jnp.concatenate([conv_prev.astype(ug.dtype), ug], axis=1)
    cw = prm["ffn_conv_w"]
    conv = sum(cw[j] * up[:, j:j + T] for j in range(FFN_CONV)) + prm["ffn_conv_b"]
    a = jax.nn.gelu(conv, approximate=False) * uv
    x = x + gt2 * rms_norm(a @ prm["w_ffn_down"], prm["norm_post_ffn"])
    new_conv = up[:, T:]
    return x, (new_k, new_v, c_new, n_new, m_new, new_conv)


def setup_inputs(seed: int = 0) -> dict:
    key = jax.random.key(seed)
    ks = jax.random.split(key, 32)
    f32 = jnp.float32
    nrm = lambda k, shape, s: jax.random.normal(k, shape, f32) * s
    rows = min(ATT_REACH, PAST_LEN)
    return {
        "x_prompt": nrm(ks[0], (BATCH, SEQ, D_MODEL), 1.0),
        "x_sample": nrm(ks[1], (DEC_BATCH, DEC_SEQ, D_MODEL), 1.0),
        "c_prompt": nrm(ks[2], (BATCH, D_MODEL), 1.0),
        "c_sample": nrm(ks[3], (DEC_BATCH, D_MODEL), 1.0),
        "cache_att_k": nrm(ks[4], (DEPTH, DEC_BATCH, rows, ATT_HEADS, ATT_HEAD_DIM), 1.0),
        "cache_att_v": nrm(ks[5], (DEPTH, DEC_BATCH, rows, ATT_HEADS, ATT_HEAD_DIM), 1.0),
        "state_mlstm_C": nrm(ks[6], (DEPTH, DEC_BATCH, MLSTM_HEADS, MLSTM_HEAD_DIM, MLSTM_HEAD_DIM), 0.05),
        "state_mlstm_n": nrm(ks[7], (DEPTH, DEC_BATCH, MLSTM_HEADS, MLSTM_HEAD_DIM), 0.1),
        "state_mlstm_m": nrm(ks[8], (DEPTH, DEC_BATCH, MLSTM_HEADS), 0.5),
        "state_ffn_conv": nrm(ks[9], (DEPTH, DEC_BATCH, FFN_CONV - 1, D_FF), 1.0),
        "ada_w": nrm(ks[10], (DEPTH, D_MODEL, 6 * D_MODEL), 0.5 * D_MODEL ** -0.5),
        "ada_b": nrm(ks[11], (DEPTH, 6 * D_MODEL), 0.02),
        "norm_pre_mix": 1.0 + nrm(ks[12], (DEPTH, D_MODEL), 0.02),
        "norm_post_mix": 1.0 + nrm(ks[13], (DEPTH, D_MODEL), 0.02),
        "norm_pre_ffn": 1.0 + nrm(ks[14], (DEPTH, D_MODEL), 0.02),
        "norm_post_ffn": 1.0 + nrm(ks[15], (DEPTH, D_MODEL), 0.02),
        "w_in": nrm(ks[16], (DEPTH, D_MODEL, IN_COLS), D_MODEL ** -0.5),
        "b_igate": nrm(ks[17], (DEPTH, MLSTM_HEADS), 0.1),
        "b_fgate": jnp.linspace(3.0, 6.0, MLSTM_HEADS, dtype=f32)[None, :] + nrm(ks[18], (DEPTH, MLSTM_HEADS), 0.1),
        "att_rel_bias": nrm(ks[19], (DEPTH, ATT_HEADS, N_REL), 0.5),
        "mlstm_norm": 1.0 + nrm(ks[20], (DEPTH, MLSTM_WIDTH), 0.02),
        "w_out": nrm(ks[21], (DEPTH, MIX_WIDTH, D_MODEL), MIX_WIDTH ** -0.5),
        "w_ffn_gate": nrm(ks[22], (DEPTH, D_MODEL, D_FF), D_MODEL ** -0.5),
        "w_ffn_up": nrm(ks[23], (DEPTH, D_MODEL, D_FF), D_MODEL ** -0.5),
        "ffn_conv_w": nrm(ks[24], (DEPTH, FFN_CONV, D_FF), FFN_CONV ** -0.5),
        "ffn_conv_b": nrm(ks[25], (DEPTH, D_FF), 0.02),
        "w_ffn_down": nrm(ks[26], (DEPTH, D_FF, D_MODEL), D_FF ** -0.5),
    }


def reference(x_prompt, x_sample, c_prompt, c_sample,
              cache_att_k, cache_att_v, state_mlstm_C, state_mlstm_n, state_mlstm_m, state_ffn_conv,
              ada_w, ada_b, norm_pre_mix, norm_post_mix, norm_pre_ffn, norm_post_ffn,
              w_in, b_igate, b_fgate, att_rel_bias, mlstm_norm, w_out,
              w_ffn_gate, w_ffn_up, ffn_conv_w, ffn_conv_b, w_ffn_down):
    f32 = jnp.float32
    xp, xs = x_prompt, x_sample
    bp = x_prompt.shape[0]
    p_states, s_states = [], []
    for l in range(DEPTH):
        prm = {
            "ada_w": ada_w[l], "ada_b": ada_b[l],
            "norm_pre_mix": norm_pre_mix[l], "norm_post_mix": norm_post_mix[l],
            "norm_pre_ffn": norm_pre_ffn[l], "norm_post_ffn": norm_post_ffn[l],
            "w_in": w_in[l], "b_igate": b_igate[l], "b_fgate": b_fgate[l],
            "att_rel_bias": att_rel_bias[l], "mlstm_norm": mlstm_norm[l], "w_out": w_out[l],
            "w_ffn_gate": w_ffn_gate[l], "w_ffn_up": w_ffn_up[l],
            "ffn_conv_w": ffn_conv_w[l], "ffn_conv_b": ffn_conv_b[l], "w_ffn_down": w_ffn_down[l],
        }
        zero_state = (jnp.zeros((bp, MLSTM_HEADS, MLSTM_HEAD_DIM, MLSTM_HEAD_DIM), f32),
                      jnp.zeros((bp, MLSTM_HEADS, MLSTM_HEAD_DIM), f32),
                      jnp.zeros((bp, MLSTM_HEADS), f32))
        xp, st_p = hybrid_layer(xp, c_prompt, prm, None, zero_state, None)
        xs, st_s = hybrid_layer(xs, c_sample, prm, (cache_att_k[l], cache_att_v[l]),
                                (state_mlstm_C[l], state_mlstm_n[l], state_mlstm_m[l]),
                                state_ffn_conv[l])
        p_states.append(st_p)
        s_states.append(st_s)
    p_k, p_v, p_C, p_n, p_m, p_conv = [jnp.stack(t, axis=0) for t in zip(*p_states)]
    s_k, s_v, s_C, s_n, s_m, s_conv = [jnp.stack(t, axis=0) for t in zip(*s_states)]
    return (xp, xs, p_k, p_v, p_C, p_n, p_m, p_conv, s_k, s_v, s_C, s_n, s_m, s_conv)
```

```python
import contextlib
import numpy as np
import concourse.bass as bass
import concourse.mybir as mybir
from concourse.bass_utils import run_bass_kernel_spmd

F32 = mybir.dt.float32
BF16 = mybir.dt.bfloat16
AF = mybir.ActivationFunctionType
ALU = mybir.AluOpType
AX = mybir.AxisListType

D = 2048
KT = 16
DFF = 5632
FT = 44
INC = 7176
EPS = 1e-6
NEG = -30000.0


class Prog:
    ENG = ("pe", "act", "dve", "pool", "sp")

    def __init__(self, nc, n_dma=24):
        self.nc = nc
        self.q = {e: [] for e in self.ENG}
        self.cnt = {e: 0 for e in self.ENG}
        self.waited = {}
        self.lastw = {}
        self.readers = {}
        self.n_dma = n_dma
        self.dma_val = {}
        self.dma_rr = {e: 0 for e in self.ENG}
        self.semh = {}

    def _deps(self, eng, reads, writes):
        deps = []
        for k in reads:
            deps += self.lastw.get(k, [])
        for k in writes:
            deps += self.lastw.get(k, [])
            deps += self.readers.get(k, [])
        waits = []
        for semkey, val in deps:
            if eng == "pe" and semkey == ("e", "pe"):
                continue
            if self.waited.get((eng, semkey), 0) >= val:
                continue
            self.waited[(eng, semkey)] = val
            waits.append((semkey, val))
        return waits

    def _record(self, tok, reads, writes):
        for k in writes:
            self.lastw[k] = [tok]
            self.readers[k] = []
        for k in reads:
            self.readers.setdefault(k, []).append(tok)

    def op(self, eng, fn, reads=(), writes=(), signal=True):
        ex = [k for k in reads if k.startswith("ps")]
        if ex:
            reads = [k for k in reads if not k.startswith("ps")]
            writes = list(writes) + ex
        waits = self._deps(eng, reads, writes)
        if signal:
            self.cnt[eng] += 1
            tok = (("e", eng), self.cnt[eng])
        else:
            tok = (("e", eng), self.cnt[eng] + 1)
        self.q[eng].append((waits, fn, signal))
        self._record(tok, reads, writes)
        return tok

    def dma(self, qeng, out, in_, reads=(), writes=()):
        waits = self._deps(qeng, reads, writes)
        idx = self.dma_rr[qeng] % self.n_dma
        self.dma_rr[qeng] += 1
        semkey = ("d", qeng, idx)
        v = self.dma_val.get(semkey, 0)
        if v > 0 and self.waited.get((qeng, semkey), 0) < v:
            self.waited[(qeng, semkey)] = v
            waits.append((semkey, v))
        self.dma_val[semkey] = v + 16
        tok = (semkey, v + 16)

        def fn(e, out=out, in_=in_, semkey=semkey):
            return e.dma_start(out=out, in_=in_).then_inc(self.semh[semkey], 16)

        self.q[qeng].append((waits, fn, False))
        self._record(tok, reads, writes)
        return tok

    def barrier(self):
        for eng in self.ENG:
            waits = []
            for semkey, v in self.dma_val.items():
                if self.waited.get((eng, semkey), 0) < v:
                    self.waited[(eng, semkey)] = v
                    waits.append((semkey, v))
            for e in self.ENG:
                if e == eng or self.cnt[e] == 0:
                    continue
                semkey = ("e", e)
                if self.waited.get((eng, semkey), 0) < self.cnt[e]:
                    self.waited[(eng, semkey)] = self.cnt[e]
                    waits.append((semkey, self.cnt[e]))
            if waits:
                self.q[eng].append((waits, None, False))
        self.lastw = {}
        self.readers = {}

    def run(self):
        nc = self.nc
        with contextlib.ExitStack() as st:
            for e in self.ENG:
                self.semh[("e", e)] = st.enter_context(nc.semaphore("s_" + e))
            for qe in self.ENG:
                for i in range(min(self.dma_rr[qe], self.n_dma)):
                    self.semh[("d", qe, i)] = st.enter_context(nc.semaphore("d_%s_%d" % (qe, i)))
            block = st.enter_context(nc.Block())

            def runq(ename, e):
                for waits, fn, signal in self.q[ename]:
                    for semkey, val in waits:
                        e.wait_ge(self.semh[semkey], val)
                    if fn is None:
                        continue
                    ins = fn(e)
                    if signal:
                        ins.then_inc(self.semh[("e", ename)], 1)

            @block.tensor
            def _(e):
                runq("pe", e)

            @block.scalar
            def _(e):
                runq("act", e)

            @block.vector
            def _(e):
                runq("dve", e)

            @block.gpsimd
            def _(e):
                runq("pool", e)

            @block.sync
            def _(e):
                runq("sp", e)


def MM(out, lhsT, rhs, start=True, stop=True):
    return lambda e: e.matmul(out, lhsT=lhsT, rhs=rhs, start=start, stop=stop)


def TR(out, in_, ident):
    return lambda e: e.transpose(out=out, in_=in_, identity=ident)


def ACT(out, in_, func, **kw):
    return lambda e: e.activation(out=out, in_=in_, func=func, **kw)


def TS(out, in0, s1, s2=None, op0=ALU.mult, op1=None):
    if op1 is None:
        return lambda e: e.tensor_scalar(out=out, in0=in0, scalar1=s1, scalar2=None, op0=op0)
    return lambda e: e.tensor_scalar(out=out, in0=in0, scalar1=s1, scalar2=s2, op0=op0, op1=op1)


def TT(out, in0, in1, op):
    return lambda e: e.tensor_tensor(out=out, in0=in0, in1=in1, op=op)


def STT(out, in0, scalar, in1, op0, op1):
    return lambda e: e.scalar_tensor_tensor(out=out, in0=in0, scalar=scalar, in1=in1, op0=op0, op1=op1)


def CP(out, in_):
    return lambda e: e.tensor_copy(out=out, in_=in_)


def MSET(ap, v):
    return lambda e: e.memset(ap, v)


class Arena:
    def __init__(self, nc, nbytes):
        self.t = nc.alloc_sbuf_tensor("arena", [128, nbytes // 4], F32)
        self.nbytes = nbytes
        self.off = 0

    def mark(self):
        return self.off

    def reset(self, m):
        self.off = m

    def alloc(self, free_shape, dtype):
        n = int(np.prod(free_shape))
        sz = 4 if dtype == F32 else 2
        nb = (n * sz + 63) // 64 * 64
        assert self.off + nb <= self.nbytes, ("SBUF arena overflow", self.off, nb, self.nbytes)
        ap = self.t[:, self.off // 4:(self.off + nb) // 4]
        self.off += nb
        if dtype != F32:
            ap = ap.bitcast(dtype)
        ap = ap[:, 0:n]
        if len(free_shape) == 2:
            ap = ap.rearrange("p (a b) -> p a b", a=free_shape[0])
        elif len(free_shape) == 3:
            ap = ap.rearrange("p (a b c) -> p a b c", a=free_shape[0], b=free_shape[1])
        elif len(free_shape) == 4:
            ap = ap.rearrange("p (a b c d) -> p a b c d", a=free_shape[0], b=free_shape[1], c=free_shape[2])
        return ap


def build(SEQ, last_stage=6, debug=False):
    OWN = SEQ // 4
    NOWN = OWN // 128
    NPRE = 3 * NOWN - 1
    NM = NOWN + 2
    NK = NM + 4
    TM = NM * 128
    TK = NK * 128
    NOUT = NOWN + 1
    assert NOWN >= 4 or NOWN == 2

    nc = bass.Bass("TRN2", target_bir_lowering=False)
    P = Prog(nc)

    def din(name, shape, dt=F32):
        return nc.dram_tensor(name, list(shape), dt, kind="ExternalInput").ap()

    def dout(name, shape, dt=F32):
        return nc.dram_tensor(name, list(shape), dt, kind="ExternalOutput").ap()

    def dscr(name, shape, dt):
        if debug:
            return nc.dram_tensor(name, list(shape), dt, kind="ExternalOutput").ap()
        return nc.dram_tensor(name, list(shape), dt).ap()

    x_main = din("x_main", [TM, D])
    x_pre = din("x_pre", [NPRE * 128, D])
    masks = din("masks", [128, 3, NPRE + NM])
    c3T = din("c3T", [128, KT, 3])
    gpreT = din("gpreT", [128, 2, KT])
    adabT = din("adabT", [128, 96])
    adab3 = din("adab3", [3, 12288])
    gpost3 = din("gpost3", [3, 2, D])
    mnormB = din("mnormB", [128, 1024])
    bgate = din("bgate", [128, 8])
    cwT = din("cwT", [128, FT, 4])
    bias_tab = din("bias_tab", [128, 8, 5, 128])
    bias_s4 = din("bias_s4", [128, 8, 128])
    consts = din("consts", [128, 8, 128])
    cache_k = din("cache_k", [2, 512, 1024])
    cache_v = din("cache_v", [2, 512, 1024])
    st_C = din("st_C", [2, 4, 256, 256])
    st_n = din("st_n", [2, 4, 256])
    st_mB = din("st_mB", [128, 2, 4])
    st_m4 = din("st_m4", [4, 2])
    conv_st = din("conv_st", [4, 5632])
    ada_w = din("ada_w", [D, 12288])
    w_in = din("w_in", [D, INC])
    w_out = din("w_out", [D, D])
    w_g = din("w_g", [D, DFF])
    w_u = din("w_u", [D, DFF])
    w_d = din("w_d", [DFF, D])

    y_main = dout("y_main", [NOUT * 128, D])
    k_out = dout("k_out", [5 * 128, 1024])
    v_out = dout("v_out", [5 * 128, 1024])
    C_out = dout("C_out", [3, 4, 256, 256])
    n_out = dout("n_out", [3, 4, 256])
    m_out = dout("m_out", [4, 3])
    conv_out = dout("conv_out", [6, 5632])

    scr_fm = dscr("scr_fm", [32, 128, TK], BF16)
    scr_va = dscr("scr_va", [NK, 128, 8 * 129], BF16)
    scr_tm = dscr("scr_tm", [NM, 128, 3072], BF16)
    scr_g = dscr("scr_g", [NM, 128, 8], F32)
    scr_gtg = nc.dram_tensor("scr_gtg", [3, 2, D], F32).ap()
    scr_mixT = dscr("scr_mixT", [NM, 128, KT * 128], BF16)
    scr_x1 = dscr("scr_x1", [NM, 128, D], F32)
    scr_h2T = dscr("scr_h2T", [128, KT, TM], BF16)
    scr_aT = dscr("scr_aT", [FT, 128, TM], BF16)
    dbg = dscr("dbg", [128, 4096], F32) if debug else None

    A = Arena(nc, 206 * 1024)
    ps = [nc.alloc_psum_tensor("ps%d" % b, [128, 512], F32) for b in range(8)]

    def psf(b):
        return ps[b][:]

    def psb(b):
        return ps[b][:].bitcast(BF16)

    PK = ["ps%d" % b for b in range(8)]

    cst = A.alloc([8, 128], F32)
    identf = cst[:, 0, :]
    tri = cst[:, 1, :]
    ind = cst[:, 2:4, :]
    sel = cst[0:3, 4:6, :]
    ind2 = cst[:, 6, 0:2]
    ones4 = cst[0:4, 7, :]
    identb = A.alloc([128], BF16)
    msk = A.alloc([3, NPRE + NM], F32)
    modT = A.alloc([4, KT, 3], F32)
    bgt = A.alloc([8], F32)
    epsc = A.alloc([1], F32)
    onec = A.alloc([1], F32)
    P.dma("sp", cst, consts, writes=["cst"])
    P.dma("sp", msk, masks, writes=["msk"])
    P.dma("sp", bgt, bgate, writes=["bgt"])
    P.op("dve", MSET(epsc, EPS), writes=["epsc"])
    P.op("dve", MSET(onec, 1.0), writes=["onec"])
    P.op("dve", CP(identb, identf), reads=["cst"], writes=["identb"])
    persist_mark = A.mark()

    siluT = A.alloc([KT, 3], BF16)
    gtg = A.alloc([2, D], F32)
    c3s = A.alloc([KT, 3], F32)
    gpre = A.alloc([2, KT], F32)
    adab = A.alloc([96], F32)
    adab3s = A.alloc([2, D], F32)
    gpost = A.alloc([2, D], F32)
    ablk = [A.alloc([KT, 512], BF16) for _ in range(3)]
    P.dma("sp", c3s, c3T, writes=["c3s"])
    P.dma("sp", gpre, gpreT, writes=["gpre"])
    P.dma("sp", adab, adabT, writes=["adab"])
    P.dma("sp", adab3s[0:3, 0, :], adab3[:, 2 * D:3 * D], writes=["adab3s0"])
    P.dma("sp", adab3s[0:3, 1, :], adab3[:, 5 * D:6 * D], writes=["adab3s1"])
    P.dma("sp", gpost[0:3], gpost3, writes=["gpost"])
    P.op("act", ACT(siluT, c3s, AF.Silu), reads=["c3s"], writes=["siluT"])

    nblk = 0
    for v, slot in ((0, 0), (1, 1), (3, 2), (4, 3)):
        for cb4 in range(4):
            bi = nblk % 3
            nblk += 1
            c0 = v * D + cb4 * 512
            P.dma("pool", ablk[bi], ada_w[:, c0:c0 + 512].rearrange("(k p) n -> p k n", p=128), writes=["ablk%d" % bi])
            for sub in range(4):
                kc = cb4 * 4 + sub
                b = kc % 4
                for k in range(KT):
                    P.op("pe", MM(psf(b)[:, 0:3], ablk[bi][:, k, sub * 128:(sub + 1) * 128], siluT[:, k, :], k == 0, k == KT - 1),
                         reads=["ablk%d" % bi, "siluT"], writes=[PK[b]], signal=(k == KT - 1))
                P.op("dve", TS(modT[:, slot, kc, :], psf(b)[:, 0:3], adab[:, v * 16 + kc:v * 16 + kc + 1], None, ALU.add),
                     reads=[PK[b], "adab"], writes=["modT"])
    for slot, gi in ((1, 0), (3, 1)):
        P.op("dve", STT(modT[:, slot], modT[:, slot], 1.0, gpre[:, gi, :].unsqueeze(2).broadcast_to([128, KT, 3]), ALU.add, ALU.mult),
             reads=["modT", "gpre"], writes=["modT"])
    for gi, v in ((0, 2), (1, 5)):
        for cb4 in range(4):
            bi = nblk % 3
            nblk += 1
            c0 = v * D + cb4 * 512
            P.dma("pool", ablk[bi], ada_w[:, c0:c0 + 512].rearrange("(k p) n -> p k n", p=128), writes=["ablk%d" % bi])
            b = 4 + cb4 % 4
            for k in range(KT):
                P.op("pe", MM(psf(b)[0:3, :], siluT[:, k, :], ablk[bi][:, k, :], k == 0, k == KT - 1),
                     reads=["ablk%d" % bi, "siluT"], writes=[PK[b]], signal=(k == KT - 1))
            P.op("dve", TT(gtg[0:3, gi, cb4 * 512:(cb4 + 1) * 512], psf(b)[0:3, :], adab3s[0:3, gi, cb4 * 512:(cb4 + 1) * 512], ALU.add),
                 reads=[PK[b], "adab3s%d" % gi], writes=["gtg%d_%d" % (gi, cb4)])
            P.op("dve", TT(gtg[0:3, gi, cb4 * 512:(cb4 + 1) * 512], gtg[0:3, gi, cb4 * 512:(cb4 + 1) * 512],
                           gpost[0:3, gi, cb4 * 512:(cb4 + 1) * 512], ALU.mult),
                 reads=["gpost", "gtg%d_%d" % (gi, cb4)], writes=["gtg%d_%d" % (gi, cb4)])
    GTG_KEYS = [["gtg%d_%d" % (gi, c) for c in range(4)] for gi in range(2)]
    for gi in range(2):
        P.dma("sp", scr_gtg[:, gi, :], gtg[0:3, gi, :], reads=GTG_KEYS[gi], writes=["scr_gtg"])

    def build_gtgB(dst_p, dst_s, gi, keyp, keys_):
        tmp = A.alloc([D], F32)
        P.dma("sp", tmp[0:3, :], scr_gtg[:, gi, :], writes=["gtgtmp"])
        for vi, dst, key in ((0, dst_p, keyp), (1, dst_s, keys_)):
            for cb4 in range(4):
                b = cb4
                P.op("pe", MM(psf(b), sel[:, vi, :], tmp[0:3, cb4 * 512:(cb4 + 1) * 512]),
                     reads=["cst", "gtgtmp"], writes=[PK[b]])
                P.op("act", ACT(dst[:, cb4 * 512:(cb4 + 1) * 512], psf(b), AF.Copy), reads=[PK[b]], writes=[key])

    P.barrier()
    A.reset(persist_mark)
    if last_stage == 0:
        if debug:
            P.dma("sp", dbg[:, 0:192], modT.rearrange("p a b c -> p (a b c)"), reads=["modT"], writes=["dbg"])
            P.dma("sp", dbg[0:3, 192:192 + 4096 - 192], gtg[0:3].rearrange("p a b -> p (a b)")[:, 0:4096 - 192], reads=sum(GTG_KEYS, []), writes=["dbg2"])
        P.barrier()
        P.run()
        return nc

    def rstd_from_ss(ss_ap, out_ap, n, keys_in, key_out):
        P.op("act", ACT(out_ap, ss_ap, AF.Ln, scale=1.0 / n, bias=epsc[:, 0:1]), reads=keys_in + ["epsc"], writes=[key_out])
        P.op("act", ACT(out_ap, out_ap, AF.Exp, scale=-0.5), reads=[key_out], writes=[key_out])

    fe = {}

    def fe_alloc(with_xs=True, nxs=2):
        if with_xs:
            fe["xs"] = [A.alloc([D], F32) for _ in range(nxs)]
        fe["xn"] = [A.alloc([D], BF16) for _ in range(2)]
        fe["junk"] = A.alloc([D], BF16)
        fe["ss"] = [A.alloc([2], F32) for _ in range(2)]
        fe["n"] = 0

    def front_end(x_src, dst, dst_key, slot_a, slot_sh, rows, tb=(0, 1), xs_in=None, xs_keys=None):
        i = fe["n"] % 2
        fe["n"] += 1
        xn, ss = fe["xn"][i], fe["ss"][i]
        kx, kn, ks = "fe_xs%d" % i, "fe_xn%d" % i, "fe_ss%d" % i
        if xs_in is None:
            xs = fe["xs"][i % len(fe["xs"])]
            kx = "fe_xs%d" % (i % len(fe["xs"]))
            P.dma("sp", xs, x_src, writes=[kx])
            kxl = [kx]
        else:
            xs, kxl = xs_in, list(xs_keys)
        P.op("act", ACT(fe["junk"], xs, AF.Square, accum_out=ss[:, 0:1]), reads=kxl, writes=["fe_junk", ks])
        rstd_from_ss(ss[:, 0:1], ss[:, 1:2], D, [ks], ks)
        P.op("dve", TS(xn, xs, ss[:, 1:2]), reads=kxl + [ks], writes=[kn])
        for half in range(2):
            b = tb[half]
            pt = psb(b).rearrange("p (k t) -> p k t", k=8)
            for k8 in range(8):
                k = half * 8 + k8
                P.op("pe", TR(pt[:, k8, :], xn[:, k * 128:(k + 1) * 128], identb), reads=[kn, "identb"], writes=[PK[b]],
                     signal=(k8 == 7))
            for k8 in range(8):
                k = half * 8 + k8
                eng = "act" if half == 0 else "dve"
                if rows[0] == rows[1]:
                    segs = [(0, 128, rows[0])]
                else:
                    segs = [(0, 64, rows[0]), (64, 128, rows[1])]
                for (t0, t1, r) in segs:
                    a_ap = modT[:, slot_a, k, r:r + 1]
                    s_ap = modT[:, slot_sh, k, r:r + 1]
                    if eng == "act":
                        P.op("act", ACT(dst[:, k, t0:t1], pt[:, k8, t0:t1], AF.Identity, scale=a_ap, bias=s_ap),
                             reads=[PK[b], "modT"], writes=["%s_k%d_%d" % (dst_key, k, t0)])
                    else:
                        P.op("dve", TS(dst[:, k, t0:t1], pt[:, k8, t0:t1], a_ap, s_ap, ALU.mult, ALU.add),
                             reads=[PK[b], "modT"], writes=["%s_k%d_%d" % (dst_key, k, t0)])

    gm = {}

    def gate_alloc():
        gm["gsb"] = [A.alloc([8], F32) for _ in range(2)]
        gm["lfm"] = [A.alloc([4], F32) for _ in range(2)]
        gm["pre"] = [A.alloc([16], F32) for _ in range(2)]
        gm["ex"] = [A.alloc([16], F32) for _ in range(2)]
        gm["gbf"] = [A.alloc([4], BF16) for _ in range(2)]
        gm["amax"] = [A.alloc([2], F32) for _ in range(2)]
        gm["B4"] = [A.alloc([2], F32) for _ in range(2)]
        gm["t4"] = A.alloc([2], F32)
        gm["n"] = 0

    def gate_math(g_src, g_keys, mcol, gb, gb2):
        i = gm["n"] % 2
        gm["n"] += 1
        gsb, lfm, pre, ex, gbf, amax, B4 = (gm[k][i] for k in ("gsb", "lfm", "pre", "ex", "gbf", "amax", "B4"))
        K = "gm%d_" % i
        P.op("dve", TT(gsb, g_src, bgt, ALU.add), reads=g_keys + ["bgt"], writes=[K + "gsb"])
        P.op("act", ACT(lfm, gsb[:, 4:8], AF.Exp, scale=-1.0), reads=[K + "gsb"], writes=[K + "lfm"])
        P.op("act", ACT(lfm, lfm, AF.Ln, bias=onec[:, 0:1]), reads=[K + "lfm", "onec"], writes=[K + "lfm"])
        P.op("dve", TS(lfm, lfm, msk[:, 1, mcol:mcol + 1]), reads=[K + "lfm", "msk"], writes=[K + "lfm"])
        g0 = psf(gb)
        P.op("pe", MM(g0[:, 0:4], tri, lfm), reads=["cst", K + "lfm"], writes=[PK[gb]], signal=False)
        P.op("pe", MM(g0[:, 4:8], ind[:, 0, :], lfm), reads=["cst", K + "lfm"], writes=[PK[gb]], signal=False)
        P.op("pe", MM(g0[:, 8:12], ind[:, 1, :], lfm), reads=["cst", K + "lfm"], writes=[PK[gb]], signal=False)
        P.op("pe", MM(g0[0:4, 12:14], lfm, ind2), reads=["cst", K + "lfm"], writes=[PK[gb]])
        P.op("dve", CP(pre[:, 4:16], g0[:, 0:12]), reads=[PK[gb]], writes=[K + "pre"])
        P.op("dve", CP(B4[0:4, :], g0[0:4, 12:14]), reads=[PK[gb]], writes=[K + "B4"])
        P.op("dve", TT(pre[:, 0:4], gsb[:, 0:4], pre[:, 4:8], ALU.subtract), reads=[K + "gsb", K + "pre"], writes=[K + "pre"])
        P.op("dve", TS(pre[:, 0:4], pre[:, 0:4], msk[:, 0, mcol:mcol + 1], msk[:, 2, mcol:mcol + 1], ALU.mult, ALU.add),
             reads=[K + "pre", "msk"], writes=[K + "pre"])
        P.op("act", ACT(ex, pre, AF.Exp), reads=[K + "pre"], writes=[K + "ex"])
        P.op("dve", CP(gbf, ex[:, 0:4]), reads=[K + "ex"], writes=[K + "gbf"])
        g1 = psf(gb2)
        P.op("pe", TR(g1[0:4, 0:128], pre[:, 0:4], identf), reads=[K + "pre", "cst"], writes=[PK[gb2]])
        P.op("dve", lambda e, o=amax[0:4, :], s=g1[0:4, 0:128].rearrange("p (c t) -> p c t", c=2):
             e.tensor_reduce(out=o, in_=s, axis=AX.X, op=ALU.max), reads=[PK[gb2]], writes=[K + "amax"])
        return dict(ex=ex, gbf=gbf, amax=amax, B4=B4, K=K)

    def m_update(mt, mkey, G, c):
        K = G["K"]
        t4 = gm["t4"]
        P.op("dve", TT(t4[0:4, 0:1], G["amax"][0:4, c:c + 1], G["B4"][0:4, c:c + 1], ALU.add),
             reads=[K + "amax", K + "B4"], writes=["t4"])
        P.op("dve", TT(mt, mt, G["B4"][0:4, c:c + 1], ALU.add), reads=[mkey, K + "B4"], writes=[mkey])
        P.op("dve", TT(mt, mt, t4[0:4, 0:1], ALU.max), reads=[mkey, "t4"], writes=[mkey])

    def state_update(Sf, Sf_key, nf, nf_key, DSb, G, c, h, km_ap, km_key, vp_ap, vp_key, Sb=None, Sb_key=None, nb=None, nb_key=None,
                     S_src=None, S_src_key=None, n_src=None, n_src_key=None):
        K = G["K"]
        if S_src is None:
            S_src, S_src_key, n_src, n_src_key = Sf, Sf_key, nf, nf_key
        r0, r1 = 64 * c, 64 * c + 64
        eB = G["ex"][:, 8 + 4 * c + h:8 + 4 * c + h + 1]
        DS = psf(DSb)
        for d in range(2):
            P.op("pe", MM(DS[:, d * 256:(d + 1) * 256], km_ap[r0:r1, h * 256 + d * 128:h * 256 + (d + 1) * 128], vp_ap[r0:r1, h, :]),
                 reads=[km_key, vp_key], writes=[PK[DSb]], signal=(d == 1))
        Sh = Sf[:, h].rearrange("p d e -> p (d e)")
        Ssh = S_src[:, h].rearrange("p d e -> p (d e)")
        P.op("act", ACT(Sh, Ssh, AF.Copy, scale=eB), reads=[S_src_key + str(h), K + "ex"], writes=[Sf_key + str(h)])
        P.op("dve", STT(Sh, DS, eB, Sh, ALU.mult, ALU.add), reads=[PK[DSb], K + "ex", Sf_key + str(h)], writes=[Sf_key + str(h)])
        if Sb is not None:
            P.op("pool", CP(Sb[:, h].rearrange("p d e -> p (d e)"), Sh), reads=[Sf_key + str(h)], writes=[Sb_key + str(h)])

    def n_update(nf, nf_key, NBb, G, c, km_ap, km_key, nb=None, nb_key=None, n_src=None, n_src_key=None):
        K = G["K"]
        if n_src is None:
            n_src, n_src_key = nf, nf_key
        r0, r1 = 64 * c, 64 * c + 64
        NB = psf(NBb)
        for h in range(4):
            for d in range(2):
                P.op("pe", MM(NB[:, 256 + h * 2 + d:256 + h * 2 + d + 1], km_ap[r0:r1, h * 256 + d * 128:h * 256 + (d + 1) * 128],
                              G["gbf"][r0:r1, h:h + 1]),
                     reads=[km_key, K + "gbf"], writes=[PK[NBb]], signal=(h == 3 and d == 1))
        eB4 = G["ex"][:, 8 + 4 * c:12 + 4 * c].unsqueeze(2).broadcast_to([128, 4, 2])
        P.op("dve", TT(nf, n_src, NB[:, 256:264].rearrange("p (h d) -> p h d", h=4), ALU.add),
             reads=[n_src_key, PK[NBb]], writes=[nf_key])
        P.op("dve", TT(nf, nf, eB4, ALU.mult), reads=[nf_key, K + "ex"], writes=[nf_key])
        if nb is not None:
            P.op("dve", CP(nb, nf), reads=[nf_key], writes=[nb_key])

    base_mark = A.mark()
    SX = A.alloc([4, 2, 256], F32)
    nX = A.alloc([4, 2], F32)
    mP = A.alloc([2], F32)
    P.op("pool", MSET(SX, 0.0), writes=["SX%d" % h for h in range(4)])
    P.op("pool", MSET(nX, 0.0), writes=["nX"])
    P.op("pool", MSET(mP, 0.0), writes=["mP"])
    persist_mark = A.mark()

    hT_halo = A.alloc([KT, 512], BF16)
    s1_mark = A.mark()
    wpre = A.alloc([KT, 2056], BF16)
    P.dma("pool", wpre[:, :, 0:1024], w_in[:, 4096:5120].rearrange("(k p) n -> p k n", p=128), writes=["wpre_km"])
    P.dma("pool", wpre[:, :, 1024:2048], w_in[:, 5120:6144].rearrange("(k p) n -> p k n", p=128), writes=["wpre_vm"])
    P.dma("pool", wpre[:, :, 2048:2056], w_in[:, 7168:7176].rearrange("(k p) n -> p k n", p=128), writes=["wpre_g"])
    fe_alloc()
    gate_alloc()
    hT_pre = [A.alloc([KT, 128], BF16) for _ in range(2)]
    km_sb = [A.alloc([1024], BF16) for _ in range(2)]
    vp_sb = [A.alloc([4, 256], BF16) for _ in range(2)]
    for t in range(NPRE):
        i2 = t % 2
        if t >= NPRE - 4:
            u = t - (NPRE - 4)
            dst, dkey = hT_halo[:, :, u * 128:(u + 1) * 128], "hT_u%d" % u
        else:
            dst, dkey = hT_pre[i2], "hT_pre%d" % i2
        front_end(x_pre[t * 128:(t + 1) * 128, :], dst, dkey, 1, 0, (0, 0), tb=(0, 1))
        for cb, b in ((0, 2), (1, 3), (2, 4), (3, 5)):
            for k in range(KT):
                P.op("pe", MM(psf(b), dst[:, k, :], wpre[:, k, cb * 512:(cb + 1) * 512], k == 0, k == KT - 1),
                     reads=[dkey + "_k%d_0" % k, "wpre_km" if cb < 2 else "wpre_vm"], writes=[PK[b]], signal=(k == KT - 1))
        for k in range(KT):
            P.op("pe", MM(psf(6)[:, 32:40], dst[:, k, :], wpre[:, k, 2048:2056], k == 0, k == KT - 1),
                 reads=[dkey + "_k%d_0" % k, "wpre_g"], writes=[PK[6]], signal=(k == KT - 1))
        G = gate_math(psf(6)[:, 32:40], [PK[6]], t, 6, 7)
        K = G["K"]
        for cb, b in ((0, 2), (1, 3)):
            P.op("act", ACT(km_sb[i2][:, cb * 512:(cb + 1) * 512], psf(b), AF.Copy, scale=0.0625), reads=[PK[b]], writes=["km_sb%d" % i2])
        for h in range(4):
            b = 4 + h // 2
            P.op("dve", TS(vp_sb[i2][:, h, :], psf(b)[:, (h % 2) * 256:(h % 2 + 1) * 256], G["ex"][:, h:h + 1]),
                 reads=[PK[b], K + "ex"], writes=["vp_sb%d" % i2])
        for c in range(2):
            n_update(nX, "nX", 7, G, c, km_sb[i2], "km_sb%d" % i2)
            for h in range(4):
                state_update(SX, "SX", nX, "nX", 0 + (h % 2), G, c, h, km_sb[i2], "km_sb%d" % i2, vp_sb[i2], "vp_sb%d" % i2)
            m_update(mP[0:4, 0:1], "mP", G, c)
    P.barrier()
    A.reset(s1_mark)
    if last_stage == 1:
        if debug:
            P.dma("sp", dbg[:, 0:2048], SX.rearrange("p a b c -> p (a b c)"), reads=["SX%d" % h for h in range(4)], writes=["dbg"])
            P.dma("sp", dbg[:, 2048:2056], nX.rearrange("p a b -> p (a b)"), reads=["nX"], writes=["dbg1"])
            P.dma("sp", dbg[0:4, 2056:2058], mP[0:4, 0:2], reads=["mP"], writes=["dbg2"])
        P.barrier()
        P.run()
        return nc


    hT_main = A.alloc([KT, TM], BF16)

    def hTt(u):
        return hT_halo[:, :, u * 128:(u + 1) * 128] if u < 4 else hT_main[:, :, (u - 4) * 128:(u - 3) * 128]

    def hTr(k, t0, t1):
        return hT_halo[:, k, t0:t1] if t1 <= 512 else hT_main[:, k, t0 - 512:t1 - 512]

    fe_alloc(nxs=1)
    wblk = [A.alloc([KT, 512], BF16) for _ in range(3)]
    wg8 = A.alloc([KT, 8], BF16)
    fmst = [A.alloc([TK], BF16) for _ in range(2)]
    tmst = [A.alloc([516], BF16) for _ in range(4)]
    f32st = [A.alloc([512], F32) for _ in range(3)]
    gst = [A.alloc([8], F32) for _ in range(2)]
    cnt = {"wb": 0, "bank": 0, "ev": 0, "fm": 0, "tm": 0, "f32": 0, "g": 0}

    def hkeys(u, k):
        if u == NK - 1:
            return ["hT_u%d_k%d_0" % (u, k), "hT_u%d_k%d_64" % (u, k)]
        return ["hT_u%d_k%d_0" % (u, k)]

    for i in range(NM):
        u = 4 + i
        rows = (0, 0) if i < NM - 1 else (1, 2)
        front_end(x_main[i * 128:(i + 1) * 128, :], hTt(u), "hT_u%d" % u, 1, 0, rows, tb=(0, 1))

    def load_wblk(col0):
        bi = cnt["wb"] % 3
        cnt["wb"] += 1
        P.dma("pool", wblk[bi], w_in[:, col0:col0 + 512].rearrange("(k p) n -> p k n", p=128), writes=["wblk%d" % bi])
        return wblk[bi], "wblk%d" % bi

    def next_bank():
        b = 2 + cnt["bank"] % 6
        cnt["bank"] += 1
        return b

    def ev_eng():
        cnt["ev"] += 1
        return "act" if cnt["ev"] % 2 == 0 else "dve"

    def evac(out, in_, keys_r, keys_w, scale=None, eng=None):
        eng = eng or ev_eng()
        if eng == "act":
            if scale is None:
                P.op("act", ACT(out, in_, AF.Copy), reads=keys_r, writes=keys_w)
            else:
                P.op("act", ACT(out, in_, AF.Copy, scale=scale), reads=keys_r, writes=keys_w)
        else:
            if scale is None:
                P.op("dve", CP(out, in_), reads=keys_r, writes=keys_w)
            else:
                P.op("dve", TS(out, in_, scale), reads=keys_r, writes=keys_w)

    fm_blocks = [(0, 128 ** -0.5, 4), (512, 128 ** -0.5, 4), (1024, None, 0), (1536, None, 0),
                 (3072, None, 4), (3584, None, 4), (4096, 0.0625, 4), (4608, 0.0625, 4)]
    for blk, (col0, scale, u_lo) in enumerate(fm_blocks):
        wb, wkey = load_wblk(col0)
        t_lo = u_lo * 128
        for sub in range(4):
            cb = blk * 4 + sub
            fi = cnt["fm"] % 2
            cnt["fm"] += 1
            stkeys = []
            for tbi, t0 in enumerate(range(t_lo, TK, 512)):
                t1 = min(t0 + 512, TK)
                b = next_bank()
                for k in range(KT):
                    rk = [wkey]
                    for u in range(t0 // 128, t1 // 128):
                        rk += hkeys(u, k)
                    P.op("pe", MM(psf(b)[:, 0:t1 - t0], wb[:, k, sub * 128:(sub + 1) * 128], hTr(k, t0, t1), k == 0, k == KT - 1),
                         reads=rk, writes=[PK[b]], signal=(k == KT - 1))
                key = "fmst%d_%d" % (fi, tbi)
                stkeys.append(key)
                evac(fmst[fi][:, t0:t1], psf(b)[:, 0:t1 - t0], [PK[b]], [key], scale)
            P.dma("sp", scr_fm[cb, :, t_lo:TK], fmst[fi][:, t_lo:TK], reads=stkeys, writes=["scr_fm"])

    def out_index(i):
        if NOWN - 3 <= i <= NOWN:
            return i - (NOWN - 3)
        if i == NM - 1:
            return 4
        return None

    def mcol_of(u):
        return NPRE - 4 + u if u < 4 else NPRE + (u - 4)

    for wbi in range(2):
        wb, wkey = load_wblk(2048 + wbi * 512)
        for u in range(NK):
            b = next_bank()
            for k in range(KT):
                P.op("pe", MM(psf(b), hTt(u)[:, k, :], wb[:, k, :], k == 0, k == KT - 1),
                     reads=[wkey] + hkeys(u, k), writes=[PK[b]], signal=(k == KT - 1))
            ti = cnt["tm"] % 4
            cnt["tm"] += 1
            st = tmst[ti].rearrange("p (h e) -> p h e", h=4)
            mc = mcol_of(u)
            P.op("dve", TS(st[:, :, 0:128], psf(b).rearrange("p (h e) -> p h e", h=4), msk[:, 0, mc:mc + 1]),
                 reads=[PK[b], "msk"], writes=["tmst%d" % ti])
            P.op("pool", CP(st[:, :, 128], msk[:, 0, mc:mc + 1].broadcast_to([128, 4])), reads=["msk"], writes=["tmst%dx" % ti])
            P.dma("sp", scr_va[u, :, wbi * 516:(wbi + 1) * 516], tmst[ti], reads=["tmst%d" % ti, "tmst%dx" % ti], writes=["scr_va"])
            oi = out_index(u - 4) if u >= 4 else None
            if oi is not None:
                fi = cnt["f32"] % 3
                cnt["f32"] += 1
                P.op("act", ACT(f32st[fi], psf(b), AF.Copy), reads=[PK[b]], writes=["f32st%d" % fi])
                P.dma("sp", v_out[oi * 128:(oi + 1) * 128, wbi * 512:(wbi + 1) * 512], f32st[fi], reads=["f32st%d" % fi], writes=["v_out"])
    for (col0, off, scale) in ((4096, 0, 0.0625), (5120, 1024, None), (6144, 2048, None)):
        for wbi in range(2):
            wb, wkey = load_wblk(col0 + wbi * 512)
            for i in range(NM):
                u = 4 + i
                b = next_bank()
                for k in range(KT):
                    P.op("pe", MM(psf(b), hTt(u)[:, k, :], wb[:, k, :], k == 0, k == KT - 1),
                         reads=[wkey] + hkeys(u, k), writes=[PK[b]], signal=(k == KT - 1))
                ti = cnt["tm"] % 4
                cnt["tm"] += 1
                evac(tmst[ti][:, 0:512], psf(b), [PK[b]], ["tmst%d" % ti], scale)
                P.dma("sp", scr_tm[i, :, off + wbi * 512:off + (wbi + 1) * 512], tmst[ti][:, 0:512], reads=["tmst%d" % ti], writes=["scr_tm"])
    P.dma("pool", wg8, w_in[:, 7168:7176].rearrange("(k p) n -> p k n", p=128), writes=["wg8"])
    for i in range(NM):
        u = 4 + i
        b = next_bank()
        for k in range(KT):
            P.op("pe", MM(psf(b)[:, 0:8], hTt(u)[:, k, :], wg8[:, k, :], k == 0, k == KT - 1),
                 reads=["wg8"] + hkeys(u, k), writes=[PK[b]], signal=(k == KT - 1))
        gi = cnt["g"] % 2
        cnt["g"] += 1
        P.op("dve", CP(gst[gi], psf(b)[:, 0:8]), reads=[PK[b]], writes=["gst%d" % gi])
        P.dma("sp", scr_g[i], gst[gi], reads=["gst%d" % gi], writes=["scr_g"])
    for wbi in range(2):
        wb, wkey = load_wblk(1024 + wbi * 512)
        for i in range(NM):
            oi = out_index(i)
            if oi is None:
                continue
            u = 4 + i
            b = next_bank()
            for k in range(KT):
                P.op("pe", MM(psf(b), hTt(u)[:, k, :], wb[:, k, :], k == 0, k == KT - 1),
                     reads=[wkey] + hkeys(u, k), writes=[PK[b]], signal=(k == KT - 1))
            fi = cnt["f32"] % 3
            cnt["f32"] += 1
            P.op("act", ACT(f32st[fi], psf(b), AF.Copy), reads=[PK[b]], writes=["f32st%d" % fi])
            P.dma("sp", k_out[oi * 128:(oi + 1) * 128, wbi * 512:(wbi + 1) * 512], f32st[fi], reads=["f32st%d" % fi], writes=["k_out"])
    P.barrier()
    A.reset(persist_mark)
    if last_stage == 2:
        P.run()
        return nc

    gate_alloc()
    RS = 6
    mnb = A.alloc([1024], F32)
    btab = A.alloc([8, 5, 128], BF16)
    bs4 = A.alloc([8, 128], BF16)
    P.dma("sp", mnb, mnormB, writes=["mnb"])
    P.dma("pool", btab, bias_tab, writes=["btab"])
    P.dma("pool", bs4, bias_s4, writes=["bs4"])
    kring = [A.alloc([8, 128], BF16) for _ in range(RS)]
    vring = [A.alloc([8, 129], BF16) for _ in range(RS)]
    qbuf = [A.alloc([8, 128], BF16) for _ in range(2)]
    mbuf = [A.alloc([16, 128], BF16) for _ in range(2)]
    tbuf = [A.alloc([3072], BF16) for _ in range(2)]
    gbuf = [A.alloc([8], F32) for _ in range(2)]
    expT = [A.alloc([5, 128], BF16) for _ in range(2)]
    es0 = [A.alloc([4, 128], BF16) for _ in range(2)]
    es1 = [A.alloc([4, 128], BF16) for _ in range(2)]
    vp = [A.alloc([4, 256], BF16) for _ in range(2)]
    qz = [[A.alloc([8, 128], BF16) for _c in range(2)] for _ in range(2)]
    qkm = [A.alloc([4, 128], BF16) for _ in range(2)]
    mix = [A.alloc([2048], BF16) for _ in range(2)]
    mixTs = [A.alloc([KT, 128], BF16) for _ in range(2)]
    gsn = A.alloc([1024], F32)
    SY = A.alloc([4, 2, 256], F32)
    SXb = A.alloc([4, 2, 256], BF16)
    SYb = A.alloc([4, 2, 256], BF16)
    nY = A.alloc([4, 2], F32)
    nXb = A.alloc([4, 2], BF16)
    nYb = A.alloc([4, 2], BF16)
    rden = A.alloc([8], F32)
    dd = A.alloc([4], F32)
    ddn = A.alloc([4], F32)
    ssh = A.alloc([8], F32)
    scl = A.alloc([4], F32)
    hjunk = A.alloc([256], BF16)
    ms = A.alloc([2], F32)
    mcolt = A.alloc([4], F32)
    d4 = A.alloc([4], F32)
    emn = A.alloc([4], F32)
    em_in = A.alloc([2, 4], F32)
    Cst = A.alloc([4, 2, 256], F32)
    nst = A.alloc([4, 2], F32)
    nst8 = A.alloc([128], F32)
    nld = A.alloc([128], F32)
    kcT = [A.alloc([8, 512], BF16) for _ in range(2)]
    vc = [A.alloc([4, 8, 129], BF16) for _ in range(2)]
    kld = [A.alloc([1024], BF16) for _ in range(2)]
    SXK = ["SX%d" % h for h in range(4)]
    SYK = ["SY%d" % h for h in range(4)]
    for h in range(4):
        P.op("pool", CP(SXb[:, h], SX[:, h]), reads=["SX%d" % h], writes=["SXb%d" % h])
    P.op("pool", CP(nXb, nX), reads=["nX"], writes=["nXb"])
    for c in range(2):
        for i2 in range(2):
            P.op("pool", MSET(qz[i2][c], 0.0), writes=["qz%d_%d" % (i2, c)])
            P.op("pool", MSET(es0[i2], 0.0), writes=["es0_%d" % i2])
            P.op("pool", MSET(es1[i2], 0.0), writes=["es1_%d" % i2])

    def load_kv(u):
        s = u % RS
        P.dma("sp", kring[s], scr_fm[8:16, :, u * 128:(u + 1) * 128].rearrange("c p t -> p c t"), writes=["kring%d" % s])
        P.dma("sp", vring[s], scr_va[u].rearrange("p (h e) -> p h e", h=8), writes=["vring%d" % s])

    for u in range(4):
        load_kv(u)

    def emit_state(w, Sf, Skeys, nf, nkey, mt, mkey):
        P.op("dve", CP(mcolt[0:4, w:w + 1], mt), reads=[mkey], writes=["mcolt"])
        P.op("dve", TS(d4[0:4, 0:4], identf[0:4, 0:4], mt), reads=[mkey, "cst"], writes=["d4"])
        P.op("pe", MM(psf(6)[:, 0:4], ones4, d4[0:4, 0:4]), reads=["cst", "d4"], writes=[PK[6]])
        P.op("act", ACT(emn, psf(6)[:, 0:4], AF.Exp, scale=-1.0), reads=[PK[6]], writes=["emn"])
        for h in range(4):
            P.op("dve", TS(Cst[:, h].rearrange("p d e -> p (d e)"), Sf[:, h].rearrange("p d e -> p (d e)"), emn[:, h:h + 1]),
                 reads=[Skeys[h], "emn"], writes=["Cst"])
        P.dma("sp", C_out[w].rearrange("h (d p) e -> p h d e", p=128), Cst, reads=["Cst"], writes=["C_out"])
        P.op("dve", TT(nst, nf, emn.unsqueeze(2).broadcast_to([128, 4, 2]), ALU.mult), reads=[nkey, "emn"], writes=["nst"])
        P.op("pe", TR(psf(6)[0:8, 128:256], nst.rearrange("p h d -> p (h d)"), identf), reads=["nst", "cst"], writes=[PK[6]])
        P.op("dve", CP(nst8[0:8, :], psf(6)[0:8, 128:256]), reads=[PK[6]], writes=["nst8"])
        P.dma("sp", n_out[w].rearrange("h (d p) -> (h d) p", p=128), nst8[0:8, :], reads=["nst8"], writes=["n_out"])

    for i in range(NM):
        u = 4 + i
        i2 = i % 2
        sample = (i == NM - 1)
        slot = u % RS
        load_kv(u)
        P.dma("sp", qbuf[i2], scr_fm[0:8, :, u * 128:(u + 1) * 128].rearrange("c p t -> p c t"), writes=["qbuf%d" % i2])
        P.dma("sp", mbuf[i2], scr_fm[16:32, :, u * 128:(u + 1) * 128].rearrange("c p t -> p c t"), writes=["mbuf%d" % i2])
        P.dma("sp", tbuf[i2], scr_tm[i], writes=["tbuf%d" % i2])
        P.dma("sp", gbuf[i2], scr_g[i], writes=["gbuf%d" % i2])
        kmT = mbuf[i2][:, 8:16, :]
        qmT = mbuf[i2][:, 0:8, :]
        km_tm = tbuf[i2][:, 0:1024]
        vm_tm = tbuf[i2][:, 1024:2048].rearrange("p (h e) -> p h e", h=4)
        om_tm = tbuf[i2][:, 2048:3072]
        tk, mk_, qk_ = "tbuf%d" % i2, "mbuf%d" % i2, "qbuf%d" % i2
        if sample:
            for seq in range(2):
                for kt in range(4):
                    li = (seq * 4 + kt) % 2
                    P.dma("pool", kld[li], cache_k[seq, kt * 128:(kt + 1) * 128, :], writes=["kld%d" % li])
                    for hh2 in range(2):
                        b = 4 + hh2
                        pt = psb(b).rearrange("p (k t) -> p k t", k=8)
                        for h4 in range(4):
                            h = hh2 * 4 + h4
                            P.op("pe", TR(pt[:, h4, :], kld[li][:, h * 128:(h + 1) * 128], identb), reads=["kld%d" % li, "identb"],
                                 writes=[PK[b]], signal=(h4 == 3))
                        evac(kcT[seq][:, hh2 * 4:(hh2 + 1) * 4, kt * 128:(kt + 1) * 128], pt[:, 0:4, :], [PK[b]], ["kcT%d" % seq],
                             eng=("act" if hh2 == 0 else "dve"))
                for kt in range(4):
                    P.dma("pool", vc[seq][:, kt, :, 0:128], cache_v[seq, kt * 128:(kt + 1) * 128, :].rearrange("p (h e) -> p h e", h=8),
                          writes=["vc%d_%d" % (seq, kt)])
                P.op("pool", MSET(vc[seq][:, :, :, 128], 1.0), writes=["vc%dx" % seq])
            P.dma("sp", em_in, st_mB, writes=["em_in"])
            P.op("act", ACT(em_in, em_in, AF.Exp), reads=["em_in"], writes=["em_in"])
            P.dma("sp", ms[0:4, 0:2], st_m4, writes=["ms0", "ms1"])
            for seq, (Sf, SK, Sb_, SbK, nf, nK, nb_, nbK) in enumerate(((SX, "SX", SXb, "SXb", nX, "nX", nXb, "nXb"),
                                                                         (SY, "SY", SYb, "SYb", nY, "nY", nYb, "nYb"))):
                P.dma("sp", Sf, st_C[seq].rearrange("h (d p) e -> p h d e", p=128), writes=[SK + str(h) for h in range(4)])
                P.dma("sp", nld[0:8, :], st_n[seq].rearrange("h (d p) -> (h d) p", p=128), writes=["nld"])
                P.op("pe", TR(psf(6)[:, 0:8], nld[0:8, :], identf[0:8, 0:8]), reads=["nld", "cst"], writes=[PK[6]])
                P.op("dve", TT(nf, psf(6)[:, 0:8].rearrange("p (h d) -> p h d", h=4), em_in[:, seq, :].unsqueeze(2).broadcast_to([128, 4, 2]), ALU.mult),
                     reads=[PK[6], "em_in"], writes=[nK])
                P.op("dve", CP(nb_, nf), reads=[nK], writes=[nbK])
                for h in range(4):
                    P.op("dve", TS(Sf[:, h].rearrange("p d e -> p (d e)"), Sf[:, h].rearrange("p d e -> p (d e)"), em_in[:, seq, h:h + 1]),
                         reads=[SK + str(h), "em_in"], writes=[SK + str(h)])
                    P.op("pool", CP(Sb_[:, h], Sf[:, h]), reads=[SK + str(h)], writes=[SbK + str(h)])
        G = gate_math(gbuf[i2], ["gbuf%d" % i2], NPRE + i, 6, 7)
        K = G["K"]
        P.op("dve", TT(vp[i2], vm_tm, G["ex"][:, 0:4].unsqueeze(2).broadcast_to([128, 4, 256]), ALU.mult),
             reads=[tk, K + "ex"], writes=["vp%d" % i2])
        for c in range(2):
            P.op("pool", CP(qz[i2][c][:, :, 64 * c:64 * c + 64], qmT[:, :, 64 * c:64 * c + 64]), reads=[mk_], writes=["qz%d_%d" % (i2, c)])
        for h in range(4):
            for d in range(2):
                P.op("pe", MM(psf(0)[:, h * 128:(h + 1) * 128], kmT[:, h * 2 + d, :], qmT[:, h * 2 + d, :], d == 0, d == 1),
                     reads=[mk_], writes=[PK[0]], signal=(h == 3 and d == 1))
        P.op("dve", TT(qkm[i2], psf(0).rearrange("p (h t) -> p h t", h=4), tri.unsqueeze(1).broadcast_to([128, 4, 128]), ALU.mult),
             reads=[PK[0], "cst"], writes=["qkm%d" % i2])
        if not sample:
            n_update(nY, "nY", 7, G, 0, km_tm, tk, nb=nYb, nb_key="nYb", n_src=nX, n_src_key="nX")
            for h in range(4):
                state_update(SY, "SY", None, None, 1 + (h % 2), G, 0, h, km_tm, tk, vp[i2], "vp%d" % i2, Sb=SYb, Sb_key="SYb",
                             S_src=SX, S_src_key="SX")
        for h in range(8):
            e = h % 2
            bx, by = (2, 3) if e == 0 else (4, 5)
            X, Y = psf(bx), psf(by)
            if not sample:
                for kt in range(5):
                    ks = (u - 4 + kt) % RS
                    o = X[:, kt * 128:(kt + 1) * 128] if kt < 4 else Y[:, 0:128]
                    bk = PK[bx] if kt < 4 else PK[by]
                    P.op("pe", MM(o, kring[ks][:, h, :], qbuf[i2][:, h, :], True, False), reads=["kring%d" % ks, qk_], writes=[bk], signal=False)
                    P.op("pe", MM(o, identb, btab[:, h, kt, :], False, True), reads=["identb", "btab"], writes=[bk], signal=(kt >= 3))
                P.op("act", ACT(expT[e][:, 0:4, :], X.rearrange("p (k t) -> p k t", k=4), AF.Exp), reads=[PK[bx]], writes=["expT%da" % e])
                P.op("act", ACT(expT[e][:, 4, :], Y[:, 0:128], AF.Exp), reads=[PK[by]], writes=["expT%db" % e])
            else:
                for kt in range(4):
                    for seq in range(2):
                        o = X[:, kt * 128 + seq * 64:kt * 128 + seq * 64 + 64]
                        P.op("pe", MM(o, kcT[seq][:, h, kt * 128:(kt + 1) * 128], qbuf[i2][:, h, seq * 64:seq * 64 + 64], True, False),
                             reads=["kcT%d" % seq, qk_], writes=[PK[bx]], signal=False)
                        P.op("pe", MM(o, identb, btab[:, h, kt, 0:64], False, True), reads=["identb", "btab"], writes=[PK[bx]],
                             signal=(kt == 3 and seq == 1))
                P.op("pe", MM(Y[:, 0:128], kring[slot][:, h, :], qbuf[i2][:, h, :], True, False), reads=["kring%d" % slot, qk_], writes=[PK[by]], signal=False)
                P.op("pe", MM(Y[:, 0:128], identb, bs4[:, h, :], False, True), reads=["identb", "bs4"], writes=[PK[by]])
                X3 = X.rearrange("p (k t) -> p k t", k=4)
                P.op("act", ACT(es0[e][:, :, 0:64], X3[:, :, 0:64], AF.Exp), reads=[PK[bx]], writes=["es0_%d" % e])
                P.op("act", ACT(es1[e][:, :, 64:128], X3[:, :, 64:128], AF.Exp), reads=[PK[bx]], writes=["es1_%d" % e])
                P.op("act", ACT(expT[e][:, 4, :], Y[:, 0:128], AF.Exp), reads=[PK[by]], writes=["expT%db" % e])
            g = h // 3
            pb = 6 + g % 2
            hh = h % 3
            PV = psf(pb)
            o = PV[:, hh * 129:(hh + 1) * 129]
            if not sample:
                for kt in range(5):
                    ks = (u - 4 + kt) % RS
                    P.op("pe", MM(o, expT[e][:, kt, :], vring[ks][:, h, :], kt == 0, kt == 4),
                         reads=["expT%da" % e if kt < 4 else "expT%db" % e, "vring%d" % ks], writes=[PK[pb]], signal=(kt == 4))
            else:
                for kt in range(4):
                    P.op("pe", MM(o, es0[e][:, kt, :], vc[0][:, kt, h, :], kt == 0, False), reads=["es0_%d" % e, "vc0_%d" % kt, "vc0x"], writes=[PK[pb]], signal=False)
                    P.op("pe", MM(o, es1[e][:, kt, :], vc[1][:, kt, h, :], False, False), reads=["es1_%d" % e, "vc1_%d" % kt, "vc1x"], writes=[PK[pb]], signal=False)
                P.op("pe", MM(o, expT[e][:, 4, :], vring[slot][:, h, :], False, True), reads=["expT%db" % e, "vring%d" % slot], writes=[PK[pb]])
            if h in (2, 5, 7):
                nh = hh + 1
                h0 = h - hh
                PV3 = PV[:, 0:nh * 129].rearrange("p (h e) -> p h e", h=nh)
                P.op("dve", TS(rden[:, 0:nh], PV3[:, :, 128], 1e-30, None, ALU.add), reads=[PK[pb]], writes=["rden"])
                P.op("dve", lambda e_, o_=rden[:, 0:nh]: e_.reciprocal(out=o_, in_=o_), reads=["rden"], writes=["rden"])
                P.op("dve", TT(mix[i2][:, h0 * 128:(h0 + nh) * 128].rearrange("p (h e) -> p h e", h=nh), PV3[:, :, 0:128],
                               rden[:, 0:nh].unsqueeze(2).broadcast_to([128, nh, 128]), ALU.mult),
                     reads=[PK[pb], "rden"], writes=["mix%d_a%d" % (i2, g)])
        S0b, S0bK, n0b, n0bK = SXb, "SXb", nXb, "nXb"
        S1b, S1bK, n1b, n1bK = SYb, "SYb", nYb, "nYb"
        for h in range(4):
            hb = 0 + h // 2
            Hh = psf(hb)[:, (h % 2) * 256:(h % 2 + 1) * 256]
            P.op("pe", MM(Hh, qkm[i2][:, h, :], vp[i2][:, h, :], True, False), reads=["qkm%d" % i2, "vp%d" % i2], writes=[PK[hb]], signal=False)
            for c, (Sb_, SbK) in enumerate(((S0b, S0bK), (S1b, S1bK))):
                for d in range(2):
                    P.op("pe", MM(Hh, qz[i2][c][:, h * 2 + d, :], Sb_[:, h, d, :], False, c == 1 and d == 1),
                         reads=["qz%d_%d" % (i2, c), SbK + str(h)], writes=[PK[hb]], signal=(c == 1 and d == 1))
        DEN = psf(7)[:, 300:304]
        for h in range(4):
            o = DEN[:, h:h + 1]
            P.op("pe", MM(o, qkm[i2][:, h, :], G["gbf"][:, h:h + 1], True, False), reads=["qkm%d" % i2, K + "gbf"], writes=[PK[7]], signal=False)
            for c, (nb_, nbK) in enumerate(((n0b, n0bK), (n1b, n1bK))):
                for d in range(2):
                    P.op("pe", MM(o, qz[i2][c][:, h * 2 + d, :], nb_[:, h, d:d + 1], False, c == 1 and d == 1),
                         reads=["qz%d_%d" % (i2, c), nbK], writes=[PK[7]], signal=(h == 3 and c == 1 and d == 1))
        eb = G["ex"][:, 4:8]
        P.op("dve", TT(dd, DEN, eb, ALU.mult), reads=[PK[7], K + "ex"], writes=["dd"])
        P.op("dve", TS(ddn, dd, -1.0), reads=["dd"], writes=["ddn"])
        P.op("dve", TT(dd, dd, ddn, ALU.max), reads=["dd", "ddn"], writes=["dd"])
        P.op("dve", TS(dd, dd, 1.0, None, ALU.max), reads=["dd"], writes=["dd"])
        P.op("dve", lambda e_, o_=dd, i_=dd: e_.reciprocal(out=o_, in_=i_), reads=["dd"], writes=["dd"])
        P.op("dve", TT(dd, dd, eb, ALU.mult), reads=["dd", K + "ex"], writes=["dd"])
        for h in range(4):
            hb = 0 + h // 2
            Hh = psf(hb)[:, (h % 2) * 256:(h % 2 + 1) * 256]
            P.op("act", ACT(hjunk, Hh, AF.Square, scale=dd[:, h:h + 1], accum_out=ssh[:, h:h + 1]), reads=[PK[hb], "dd"], writes=["hjunk", "ssh%d" % h])
        rstd_from_ss(ssh[:, 0:4], ssh[:, 4:8], 256, ["ssh%d" % h for h in range(4)], "sshr")
        P.op("dve", TT(scl, dd, ssh[:, 4:8], ALU.mult), reads=["dd", "sshr"], writes=["scl"])
        P.op("act", ACT(gsn, om_tm, AF.Sigmoid), reads=[tk], writes=["gsn"])
        P.op("pool", TT(gsn, gsn, mnb, ALU.mult), reads=["gsn", "mnb"], writes=["gsn"])
        for h in range(4):
            hb = 0 + h // 2
            Hh = psf(hb)[:, (h % 2) * 256:(h % 2 + 1) * 256]
            P.op("dve", STT(mix[i2][:, 1024 + h * 256:1024 + (h + 1) * 256], Hh, scl[:, h:h + 1], gsn[:, h * 256:(h + 1) * 256], ALU.mult, ALU.mult),
                 reads=[PK[hb], "scl", "gsn"], writes=["mix%d_m%d" % (i2, h)])
        if not sample:
            n_update(nX, "nX", 7, G, 1, km_tm, tk, nb=nXb, nb_key="nXb", n_src=nY, n_src_key="nY")
            for h in range(4):
                state_update(SX, "SX", None, None, 2 + (h % 2), G, 1, h, km_tm, tk, vp[i2], "vp%d" % i2, Sb=SXb, Sb_key="SXb",
                             S_src=SY, S_src_key="SY")
            m_update(mP[0:4, 0:1], "mP", G, 0)
            m_update(mP[0:4, 0:1], "mP", G, 1)
            if i == NOWN:
                emit_state(0, SX, SXK, nX, "nX", mP[0:4, 0:1], "mP")
        else:
            n_update(nX, "nX", 7, G, 0, km_tm, tk)
            n_update(nY, "nY", 7, G, 1, km_tm, tk)
            for h in range(4):
                state_update(SX, "SX", None, None, 2 + (h % 2), G, 0, h, km_tm, tk, vp[i2], "vp%d" % i2)
            for h in range(4):
                state_update(SY, "SY", None, None, 2 + (h % 2), G, 1, h, km_tm, tk, vp[i2], "vp%d" % i2)
            m_update(ms[0:4, 0:1], "ms0", G, 0)
            m_update(ms[0:4, 1:2], "ms1", G, 1)
            emit_state(1, SX, SXK, nX, "nX", ms[0:4, 0:1], "ms0")
            emit_state(2, SY, SYK, nY, "nY", ms[0:4, 1:2], "ms1")
        mixkeys = ["mix%d_a%d" % (i2, g) for g in range(3)] + ["mix%d_m%d" % (i2, h) for h in range(4)]
        for half in range(2):
            b = 4 + half
            pt = psb(b).rearrange("p (k t) -> p k t", k=8)
            for k8 in range(8):
                k = half * 8 + k8
                P.op("pe", TR(pt[:, k8, :], mix[i2][:, k * 128:(k + 1) * 128], identb), reads=mixkeys + ["identb"], writes=[PK[b]], signal=(k8 == 7))
            evac(mixTs[i2][:, half * 8:(half + 1) * 8, :], pt, [PK[b]], ["mixTs%d_%d" % (i2, half)], eng=("act" if half == 0 else "dve"))
        P.dma("sp", scr_mixT[i], mixTs[i2].rearrange("p k t -> p (k t)"), reads=["mixTs%d_0" % i2, "mixTs%d_1" % i2], writes=["scr_mixT"])
    P.dma("sp", m_out, mcolt[0:4, 0:3], reads=["mcolt"], writes=["m_out"])
    P.barrier()
    A.reset(base_mark)
    if last_stage == 3:
        P.run()
        return nc


    wo = A.alloc([KT, D], BF16)
    for q in range(4):
        P.dma("pool", wo[:, :, q * 512:(q + 1) * 512], w_out[:, q * 512:(q + 1) * 512].rearrange("(k p) n -> p k n", p=128), writes=["wo%d" % q])
    gB1 = [A.alloc([D], F32) for _ in range(2)]
    build_gtgB(gB1[0], gB1[1], 0, "gB1p", "gB1s")
    fe_alloc(with_xs=False)
    xs4 = [A.alloc([D], F32) for _ in range(2)]
    x1s = [A.alloc([D], F32) for _ in range(2)]
    mT = [A.alloc([KT, 128], BF16) for _ in range(2)]
    h2s = [A.alloc([KT, 128], BF16) for _ in range(2)]
    ssq = [A.alloc([8], F32) for _ in range(2)]
    junk4 = A.alloc([512], BF16)
    for i in range(NM):
        i2 = i % 2
        sample = (i == NM - 1)
        P.dma("sp", mT[i2], scr_mixT[i].rearrange("p (k t) -> p k t", k=KT), writes=["mT%d" % i2])
        P.dma("sp", xs4[i2], x_main[i * 128:(i + 1) * 128, :], writes=["xs4_%d" % i2])
        for cb in range(4):
            for k in range(KT):
                P.op("pe", MM(psf(cb), mT[i2][:, k, :], wo[:, k, cb * 512:(cb + 1) * 512], k == 0, k == KT - 1),
                     reads=["mT%d" % i2, "wo%d" % cb], writes=[PK[cb]], signal=(k == KT - 1))
        for cb in range(4):
            P.op("act", ACT(junk4, psf(cb), AF.Square, accum_out=ssq[i2][:, cb:cb + 1]), reads=[PK[cb]], writes=["junk4", "ssq%d_%d" % (i2, cb)])
        P.op("dve", lambda e_, o_=ssq[i2][:, 4:5], i_=ssq[i2][:, 0:4]: e_.tensor_reduce(out=o_, in_=i_, axis=AX.X, op=ALU.add),
             reads=["ssq%d_%d" % (i2, cb) for cb in range(4)], writes=["ssq%d_s" % i2])
        rstd_from_ss(ssq[i2][:, 4:5], ssq[i2][:, 5:6], D, ["ssq%d_s" % i2], "ssq%d_r" % i2)
        gB, gBk = (gB1[1], "gB1s") if sample else (gB1[0], "gB1p")
        x1keys = []
        for cb in range(4):
            sl = slice(cb * 512, (cb + 1) * 512)
            key = "x1s%d_%d" % (i2, cb)
            x1keys.append(key)
            P.op("dve", STT(x1s[i2][:, sl], psf(cb), ssq[i2][:, 5:6], gB[:, sl], ALU.mult, ALU.mult), reads=[PK[cb], "ssq%d_r" % i2, gBk], writes=[key])
            P.op("pool", TT(x1s[i2][:, sl], x1s[i2][:, sl], xs4[i2][:, sl], ALU.add), reads=[key, "xs4_%d" % i2], writes=[key])
        if i > 0:
            P.dma("sp", scr_x1[i], x1s[i2], reads=x1keys, writes=["scr_x1"])
        rows = (1, 2) if sample else (0, 0)
        front_end(None, h2s[i2], "h2s%d" % i2, 3, 2, rows, tb=(4, 5), xs_in=x1s[i2], xs_keys=x1keys)
        hk = []
        for k in range(KT):
            hk.append("h2s%d_k%d_0" % (i2, k))
            if sample:
                hk.append("h2s%d_k%d_64" % (i2, k))
        P.dma("sp", scr_h2T[:, :, i * 128:(i + 1) * 128], h2s[i2], reads=hk, writes=["scr_h2T"])
    P.barrier()
    A.reset(base_mark)
    if last_stage == 4:
        P.run()
        return nc

    TMp = 128 + OWN
    base_s0 = 2 + TMp + 2
    base_s1 = base_s0 + 64 + 2
    ROW = base_s1 + 64
    h2T_all = A.alloc([KT, TM], BF16)
    for q in range(4):
        P.dma("sp", h2T_all[:, q * 4:(q + 1) * 4, :], scr_h2T[:, q * 4:(q + 1) * 4, :], writes=["h2T_%d" % q])
    H2K = ["h2T_%d" % q for q in range(4)]
    cw = A.alloc([FT, 4], F32)
    P.dma("sp", cw, cwT, writes=["cw"])
    cst_tm = A.alloc([DFF], F32)
    cstT = A.alloc([FT, 4], F32)
    P.dma("sp", cst_tm[0:4, :], conv_st, writes=["tmp22"])
    for ft in range(FT):
        P.op("pe", TR(psf(7)[:, ft * 4:(ft + 1) * 4], cst_tm[0:4, ft * 128:(ft + 1) * 128], identf[0:4, 0:4]), reads=["tmp22", "cst"],
             writes=[PK[7]], signal=(ft == FT - 1))
    P.op("dve", CP(cstT.rearrange("p f c -> p (f c)"), psf(7)[:, 0:FT * 4]), reads=[PK[7]], writes=["cstT"])
    ugrow = [A.alloc([ROW], F32) for _ in range(2)]
    acc = [A.alloc([512], F32) for _ in range(2)]
    gl = [A.alloc([512], F32) for _ in range(2)]
    arow = [A.alloc([TM], BF16) for _ in range(2)]
    convsave = A.alloc([FT, 6], F32)
    cso = cst_tm
    wgb = [A.alloc([KT, 512], BF16) for _ in range(2)]
    wub = [A.alloc([KT, 512], BF16) for _ in range(2)]
    for i2 in range(2):
        P.op("pool", MSET(ugrow[i2][:, 0:2], 0.0), writes=["ug%d_pad" % i2])
    blocks = []
    for m0 in range(0, TMp, 512):
        m1 = min(m0 + 512, TMp)
        blocks.append((m0, m1, [(m0, m1, 2 + m0)]))
    blocks.append((TMp, TMp + 128, [(TMp, TMp + 64, base_s0), (TMp + 64, TMp + 128, base_s1)]))
    nG = 0
    nU = 0
    npc = 0
    for ftg in range(FT // 4):
        wi = ftg % 2
        P.dma("pool", wgb[wi], w_g[:, ftg * 512:(ftg + 1) * 512].rearrange("(k p) n -> p k n", p=128), writes=["wgb%d" % wi])
        P.dma("pool", wub[wi], w_u[:, ftg * 512:(ftg + 1) * 512].rearrange("(k p) n -> p k n", p=128), writes=["wub%d" % wi])
        for sub in range(4):
            ft = ftg * 4 + sub
            u2 = ft % 2
            ug = ugrow[u2]
            P.op("pool", CP(ug[:, base_s0 - 2:base_s0], cstT[:, ft, 0:2]), reads=["cstT"], writes=["ug%d_s0" % u2])
            P.op("pool", CP(ug[:, base_s1 - 2:base_s1], cstT[:, ft, 2:4]), reads=["cstT"], writes=["ug%d_s1" % u2])
            akeys = []
            for bi, (m0, m1, pieces) in enumerate(blocks):
                n = m1 - m0
                gb = nG % 3
                nG += 1
                ub = 3 + nU % 4
                nU += 1
                for k in range(KT):
                    P.op("pe", MM(psf(gb)[:, 0:n], wgb[wi][:, k, sub * 128:(sub + 1) * 128], h2T_all[:, k, m0:m1], k == 0, k == KT - 1),
                         reads=["wgb%d" % wi, H2K[k // 4]], writes=[PK[gb]], signal=(k == KT - 1))
                for k in range(KT):
                    P.op("pe", MM(psf(ub)[:, 0:n], wub[wi][:, k, sub * 128:(sub + 1) * 128], h2T_all[:, k, m0:m1], k == 0, k == KT - 1),
                         reads=["wub%d" % wi, H2K[k // 4]], writes=[PK[ub]], signal=(k == KT - 1))
                for (p0, p1, c0) in pieces:
                    pn = p1 - p0
                    ukey = "ug%d_b%d_%d" % (u2, bi, p0)
                    P.op("act", ACT(ug[:, c0:c0 + pn], psf(gb)[:, p0 - m0:p1 - m0], AF.Copy), reads=[PK[gb]], writes=[ukey])
                    prev = ["ug%d_pad" % u2, "ug%d_s0" % u2, "ug%d_s1" % u2]
                    if bi > 0:
                        prev += ["ug%d_b%d_%d" % (u2, bi - 1, blocks[bi - 1][2][-1][0])]
                    if bi == 0:
                        P.op("pool", TS(ug[:, 2:130], ug[:, 2:130], msk[:, 0, NPRE:NPRE + 1]), reads=[ukey, "msk"], writes=[ukey])
                    a_ = acc[npc % 2]
                    g_ = gl[npc % 2]
                    ak, gk = "acc%d" % (npc % 2), "gl%d" % (npc % 2)
                    npc += 1
                    P.op("act", ACT(a_[:, 0:pn], ug[:, c0 - 2:c0 - 2 + pn], AF.Identity, scale=cw[:, ft, 0:1], bias=cw[:, ft, 3:4]),
                         reads=[ukey, "cw"] + prev, writes=[ak])
                    P.op("dve", STT(a_[:, 0:pn], ug[:, c0 - 1:c0 - 1 + pn], cw[:, ft, 1:2], a_[:, 0:pn], ALU.mult, ALU.add),
                         reads=[ukey, "cw", ak] + prev, writes=[ak])
                    P.op("dve", STT(a_[:, 0:pn], ug[:, c0:c0 + pn], cw[:, ft, 2:3], a_[:, 0:pn], ALU.mult, ALU.add), reads=[ukey, "cw", ak], writes=[ak])
                    P.op("act", ACT(g_[:, 0:pn], a_[:, 0:pn], AF.Gelu), reads=[ak], writes=[gk])
                    akey = "arow%d_%d" % (u2, p0)
                    akeys.append(akey)
                    P.op("dve", TT(arow[u2][:, p0:p1], g_[:, 0:pn], psf(ub)[:, p0 - m0:p1 - m0], ALU.mult), reads=[gk, PK[ub]], writes=[akey])
            allug = ["ug%d_b%d_%d" % (u2, bi, pc[0]) for bi, (_, _, pcs) in enumerate(blocks) for pc in pcs]
            P.op("pool", CP(convsave[:, ft, 0:2], ug[:, 2 + TMp - 2:2 + TMp]), reads=allug, writes=["convsave"])
            P.op("pool", CP(convsave[:, ft, 2:4], ug[:, base_s0 + 62:base_s0 + 64]), reads=allug, writes=["convsave"])
            P.op("pool", CP(convsave[:, ft, 4:6], ug[:, base_s1 + 62:base_s1 + 64]), reads=allug, writes=["convsave"])
            P.dma("sp", scr_aT[ft], arow[u2], reads=akeys, writes=["scr_aT"])
    for g4 in range(FT // 4):
        b = g4 % 2
        for s4 in range(4):
            ft = g4 * 4 + s4
            P.op("pe", TR(psf(b)[0:6, s4 * 128:(s4 + 1) * 128], convsave[:, ft, :], identf), reads=["convsave", "cst"], writes=[PK[b]], signal=(s4 == 3))
        P.op("dve", CP(cso[0:6, g4 * 512:(g4 + 1) * 512], psf(b)[0:6, :]), reads=[PK[b]], writes=["tmp22"])
    P.dma("sp", conv_out, cso[0:6, :], reads=["tmp22"], writes=["conv_out"])
    P.barrier()
    A.reset(base_mark)
    if last_stage == 5:
        P.run()
        return nc

    GS = 6
    gB2 = [A.alloc([D], F32) for _ in range(2)]
    build_gtgB(gB2[0], gB2[1], 1, "gB2p", "gB2s")
    aTg = A.alloc([FT, GS * 128], BF16)
    ystage = [A.alloc([D], F32) for _ in range(GS)]
    x1t = [A.alloc([D], F32) for _ in range(2)]
    NWD = 5
    wd = [A.alloc([4, 512], BF16) for _ in range(NWD)]
    ssq6 = A.alloc([GS, 8], F32)
    junk6 = A.alloc([512], BF16)
    out_tiles = list(range(1, NM))
    nwd = 0
    nx1 = 0
    for g0 in range(0, len(out_tiles), GS):
        tiles = out_tiles[g0:g0 + GS]
        nt = len(tiles)
        tok0 = tiles[0] * 128
        ntok = nt * 128
        for q in range(4):
            P.dma("sp", aTg[:, q * 11:(q + 1) * 11, 0:ntok], scr_aT[q * 11:(q + 1) * 11, :, tok0:tok0 + ntok].rearrange("f p t -> p f t"),
                  writes=["aTg_%d" % q])
        for cb in range(4):
            for ftq in range(FT // 4):
                wi = nwd % NWD
                nwd += 1
                P.dma("pool", wd[wi], w_d[ftq * 512:(ftq + 1) * 512, cb * 512:(cb + 1) * 512].rearrange("(f p) n -> p f n", p=128), writes=["wd%d" % wi])
                for s4 in range(4):
                    ft = ftq * 4 + s4
                    for ti in range(nt):
                        P.op("pe", MM(psf(ti), aTg[:, ft, ti * 128:(ti + 1) * 128], wd[wi][:, s4, :], ft == 0, ft == FT - 1),
                             reads=["aTg_%d" % (ft // 11), "wd%d" % wi], writes=[PK[ti]], signal=(ft == FT - 1 or (s4 == 3 and ti == nt - 1)))
            for ti in range(nt):
                P.op("act", ACT(junk6, psf(ti), AF.Square, accum_out=ssq6[:, ti, cb:cb + 1]), reads=[PK[ti]], writes=["junk6", "ssq6_%d_%d" % (ti, cb)])
                P.op("dve", CP(ystage[ti][:, cb * 512:(cb + 1) * 512], psf(ti)), reads=[PK[ti]], writes=["ys%d_%d" % (ti, cb)])
        for ti, tile in enumerate(tiles):
            sample = (tile == NM - 1)
            xi = nx1 % 2
            nx1 += 1
            P.dma("sp", x1t[xi], scr_x1[tile], writes=["x1t%d" % xi])
            P.op("dve", lambda e_, o_=ssq6[:, ti, 4:5], i_=ssq6[:, ti, 0:4]: e_.tensor_reduce(out=o_, in_=i_, axis=AX.X, op=ALU.add),
                 reads=["ssq6_%d_%d" % (ti, cb) for cb in range(4)], writes=["ssq6s_%d" % ti])
            rstd_from_ss(ssq6[:, ti, 4:5], ssq6[:, ti, 5:6], D, ["ssq6s_%d" % ti], "ssq6r_%d" % ti)
            gB, gBk = (gB2[1], "gB2s") if sample else (gB2[0], "gB2p")
            yk = ["ys%d_%d" % (ti, cb) for cb in range(4)]
            P.op("dve", STT(ystage[ti], ystage[ti], ssq6[:, ti, 5:6], gB, ALU.mult, ALU.mult), reads=yk + ["ssq6r_%d" % ti, gBk], writes=yk)
            P.op("pool", TT(ystage[ti], ystage[ti], x1t[xi], ALU.add), reads=yk + ["x1t%d" % xi], writes=yk)
            P.dma("sp", y_main[(tile - 1) * 128:tile * 128, :], ystage[ti], reads=yk, writes=["y_main"])
    P.barrier()
    P.run()
    return nc


def make_consts():
    c = np.zeros((128, 8, 128), np.float32)
    c[:, 0, :] = np.eye(128, dtype=np.float32)
    s = np.arange(128)[:, None]
    t = np.arange(128)[None, :]
    c[:, 1, :] = ((s // 64 == t // 64) & (s <= t)).astype(np.float32)
    c[0:64, 2, :] = 1.0
    c[64:128, 3, :] = 1.0
    c[0, 4, :] = 1.0
    c[1, 5, 0:64] = 1.0
    c[2, 5, 64:128] = 1.0
    c[0:64, 6, 0] = 1.0
    c[64:128, 6, 1] = 1.0
    c[:, 7, :] = 1.0
    return c


def make_consts2():
    return None


def prep_inputs(inp, SEQ):
    OWN = SEQ // 4
    NOWN = OWN // 128
    NPRE = 3 * NOWN - 1
    NM = NOWN + 2
    f32 = np.float32
    xp = np.asarray(inp["x_prompt"], f32)
    xsamp = np.asarray(inp["x_sample"], f32)
    relb = np.asarray(inp["att_rel_bias"], f32)[0]
    row = np.arange(128)[:, None, None]
    kk = np.arange(5)[None, :, None]
    qc = np.arange(128)[None, None, :]
    p = 128 * kk + row - 64 * (qc // 64)
    dist = 512 + (qc % 64) - p
    idx = np.clip(dist, -256, 256) + 256
    valid = (p >= 0) & (p < 576)
    bias_tab = np.empty((128, 8, 5, 128), f32)
    for h in range(8):
        bias_tab[:, h] = np.where(valid, relb[h][idx], NEG)
    rr = np.arange(128)[:, None]
    qq = np.arange(128)[None, :]
    same = (rr // 64) == (qq // 64)
    idx4 = np.clip((qq % 64) - (rr % 64), -256, 256) + 256
    bias_s4 = np.empty((128, 8, 128), f32)
    for h in range(8):
        bias_s4[:, h] = np.where(same, relb[h][idx4], NEG)
    consts = make_consts()
    shared = {
        "adabT": np.ascontiguousarray(np.asarray(inp["ada_b"], f32)[0].reshape(96, 128).T),
        "adab3": np.ascontiguousarray(np.broadcast_to(np.asarray(inp["ada_b"], f32)[0][None], (3, 12288))),
        "gpreT": np.ascontiguousarray(np.stack([np.asarray(inp["norm_pre_mix"], f32)[0].reshape(16, 128).T,
                                                 np.asarray(inp["norm_pre_ffn"], f32)[0].reshape(16, 128).T], axis=1)),
        "gpost3": np.ascontiguousarray(np.broadcast_to(np.stack([np.asarray(inp["norm_post_mix"], f32)[0],
                                                                  np.asarray(inp["norm_post_ffn"], f32)[0]])[None], (3, 2, D))),
        "mnormB": np.ascontiguousarray(np.broadcast_to(np.asarray(inp["mlstm_norm"], f32)[0][None], (128, 1024))),
        "bgate": np.ascontiguousarray(np.broadcast_to(np.concatenate([np.asarray(inp["b_igate"], f32)[0],
                                                                       np.asarray(inp["b_fgate"], f32)[0]])[None], (128, 8))),
        "cwT": np.ascontiguousarray(np.concatenate([np.asarray(inp["ffn_conv_w"], f32)[0].reshape(3, FT, 128),
                                                    np.asarray(inp["ffn_conv_b"], f32)[0].reshape(1, FT, 128)], 0).transpose(2, 1, 0)),
        "bias_tab": bias_tab,
        "bias_s4": bias_s4,
        "ada_w": np.asarray(inp["ada_w"], f32)[0],
        "w_in": np.asarray(inp["w_in"], f32)[0],
        "w_out": np.asarray(inp["w_out"], f32)[0],
        "w_g": np.asarray(inp["w_ffn_gate"], f32)[0],
        "w_u": np.asarray(inp["w_ffn_up"], f32)[0],
        "w_d": np.asarray(inp["w_ffn_down"], f32)[0],
    }
    maps = []
    for r in range(8):
        b, j = r // 4, r % 4
        s0 = j * OWN
        m = dict(shared)
        xm = np.zeros((NM * 128, D), f32)
        if j > 0:
            xm[0:128] = xp[b, s0 - 128:s0]
        xm[128:128 + OWN] = xp[b, s0:s0 + OWN]
        xm[128 + OWN:] = xsamp[2 * r:2 * r + 2].reshape(128, D)
        m["x_main"] = xm
        xpre = np.zeros((NPRE * 128, D), f32)
        lo = s0 - 128 - NPRE * 128
        mk = np.zeros((128, 3, NPRE + NM), f32)
        for t in range(NPRE):
            a = lo + t * 128
            if a >= 0:
                xpre[t * 128:(t + 1) * 128] = xp[b, a:a + 128]
                mk[:, 0, t] = 1.0
        mk[:, 0, NPRE] = 1.0 if j > 0 else 0.0
        mk[:, 0, NPRE + 1:] = 1.0
        mk[:, 1] = -mk[:, 0]
        mk[:, 2] = np.where(mk[:, 0] > 0, 0.0, -1e30)
        m["x_pre"] = xpre
        m["masks"] = mk
        c3 = np.stack([np.asarray(inp["c_prompt"], f32)[b], np.asarray(inp["c_sample"], f32)[2 * r],
                       np.asarray(inp["c_sample"], f32)[2 * r + 1]])
        m["c3T"] = np.ascontiguousarray(c3.reshape(3, 16, 128).transpose(2, 1, 0))
        cc = consts.copy()
        m["consts"] = cc
        m["cache_k"] = np.ascontiguousarray(np.asarray(inp["cache_att_k"], f32)[0, 2 * r:2 * r + 2].reshape(2, 512, 1024))
        m["cache_v"] = np.ascontiguousarray(np.asarray(inp["cache_att_v"], f32)[0, 2 * r:2 * r + 2].reshape(2, 512, 1024))
        m["st_C"] = np.ascontiguousarray(np.asarray(inp["state_mlstm_C"], f32)[0, 2 * r:2 * r + 2])
        m["st_n"] = np.ascontiguousarray(np.asarray(inp["state_mlstm_n"], f32)[0, 2 * r:2 * r + 2])
        sm = np.asarray(inp["state_mlstm_m"], f32)[0, 2 * r:2 * r + 2]
        m["st_mB"] = np.ascontiguousarray(np.broadcast_to(sm[None], (128, 2, 4)))
        m["st_m4"] = np.ascontiguousarray(sm.T)
        m["conv_st"] = np.ascontiguousarray(np.asarray(inp["state_ffn_conv"], f32)[0, 2 * r:2 * r + 2].reshape(4, 5632))
        maps.append(m)
    return maps


SEQ_FULL = 8192
_CACHE = {}


def kernel(**inputs):
    SEQ = int(np.asarray(inputs["x_prompt"]).shape[1])
    OWN = SEQ // 4
    if SEQ not in _CACHE:
        _CACHE[SEQ] = None
    nc = build(SEQ)
    maps = prep_inputs(inputs, SEQ)
    res = run_bass_kernel_spmd(nc, maps, core_ids=list(range(8)))
    R_ = res.results
    f32 = np.float32
    y_p = np.empty((2, SEQ, D), f32)
    y_s = np.empty((16, 64, D), f32)
    p_k = np.empty((1, 2, 512, 8, 128), f32)
    p_v = np.empty((1, 2, 512, 8, 128), f32)
    p_C = np.empty((1, 2, 4, 256, 256), f32)
    p_n = np.empty((1, 2, 4, 256), f32)
    p_m = np.empty((1, 2, 4), f32)
    p_conv = np.empty((1, 2, 2, DFF), f32)
    s_k = np.empty((1, 16, 64, 8, 128), f32)
    s_v = np.empty((1, 16, 64, 8, 128), f32)
    s_C = np.empty((1, 16, 4, 256, 256), f32)
    s_n = np.empty((1, 16, 4, 256), f32)
    s_m = np.empty((1, 16, 4), f32)
    s_conv = np.empty((1, 16, 2, DFF), f32)
    for r in range(8):
        b, j = r // 4, r % 4
        o = R_[r]
        ym = np.asarray(o["y_main"])
        y_p[b, j * OWN:(j + 1) * OWN] = ym[:OWN]
        y_s[2 * r:2 * r + 2] = ym[OWN:].reshape(2, 64, D)
        ko, vo = np.asarray(o["k_out"]), np.asarray(o["v_out"])
        s_k[0, 2 * r:2 * r + 2] = ko[512:640].reshape(2, 64, 8, 128)
        s_v[0, 2 * r:2 * r + 2] = vo[512:640].reshape(2, 64, 8, 128)
        Co, no, mo = np.asarray(o["C_out"]), np.asarray(o["n_out"]), np.asarray(o["m_out"])
        s_C[0, 2 * r:2 * r + 2] = Co[1:3]
        s_n[0, 2 * r:2 * r + 2] = no[1:3]
        s_m[0, 2 * r:2 * r + 2] = mo[:, 1:3].T
        co = np.asarray(o["conv_out"]).reshape(3, 2, DFF)
        s_conv[0, 2 * r:2 * r + 2] = co[1:3]
        if j == 3:
            p_k[0, b] = ko[0:512].reshape(512, 8, 128)
            p_v[0, b] = vo[0:512].reshape(512, 8, 128)
            p_C[0, b] = Co[0]
            p_n[0, b] = no[0]
            p_m[0, b] = mo[:, 0]
            p_conv[0, b] = co[0]
    return (y_p, y_s, p_k, p_v, p_C, p_n, p_m, p_conv, s_k, s_v, s_C, s_n, s_m, s_conv)
```

```python
import contextlib
import numpy as np
import concourse.bass as bass
import concourse.mybir as mybir
from concourse.bass_utils import run_bass_kernel_spmd

F32 = mybir.dt.float32
BF16 = mybir.dt.bfloat16
AF = mybir.ActivationFunctionType
ALU = mybir.AluOpType
AX = mybir.AxisListType

D = 2048
KT = 16
DFF = 5632
FT = 44
INC = 7176
EPS = 1e-6
NEG = -30000.0


class Prog:
    ENG = ("pe", "act", "dve", "pool", "sp")

    def __init__(self, nc, n_dma=24):
        self.nc = nc
        self.q = {e: [] for e in self.ENG}
        self.cnt = {e: 0 for e in self.ENG}
        self.waited = {}
        self.lastw = {}
        self.readers = {}
        self.n_dma = n_dma
        self.dma_val = {}
        self.dma_rr = {e: 0 for e in self.ENG}
        self.semh = {}

    def _deps(self, eng, reads, writes):
        deps = []
        for k in reads:
            deps += self.lastw.get(k, [])
        for k in writes:
            deps += self.lastw.get(k, [])
            deps += self.readers.get(k, [])
        waits = []
        for semkey, val in deps:
            if eng == "pe" and semkey == ("e", "pe"):
                continue
            if self.waited.get((eng, semkey), 0) >= val:
                continue
            self.waited[(eng, semkey)] = val
            waits.append((semkey, val))
        return waits

    def _record(self, tok, reads, writes):
        for k in writes:
            self.lastw[k] = [tok]
            self.readers[k] = []
        for k in reads:
            self.readers.setdefault(k, []).append(tok)

    def op(self, eng, fn, reads=(), writes=(), signal=True):
        ex = [k for k in reads if k.startswith("ps")]
        if ex:
            reads = [k for k in reads if not k.startswith("ps")]
            writes = list(writes) + ex
        waits = self._deps(eng, reads, writes)
        if signal:
            self.cnt[eng] += 1
            tok = (("e", eng), self.cnt[eng])
        else:
            tok = (("e", eng), self.cnt[eng] + 1)
        self.q[eng].append((waits, fn, signal))
        self._record(tok, reads, writes)
        return tok

    def dma(self, qeng, out, in_, reads=(), writes=(), custom=None, inc=16):
        waits = self._deps(qeng, reads, writes)
        idx = self.dma_rr[qeng] % self.n_dma
        self.dma_rr[qeng] += 1
        semkey = ("d", qeng, idx)
        v = self.dma_val.get(semkey, 0)
        if v > 0 and self.waited.get((qeng, semkey), 0) < v:
            self.waited[(qeng, semkey)] = v
            waits.append((semkey, v))
        self.dma_val[semkey] = v + inc
        tok = (semkey, v + inc)

        def fn(e, out=out, in_=in_, semkey=semkey, custom=custom):
            if custom is not None:
                return custom(e).then_inc(self.semh[semkey], inc)
            return e.dma_start(out=out, in_=in_).then_inc(self.semh[semkey], 16)

        self.q[qeng].append((waits, fn, False))
        self._record(tok, reads, writes)
        return tok

    def barrier(self):
        for eng in self.ENG:
            waits = []
            for semkey, v in self.dma_val.items():
                if self.waited.get((eng, semkey), 0) < v:
                    self.waited[(eng, semkey)] = v
                    waits.append((semkey, v))
            for e in self.ENG:
                if e == eng or self.cnt[e] == 0:
                    continue
                semkey = ("e", e)
                if self.waited.get((eng, semkey), 0) < self.cnt[e]:
                    self.waited[(eng, semkey)] = self.cnt[e]
                    waits.append((semkey, self.cnt[e]))
            if waits:
                self.q[eng].append((waits, None, False))
        self.lastw = {}
        self.readers = {}

    def run(self):
        nc = self.nc
        with contextlib.ExitStack() as st:
            for e in self.ENG:
                self.semh[("e", e)] = st.enter_context(nc.semaphore("s_" + e))
            for qe in self.ENG:
                for i in range(min(self.dma_rr[qe], self.n_dma)):
                    self.semh[("d", qe, i)] = st.enter_context(nc.semaphore("d_%s_%d" % (qe, i)))
            block = st.enter_context(nc.Block())

            def runq(ename, e):
                for waits, fn, signal in self.q[ename]:
                    for semkey, val in waits:
                        e.wait_ge(self.semh[semkey], val)
                    if fn is None:
                        continue
                    ins = fn(e)
                    if signal:
                        ins.then_inc(self.semh[("e", ename)], 1)

            @block.tensor
            def _(e):
                runq("pe", e)

            @block.scalar
            def _(e):
                runq("act", e)

            @block.vector
            def _(e):
                runq("dve", e)

            @block.gpsimd
            def _(e):
                runq("pool", e)

            @block.sync
            def _(e):
                runq("sp", e)


def MM(out, lhsT, rhs, start=True, stop=True):
    return lambda e: e.matmul(out, lhsT=lhsT, rhs=rhs, start=start, stop=stop)


def TR(out, in_, ident):
    return lambda e: e.transpose(out=out, in_=in_, identity=ident)


def ACT(out, in_, func, **kw):
    return lambda e: e.activation(out=out, in_=in_, func=func, **kw)


def TS(out, in0, s1, s2=None, op0=ALU.mult, op1=None):
    if op1 is None:
        return lambda e: e.tensor_scalar(out=out, in0=in0, scalar1=s1, scalar2=None, op0=op0)
    return lambda e: e.tensor_scalar(out=out, in0=in0, scalar1=s1, scalar2=s2, op0=op0, op1=op1)


def TT(out, in0, in1, op):
    return lambda e: e.tensor_tensor(out=out, in0=in0, in1=in1, op=op)


def STT(out, in0, scalar, in1, op0, op1):
    return lambda e: e.scalar_tensor_tensor(out=out, in0=in0, scalar=scalar, in1=in1, op0=op0, op1=op1)


def CP(out, in_):
    return lambda e: e.tensor_copy(out=out, in_=in_)


def MSET(ap, v):
    return lambda e: e.memset(ap, v)


class Arena:
    def __init__(self, nc, nbytes):
        self.t = nc.alloc_sbuf_tensor("arena", [128, nbytes // 4], F32)
        self.nbytes = nbytes
        self.off = 0

    def mark(self):
        return self.off

    def reset(self, m):
        self.off = m

    def alloc(self, free_shape, dtype):
        n = int(np.prod(free_shape))
        sz = 4 if dtype == F32 else 2
        nb = (n * sz + 63) // 64 * 64
        assert self.off + nb <= self.nbytes, ("SBUF arena overflow", self.off, nb, self.nbytes)
        ap = self.t[:, self.off // 4:(self.off + nb) // 4]
        self.off += nb
        if dtype != F32:
            ap = ap.bitcast(dtype)
        ap = ap[:, 0:n]
        if len(free_shape) == 2:
            ap = ap.rearrange("p (a b) -> p a b", a=free_shape[0])
        elif len(free_shape) == 3:
            ap = ap.rearrange("p (a b c) -> p a b c", a=free_shape[0], b=free_shape[1])
        elif len(free_shape) == 4:
            ap = ap.rearrange("p (a b c d) -> p a b c d", a=free_shape[0], b=free_shape[1], c=free_shape[2])
        return ap


def build(SEQ, last_stage=6, debug=False):
    OWN = SEQ // 4
    NOWN = OWN // 128
    NPRE = 4
    NM = NOWN + 2
    NK = NM + 4
    TM = NM * 128
    TK = NK * 128
    NOUT = NOWN + 1
    assert NOWN >= 4 or NOWN == 2

    nc = bass.Bass("TRN2", target_bir_lowering=False)
    P = Prog(nc)

    def din(name, shape, dt=F32):
        return nc.dram_tensor(name, list(shape), dt, kind="ExternalInput").ap()

    def dout(name, shape, dt=F32):
        return nc.dram_tensor(name, list(shape), dt, kind="ExternalOutput").ap()

    def dscr(name, shape, dt):
        if debug:
            return nc.dram_tensor(name, list(shape), dt, kind="ExternalOutput").ap()
        return nc.dram_tensor(name, list(shape), dt).ap()

    x_main = din("x_main", [TM, D])
    x_pre = din("x_pre", [NPRE * 128, D])
    masks = din("masks", [128, 3, NPRE + NM])
    fmask = din("fmask", [128, 2, 8])
    c3T = din("c3T", [128, KT, 3])
    gpreT = din("gpreT", [128, 2, KT])
    adabT = din("adabT", [128, 96])
    adab3 = din("adab3", [3, 12288])
    gpost3 = din("gpost3", [3, 2, D])
    mnormB = din("mnormB", [128, 1024])
    bgate = din("bgate", [128, 8])
    cwT = din("cwT", [128, FT, 4])
    bias_tab = din("bias_tab", [128, 8, 5, 128])
    bias_s4 = din("bias_s4", [128, 8, 128])
    consts = din("consts", [128, 8, 128])
    cache_k = din("cache_k", [2, 512, 1024])
    cache_v = din("cache_v", [2, 512, 1024])
    st_C = din("st_C", [2, 4, 256, 256])
    st_n = din("st_n", [2, 4, 256])
    st_mB = din("st_mB", [128, 2, 4])
    st_m4 = din("st_m4", [4, 2])
    conv_st = din("conv_st", [4, 5632])
    ada_w = din("ada_w", [D, 12288])
    w_in = din("w_in", [D, INC])
    w_out = din("w_out", [D, D])
    w_g = din("w_g", [D, DFF])
    w_u = din("w_u", [D, DFF])
    w_d = din("w_d", [DFF, D])

    y_main = dout("y_main", [NOUT * 128, D])
    k_out = dout("k_out", [5 * 128, 1024])
    v_out = dout("v_out", [5 * 128, 1024])
    C_out = dout("C_out", [3, 4, 256, 256])
    n_out = dout("n_out", [3, 4, 256])
    m_out = dout("m_out", [4, 3])
    conv_out = dout("conv_out", [6, 5632])

    scr_fm = dscr("scr_fm", [32, 128, TK], BF16)
    scr_va = dscr("scr_va", [NK, 128, 8 * 129], BF16)
    scr_tm = dscr("scr_tm", [NM, 128, 3072], BF16)
    scr_g = dscr("scr_g", [NM, 128, 8], F32)
    scr_gtg = nc.dram_tensor("scr_gtg", [3, 2, D], F32).ap()
    cc_in = nc.dram_tensor("cc_in", [128, 2064], F32).ap()
    cc_out = nc.dram_tensor("cc_out", [1024, 2064], F32).ap()
    scr_mixT = dscr("scr_mixT", [NM, 128, KT * 128], BF16)
    scr_x1 = dscr("scr_x1", [NM, 128, D], F32)
    scr_h2T = dscr("scr_h2T", [128, KT, TM], BF16)
    scr_aT = dscr("scr_aT", [FT, 128, TM], BF16)
    dbg = dscr("dbg", [128, 4096], F32) if debug else None

    A = Arena(nc, 206 * 1024)
    ps = [nc.alloc_psum_tensor("ps%d" % b, [128, 512], F32) for b in range(8)]

    def psf(b):
        return ps[b][:]

    def psb(b):
        return ps[b][:].bitcast(BF16)

    PK = ["ps%d" % b for b in range(8)]

    cst = A.alloc([8, 128], F32)
    identf = cst[:, 0, :]
    tri = cst[:, 1, :]
    ind = cst[:, 2:4, :]
    sel = cst[0:3, 4:6, :]
    ind2 = cst[:, 6, 0:2]
    ones4 = cst[0:4, 7, :]
    identb = A.alloc([128], BF16)
    msk = A.alloc([3, NPRE + NM], F32)
    modT = A.alloc([4, KT, 3], F32)
    bgt = A.alloc([8], F32)
    epsc = A.alloc([1], F32)
    onec = A.alloc([1], F32)
    P.dma("sp", cst, consts, writes=["cst"])
    P.dma("sp", msk, masks, writes=["msk"])
    P.dma("sp", bgt, bgate, writes=["bgt"])
    P.op("dve", MSET(epsc, EPS), writes=["epsc"])
    P.op("dve", MSET(onec, 1.0), writes=["onec"])
    P.op("dve", CP(identb, identf), reads=["cst"], writes=["identb"])
    persist_mark = A.mark()

    siluT = A.alloc([KT, 3], BF16)
    gtg = A.alloc([2, D], F32)
    c3s = A.alloc([KT, 3], F32)
    gpre = A.alloc([2, KT], F32)
    adab = A.alloc([96], F32)
    adab3s = A.alloc([2, D], F32)
    gpost = A.alloc([2, D], F32)
    ablk = [A.alloc([KT, 512], BF16) for _ in range(3)]
    P.dma("sp", c3s, c3T, writes=["c3s"])
    P.dma("sp", gpre, gpreT, writes=["gpre"])
    P.dma("sp", adab, adabT, writes=["adab"])
    P.dma("sp", adab3s[0:3, 0, :], adab3[:, 2 * D:3 * D], writes=["adab3s0"])
    P.dma("sp", adab3s[0:3, 1, :], adab3[:, 5 * D:6 * D], writes=["adab3s1"])
    P.dma("sp", gpost[0:3], gpost3, writes=["gpost"])
    P.op("act", ACT(siluT, c3s, AF.Silu), reads=["c3s"], writes=["siluT"])

    nblk = 0
    for v, slot in ((0, 0), (1, 1), (3, 2), (4, 3)):
        for cb4 in range(4):
            bi = nblk % 3
            nblk += 1
            c0 = v * D + cb4 * 512
            P.dma("pool", ablk[bi], ada_w[:, c0:c0 + 512].rearrange("(k p) n -> p k n", p=128), writes=["ablk%d" % bi])
            for sub in range(4):
                kc = cb4 * 4 + sub
                b = kc % 4
                for k in range(KT):
                    P.op("pe", MM(psf(b)[:, 0:3], ablk[bi][:, k, sub * 128:(sub + 1) * 128], siluT[:, k, :], k == 0, k == KT - 1),
                         reads=["ablk%d" % bi, "siluT"], writes=[PK[b]], signal=(k == KT - 1))
                P.op("dve", TS(modT[:, slot, kc, :], psf(b)[:, 0:3], adab[:, v * 16 + kc:v * 16 + kc + 1], None, ALU.add),
                     reads=[PK[b], "adab"], writes=["modT"])
    for slot, gi in ((1, 0), (3, 1)):
        P.op("dve", STT(modT[:, slot], modT[:, slot], 1.0, gpre[:, gi, :].unsqueeze(2).broadcast_to([128, KT, 3]), ALU.add, ALU.mult),
             reads=["modT", "gpre"], writes=["modT"])
    for gi, v in ((0, 2), (1, 5)):
        for cb4 in range(4):
            bi = nblk % 3
            nblk += 1
            c0 = v * D + cb4 * 512
            P.dma("pool", ablk[bi], ada_w[:, c0:c0 + 512].rearrange("(k p) n -> p k n", p=128), writes=["ablk%d" % bi])
            b = 4 + cb4 % 4
            for k in range(KT):
                P.op("pe", MM(psf(b)[0:3, :], siluT[:, k, :], ablk[bi][:, k, :], k == 0, k == KT - 1),
                     reads=["ablk%d" % bi, "siluT"], writes=[PK[b]], signal=(k == KT - 1))
            P.op("dve", TT(gtg[0:3, gi, cb4 * 512:(cb4 + 1) * 512], psf(b)[0:3, :], adab3s[0:3, gi, cb4 * 512:(cb4 + 1) * 512], ALU.add),
                 reads=[PK[b], "adab3s%d" % gi], writes=["gtg%d_%d" % (gi, cb4)])
            P.op("dve", TT(gtg[0:3, gi, cb4 * 512:(cb4 + 1) * 512], gtg[0:3, gi, cb4 * 512:(cb4 + 1) * 512],
                           gpost[0:3, gi, cb4 * 512:(cb4 + 1) * 512], ALU.mult),
                 reads=["gpost", "gtg%d_%d" % (gi, cb4)], writes=["gtg%d_%d" % (gi, cb4)])
    GTG_KEYS = [["gtg%d_%d" % (gi, c) for c in range(4)] for gi in range(2)]
    for gi in range(2):
        P.dma("sp", scr_gtg[:, gi, :], gtg[0:3, gi, :], reads=GTG_KEYS[gi], writes=["scr_gtg"])

    def build_gtgB(dst_p, dst_s, gi, keyp, keys_):
        tmp = A.alloc([D], F32)
        P.dma("sp", tmp[0:3, :], scr_gtg[:, gi, :], writes=["gtgtmp"])
        for vi, dst, key in ((0, dst_p, keyp), (1, dst_s, keys_)):
            for cb4 in range(4):
                b = cb4
                P.op("pe", MM(psf(b), sel[:, vi, :], tmp[0:3, cb4 * 512:(cb4 + 1) * 512]),
                     reads=["cst", "gtgtmp"], writes=[PK[b]])
                P.op("act", ACT(dst[:, cb4 * 512:(cb4 + 1) * 512], psf(b), AF.Copy), reads=[PK[b]], writes=[key])

    P.barrier()
    A.reset(persist_mark)
    if last_stage == 0:
        if debug:
            P.dma("sp", dbg[:, 0:192], modT.rearrange("p a b c -> p (a b c)"), reads=["modT"], writes=["dbg"])
            P.dma("sp", dbg[0:3, 192:192 + 4096 - 192], gtg[0:3].rearrange("p a b -> p (a b)")[:, 0:4096 - 192], reads=sum(GTG_KEYS, []), writes=["dbg2"])
        P.barrier()
        P.run()
        return nc

    def rstd_from_ss(ss_ap, out_ap, n, keys_in, key_out):
        P.op("act", ACT(out_ap, ss_ap, AF.Ln, scale=1.0 / n, bias=epsc[:, 0:1]), reads=keys_in + ["epsc"], writes=[key_out])
        P.op("act", ACT(out_ap, out_ap, AF.Exp, scale=-0.5), reads=[key_out], writes=[key_out])

    fe = {}

    def fe_alloc(with_xs=True, nxs=2):
        if with_xs:
            fe["xs"] = [A.alloc([D], F32) for _ in range(nxs)]
        fe["xn"] = [A.alloc([D], BF16) for _ in range(2)]
        fe["junk"] = A.alloc([D], BF16)
        fe["ss"] = [A.alloc([2], F32) for _ in range(2)]
        fe["n"] = 0

    def front_end(x_src, dst, dst_key, slot_a, slot_sh, rows, tb=(0, 1), xs_in=None, xs_keys=None):
        i = fe["n"] % 2
        fe["n"] += 1
        xn, ss = fe["xn"][i], fe["ss"][i]
        kx, kn, ks = "fe_xs%d" % i, "fe_xn%d" % i, "fe_ss%d" % i
        if xs_in is None:
            xs = fe["xs"][i % len(fe["xs"])]
            kx = "fe_xs%d" % (i % len(fe["xs"]))
            P.dma("sp", xs, x_src, writes=[kx])
            kxl = [kx]
        else:
            xs, kxl = xs_in, list(xs_keys)
        P.op("act", ACT(fe["junk"], xs, AF.Square, accum_out=ss[:, 0:1]), reads=kxl, writes=["fe_junk", ks])
        rstd_from_ss(ss[:, 0:1], ss[:, 1:2], D, [ks], ks)
        P.op("dve", TS(xn, xs, ss[:, 1:2]), reads=kxl + [ks], writes=[kn])
        for half in range(2):
            b = tb[half]
            pt = psb(b).rearrange("p (k t) -> p k t", k=8)
            for k8 in range(8):
                k = half * 8 + k8
                P.op("pe", TR(pt[:, k8, :], xn[:, k * 128:(k + 1) * 128], identb), reads=[kn, "identb"], writes=[PK[b]],
                     signal=(k8 == 7))
            for k8 in range(8):
                k = half * 8 + k8
                eng = "act" if half == 0 else "dve"
                if rows[0] == rows[1]:
                    segs = [(0, 128, rows[0])]
                else:
                    segs = [(0, 64, rows[0]), (64, 128, rows[1])]
                for (t0, t1, r) in segs:
                    a_ap = modT[:, slot_a, k, r:r + 1]
                    s_ap = modT[:, slot_sh, k, r:r + 1]
                    if eng == "act":
                        P.op("act", ACT(dst[:, k, t0:t1], pt[:, k8, t0:t1], AF.Identity, scale=a_ap, bias=s_ap),
                             reads=[PK[b], "modT"], writes=["%s_k%d_%d" % (dst_key, k, t0)])
                    else:
                        P.op("dve", TS(dst[:, k, t0:t1], pt[:, k8, t0:t1], a_ap, s_ap, ALU.mult, ALU.add),
                             reads=[PK[b], "modT"], writes=["%s_k%d_%d" % (dst_key, k, t0)])

    gm = {}

    def gate_alloc():
        gm["gsb"] = [A.alloc([8], F32) for _ in range(2)]
        gm["lfm"] = [A.alloc([4], F32) for _ in range(2)]
        gm["pre"] = [A.alloc([16], F32) for _ in range(2)]
        gm["ex"] = [A.alloc([16], F32) for _ in range(2)]
        gm["gbf"] = [A.alloc([4], BF16) for _ in range(2)]
        gm["amax"] = [A.alloc([2], F32) for _ in range(2)]
        gm["B4"] = [A.alloc([2], F32) for _ in range(2)]
        gm["t4"] = A.alloc([2], F32)
        gm["n"] = 0

    def gate_math(g_src, g_keys, mcol, gb, gb2):
        i = gm["n"] % 2
        gm["n"] += 1
        gsb, lfm, pre, ex, gbf, amax, B4 = (gm[k][i] for k in ("gsb", "lfm", "pre", "ex", "gbf", "amax", "B4"))
        K = "gm%d_" % i
        P.op("dve", TT(gsb, g_src, bgt, ALU.add), reads=g_keys + ["bgt"], writes=[K + "gsb"])
        P.op("act", ACT(lfm, gsb[:, 4:8], AF.Exp, scale=-1.0), reads=[K + "gsb"], writes=[K + "lfm"])
        P.op("act", ACT(lfm, lfm, AF.Ln, bias=onec[:, 0:1]), reads=[K + "lfm", "onec"], writes=[K + "lfm"])
        P.op("dve", TS(lfm, lfm, msk[:, 1, mcol:mcol + 1]), reads=[K + "lfm", "msk"], writes=[K + "lfm"])
        g0 = psf(gb)
        P.op("pe", MM(g0[:, 0:4], tri, lfm), reads=["cst", K + "lfm"], writes=[PK[gb]], signal=False)
        P.op("pe", MM(g0[:, 4:8], ind[:, 0, :], lfm), reads=["cst", K + "lfm"], writes=[PK[gb]], signal=False)
        P.op("pe", MM(g0[:, 8:12], ind[:, 1, :], lfm), reads=["cst", K + "lfm"], writes=[PK[gb]], signal=False)
        P.op("pe", MM(g0[0:4, 12:14], lfm, ind2), reads=["cst", K + "lfm"], writes=[PK[gb]])
        P.op("dve", CP(pre[:, 4:16], g0[:, 0:12]), reads=[PK[gb]], writes=[K + "pre"])
        P.op("dve", CP(B4[0:4, :], g0[0:4, 12:14]), reads=[PK[gb]], writes=[K + "B4"])
        P.op("dve", TT(pre[:, 0:4], gsb[:, 0:4], pre[:, 4:8], ALU.subtract), reads=[K + "gsb", K + "pre"], writes=[K + "pre"])
        P.op("dve", TS(pre[:, 0:4], pre[:, 0:4], msk[:, 0, mcol:mcol + 1], msk[:, 2, mcol:mcol + 1], ALU.mult, ALU.add),
             reads=[K + "pre", "msk"], writes=[K + "pre"])
        P.op("act", ACT(ex, pre, AF.Exp), reads=[K + "pre"], writes=[K + "ex"])
        P.op("dve", CP(gbf, ex[:, 0:4]), reads=[K + "ex"], writes=[K + "gbf"])
        g1 = psf(gb2)
        P.op("pe", TR(g1[0:4, 128:256], pre[:, 0:4], identf), reads=[K + "pre", "cst"], writes=[PK[gb2]])
        P.op("dve", lambda e, o=amax[0:4, :], s=g1[0:4, 128:256].rearrange("p (c t) -> p c t", c=2):
             e.tensor_reduce(out=o, in_=s, axis=AX.X, op=ALU.max), reads=[PK[gb2]], writes=[K + "amax"])
        return dict(ex=ex, gbf=gbf, amax=amax, B4=B4, K=K, pre=pre)

    def m_update(mt, mkey, G, c):
        K = G["K"]
        t4 = gm["t4"]
        P.op("dve", TT(t4[0:4, 0:1], G["amax"][0:4, c:c + 1], G["B4"][0:4, c:c + 1], ALU.add),
             reads=[K + "amax", K + "B4"], writes=["t4"])
        P.op("dve", TT(mt, mt, G["B4"][0:4, c:c + 1], ALU.add), reads=[mkey, K + "B4"], writes=[mkey])
        P.op("dve", TT(mt, mt, t4[0:4, 0:1], ALU.max), reads=[mkey, "t4"], writes=[mkey])

    def state_update(Sf, Sf_key, nf, nf_key, DSb, G, c, h, km_ap, km_key, vp_ap, vp_key, Sb=None, Sb_key=None, nb=None, nb_key=None,
                     S_src=None, S_src_key=None, n_src=None, n_src_key=None):
        K = G["K"]
        if S_src is None:
            S_src, S_src_key, n_src, n_src_key = Sf, Sf_key, nf, nf_key
        r0, r1 = 64 * c, 64 * c + 64
        eB = G["ex"][:, 8 + 4 * c + h:8 + 4 * c + h + 1]
        DS = psf(DSb)
        for d in range(2):
            P.op("pe", MM(DS[:, d * 256:(d + 1) * 256], km_ap[r0:r1, h * 256 + d * 128:h * 256 + (d + 1) * 128], vp_ap[r0:r1, h, :]),
                 reads=[km_key, vp_key], writes=[PK[DSb]], signal=(d == 1))
        Sh = Sf[:, h].rearrange("p d e -> p (d e)")
        Ssh = S_src[:, h].rearrange("p d e -> p (d e)")
        P.op("act", ACT(Sh, Ssh, AF.Copy, scale=eB), reads=[S_src_key + str(h), K + "ex"], writes=[Sf_key + str(h)])
        P.op("dve", STT(Sh, DS, eB, Sh, ALU.mult, ALU.add), reads=[PK[DSb], K + "ex", Sf_key + str(h)], writes=[Sf_key + str(h)])
        if Sb is not None:
            P.op("pool", CP(Sb[:, h].rearrange("p d e -> p (d e)"), Sh), reads=[Sf_key + str(h)], writes=[Sb_key + str(h)])

    def n_update(nf, nf_key, NBb, G, c, km_ap, km_key, nb=None, nb_key=None, n_src=None, n_src_key=None):
        K = G["K"]
        if n_src is None:
            n_src, n_src_key = nf, nf_key
        r0, r1 = 64 * c, 64 * c + 64
        NB = psf(NBb)
        for h in range(4):
            for d in range(2):
                P.op("pe", MM(NB[:, 256 + h * 2 + d:256 + h * 2 + d + 1], km_ap[r0:r1, h * 256 + d * 128:h * 256 + (d + 1) * 128],
                              G["gbf"][r0:r1, h:h + 1]),
                     reads=[km_key, K + "gbf"], writes=[PK[NBb]], signal=(h == 3 and d == 1))
        eB4 = G["ex"][:, 8 + 4 * c:12 + 4 * c].unsqueeze(2).broadcast_to([128, 4, 2])
        P.op("dve", TT(nf, n_src, NB[:, 256:264].rearrange("p (h d) -> p h d", h=4), ALU.add),
             reads=[n_src_key, PK[NBb]], writes=[nf_key])
        P.op("dve", TT(nf, nf, eB4, ALU.mult), reads=[nf_key, K + "ex"], writes=[nf_key])
        if nb is not None:
            P.op("dve", CP(nb, nf), reads=[nf_key], writes=[nb_key])

    base_mark = A.mark()
    SX = A.alloc([4, 2, 256], F32)
    nX = A.alloc([4, 2], F32)
    mP = A.alloc([2], F32)
    P.op("pool", MSET(SX, 0.0), writes=["SX%d" % h for h in range(4)])
    P.op("pool", MSET(nX, 0.0), writes=["nX"])
    P.op("pool", MSET(mP, 0.0), writes=["mP"])
    persist_mark = A.mark()

    hT_halo = A.alloc([KT, 512], BF16)
    s1_mark = A.mark()

    hT_main = A.alloc([KT, TM], BF16)

    def hTt(u):
        return hT_halo[:, :, u * 128:(u + 1) * 128] if u < 4 else hT_main[:, :, (u - 4) * 128:(u - 3) * 128]

    def hTr(k, t0, t1):
        return hT_halo[:, k, t0:t1] if t1 <= 512 else hT_main[:, k, t0 - 512:t1 - 512]

    fe_alloc(nxs=2)
    wblk = [A.alloc([KT, 512], BF16) for _ in range(3)]
    wg8 = A.alloc([KT, 8], BF16)
    fmst = [A.alloc([TK], BF16) for _ in range(2)]
    tmst = [A.alloc([516], BF16) for _ in range(4)]
    f32st = [A.alloc([512], F32) for _ in range(3)]
    gst = [A.alloc([8], F32) for _ in range(2)]
    cnt = {"wb": 0, "bank": 0, "ev": 0, "fm": 0, "tm": 0, "f32": 0, "g": 0}

    def hkeys(u, k):
        if u == NK - 1:
            return ["hT_u%d_k%d_0" % (u, k), "hT_u%d_k%d_64" % (u, k)]
        return ["hT_u%d_k%d_0" % (u, k)]

    for u in range(4):
        front_end(x_pre[u * 128:(u + 1) * 128, :], hTt(u), "hT_u%d" % u, 1, 0, (0, 0), tb=(0, 1))
    for i in range(NM):
        u = 4 + i
        rows = (0, 0) if i < NM - 1 else (1, 2)
        front_end(x_main[i * 128:(i + 1) * 128, :], hTt(u), "hT_u%d" % u, 1, 0, rows, tb=(0, 1))

    def load_wblk(col0):
        bi = cnt["wb"] % 3
        cnt["wb"] += 1
        P.dma("pool", wblk[bi], w_in[:, col0:col0 + 512].rearrange("(k p) n -> p k n", p=128), writes=["wblk%d" % bi])
        return wblk[bi], "wblk%d" % bi

    def next_bank():
        b = 2 + cnt["bank"] % 6
        cnt["bank"] += 1
        return b

    def ev_eng():
        cnt["ev"] += 1
        return "act" if cnt["ev"] % 2 == 0 else "dve"

    def evac(out, in_, keys_r, keys_w, scale=None, eng=None):
        eng = eng or ev_eng()
        if eng == "act":
            if scale is None:
                P.op("act", ACT(out, in_, AF.Copy), reads=keys_r, writes=keys_w)
            else:
                P.op("act", ACT(out, in_, AF.Copy, scale=scale), reads=keys_r, writes=keys_w)
        else:
            if scale is None:
                P.op("dve", CP(out, in_), reads=keys_r, writes=keys_w)
            else:
                P.op("dve", TS(out, in_, scale), reads=keys_r, writes=keys_w)

    fm_blocks = [(0, 128 ** -0.5, 4), (512, 128 ** -0.5, 4), (1024, None, 0), (1536, None, 0),
                 (3072, None, 4), (3584, None, 4), (4096, 0.0625, 4), (4608, 0.0625, 4)]
    for blk, (col0, scale, u_lo) in enumerate(fm_blocks):
        wb, wkey = load_wblk(col0)
        t_lo = u_lo * 128
        for sub in range(4):
            cb = blk * 4 + sub
            fi = cnt["fm"] % 2
            cnt["fm"] += 1
            stkeys = []
            for tbi, t0 in enumerate(range(t_lo, TK, 512)):
                t1 = min(t0 + 512, TK)
                b = next_bank()
                for k in range(KT):
                    rk = [wkey]
                    for u in range(t0 // 128, t1 // 128):
                        rk += hkeys(u, k)
                    P.op("pe", MM(psf(b)[:, 0:t1 - t0], wb[:, k, sub * 128:(sub + 1) * 128], hTr(k, t0, t1), k == 0, k == KT - 1),
                         reads=rk, writes=[PK[b]], signal=(k == KT - 1))
                key = "fmst%d_%d" % (fi, tbi)
                stkeys.append(key)
                evac(fmst[fi][:, t0:t1], psf(b)[:, 0:t1 - t0], [PK[b]], [key], scale)
            P.dma("sp", scr_fm[cb, :, t_lo:TK], fmst[fi][:, t_lo:TK], reads=stkeys, writes=["scr_fm"])

    def out_index(i):
        if NOWN - 3 <= i <= NOWN:
            return i - (NOWN - 3)
        if i == NM - 1:
            return 4
        return None

    def mcol_of(u):
        return NPRE - 4 + u if u < 4 else NPRE + (u - 4)

    for wbi in range(2):
        wb, wkey = load_wblk(2048 + wbi * 512)
        for u in range(NK):
            b = next_bank()
            for k in range(KT):
                P.op("pe", MM(psf(b), hTt(u)[:, k, :], wb[:, k, :], k == 0, k == KT - 1),
                     reads=[wkey] + hkeys(u, k), writes=[PK[b]], signal=(k == KT - 1))
            ti = cnt["tm"] % 4
            cnt["tm"] += 1
            st = tmst[ti].rearrange("p (h e) -> p h e", h=4)
            mc = mcol_of(u)
            P.op("dve", TS(st[:, :, 0:128], psf(b).rearrange("p (h e) -> p h e", h=4), msk[:, 0, mc:mc + 1]),
                 reads=[PK[b], "msk"], writes=["tmst%d" % ti])
            P.op("pool", CP(st[:, :, 128], msk[:, 0, mc:mc + 1].broadcast_to([128, 4])), reads=["msk"], writes=["tmst%dx" % ti])
            P.dma("sp", scr_va[u, :, wbi * 516:(wbi + 1) * 516], tmst[ti], reads=["tmst%d" % ti, "tmst%dx" % ti], writes=["scr_va"])
            oi = out_index(u - 4) if u >= 4 else None
            if oi is not None:
                fi = cnt["f32"] % 3
                cnt["f32"] += 1
                P.op("act", ACT(f32st[fi], psf(b), AF.Copy), reads=[PK[b]], writes=["f32st%d" % fi])
                P.dma("sp", v_out[oi * 128:(oi + 1) * 128, wbi * 512:(wbi + 1) * 512], f32st[fi], reads=["f32st%d" % fi], writes=["v_out"])
    for (col0, off, scale) in ((4096, 0, 0.0625), (5120, 1024, None), (6144, 2048, None)):
        for wbi in range(2):
            wb, wkey = load_wblk(col0 + wbi * 512)
            for i in range(NM):
                u = 4 + i
                b = next_bank()
                for k in range(KT):
                    P.op("pe", MM(psf(b), hTt(u)[:, k, :], wb[:, k, :], k == 0, k == KT - 1),
                         reads=[wkey] + hkeys(u, k), writes=[PK[b]], signal=(k == KT - 1))
                ti = cnt["tm"] % 4
                cnt["tm"] += 1
                evac(tmst[ti][:, 0:512], psf(b), [PK[b]], ["tmst%d" % ti], scale)
                P.dma("sp", scr_tm[i, :, off + wbi * 512:off + (wbi + 1) * 512], tmst[ti][:, 0:512], reads=["tmst%d" % ti], writes=["scr_tm"])
    P.dma("pool", wg8, w_in[:, 7168:7176].rearrange("(k p) n -> p k n", p=128), writes=["wg8"])
    for i in range(NM):
        u = 4 + i
        b = next_bank()
        for k in range(KT):
            P.op("pe", MM(psf(b)[:, 0:8], hTt(u)[:, k, :], wg8[:, k, :], k == 0, k == KT - 1),
                 reads=["wg8"] + hkeys(u, k), writes=[PK[b]], signal=(k == KT - 1))
        gi = cnt["g"] % 2
        cnt["g"] += 1
        P.op("dve", CP(gst[gi], psf(b)[:, 0:8]), reads=[PK[b]], writes=["gst%d" % gi])
        P.dma("sp", scr_g[i], gst[gi], reads=["gst%d" % gi], writes=["scr_g"])
    for wbi in range(2):
        wb, wkey = load_wblk(1024 + wbi * 512)
        for i in range(NM):
            oi = out_index(i)
            if oi is None:
                continue
            u = 4 + i
            b = next_bank()
            for k in range(KT):
                P.op("pe", MM(psf(b), hTt(u)[:, k, :], wb[:, k, :], k == 0, k == KT - 1),
                     reads=[wkey] + hkeys(u, k), writes=[PK[b]], signal=(k == KT - 1))
            fi = cnt["f32"] % 3
            cnt["f32"] += 1
            P.op("act", ACT(f32st[fi], psf(b), AF.Copy), reads=[PK[b]], writes=["f32st%d" % fi])
            P.dma("sp", k_out[oi * 128:(oi + 1) * 128, wbi * 512:(wbi + 1) * 512], f32st[fi], reads=["f32st%d" % fi], writes=["k_out"])
    P.barrier()
    A.reset(persist_mark)
    if last_stage == 2:
        P.run()
        return nc

    gate_alloc()
    pay = A.alloc([2064], F32)
    Sl = pay[:, 0:2048].rearrange("p (h d e) -> p h d e", h=4, d=2)
    nl = pay[:, 2048:2056].rearrange("p (h d) -> p h d", h=4)
    Bt = pay[:, 2056:2060]
    ab = pay[:, 2060:2064]
    ml = A.alloc([2], F32)
    d4l = A.alloc([4], F32)
    fm_ = A.alloc([2, 8], F32)
    P.dma("sp", fm_, fmask, writes=["fm_"])
    P.op("pool", MSET(pay, 0.0), writes=["SL%d" % h for h in range(4)] + ["nL", "BtL", "abL"])
    P.op("pool", MSET(ml, -1e30), writes=["mL"])
    tbl = [A.alloc([2048], BF16) for _ in range(2)]
    gbl = [A.alloc([8], F32) for _ in range(2)]
    vpl = [A.alloc([4, 256], BF16) for _ in range(2)]
    payq = [A.alloc([2064], F32) for _ in range(2)]
    eq = A.alloc([8], F32)
    mB = A.alloc([4], F32)
    tq = A.alloc([4], F32)
    P.op("pool", MSET(mB, 0.0), writes=["mB"])

    def loadL(i):
        P.dma("sp", tbl[i % 2], scr_tm[i, :, 0:2048], writes=["tbl%d" % (i % 2)])
        P.dma("sp", gbl[i % 2], scr_g[i], writes=["gbl%d" % (i % 2)])

    loadL(0)
    for i in range(NOWN):
        i2 = i % 2
        if i + 1 < NOWN:
            loadL(i + 1)
        tk = "tbl%d" % i2
        km_tm = tbl[i2][:, 0:1024]
        vm_tm = tbl[i2][:, 1024:2048].rearrange("p (h e) -> p h e", h=4)
        G = gate_math(gbl[i2], ["gbl%d" % i2], NPRE + i, 6, 7)
        K = G["K"]
        P.op("dve", TT(vpl[i2], vm_tm, G["ex"][:, 0:4].unsqueeze(2).broadcast_to([128, 4, 256]), ALU.mult),
             reads=[tk, K + "ex"], writes=["vpl%d" % i2])
        P.op("dve", TT(Bt, Bt, G["pre"][:, 8:12], ALU.add), reads=["BtL", K + "pre"], writes=["BtL"])
        P.op("dve", TT(Bt, Bt, G["pre"][:, 12:16], ALU.add), reads=["BtL", K + "pre"], writes=["BtL"])
        for c in range(2):
            n_update(nl, "nL", 7, G, c, km_tm, tk)
            for h in range(4):
                state_update(Sl, "SL", None, None, h, G, c, h, km_tm, tk, vpl[i2], "vpl%d" % i2)
            m_update(ml[0:4, 0:1], "mL", G, c)
    P.op("dve", TS(d4l[0:4, 0:4], identf[0:4, 0:4], ml[0:4, 0:1]), reads=["mL", "cst"], writes=["d4l"])
    P.op("pe", MM(psf(6)[:, 0:4], ones4, d4l[0:4, 0:4]), reads=["cst", "d4l"], writes=[PK[6]])
    P.op("dve", CP(ab, psf(6)[:, 0:4]), reads=[PK[6]], writes=["abL"])
    P.dma("sp", cc_in, pay, reads=["SL%d" % h for h in range(4)] + ["nL", "BtL", "abL"], writes=["cc_in"])
    P.dma("pool", None, None, reads=["cc_in"], writes=["cc_out"], inc=1,
          custom=lambda e: e.collective_compute("AllGather", ALU.bypass, replica_groups=[list(range(8))], ins=[cc_in], outs=[cc_out]))
    for q in range(8):
        pq = payq[q % 2]
        pk = "payq%d" % (q % 2)
        P.dma("sp", pq, cc_out[q * 128:(q + 1) * 128, :], reads=["cc_out"], writes=[pk])
        act_q = fm_[:, 0, q:q + 1]
        P.op("dve", TS(eq[:, 0:4], pq[:, 2056:2060], act_q), reads=[pk, "fm_"], writes=["eq"])
        P.op("act", ACT(eq[:, 4:8], eq[:, 0:4], AF.Exp), reads=["eq"], writes=["eqe"])
        Sq = pq[:, 0:2048].rearrange("p (h de) -> p h de", h=4)
        for h in range(4):
            Sh = SX[:, h].rearrange("p d e -> p (d e)")
            P.op("act", ACT(Sh, Sh, AF.Copy, scale=eq[:, 4 + h:5 + h]), reads=["SX%d" % h, "eqe"], writes=["SX%d" % h])
            P.op("dve", STT(Sh, Sq[:, h, :], act_q, Sh, ALU.mult, ALU.add), reads=[pk, "fm_", "SX%d" % h], writes=["SX%d" % h])
        P.op("dve", TT(nX, nX, eq[:, 4:8].unsqueeze(2).broadcast_to([128, 4, 2]), ALU.mult), reads=["nX", "eqe"], writes=["nX"])
        P.op("dve", STT(nX, pq[:, 2048:2056].rearrange("p (h d) -> p h d", h=4), act_q, nX, ALU.mult, ALU.add), reads=[pk, "fm_", "nX"], writes=["nX"])
        P.op("dve", TT(mB, mB, eq[:, 0:4], ALU.add), reads=["mB", "eq"], writes=["mB"])
        P.op("dve", TS(tq, pq[:, 2060:2064], act_q, fm_[:, 1, q:q + 1], ALU.mult, ALU.add), reads=[pk, "fm_"], writes=["tq"])
        P.op("dve", TT(mB, mB, tq, ALU.max), reads=["mB", "tq"], writes=["mB"])
    P.op("pe", TR(psf(7)[0:4, 0:128], mB, identf), reads=["mB", "cst"], writes=[PK[7]])
    P.op("dve", CP(mP[0:4, 0:1], psf(7)[0:4, 0:1]), reads=[PK[7]], writes=["mP"])
    P.barrier()
    A.reset(persist_mark)

    gate_alloc()
    RS = 6
    mnb = A.alloc([1024], F32)
    btab = A.alloc([8, 5, 128], BF16)
    bs4 = A.alloc([8, 128], BF16)
    P.dma("sp", mnb, mnormB, writes=["mnb"])
    P.dma("pool", btab, bias_tab, writes=["btab"])
    P.dma("pool", bs4, bias_s4, writes=["bs4"])
    kring = [A.alloc([8, 128], BF16) for _ in range(RS)]
    vring = [A.alloc([8, 129], BF16) for _ in range(RS)]
    qbuf = [A.alloc([8, 128], BF16) for _ in range(2)]
    mbuf = [A.alloc([16, 128], BF16) for _ in range(2)]
    tbuf = [A.alloc([3072], BF16) for _ in range(2)]
    gbuf = [A.alloc([8], F32) for _ in range(2)]
    expT = [A.alloc([5, 128], BF16) for _ in range(2)]
    es0 = [A.alloc([4, 128], BF16) for _ in range(2)]
    es1 = [A.alloc([4, 128], BF16) for _ in range(2)]
    vp = [A.alloc([4, 256], BF16) for _ in range(2)]
    qz = [[A.alloc([8, 128], BF16) for _c in range(2)] for _ in range(2)]
    qkm = [A.alloc([4, 128], BF16) for _ in range(2)]
    mix = [A.alloc([2048], BF16) for _ in range(2)]
    mixTs = [A.alloc([KT, 128], BF16) for _ in range(2)]
    gsn = A.alloc([1024], F32)
    SY = A.alloc([4, 2, 256], F32)
    SXb = A.alloc([4, 2, 256], BF16)
    SYb = A.alloc([4, 2, 256], BF16)
    nY = A.alloc([4, 2], F32)
    nXb = A.alloc([4, 2], BF16)
    nYb = A.alloc([4, 2], BF16)
    rden = A.alloc([8], F32)
    dd = A.alloc([4], F32)
    ddn = A.alloc([4], F32)
    ssh = A.alloc([8], F32)
    scl = A.alloc([4], F32)
    hjunk = A.alloc([256], BF16)
    ms = A.alloc([2], F32)
    mcolt = A.alloc([4], F32)
    d4 = A.alloc([4], F32)
    emn = A.alloc([4], F32)
    em_in = A.alloc([2, 4], F32)
    Cst = A.alloc([4, 2, 256], F32)
    nst = A.alloc([4, 2], F32)
    nst8 = A.alloc([128], F32)
    nld = A.alloc([128], F32)
    kcT = [A.alloc([8, 512], BF16) for _ in range(2)]
    vc = [A.alloc([4, 8, 129], BF16) for _ in range(2)]
    kld = [A.alloc([1024], BF16) for _ in range(2)]
    SXK = ["SX%d" % h for h in range(4)]
    SYK = ["SY%d" % h for h in range(4)]
    for h in range(4):
        P.op("pool", CP(SXb[:, h], SX[:, h]), reads=["SX%d" % h], writes=["SXb%d" % h])
    P.op("pool", CP(nXb, nX), reads=["nX"], writes=["nXb"])
    for c in range(2):
        for i2 in range(2):
            P.op("pool", MSET(qz[i2][c], 0.0), writes=["qz%d_%d" % (i2, c)])
            P.op("pool", MSET(es0[i2], 0.0), writes=["es0_%d" % i2])
            P.op("pool", MSET(es1[i2], 0.0), writes=["es1_%d" % i2])

    def load_kv(u):
        s = u % RS
        P.dma("sp", kring[s], scr_fm[8:16, :, u * 128:(u + 1) * 128].rearrange("c p t -> p c t"), writes=["kring%d" % s])
        P.dma("sp", vring[s], scr_va[u].rearrange("p (h e) -> p h e", h=8), writes=["vring%d" % s])

    for u in range(4):
        load_kv(u)

    def emit_state(w, Sf, Skeys, nf, nkey, mt, mkey):
        P.op("dve", CP(mcolt[0:4, w:w + 1], mt), reads=[mkey], writes=["mcolt"])
        P.op("dve", TS(d4[0:4, 0:4], identf[0:4, 0:4], mt), reads=[mkey, "cst"], writes=["d4"])
        P.op("pe", MM(psf(1)[:, 0:4], ones4, d4[0:4, 0:4]), reads=["cst", "d4"], writes=[PK[1]])
        P.op("act", ACT(emn, psf(1)[:, 0:4], AF.Exp, scale=-1.0), reads=[PK[1]], writes=["emn"])
        for h in range(4):
            P.op("dve", TS(Cst[:, h].rearrange("p d e -> p (d e)"), Sf[:, h].rearrange("p d e -> p (d e)"), emn[:, h:h + 1]),
                 reads=[Skeys[h], "emn"], writes=["Cst"])
        P.dma("sp", C_out[w].rearrange("h (d p) e -> p h d e", p=128), Cst, reads=["Cst"], writes=["C_out"])
        P.op("dve", TT(nst, nf, emn.unsqueeze(2).broadcast_to([128, 4, 2]), ALU.mult), reads=[nkey, "emn"], writes=["nst"])
        P.op("pe", TR(psf(1)[0:8, 128:256], nst.rearrange("p h d -> p (h d)"), identf), reads=["nst", "cst"], writes=[PK[1]])
        P.op("dve", CP(nst8[0:8, :], psf(1)[0:8, 128:256]), reads=[PK[1]], writes=["nst8"])
        P.dma("sp", n_out[w].rearrange("h (d p) -> (h d) p", p=128), nst8[0:8, :], reads=["nst8"], writes=["n_out"])

    def load3(i):
        u = 4 + i
        i2 = i % 2
        load_kv(u)
        P.dma("sp", qbuf[i2], scr_fm[0:8, :, u * 128:(u + 1) * 128].rearrange("c p t -> p c t"), writes=["qbuf%d" % i2])
        P.dma("sp", mbuf[i2], scr_fm[16:32, :, u * 128:(u + 1) * 128].rearrange("c p t -> p c t"), writes=["mbuf%d" % i2])
        P.dma("sp", tbuf[i2], scr_tm[i], writes=["tbuf%d" % i2])
        P.dma("sp", gbuf[i2], scr_g[i], writes=["gbuf%d" % i2])

    GL = {}

    def views(i):
        i2 = i % 2
        return dict(i2=i2, u=4 + i, sample=(i == NM - 1), slot=(4 + i) % RS,
                    kmT=mbuf[i2][:, 8:16, :], qmT=mbuf[i2][:, 0:8, :], km_tm=tbuf[i2][:, 0:1024],
                    vm_tm=tbuf[i2][:, 1024:2048].rearrange("p (h e) -> p h e", h=4), om_tm=tbuf[i2][:, 2048:3072],
                    tk="tbuf%d" % i2, mk="mbuf%d" % i2, qk="qbuf%d" % i2)

    def sample_setup():
        for seq in range(2):
            for kt in range(4):
                li = (seq * 4 + kt) % 2
                P.dma("pool", kld[li], cache_k[seq, kt * 128:(kt + 1) * 128, :], writes=["kld%d" % li])
                for hh2 in range(2):
                    b = 4 + hh2
                    pt = psb(b).rearrange("p (k t) -> p k t", k=8)
                    for h4 in range(4):
                        h = hh2 * 4 + h4
                        P.op("pe", TR(pt[:, h4, :], kld[li][:, h * 128:(h + 1) * 128], identb), reads=["kld%d" % li, "identb"],
                             writes=[PK[b]], signal=(h4 == 3))
                    evac(kcT[seq][:, hh2 * 4:(hh2 + 1) * 4, kt * 128:(kt + 1) * 128], pt[:, 0:4, :], [PK[b]], ["kcT%d" % seq],
                         eng=("act" if hh2 == 0 else "dve"))
            for kt in range(4):
                P.dma("pool", vc[seq][:, kt, :, 0:128], cache_v[seq, kt * 128:(kt + 1) * 128, :].rearrange("p (h e) -> p h e", h=8),
                      writes=["vc%d_%d" % (seq, kt)])
            P.op("pool", MSET(vc[seq][:, :, :, 128], 1.0), writes=["vc%dx" % seq])
        P.dma("sp", em_in, st_mB, writes=["em_in"])
        P.op("act", ACT(em_in, em_in, AF.Exp), reads=["em_in"], writes=["em_in"])
        P.dma("sp", ms[0:4, 0:2], st_m4, writes=["ms0", "ms1"])
        for seq, (Sf, SK, Sb_, SbK, nf, nK, nb_, nbK) in enumerate(((SX, "SX", SXb, "SXb", nX, "nX", nXb, "nXb"),
                                                                     (SY, "SY", SYb, "SYb", nY, "nY", nYb, "nYb"))):
            P.dma("sp", Sf, st_C[seq].rearrange("h (d p) e -> p h d e", p=128), writes=[SK + str(h) for h in range(4)])
            P.dma("sp", nld[0:8, :], st_n[seq].rearrange("h (d p) -> (h d) p", p=128), writes=["nld"])
            P.op("pe", TR(psf(1)[:, 0:8], nld[0:8, :], identf[0:8, 0:8]), reads=["nld", "cst"], writes=[PK[1]])
            P.op("dve", TT(nf, psf(1)[:, 0:8].rearrange("p (h d) -> p h d", h=4), em_in[:, seq, :].unsqueeze(2).broadcast_to([128, 4, 2]), ALU.mult),
                 reads=[PK[1], "em_in"], writes=[nK])
            P.op("dve", CP(nb_, nf), reads=[nK], writes=[nbK])
            for h in range(4):
                P.op("dve", TS(Sf[:, h].rearrange("p d e -> p (d e)"), Sf[:, h].rearrange("p d e -> p (d e)"), em_in[:, seq, h:h + 1]),
                     reads=[SK + str(h), "em_in"], writes=[SK + str(h)])
                P.op("pool", CP(Sb_[:, h], Sf[:, h]), reads=[SK + str(h)], writes=[SbK + str(h)])

    def PA(i):
        V = views(i)
        i2 = V["i2"]
        G = gate_math(gbuf[i2], ["gbuf%d" % i2], NPRE + i, 1, 1)
        GL[i] = G
        K = G["K"]
        P.op("dve", TT(vp[i2], V["vm_tm"], G["ex"][:, 0:4].unsqueeze(2).broadcast_to([128, 4, 256]), ALU.mult),
             reads=[V["tk"], K + "ex"], writes=["vp%d" % i2])
        for c in range(2):
            P.op("pool", CP(qz[i2][c][:, :, 64 * c:64 * c + 64], V["qmT"][:, :, 64 * c:64 * c + 64]), reads=[V["mk"]], writes=["qz%d_%d" % (i2, c)])
        for h in range(4):
            for d in range(2):
                P.op("pe", MM(psf(0)[:, h * 128:(h + 1) * 128], V["kmT"][:, h * 2 + d, :], V["qmT"][:, h * 2 + d, :], d == 0, d == 1),
                     reads=[V["mk"]], writes=[PK[0]], signal=(h == 3 and d == 1))
        P.op("dve", TT(qkm[i2], psf(0).rearrange("p (h t) -> p h t", h=4), tri.unsqueeze(1).broadcast_to([128, 4, 128]), ALU.mult),
             reads=[PK[0], "cst"], writes=["qkm%d" % i2])

    def PB(i):
        V = views(i)
        i2 = V["i2"]
        G = GL[i]
        n_update(nY, "nY", 1, G, 0, V["km_tm"], V["tk"], nb=nYb, nb_key="nYb", n_src=nX, n_src_key="nX")
        for h in range(4):
            state_update(SY, "SY", None, None, h % 2, G, 0, h, V["km_tm"], V["tk"], vp[i2], "vp%d" % i2, Sb=SYb, Sb_key="SYb",
                         S_src=SX, S_src_key="SX")

    def ATT(i, heads):
        V = views(i)
        i2, u, sample, slot, qk_ = V["i2"], V["u"], V["sample"], V["slot"], V["qk"]
        for h in heads:
            e = h % 2
            bx, by = (2, 3) if e == 0 else (4, 5)
            X, Y = psf(bx), psf(by)
            if not sample:
                for kt in range(5):
                    ks = (u - 4 + kt) % RS
                    o = X[:, kt * 128:(kt + 1) * 128] if kt < 4 else Y[:, 0:128]
                    bk = PK[bx] if kt < 4 else PK[by]
                    P.op("pe", MM(o, kring[ks][:, h, :], qbuf[i2][:, h, :], True, False), reads=["kring%d" % ks, qk_], writes=[bk], signal=False)
                    P.op("pe", MM(o, identb, btab[:, h, kt, :], False, True), reads=["identb", "btab"], writes=[bk], signal=(kt >= 3))
                P.op("act", ACT(expT[e][:, 0:4, :], X.rearrange("p (k t) -> p k t", k=4), AF.Exp), reads=[PK[bx]], writes=["expT%da" % e])
                P.op("act", ACT(expT[e][:, 4, :], Y[:, 0:128], AF.Exp), reads=[PK[by]], writes=["expT%db" % e])
            else:
                for kt in range(4):
                    for seq in range(2):
                        o = X[:, kt * 128 + seq * 64:kt * 128 + seq * 64 + 64]
                        P.op("pe", MM(o, kcT[seq][:, h, kt * 128:(kt + 1) * 128], qbuf[i2][:, h, seq * 64:seq * 64 + 64], True, False),
                             reads=["kcT%d" % seq, qk_], writes=[PK[bx]], signal=False)
                        P.op("pe", MM(o, identb, btab[:, h, kt, 0:64], False, True), reads=["identb", "btab"], writes=[PK[bx]],
                             signal=(kt == 3 and seq == 1))
                P.op("pe", MM(Y[:, 0:128], kring[slot][:, h, :], qbuf[i2][:, h, :], True, False), reads=["kring%d" % slot, qk_], writes=[PK[by]], signal=False)
                P.op("pe", MM(Y[:, 0:128], identb, bs4[:, h, :], False, True), reads=["identb", "bs4"], writes=[PK[by]])
                X3 = X.rearrange("p (k t) -> p k t", k=4)
                P.op("act", ACT(es0[e][:, :, 0:64], X3[:, :, 0:64], AF.Exp), reads=[PK[bx]], writes=["es0_%d" % e])
                P.op("act", ACT(es1[e][:, :, 64:128], X3[:, :, 64:128], AF.Exp), reads=[PK[bx]], writes=["es1_%d" % e])
                P.op("act", ACT(expT[e][:, 4, :], Y[:, 0:128], AF.Exp), reads=[PK[by]], writes=["expT%db" % e])
            g = h // 3
            pb = 6 + g % 2
            hh = h % 3
            PV = psf(pb)
            o = PV[:, hh * 129:(hh + 1) * 129]
            if not sample:
                for kt in range(5):
                    ks = (u - 4 + kt) % RS
                    P.op("pe", MM(o, expT[e][:, kt, :], vring[ks][:, h, :], kt == 0, kt == 4),
                         reads=["expT%da" % e if kt < 4 else "expT%db" % e, "vring%d" % ks], writes=[PK[pb]], signal=(kt == 4))
            else:
                for kt in range(4):
                    P.op("pe", MM(o, es0[e][:, kt, :], vc[0][:, kt, h, :], kt == 0, False), reads=["es0_%d" % e, "vc0_%d" % kt, "vc0x"], writes=[PK[pb]], signal=False)
                    P.op("pe", MM(o, es1[e][:, kt, :], vc[1][:, kt, h, :], False, False), reads=["es1_%d" % e, "vc1_%d" % kt, "vc1x"], writes=[PK[pb]], signal=False)
                P.op("pe", MM(o, expT[e][:, 4, :], vring[slot][:, h, :], False, True), reads=["expT%db" % e, "vring%d" % slot], writes=[PK[pb]])
            if h in (2, 5, 7):
                nh = hh + 1
                h0 = h - hh
                PV3 = PV[:, 0:nh * 129].rearrange("p (h e) -> p h e", h=nh)
                P.op("dve", TS(rden[:, 0:nh], PV3[:, :, 128], 1e-30, None, ALU.add), reads=[PK[pb]], writes=["rden"])
                P.op("dve", lambda e_, o_=rden[:, 0:nh]: e_.reciprocal(out=o_, in_=o_), reads=["rden"], writes=["rden"])
                P.op("dve", TT(mix[i2][:, h0 * 128:(h0 + nh) * 128].rearrange("p (h e) -> p h e", h=nh), PV3[:, :, 0:128],
                               rden[:, 0:nh].unsqueeze(2).broadcast_to([128, nh, 128]), ALU.mult),
                     reads=[PK[pb], "rden"], writes=["mix%d_a%d" % (i2, g)])

    def EPI(i):
        V = views(i)
        i2, sample, tk, km_tm, om_tm = V["i2"], V["sample"], V["tk"], V["km_tm"], V["om_tm"]
        G = GL[i]
        K = G["K"]
        for h in range(4):
            hb = h // 2
            Hh = psf(hb)[:, (h % 2) * 256:(h % 2 + 1) * 256]
            P.op("pe", MM(Hh, qkm[i2][:, h, :], vp[i2][:, h, :], True, False), reads=["qkm%d" % i2, "vp%d" % i2], writes=[PK[hb]], signal=False)
            for c, (Sb_, SbK) in enumerate(((SXb, "SXb"), (SYb, "SYb"))):
                for d in range(2):
                    P.op("pe", MM(Hh, qz[i2][c][:, h * 2 + d, :], Sb_[:, h, d, :], False, c == 1 and d == 1),
                         reads=["qz%d_%d" % (i2, c), SbK + str(h)], writes=[PK[hb]], signal=(c == 1 and d == 1))
        DEN = psf(6)[:, 400:404]
        for h in range(4):
            o = DEN[:, h:h + 1]
            P.op("pe", MM(o, qkm[i2][:, h, :], G["gbf"][:, h:h + 1], True, False), reads=["qkm%d" % i2, K + "gbf"], writes=[PK[6]], signal=False)
            for c, (nb_, nbK) in enumerate(((nXb, "nXb"), (nYb, "nYb"))):
                for d in range(2):
                    P.op("pe", MM(o, qz[i2][c][:, h * 2 + d, :], nb_[:, h, d:d + 1], False, c == 1 and d == 1),
                         reads=["qz%d_%d" % (i2, c), nbK], writes=[PK[6]], signal=(h == 3 and c == 1 and d == 1))
        eb = G["ex"][:, 4:8]
        P.op("dve", TT(dd, DEN, eb, ALU.mult), reads=[PK[6], K + "ex"], writes=["dd"])
        P.op("dve", TS(ddn, dd, -1.0), reads=["dd"], writes=["ddn"])
        P.op("dve", TT(dd, dd, ddn, ALU.max), reads=["dd", "ddn"], writes=["dd"])
        P.op("dve", TS(dd, dd, 1.0, None, ALU.max), reads=["dd"], writes=["dd"])
        P.op("dve", lambda e_, o_=dd, i_=dd: e_.reciprocal(out=o_, in_=i_), reads=["dd"], writes=["dd"])
        P.op("dve", TT(dd, dd, eb, ALU.mult), reads=["dd", K + "ex"], writes=["dd"])
        for h in range(4):
            hb = h // 2
            Hh = psf(hb)[:, (h % 2) * 256:(h % 2 + 1) * 256]
            P.op("act", ACT(hjunk, Hh, AF.Square, scale=dd[:, h:h + 1], accum_out=ssh[:, h:h + 1]), reads=[PK[hb], "dd"], writes=["hjunk", "ssh%d" % h])
        rstd_from_ss(ssh[:, 0:4], ssh[:, 4:8], 256, ["ssh%d" % h for h in range(4)], "sshr")
        P.op("dve", TT(scl, dd, ssh[:, 4:8], ALU.mult), reads=["dd", "sshr"], writes=["scl"])
        P.op("act", ACT(gsn, om_tm, AF.Sigmoid), reads=[tk], writes=["gsn"])
        P.op("pool", TT(gsn, gsn, mnb, ALU.mult), reads=["gsn", "mnb"], writes=["gsn"])
        for h in range(4):
            hb = h // 2
            Hh = psf(hb)[:, (h % 2) * 256:(h % 2 + 1) * 256]
            P.op("dve", STT(mix[i2][:, 1024 + h * 256:1024 + (h + 1) * 256], Hh, scl[:, h:h + 1], gsn[:, h * 256:(h + 1) * 256], ALU.mult, ALU.mult),
                 reads=[PK[hb], "scl", "gsn"], writes=["mix%d_m%d" % (i2, h)])
        if not sample:
            n_update(nX, "nX", 1, G, 1, km_tm, tk, nb=nXb, nb_key="nXb", n_src=nY, n_src_key="nY")
            for h in range(4):
                state_update(SX, "SX", None, None, h % 2, G, 1, h, km_tm, tk, vp[i2], "vp%d" % i2, Sb=SXb, Sb_key="SXb",
                             S_src=SY, S_src_key="SY")
            m_update(mP[0:4, 0:1], "mP", G, 0)
            m_update(mP[0:4, 0:1], "mP", G, 1)
            if i == NOWN:
                emit_state(0, SX, SXK, nX, "nX", mP[0:4, 0:1], "mP")
        else:
            n_update(nX, "nX", 1, G, 0, km_tm, tk)
            n_update(nY, "nY", 1, G, 1, km_tm, tk)
            for h in range(4):
                state_update(SX, "SX", None, None, h % 2, G, 0, h, km_tm, tk, vp[i2], "vp%d" % i2)
            for h in range(4):
                state_update(SY, "SY", None, None, h % 2, G, 1, h, km_tm, tk, vp[i2], "vp%d" % i2)
            m_update(ms[0:4, 0:1], "ms0", G, 0)
            m_update(ms[0:4, 1:2], "ms1", G, 1)
            emit_state(1, SX, SXK, nX, "nX", ms[0:4, 0:1], "ms0")
            emit_state(2, SY, SYK, nY, "nY", ms[0:4, 1:2], "ms1")
        mixkeys = ["mix%d_a%d" % (i2, g) for g in range(3)] + ["mix%d_m%d" % (i2, h) for h in range(4)]
        for half in range(2):
            b = 4 + half
            pt = psb(b).rearrange("p (k t) -> p k t", k=8)
            for k8 in range(8):
                k = half * 8 + k8
                P.op("pe", TR(pt[:, k8, :], mix[i2][:, k * 128:(k + 1) * 128], identb), reads=mixkeys + ["identb"], writes=[PK[b]], signal=(k8 == 7))
            evac(mixTs[i2][:, half * 8:(half + 1) * 8, :], pt, [PK[b]], ["mixTs%d_%d" % (i2, half)], eng=("act" if half == 0 else "dve"))
        P.dma("act", scr_mixT[i], mixTs[i2].rearrange("p k t -> p (k t)"), reads=["mixTs%d_0" % i2, "mixTs%d_1" % i2], writes=["scr_mixT"])

    load3(0)
    PA(0)
    PB(0)
    for i in range(NM):
        if i + 1 < NM:
            load3(i + 1)
        ATT(i, range(0, 4))
        if i + 1 < NM:
            PA(i + 1)
        ATT(i, range(4, 8))
        EPI(i)
        if i + 1 < NM:
            if i + 1 == NM - 1:
                sample_setup()
            else:
                PB(i + 1)
    P.dma("sp", m_out, mcolt[0:4, 0:3], reads=["mcolt"], writes=["m_out"])
    P.barrier()
    A.reset(base_mark)
    if last_stage == 3:
        P.run()
        return nc


    wo = A.alloc([KT, D], BF16)
    for q in range(4):
        P.dma("pool", wo[:, :, q * 512:(q + 1) * 512], w_out[:, q * 512:(q + 1) * 512].rearrange("(k p) n -> p k n", p=128), writes=["wo%d" % q])
    gB1 = [A.alloc([D], F32) for _ in range(2)]
    build_gtgB(gB1[0], gB1[1], 0, "gB1p", "gB1s")
    fe_alloc(with_xs=False)
    xs4 = [A.alloc([D], F32) for _ in range(2)]
    x1s = [A.alloc([D], F32) for _ in range(2)]
    mT = [A.alloc([KT, 128], BF16) for _ in range(2)]
    h2s = [A.alloc([KT, 128], BF16) for _ in range(2)]
    ssq = [A.alloc([8], F32) for _ in range(2)]
    junk4 = A.alloc([512], BF16)
    def load4(i):
        P.dma("sp", mT[i % 2], scr_mixT[i].rearrange("p (k t) -> p k t", k=KT), writes=["mT%d" % (i % 2)])
        P.dma("sp", xs4[i % 2], x_main[i * 128:(i + 1) * 128, :], writes=["xs4_%d" % (i % 2)])

    load4(0)
    for i in range(NM):
        i2 = i % 2
        sample = (i == NM - 1)
        if i + 1 < NM:
            load4(i + 1)
        bo = 4 * i2
        for cb in range(4):
            for k in range(KT):
                P.op("pe", MM(psf(bo + cb), mT[i2][:, k, :], wo[:, k, cb * 512:(cb + 1) * 512], k == 0, k == KT - 1),
                     reads=["mT%d" % i2, "wo%d" % cb], writes=[PK[bo + cb]], signal=(k == KT - 1))
        for cb in range(4):
            P.op("act", ACT(junk4, psf(bo + cb), AF.Square, accum_out=ssq[i2][:, cb:cb + 1]), reads=[PK[bo + cb]], writes=["junk4", "ssq%d_%d" % (i2, cb)])
        P.op("dve", lambda e_, o_=ssq[i2][:, 4:5], i_=ssq[i2][:, 0:4]: e_.tensor_reduce(out=o_, in_=i_, axis=AX.X, op=ALU.add),
             reads=["ssq%d_%d" % (i2, cb) for cb in range(4)], writes=["ssq%d_s" % i2])
        rstd_from_ss(ssq[i2][:, 4:5], ssq[i2][:, 5:6], D, ["ssq%d_s" % i2], "ssq%d_r" % i2)
        gB, gBk = (gB1[1], "gB1s") if sample else (gB1[0], "gB1p")
        x1keys = []
        for cb in range(4):
            sl = slice(cb * 512, (cb + 1) * 512)
            key = "x1s%d_%d" % (i2, cb)
            x1keys.append(key)
            P.op("dve", STT(x1s[i2][:, sl], psf(bo + cb), ssq[i2][:, 5:6], gB[:, sl], ALU.mult, ALU.mult), reads=[PK[bo + cb], "ssq%d_r" % i2, gBk], writes=[key])
            P.op("pool", TT(x1s[i2][:, sl], x1s[i2][:, sl], xs4[i2][:, sl], ALU.add), reads=[key, "xs4_%d" % i2], writes=[key])
        if i > 0:
            P.dma("act", scr_x1[i], x1s[i2], reads=x1keys, writes=["scr_x1"])
        rows = (1, 2) if sample else (0, 0)
        front_end(None, h2s[i2], "h2s%d" % i2, 3, 2, rows, tb=(bo, bo + 1), xs_in=x1s[i2], xs_keys=x1keys)
        hk = []
        for k in range(KT):
            hk.append("h2s%d_k%d_0" % (i2, k))
            if sample:
                hk.append("h2s%d_k%d_64" % (i2, k))
        P.dma("act", scr_h2T[:, :, i * 128:(i + 1) * 128], h2s[i2], reads=hk, writes=["scr_h2T"])
    P.barrier()
    A.reset(base_mark)
    if last_stage == 4:
        P.run()
        return nc

    TMp = 128 + OWN
    base_s0 = 2 + TMp + 2
    base_s1 = base_s0 + 64 + 2
    ROW = base_s1 + 64
    h2T_all = A.alloc([KT, TM], BF16)
    for q in range(4):
        P.dma("sp", h2T_all[:, q * 4:(q + 1) * 4, :], scr_h2T[:, q * 4:(q + 1) * 4, :], writes=["h2T_%d" % q])
    H2K = ["h2T_%d" % q for q in range(4)]
    cw = A.alloc([FT, 4], F32)
    P.dma("sp", cw, cwT, writes=["cw"])
    cst_tm = A.alloc([DFF], F32)
    cstT = A.alloc([FT, 4], F32)
    P.dma("sp", cst_tm[0:4, :], conv_st, writes=["tmp22"])
    for ft in range(FT):
        P.op("pe", TR(psf(7)[:, ft * 4:(ft + 1) * 4], cst_tm[0:4, ft * 128:(ft + 1) * 128], identf[0:4, 0:4]), reads=["tmp22", "cst"],
             writes=[PK[7]], signal=(ft == FT - 1))
    P.op("dve", CP(cstT.rearrange("p f c -> p (f c)"), psf(7)[:, 0:FT * 4]), reads=[PK[7]], writes=["cstT"])
    ugrow = [A.alloc([ROW], F32) for _ in range(2)]
    acc = [A.alloc([512], F32) for _ in range(2)]
    gl = [A.alloc([512], F32) for _ in range(2)]
    arow = [A.alloc([TM], BF16) for _ in range(2)]
    convsave = A.alloc([FT, 6], F32)
    cso = cst_tm
    wgb = [A.alloc([KT, 512], BF16) for _ in range(2)]
    wub = [A.alloc([KT, 512], BF16) for _ in range(2)]
    for i2 in range(2):
        P.op("pool", MSET(ugrow[i2][:, 0:2], 0.0), writes=["ug%d_pad" % i2])
    blocks = []
    for m0 in range(0, TMp, 512):
        m1 = min(m0 + 512, TMp)
        blocks.append((m0, m1, [(m0, m1, 2 + m0)]))
    blocks.append((TMp, TMp + 128, [(TMp, TMp + 64, base_s0), (TMp + 64, TMp + 128, base_s1)]))
    nG = 0
    nU = 0
    npc = 0
    for ftg in range(FT // 4):
        wi = ftg % 2
        P.dma("pool", wgb[wi], w_g[:, ftg * 512:(ftg + 1) * 512].rearrange("(k p) n -> p k n", p=128), writes=["wgb%d" % wi])
        P.dma("pool", wub[wi], w_u[:, ftg * 512:(ftg + 1) * 512].rearrange("(k p) n -> p k n", p=128), writes=["wub%d" % wi])
        for sub in range(4):
            ft = ftg * 4 + sub
            u2 = ft % 2
            ug = ugrow[u2]
            P.op("pool", CP(ug[:, base_s0 - 2:base_s0], cstT[:, ft, 0:2]), reads=["cstT"], writes=["ug%d_s0" % u2])
            P.op("pool", CP(ug[:, base_s1 - 2:base_s1], cstT[:, ft, 2:4]), reads=["cstT"], writes=["ug%d_s1" % u2])
            akeys = []
            for bi, (m0, m1, pieces) in enumerate(blocks):
                n = m1 - m0
                gb = nG % 3
                nG += 1
                ub = 3 + nU % 4
                nU += 1
                for k in range(KT):
                    P.op("pe", MM(psf(gb)[:, 0:n], wgb[wi][:, k, sub * 128:(sub + 1) * 128], h2T_all[:, k, m0:m1], k == 0, k == KT - 1),
                         reads=["wgb%d" % wi, H2K[k // 4]], writes=[PK[gb]], signal=(k == KT - 1))
                for k in range(KT):
                    P.op("pe", MM(psf(ub)[:, 0:n], wub[wi][:, k, sub * 128:(sub + 1) * 128], h2T_all[:, k, m0:m1], k == 0, k == KT - 1),
                         reads=["wub%d" % wi, H2K[k // 4]], writes=[PK[ub]], signal=(k == KT - 1))
                for (p0, p1, c0) in pieces:
                    pn = p1 - p0
                    ukey = "ug%d_b%d_%d" % (u2, bi, p0)
                    P.op("act", ACT(ug[:, c0:c0 + pn], psf(gb)[:, p0 - m0:p1 - m0], AF.Copy), reads=[PK[gb]], writes=[ukey])
                    prev = ["ug%d_pad" % u2, "ug%d_s0" % u2, "ug%d_s1" % u2]
                    if bi > 0:
                        prev += ["ug%d_b%d_%d" % (u2, bi - 1, blocks[bi - 1][2][-1][0])]
                    if bi == 0:
                        P.op("pool", TS(ug[:, 2:130], ug[:, 2:130], msk[:, 0, NPRE:NPRE + 1]), reads=[ukey, "msk"], writes=[ukey])
                    a_ = acc[npc % 2]
                    g_ = gl[npc % 2]
                    ak, gk = "acc%d" % (npc % 2), "gl%d" % (npc % 2)
                    npc += 1
                    P.op("act", ACT(a_[:, 0:pn], ug[:, c0 - 2:c0 - 2 + pn], AF.Identity, scale=cw[:, ft, 0:1], bias=cw[:, ft, 3:4]),
                         reads=[ukey, "cw"] + prev, writes=[ak])
                    P.op("dve", STT(a_[:, 0:pn], ug[:, c0 - 1:c0 - 1 + pn], cw[:, ft, 1:2], a_[:, 0:pn], ALU.mult, ALU.add),
                         reads=[ukey, "cw", ak] + prev, writes=[ak])
                    P.op("dve", STT(a_[:, 0:pn], ug[:, c0:c0 + pn], cw[:, ft, 2:3], a_[:, 0:pn], ALU.mult, ALU.add), reads=[ukey, "cw", ak], writes=[ak])
                    P.op("act", ACT(g_[:, 0:pn], a_[:, 0:pn], AF.Gelu), reads=[ak], writes=[gk])
                    akey = "arow%d_%d" % (u2, p0)
                    akeys.append(akey)
                    P.op("dve", TT(arow[u2][:, p0:p1], g_[:, 0:pn], psf(ub)[:, p0 - m0:p1 - m0], ALU.mult), reads=[gk, PK[ub]], writes=[akey])
            allug = ["ug%d_b%d_%d" % (u2, bi, pc[0]) for bi, (_, _, pcs) in enumerate(blocks) for pc in pcs]
            P.op("pool", CP(convsave[:, ft, 0:2], ug[:, 2 + TMp - 2:2 + TMp]), reads=allug, writes=["convsave"])
            P.op("pool", CP(convsave[:, ft, 2:4], ug[:, base_s0 + 62:base_s0 + 64]), reads=allug, writes=["convsave"])
            P.op("pool", CP(convsave[:, ft, 4:6], ug[:, base_s1 + 62:base_s1 + 64]), reads=allug, writes=["convsave"])
            P.dma("sp", scr_aT[ft], arow[u2], reads=akeys, writes=["scr_aT"])
    for g4 in range(FT // 4):
        b = g4 % 2
        for s4 in range(4):
            ft = g4 * 4 + s4
            P.op("pe", TR(psf(b)[0:6, s4 * 128:(s4 + 1) * 128], convsave[:, ft, :], identf), reads=["convsave", "cst"], writes=[PK[b]], signal=(s4 == 3))
        P.op("dve", CP(cso[0:6, g4 * 512:(g4 + 1) * 512], psf(b)[0:6, :]), reads=[PK[b]], writes=["tmp22"])
    P.dma("sp", conv_out, cso[0:6, :], reads=["tmp22"], writes=["conv_out"])
    P.barrier()
    A.reset(base_mark)
    if last_stage == 5:
        P.run()
        return nc

    GS = 6
    gB2 = [A.alloc([D], F32) for _ in range(2)]
    build_gtgB(gB2[0], gB2[1], 1, "gB2p", "gB2s")
    aTg = A.alloc([FT, GS * 128], BF16)
    ystage = [A.alloc([D], F32) for _ in range(GS)]
    x1t = [A.alloc([D], F32) for _ in range(2)]
    NWD = 5
    wd = [A.alloc([4, 512], BF16) for _ in range(NWD)]
    ssq6 = A.alloc([GS, 8], F32)
    junk6 = A.alloc([512], BF16)
    out_tiles = list(range(1, NM))
    nwd = 0
    nx1 = 0
    for g0 in range(0, len(out_tiles), GS):
        tiles = out_tiles[g0:g0 + GS]
        nt = len(tiles)
        tok0 = tiles[0] * 128
        ntok = nt * 128
        for q in range(4):
            P.dma("sp", aTg[:, q * 11:(q + 1) * 11, 0:ntok], scr_aT[q * 11:(q + 1) * 11, :, tok0:tok0 + ntok].rearrange("f p t -> p f t"),
                  writes=["aTg_%d" % q])
        for cb in range(4):
            for ftq in range(FT // 4):
                wi = nwd % NWD
                nwd += 1
                P.dma("pool", wd[wi], w_d[ftq * 512:(ftq + 1) * 512, cb * 512:(cb + 1) * 512].rearrange("(f p) n -> p f n", p=128), writes=["wd%d" % wi])
                for s4 in range(4):
                    ft = ftq * 4 + s4
                    for ti in range(nt):
                        P.op("pe", MM(psf(ti), aTg[:, ft, ti * 128:(ti + 1) * 128], wd[wi][:, s4, :], ft == 0, ft == FT - 1),
                             reads=["aTg_%d" % (ft // 11), "wd%d" % wi], writes=[PK[ti]], signal=(ft == FT - 1 or (s4 == 3 and ti == nt - 1)))
            for ti in range(nt):
                P.op("act", ACT(junk6, psf(ti), AF.Square, accum_out=ssq6[:, ti, cb:cb + 1]), reads=[PK[ti]], writes=["junk6", "ssq6_%d_%d" % (ti, cb)])
                P.op("dve", CP(ystage[ti][:, cb * 512:(cb + 1) * 512], psf(ti)), reads=[PK[ti]], writes=["ys%d_%d" % (ti, cb)])
        for ti, tile in enumerate(tiles):
            sample = (tile == NM - 1)
            xi = nx1 % 2
            nx1 += 1
            P.dma("sp", x1t[xi], scr_x1[tile], writes=["x1t%d" % xi])
            P.op("dve", lambda e_, o_=ssq6[:, ti, 4:5], i_=ssq6[:, ti, 0:4]: e_.tensor_reduce(out=o_, in_=i_, axis=AX.X, op=ALU.add),
                 reads=["ssq6_%d_%d" % (ti, cb) for cb in range(4)], writes=["ssq6s_%d" % ti])
            rstd_from_ss(ssq6[:, ti, 4:5], ssq6[:, ti, 5:6], D, ["ssq6s_%d" % ti], "ssq6r_%d" % ti)
            gB, gBk = (gB2[1], "gB2s") if sample else (gB2[0], "gB2p")
            yk = ["ys%d_%d" % (ti, cb) for cb in range(4)]
            P.op("dve", STT(ystage[ti], ystage[ti], ssq6[:, ti, 5:6], gB, ALU.mult, ALU.mult), reads=yk + ["ssq6r_%d" % ti, gBk], writes=yk)
            P.op("pool", TT(ystage[ti], ystage[ti], x1t[xi], ALU.add), reads=yk + ["x1t%d" % xi], writes=yk)
            P.dma("act", y_main[(tile - 1) * 128:tile * 128, :], ystage[ti], reads=yk, writes=["y_main"])
    P.barrier()
    P.run()
    return nc


def make_consts():
    c = np.zeros((128, 8, 128), np.float32)
    c[:, 0, :] = np.eye(128, dtype=np.float32)
    s = np.arange(128)[:, None]
    t = np.arange(128)[None, :]
    c[:, 1, :] = ((s // 64 == t // 64) & (s <= t)).astype(np.float32)
    c[0:64, 2, :] = 1.0
    c[64:128, 3, :] = 1.0
    c[0, 4, :] = 1.0
    c[1, 5, 0:64] = 1.0
    c[2, 5, 64:128] = 1.0
    c[0:64, 6, 0] = 1.0
    c[64:128, 6, 1] = 1.0
    c[:, 7, :] = 1.0
    return c


def make_consts2():
    return None


def prep_inputs(inp, SEQ):
    OWN = SEQ // 4
    NOWN = OWN // 128
    NPRE = 4
    NM = NOWN + 2
    f32 = np.float32
    xp = np.asarray(inp["x_prompt"], f32)
    xsamp = np.asarray(inp["x_sample"], f32)
    relb = np.asarray(inp["att_rel_bias"], f32)[0]
    row = np.arange(128)[:, None, None]
    kk = np.arange(5)[None, :, None]
    qc = np.arange(128)[None, None, :]
    p = 128 * kk + row - 64 * (qc // 64)
    dist = 512 + (qc % 64) - p
    idx = np.clip(dist, -256, 256) + 256
    valid = (p >= 0) & (p < 576)
    bias_tab = np.empty((128, 8, 5, 128), f32)
    for h in range(8):
        bias_tab[:, h] = np.where(valid, relb[h][idx], NEG)
    rr = np.arange(128)[:, None]
    qq = np.arange(128)[None, :]
    same = (rr // 64) == (qq // 64)
    idx4 = np.clip((qq % 64) - (rr % 64), -256, 256) + 256
    bias_s4 = np.empty((128, 8, 128), f32)
    for h in range(8):
        bias_s4[:, h] = np.where(same, relb[h][idx4], NEG)
    consts = make_consts()
    shared = {
        "adabT": np.ascontiguousarray(np.asarray(inp["ada_b"], f32)[0].reshape(96, 128).T),
        "adab3": np.ascontiguousarray(np.broadcast_to(np.asarray(inp["ada_b"], f32)[0][None], (3, 12288))),
        "gpreT": np.ascontiguousarray(np.stack([np.asarray(inp["norm_pre_mix"], f32)[0].reshape(16, 128).T,
                                                 np.asarray(inp["norm_pre_ffn"], f32)[0].reshape(16, 128).T], axis=1)),
        "gpost3": np.ascontiguousarray(np.broadcast_to(np.stack([np.asarray(inp["norm_post_mix"], f32)[0],
                                                                  np.asarray(inp["norm_post_ffn"], f32)[0]])[None], (3, 2, D))),
        "mnormB": np.ascontiguousarray(np.broadcast_to(np.asarray(inp["mlstm_norm"], f32)[0][None], (128, 1024))),
        "bgate": np.ascontiguousarray(np.broadcast_to(np.concatenate([np.asarray(inp["b_igate"], f32)[0],
                                                                       np.asarray(inp["b_fgate"], f32)[0]])[None], (128, 8))),
        "cwT": np.ascontiguousarray(np.concatenate([np.asarray(inp["ffn_conv_w"], f32)[0].reshape(3, FT, 128),
                                                    np.asarray(inp["ffn_conv_b"], f32)[0].reshape(1, FT, 128)], 0).transpose(2, 1, 0)),
        "bias_tab": bias_tab,
        "bias_s4": bias_s4,
        "ada_w": np.asarray(inp["ada_w"], f32)[0],
        "w_in": np.asarray(inp["w_in"], f32)[0],
        "w_out": np.asarray(inp["w_out"], f32)[0],
        "w_g": np.asarray(inp["w_ffn_gate"], f32)[0],
        "w_u": np.asarray(inp["w_ffn_up"], f32)[0],
        "w_d": np.asarray(inp["w_ffn_down"], f32)[0],
    }
    maps = []
    for r in range(8):
        b, j = r // 4, r % 4
        s0 = j * OWN
        m = dict(shared)
        xm = np.zeros((NM * 128, D), f32)
        if j > 0:
            xm[0:128] = xp[b, s0 - 128:s0]
        xm[128:128 + OWN] = xp[b, s0:s0 + OWN]
        xm[128 + OWN:] = xsamp[2 * r:2 * r + 2].reshape(128, D)
        m["x_main"] = xm
        xpre = np.zeros((NPRE * 128, D), f32)
        lo = s0 - 128 - NPRE * 128
        mk = np.zeros((128, 3, NPRE + NM), f32)
        for t in range(NPRE):
            a = lo + t * 128
            if a >= 0:
                xpre[t * 128:(t + 1) * 128] = xp[b, a:a + 128]
                mk[:, 0, t] = 1.0
        mk[:, 0, NPRE] = 1.0 if j > 0 else 0.0
        mk[:, 0, NPRE + 1:] = 1.0
        mk[:, 1] = -mk[:, 0]
        mk[:, 2] = np.where(mk[:, 0] > 0, 0.0, -1e30)
        m["x_pre"] = xpre
        m["masks"] = mk
        fmk = np.zeros((128, 2, 8), f32)
        for q in range(8):
            if q // 4 == b and q % 4 < j:
                fmk[:, 0, q] = 1.0
            else:
                fmk[:, 1, q] = -1e30
        m["fmask"] = fmk
        c3 = np.stack([np.asarray(inp["c_prompt"], f32)[b], np.asarray(inp["c_sample"], f32)[2 * r],
                       np.asarray(inp["c_sample"], f32)[2 * r + 1]])
        m["c3T"] = np.ascontiguousarray(c3.reshape(3, 16, 128).transpose(2, 1, 0))
        cc = consts.copy()
        m["consts"] = cc
        m["cache_k"] = np.ascontiguousarray(np.asarray(inp["cache_att_k"], f32)[0, 2 * r:2 * r + 2].reshape(2, 512, 1024))
        m["cache_v"] = np.ascontiguousarray(np.asarray(inp["cache_att_v"], f32)[0, 2 * r:2 * r + 2].reshape(2, 512, 1024))
        m["st_C"] = np.ascontiguousarray(np.asarray(inp["state_mlstm_C"], f32)[0, 2 * r:2 * r + 2])
        m["st_n"] = np.ascontiguousarray(np.asarray(inp["state_mlstm_n"], f32)[0, 2 * r:2 * r + 2])
        sm = np.asarray(inp["state_mlstm_m"], f32)[0, 2 * r:2 * r + 2]
        m["st_mB"] = np.ascontiguousarray(np.broadcast_to(sm[None], (128, 2, 4)))
        m["st_m4"] = np.ascontiguousarray(sm.T)
        m["conv_st"] = np.ascontiguousarray(np.asarray(inp["state_ffn_conv"], f32)[0, 2 * r:2 * r + 2].reshape(4, 5632))
        maps.append(m)
    return maps


SEQ_FULL = 8192
_CACHE = {}


def kernel(**inputs):
    SEQ = int(np.asarray(inputs["x_prompt"]).shape[1])
    OWN = SEQ // 4
    if SEQ not in _CACHE:
        _CACHE[SEQ] = None
    nc = build(SEQ)
    maps = prep_inputs(inputs, SEQ)
    res = run_bass_kernel_spmd(nc, maps, core_ids=list(range(8)))
    R_ = res.results
    f32 = np.float32
    y_p = np.empty((2, SEQ, D), f32)
    y_s = np.empty((16, 64, D), f32)
    p_k = np.empty((1, 2, 512, 8, 128), f32)
    p_v = np.empty((1, 2, 512, 8, 128), f32)
    p_C = np.empty((1, 2, 4, 256, 256), f32)
    p_n = np.empty((1, 2, 4, 256), f32)
    p_m = np.empty((1, 2, 4), f32)
    p_conv = np.empty((1, 2, 2, DFF), f32)
    s_k = np.empty((1, 16, 64, 8, 128), f32)
    s_v = np.empty((1, 16, 64, 8, 128), f32)
    s_C = np.empty((1, 16, 4, 256, 256), f32)
    s_n = np.empty((1, 16, 4, 256), f32)
    s_m = np.empty((1, 16, 4), f32)
    s_conv = np.empty((1, 16, 2, DFF), f32)
    for r in range(8):
        b, j = r // 4, r % 4
        o = R_[r]
        ym = np.asarray(o["y_main"])
        y_p[b, j * OWN:(j + 1) * OWN] = ym[:OWN]
        y_s[2 * r:2 * r + 2] = ym[OWN:].reshape(2, 64, D)
        ko, vo = np.asarray(o["k_out"]), np.asarray(o["v_out"])
        s_k[0, 2 * r:2 * r + 2] = ko[512:640].reshape(2, 64, 8, 128)
        s_v[0, 2 * r:2 * r + 2] = vo[512:640].reshape(2, 64, 8, 128)
        Co, no, mo = np.asarray(o["C_out"]), np.asarray(o["n_out"]), np.asarray(o["m_out"])
        s_C[0, 2 * r:2 * r + 2] = Co[1:3]
        s_n[0, 2 * r:2 * r + 2] = no[1:3]
        s_m[0, 2 * r:2 * r + 2] = mo[:, 1:3].T
        co = np.asarray(o["conv_out"]).reshape(3, 2, DFF)
        s_conv[0, 2 * r:2 * r + 2] = co[1:3]
        if j == 3:
            p_k[0, b] = ko[0:512].reshape(512, 8, 128)
            p_v[0, b] = vo[0:512].reshape(512, 8, 128)
            p_C[0, b] = Co[0]
            p_n[0, b] = no[0]
            p_m[0, b] = mo[:, 0]
            p_conv[0, b] = co[0]
    return (y_p, y_s, p_k, p_v, p_C, p_n, p_m, p_conv, s_k, s_v, s_C, s_n, s_m, s_conv)
```

```python
import contextlib
import numpy as np
import concourse.bass as bass
import concourse.mybir as mybir
from concourse.bass_utils import run_bass_kernel_spmd

F32 = mybir.dt.float32
BF16 = mybir.dt.bfloat16
AF = mybir.ActivationFunctionType
ALU = mybir.AluOpType
AX = mybir.AxisListType

D = 2048
KT = 16
DFF = 5632
FT = 44
INC = 7176
EPS = 1e-6
NEG = -30000.0


class Prog:
    ENG = ("pe", "act", "dve", "pool", "sp")

    def __init__(self, nc, n_dma=24):
        self.nc = nc
        self.q = {e: [] for e in self.ENG}
        self.cnt = {e: 0 for e in self.ENG}
        self.waited = {}
        self.lastw = {}
        self.readers = {}
        self.n_dma = n_dma
        self.dma_val = {}
        self.dma_rr = {e: 0 for e in self.ENG}
        self.semh = {}

    def _deps(self, eng, reads, writes):
        deps = []
        for k in reads:
            deps += self.lastw.get(k, [])
        for k in writes:
            deps += self.lastw.get(k, [])
            deps += self.readers.get(k, [])
        waits = []
        for semkey, val in deps:
            if eng == "pe" and semkey == ("e", "pe"):
                continue
            if self.waited.get((eng, semkey), 0) >= val:
                continue
            self.waited[(eng, semkey)] = val
            waits.append((semkey, val))
        return waits

    def _record(self, tok, reads, writes):
        for k in writes:
            self.lastw[k] = [tok]
            self.readers[k] = []
        for k in reads:
            self.readers.setdefault(k, []).append(tok)

    def op(self, eng, fn, reads=(), writes=(), signal=True):
        ex = [k for k in reads if k.startswith("ps")]
        if ex:
            reads = [k for k in reads if not k.startswith("ps")]
            writes = list(writes) + ex
        waits = self._deps(eng, reads, writes)
        if signal:
            self.cnt[eng] += 1
            tok = (("e", eng), self.cnt[eng])
        else:
            tok = (("e", eng), self.cnt[eng] + 1)
        self.q[eng].append((waits, fn, signal))
        self._record(tok, reads, writes)
        return tok

    def dma(self, qeng, out, in_, reads=(), writes=(), custom=None, inc=16):
        waits = self._deps(qeng, reads, writes)
        idx = self.dma_rr[qeng] % self.n_dma
        self.dma_rr[qeng] += 1
        semkey = ("d", qeng, idx)
        v = self.dma_val.get(semkey, 0)
        if v > 0 and self.waited.get((qeng, semkey), 0) < v:
            self.waited[(qeng, semkey)] = v
            waits.append((semkey, v))
        self.dma_val[semkey] = v + inc
        tok = (semkey, v + inc)

        def fn(e, out=out, in_=in_, semkey=semkey, custom=custom):
            if custom is not None:
                return custom(e).then_inc(self.semh[semkey], inc)
            return e.dma_start(out=out, in_=in_).then_inc(self.semh[semkey], 16)

        self.q[qeng].append((waits, fn, False))
        self._record(tok, reads, writes)
        return tok

    def barrier(self):
        for eng in self.ENG:
            waits = []
            for semkey, v in self.dma_val.items():
                if self.waited.get((eng, semkey), 0) < v:
                    self.waited[(eng, semkey)] = v
                    waits.append((semkey, v))
            for e in self.ENG:
                if e == eng or self.cnt[e] == 0:
                    continue
                semkey = ("e", e)
                if self.waited.get((eng, semkey), 0) < self.cnt[e]:
                    self.waited[(eng, semkey)] = self.cnt[e]
                    waits.append((semkey, self.cnt[e]))
            if waits:
                self.q[eng].append((waits, None, False))
        self.lastw = {}
        self.readers = {}

    def run(self):
        nc = self.nc
        with contextlib.ExitStack() as st:
            for e in self.ENG:
                self.semh[("e", e)] = st.enter_context(nc.semaphore("s_" + e))
            for qe in self.ENG:
                for i in range(min(self.dma_rr[qe], self.n_dma)):
                    self.semh[("d", qe, i)] = st.enter_context(nc.semaphore("d_%s_%d" % (qe, i)))
            block = st.enter_context(nc.Block())

            def runq(ename, e):
                for waits, fn, signal in self.q[ename]:
                    for semkey, val in waits:
                        e.wait_ge(self.semh[semkey], val)
                    if fn is None:
                        continue
                    ins = fn(e)
                    if signal:
                        ins.then_inc(self.semh[("e", ename)], 1)

            @block.tensor
            def _(e):
                runq("pe", e)

            @block.scalar
            def _(e):
                runq("act", e)

            @block.vector
            def _(e):
                runq("dve", e)

            @block.gpsimd
            def _(e):
                runq("pool", e)

            @block.sync
            def _(e):
                runq("sp", e)


def MM(out, lhsT, rhs, start=True, stop=True):
    return lambda e: e.matmul(out, lhsT=lhsT, rhs=rhs, start=start, stop=stop)


def TR(out, in_, ident):
    return lambda e: e.transpose(out=out, in_=in_, identity=ident)


def ACT(out, in_, func, **kw):
    return lambda e: e.activation(out=out, in_=in_, func=func, **kw)


def TS(out, in0, s1, s2=None, op0=ALU.mult, op1=None):
    if op1 is None:
        return lambda e: e.tensor_scalar(out=out, in0=in0, scalar1=s1, scalar2=None, op0=op0)
    return lambda e: e.tensor_scalar(out=out, in0=in0, scalar1=s1, scalar2=s2, op0=op0, op1=op1)


def TT(out, in0, in1, op):
    return lambda e: e.tensor_tensor(out=out, in0=in0, in1=in1, op=op)


def STT(out, in0, scalar, in1, op0, op1):
    return lambda e: e.scalar_tensor_tensor(out=out, in0=in0, scalar=scalar, in1=in1, op0=op0, op1=op1)


def CP(out, in_):
    return lambda e: e.tensor_copy(out=out, in_=in_)


def MSET(ap, v):
    return lambda e: e.memset(ap, v)


class Arena:
    def __init__(self, nc, nbytes):
        self.t = nc.alloc_sbuf_tensor("arena", [128, nbytes // 4], F32)
        self.nbytes = nbytes
        self.off = 0

    def mark(self):
        return self.off

    def reset(self, m):
        self.off = m

    def alloc(self, free_shape, dtype):
        n = int(np.prod(free_shape))
        sz = 4 if dtype == F32 else 2
        nb = (n * sz + 63) // 64 * 64
        assert self.off + nb <= self.nbytes, ("SBUF arena overflow", self.off, nb, self.nbytes)
        ap = self.t[:, self.off // 4:(self.off + nb) // 4]
        self.off += nb
        if dtype != F32:
            ap = ap.bitcast(dtype)
        ap = ap[:, 0:n]
        if len(free_shape) == 2:
            ap = ap.rearrange("p (a b) -> p a b", a=free_shape[0])
        elif len(free_shape) == 3:
            ap = ap.rearrange("p (a b c) -> p a b c", a=free_shape[0], b=free_shape[1])
        elif len(free_shape) == 4:
            ap = ap.rearrange("p (a b c d) -> p a b c d", a=free_shape[0], b=free_shape[1], c=free_shape[2])
        return ap


def build(SEQ, last_stage=6, debug=False):
    OWN = SEQ // 4
    NOWN = OWN // 128
    NPRE = 4
    NM = NOWN + 2
    NK = NM + 4
    TM = NM * 128
    TK = NK * 128
    NOUT = NOWN + 1
    assert NOWN >= 4 or NOWN == 2

    nc = bass.Bass("TRN2", target_bir_lowering=False)
    P = Prog(nc)

    def din(name, shape, dt=F32):
        return nc.dram_tensor(name, list(shape), dt, kind="ExternalInput").ap()

    def dout(name, shape, dt=F32):
        return nc.dram_tensor(name, list(shape), dt, kind="ExternalOutput").ap()

    def dscr(name, shape, dt):
        if debug:
            return nc.dram_tensor(name, list(shape), dt, kind="ExternalOutput").ap()
        return nc.dram_tensor(name, list(shape), dt).ap()

    x_main = din("x_main", [TM, D])
    x_pre = din("x_pre", [NPRE * 128, D])
    masks = din("masks", [128, 3, NPRE + NM])
    fmask = din("fmask", [128, 2, 8])
    c3T = din("c3T", [128, KT, 3])
    gpreT = din("gpreT", [128, 2, KT])
    adabT = din("adabT", [128, 96])
    adab3 = din("adab3", [3, 12288])
    gpost3 = din("gpost3", [3, 2, D])
    mnormB = din("mnormB", [128, 1024])
    bgate = din("bgate", [128, 8])
    cwT = din("cwT", [128, FT, 4])
    bias_tab = din("bias_tab", [128, 8, 5, 128])
    bias_s4 = din("bias_s4", [128, 8, 128])
    consts = din("consts", [128, 8, 128])
    cache_k = din("cache_k", [2, 512, 1024])
    cache_v = din("cache_v", [2, 512, 1024])
    st_C = din("st_C", [2, 4, 256, 256])
    st_n = din("st_n", [2, 4, 256])
    st_mB = din("st_mB", [128, 2, 4])
    st_m4 = din("st_m4", [4, 2])
    conv_st = din("conv_st", [4, 5632])
    ada_w = din("ada_w", [D, 12288])
    w_in = din("w_in", [D, INC])
    w_out = din("w_out", [D, D])
    w_g = din("w_g", [D, DFF])
    w_u = din("w_u", [D, DFF])
    w_d = din("w_d", [DFF, D])

    y_main = dout("y_main", [NOUT * 128, D])
    k_out = dout("k_out", [5 * 128, 1024])
    v_out = dout("v_out", [5 * 128, 1024])
    C_out = dout("C_out", [3, 4, 256, 256])
    n_out = dout("n_out", [3, 4, 256])
    m_out = dout("m_out", [4, 3])
    conv_out = dout("conv_out", [6, 5632])

    scr_fm = dscr("scr_fm", [32, 128, TK], BF16)
    scr_va = dscr("scr_va", [NK, 128, 8 * 129], BF16)
    scr_tm = dscr("scr_tm", [NM, 128, 3072], BF16)
    scr_g = dscr("scr_g", [NM, 128, 8], F32)
    scr_gtg = nc.dram_tensor("scr_gtg", [3, 2, D], F32).ap()
    cc_in = nc.dram_tensor("cc_in", [128, 2064], F32).ap()
    cc_out = nc.dram_tensor("cc_out", [1024, 2064], F32).ap()
    scr_mixT = dscr("scr_mixT", [NM, 128, KT * 128], BF16)
    scr_x1 = dscr("scr_x1", [NM, 128, D], F32)
    scr_h2T = dscr("scr_h2T", [128, KT, TM], BF16)
    scr_aT = dscr("scr_aT", [FT, 128, TM], BF16)
    dbg = dscr("dbg", [128, 4096], F32) if debug else None

    A = Arena(nc, 206 * 1024)
    ps = [nc.alloc_psum_tensor("ps%d" % b, [128, 512], F32) for b in range(8)]

    def psf(b):
        return ps[b][:]

    def psb(b):
        return ps[b][:].bitcast(BF16)

    PK = ["ps%d" % b for b in range(8)]

    cst = A.alloc([8, 128], F32)
    identf = cst[:, 0, :]
    tri = cst[:, 1, :]
    ind = cst[:, 2:4, :]
    sel = cst[0:3, 4:6, :]
    ind2 = cst[:, 6, 0:2]
    ones4 = cst[0:4, 7, :]
    identb = A.alloc([128], BF16)
    msk = A.alloc([3, NPRE + NM], F32)
    modT = A.alloc([4, KT, 3], F32)
    bgt = A.alloc([8], F32)
    epsc = A.alloc([1], F32)
    onec = A.alloc([1], F32)
    P.dma("sp", cst, consts, writes=["cst"])
    P.dma("sp", msk, masks, writes=["msk"])
    P.dma("sp", bgt, bgate, writes=["bgt"])
    P.op("dve", MSET(epsc, EPS), writes=["epsc"])
    P.op("dve", MSET(onec, 1.0), writes=["onec"])
    P.op("dve", CP(identb, identf), reads=["cst"], writes=["identb"])
    persist_mark = A.mark()

    siluT = A.alloc([KT, 3], BF16)
    gtg = A.alloc([2, D], F32)
    c3s = A.alloc([KT, 3], F32)
    gpre = A.alloc([2, KT], F32)
    adab = A.alloc([96], F32)
    adab3s = A.alloc([2, D], F32)
    gpost = A.alloc([2, D], F32)
    ablk = [A.alloc([KT, 512], BF16) for _ in range(3)]
    P.dma("sp", c3s, c3T, writes=["c3s"])
    P.dma("sp", gpre, gpreT, writes=["gpre"])
    P.dma("sp", adab, adabT, writes=["adab"])
    P.dma("sp", adab3s[0:3, 0, :], adab3[:, 2 * D:3 * D], writes=["adab3s0"])
    P.dma("sp", adab3s[0:3, 1, :], adab3[:, 5 * D:6 * D], writes=["adab3s1"])
    P.dma("sp", gpost[0:3], gpost3, writes=["gpost"])
    P.op("act", ACT(siluT, c3s, AF.Silu), reads=["c3s"], writes=["siluT"])

    nblk = 0
    for v, slot in ((0, 0), (1, 1), (3, 2), (4, 3)):
        for cb4 in range(4):
            bi = nblk % 3
            nblk += 1
            c0 = v * D + cb4 * 512
            P.dma("pool", ablk[bi], ada_w[:, c0:c0 + 512].rearrange("(k p) n -> p k n", p=128), writes=["ablk%d" % bi])
            for sub in range(4):
                kc = cb4 * 4 + sub
                b = kc % 4
                for k in range(KT):
                    P.op("pe", MM(psf(b)[:, 0:3], ablk[bi][:, k, sub * 128:(sub + 1) * 128], siluT[:, k, :], k == 0, k == KT - 1),
                         reads=["ablk%d" % bi, "siluT"], writes=[PK[b]], signal=(k == KT - 1))
                P.op("dve", TS(modT[:, slot, kc, :], psf(b)[:, 0:3], adab[:, v * 16 + kc:v * 16 + kc + 1], None, ALU.add),
                     reads=[PK[b], "adab"], writes=["modT"])
    for slot, gi in ((1, 0), (3, 1)):
        P.op("dve", STT(modT[:, slot], modT[:, slot], 1.0, gpre[:, gi, :].unsqueeze(2).broadcast_to([128, KT, 3]), ALU.add, ALU.mult),
             reads=["modT", "gpre"], writes=["modT"])
    for gi, v in ((0, 2), (1, 5)):
        for cb4 in range(4):
            bi = nblk % 3
            nblk += 1
            c0 = v * D + cb4 * 512
            P.dma("pool", ablk[bi], ada_w[:, c0:c0 + 512].rearrange("(k p) n -> p k n", p=128), writes=["ablk%d" % bi])
            b = 4 + cb4 % 4
            for k in range(KT):
                P.op("pe", MM(psf(b)[0:3, :], siluT[:, k, :], ablk[bi][:, k, :], k == 0, k == KT - 1),
                     reads=["ablk%d" % bi, "siluT"], writes=[PK[b]], signal=(k == KT - 1))
            P.op("dve", TT(gtg[0:3, gi, cb4 * 512:(cb4 + 1) * 512], psf(b)[0:3, :], adab3s[0:3, gi, cb4 * 512:(cb4 + 1) * 512], ALU.add),
                 reads=[PK[b], "adab3s%d" % gi], writes=["gtg%d_%d" % (gi, cb4)])
            P.op("dve", TT(gtg[0:3, gi, cb4 * 512:(cb4 + 1) * 512], gtg[0:3, gi, cb4 * 512:(cb4 + 1) * 512],
                           gpost[0:3, gi, cb4 * 512:(cb4 + 1) * 512], ALU.mult),
                 reads=["gpost", "gtg%d_%d" % (gi, cb4)], writes=["gtg%d_%d" % (gi, cb4)])
    GTG_KEYS = [["gtg%d_%d" % (gi, c) for c in range(4)] for gi in range(2)]
    for gi in range(2):
        P.dma("sp", scr_gtg[:, gi, :], gtg[0:3, gi, :], reads=GTG_KEYS[gi], writes=["scr_gtg"])

    def build_gtgB(dst_p, dst_s, gi, keyp, keys_):
        tmp = A.alloc([D], F32)
        P.dma("sp", tmp[0:3, :], scr_gtg[:, gi, :], writes=["gtgtmp"])
        for vi, dst, key in ((0, dst_p, keyp), (1, dst_s, keys_)):
            for cb4 in range(4):
                b = cb4
                P.op("pe", MM(psf(b), sel[:, vi, :], tmp[0:3, cb4 * 512:(cb4 + 1) * 512]),
                     reads=["cst", "gtgtmp"], writes=[PK[b]])
                P.op("act", ACT(dst[:, cb4 * 512:(cb4 + 1) * 512], psf(b), AF.Copy), reads=[PK[b]], writes=[key])

    P.barrier()
    A.reset(persist_mark)
    if last_stage == 0:
        if debug:
            P.dma("sp", dbg[:, 0:192], modT.rearrange("p a b c -> p (a b c)"), reads=["modT"], writes=["dbg"])
            P.dma("sp", dbg[0:3, 192:192 + 4096 - 192], gtg[0:3].rearrange("p a b -> p (a b)")[:, 0:4096 - 192], reads=sum(GTG_KEYS, []), writes=["dbg2"])
        P.barrier()
        P.run()
        return nc

    def rstd_from_ss(ss_ap, out_ap, n, keys_in, key_out):
        P.op("act", ACT(out_ap, ss_ap, AF.Ln, scale=1.0 / n, bias=epsc[:, 0:1]), reads=keys_in + ["epsc"], writes=[key_out])
        P.op("act", ACT(out_ap, out_ap, AF.Exp, scale=-0.5), reads=[key_out], writes=[key_out])

    fe = {}

    def fe_alloc(with_xs=True, nxs=2):
        if with_xs:
            fe["xs"] = [A.alloc([D], F32) for _ in range(nxs)]
        fe["xn"] = [A.alloc([D], BF16) for _ in range(2)]
        fe["junk"] = A.alloc([D], BF16)
        fe["ss"] = [A.alloc([2], F32) for _ in range(2)]
        fe["n"] = 0

    def front_end(x_src, dst, dst_key, slot_a, slot_sh, rows, tb=(0, 1), xs_in=None, xs_keys=None):
        i = fe["n"] % 2
        fe["n"] += 1
        xn, ss = fe["xn"][i], fe["ss"][i]
        kx, kn, ks = "fe_xs%d" % i, "fe_xn%d" % i, "fe_ss%d" % i
        if xs_in is None:
            xs = fe["xs"][i % len(fe["xs"])]
            kx = "fe_xs%d" % (i % len(fe["xs"]))
            P.dma("sp", xs, x_src, writes=[kx])
            kxl = [kx]
        else:
            xs, kxl = xs_in, list(xs_keys)
        P.op("act", ACT(fe["junk"], xs, AF.Square, accum_out=ss[:, 0:1]), reads=kxl, writes=["fe_junk", ks])
        rstd_from_ss(ss[:, 0:1], ss[:, 1:2], D, [ks], ks)
        P.op("dve", TS(xn, xs, ss[:, 1:2]), reads=kxl + [ks], writes=[kn])
        for half in range(2):
            b = tb[half]
            pt = psb(b).rearrange("p (k t) -> p k t", k=8)
            for k8 in range(8):
                k = half * 8 + k8
                P.op("pe", TR(pt[:, k8, :], xn[:, k * 128:(k + 1) * 128], identb), reads=[kn, "identb"], writes=[PK[b]],
                     signal=(k8 == 7))
            for k8 in range(8):
                k = half * 8 + k8
                eng = "act" if half == 0 else "dve"
                if rows[0] == rows[1]:
                    segs = [(0, 128, rows[0])]
                else:
                    segs = [(0, 64, rows[0]), (64, 128, rows[1])]
                for (t0, t1, r) in segs:
                    a_ap = modT[:, slot_a, k, r:r + 1]
                    s_ap = modT[:, slot_sh, k, r:r + 1]
                    if eng == "act":
                        P.op("act", ACT(dst[:, k, t0:t1], pt[:, k8, t0:t1], AF.Identity, scale=a_ap, bias=s_ap),
                             reads=[PK[b], "modT"], writes=["%s_k%d_%d" % (dst_key, k, t0)])
                    else:
                        P.op("dve", TS(dst[:, k, t0:t1], pt[:, k8, t0:t1], a_ap, s_ap, ALU.mult, ALU.add),
                             reads=[PK[b], "modT"], writes=["%s_k%d_%d" % (dst_key, k, t0)])

    gm = {}

    def gate_alloc():
        gm["gsb"] = [A.alloc([8], F32) for _ in range(2)]
        gm["lfm"] = [A.alloc([4], F32) for _ in range(2)]
        gm["pre"] = [A.alloc([16], F32) for _ in range(2)]
        gm["ex"] = [A.alloc([16], F32) for _ in range(2)]
        gm["gbf"] = [A.alloc([4], BF16) for _ in range(2)]
        gm["amax"] = [A.alloc([2], F32) for _ in range(2)]
        gm["B4"] = [A.alloc([2], F32) for _ in range(2)]
        gm["t4"] = A.alloc([2], F32)
        gm["n"] = 0

    def gate_math(g_src, g_keys, mcol, gb, gb2):
        i = gm["n"] % 2
        gm["n"] += 1
        gsb, lfm, pre, ex, gbf, amax, B4 = (gm[k][i] for k in ("gsb", "lfm", "pre", "ex", "gbf", "amax", "B4"))
        K = "gm%d_" % i
        P.op("dve", TT(gsb, g_src, bgt, ALU.add), reads=g_keys + ["bgt"], writes=[K + "gsb"])
        P.op("act", ACT(lfm, gsb[:, 4:8], AF.Exp, scale=-1.0), reads=[K + "gsb"], writes=[K + "lfm"])
        P.op("act", ACT(lfm, lfm, AF.Ln, bias=onec[:, 0:1]), reads=[K + "lfm", "onec"], writes=[K + "lfm"])
        P.op("dve", TS(lfm, lfm, msk[:, 1, mcol:mcol + 1]), reads=[K + "lfm", "msk"], writes=[K + "lfm"])
        g0 = psf(gb)
        P.op("pe", MM(g0[:, 0:4], tri, lfm), reads=["cst", K + "lfm"], writes=[PK[gb]], signal=False)
        P.op("pe", MM(g0[:, 4:8], ind[:, 0, :], lfm), reads=["cst", K + "lfm"], writes=[PK[gb]], signal=False)
        P.op("pe", MM(g0[:, 8:12], ind[:, 1, :], lfm), reads=["cst", K + "lfm"], writes=[PK[gb]], signal=False)
        P.op("pe", MM(g0[0:4, 12:14], lfm, ind2), reads=["cst", K + "lfm"], writes=[PK[gb]])
        P.op("dve", CP(pre[:, 4:16], g0[:, 0:12]), reads=[PK[gb]], writes=[K + "pre"])
        P.op("dve", CP(B4[0:4, :], g0[0:4, 12:14]), reads=[PK[gb]], writes=[K + "B4"])
        P.op("dve", TT(pre[:, 0:4], gsb[:, 0:4], pre[:, 4:8], ALU.subtract), reads=[K + "gsb", K + "pre"], writes=[K + "pre"])
        P.op("dve", TS(pre[:, 0:4], pre[:, 0:4], msk[:, 0, mcol:mcol + 1], msk[:, 2, mcol:mcol + 1], ALU.mult, ALU.add),
             reads=[K + "pre", "msk"], writes=[K + "pre"])
        P.op("act", ACT(ex, pre, AF.Exp), reads=[K + "pre"], writes=[K + "ex"])
        P.op("dve", CP(gbf, ex[:, 0:4]), reads=[K + "ex"], writes=[K + "gbf"])
        g1 = psf(gb2)
        P.op("pe", TR(g1[0:4, 128:256], pre[:, 0:4], identf), reads=[K + "pre", "cst"], writes=[PK[gb2]])
        P.op("dve", lambda e, o=amax[0:4, :], s=g1[0:4, 128:256].rearrange("p (c t) -> p c t", c=2):
             e.tensor_reduce(out=o, in_=s, axis=AX.X, op=ALU.max), reads=[PK[gb2]], writes=[K + "amax"])
        return dict(ex=ex, gbf=gbf, amax=amax, B4=B4, K=K, pre=pre)

    def m_update(mt, mkey, G, c):
        K = G["K"]
        t4 = gm["t4"]
        P.op("dve", TT(t4[0:4, 0:1], G["amax"][0:4, c:c + 1], G["B4"][0:4, c:c + 1], ALU.add),
             reads=[K + "amax", K + "B4"], writes=["t4"])
        P.op("dve", TT(mt, mt, G["B4"][0:4, c:c + 1], ALU.add), reads=[mkey, K + "B4"], writes=[mkey])
        P.op("dve", TT(mt, mt, t4[0:4, 0:1], ALU.max), reads=[mkey, "t4"], writes=[mkey])

    def state_update(Sf, Sf_key, nf, nf_key, DSb, G, c, h, km_ap, km_key, vp_ap, vp_key, Sb=None, Sb_key=None, nb=None, nb_key=None,
                     S_src=None, S_src_key=None, n_src=None, n_src_key=None):
        K = G["K"]
        if S_src is None:
            S_src, S_src_key, n_src, n_src_key = Sf, Sf_key, nf, nf_key
        r0, r1 = 64 * c, 64 * c + 64
        eB = G["ex"][:, 8 + 4 * c + h:8 + 4 * c + h + 1]
        DS = psf(DSb)
        for d in range(2):
            P.op("pe", MM(DS[:, d * 256:(d + 1) * 256], km_ap[r0:r1, h * 256 + d * 128:h * 256 + (d + 1) * 128], vp_ap[r0:r1, h, :]),
                 reads=[km_key, vp_key], writes=[PK[DSb]], signal=(d == 1))
        Sh = Sf[:, h].rearrange("p d e -> p (d e)")
        Ssh = S_src[:, h].rearrange("p d e -> p (d e)")
        P.op("act", ACT(Sh, Ssh, AF.Copy, scale=eB), reads=[S_src_key + str(h), K + "ex"], writes=[Sf_key + str(h)])
        P.op("dve", STT(Sh, DS, eB, Sh, ALU.mult, ALU.add), reads=[PK[DSb], K + "ex", Sf_key + str(h)], writes=[Sf_key + str(h)])
        if Sb is not None:
            P.op("act", ACT(Sb[:, h].rearrange("p d e -> p (d e)"), Sh, AF.Copy), reads=[Sf_key + str(h)], writes=[Sb_key + str(h)])

    def n_update(nf, nf_key, NBb, G, c, km_ap, km_key, nb=None, nb_key=None, n_src=None, n_src_key=None):
        K = G["K"]
        if n_src is None:
            n_src, n_src_key = nf, nf_key
        r0, r1 = 64 * c, 64 * c + 64
        NB = psf(NBb)
        for h in range(4):
            for d in range(2):
                P.op("pe", MM(NB[:, 256 + h * 2 + d:256 + h * 2 + d + 1], km_ap[r0:r1, h * 256 + d * 128:h * 256 + (d + 1) * 128],
                              G["gbf"][r0:r1, h:h + 1]),
                     reads=[km_key, K + "gbf"], writes=[PK[NBb]], signal=(h == 3 and d == 1))
        eB4 = G["ex"][:, 8 + 4 * c:12 + 4 * c].unsqueeze(2).broadcast_to([128, 4, 2])
        P.op("dve", TT(nf, n_src, NB[:, 256:264].rearrange("p (h d) -> p h d", h=4), ALU.add),
             reads=[n_src_key, PK[NBb]], writes=[nf_key])
        P.op("dve", TT(nf, nf, eB4, ALU.mult), reads=[nf_key, K + "ex"], writes=[nf_key])
        if nb is not None:
            P.op("dve", CP(nb, nf), reads=[nf_key], writes=[nb_key])

    base_mark = A.mark()
    SX = A.alloc([4, 2, 256], F32)
    nX = A.alloc([4, 2], F32)
    mP = A.alloc([2], F32)
    P.op("pool", MSET(SX, 0.0), writes=["SX%d" % h for h in range(4)])
    P.op("pool", MSET(nX, 0.0), writes=["nX"])
    P.op("pool", MSET(mP, 0.0), writes=["mP"])
    persist_mark = A.mark()

    hT_halo = A.alloc([KT, 512], BF16)
    s1_mark = A.mark()

    hT_main = A.alloc([KT, TM], BF16)

    def hTt(u):
        return hT_halo[:, :, u * 128:(u + 1) * 128] if u < 4 else hT_main[:, :, (u - 4) * 128:(u - 3) * 128]

    def hTr(k, t0, t1):
        return hT_halo[:, k, t0:t1] if t1 <= 512 else hT_main[:, k, t0 - 512:t1 - 512]

    fe_alloc(nxs=2)
    wblk = [A.alloc([KT, 512], BF16) for _ in range(3)]
    wg8 = A.alloc([KT, 8], BF16)
    fmst = [A.alloc([TK], BF16) for _ in range(2)]
    tmst = [A.alloc([516], BF16) for _ in range(4)]
    f32st = [A.alloc([512], F32) for _ in range(3)]
    gst = [A.alloc([8], F32) for _ in range(2)]
    cnt = {"wb": 0, "bank": 0, "ev": 0, "fm": 0, "tm": 0, "f32": 0, "g": 0}

    def hkeys(u, k):
        if u == NK - 1:
            return ["hT_u%d_k%d_0" % (u, k), "hT_u%d_k%d_64" % (u, k)]
        return ["hT_u%d_k%d_0" % (u, k)]

    for u in range(4):
        front_end(x_pre[u * 128:(u + 1) * 128, :], hTt(u), "hT_u%d" % u, 1, 0, (0, 0), tb=(0, 1))
    for i in range(NM):
        u = 4 + i
        rows = (0, 0) if i < NM - 1 else (1, 2)
        front_end(x_main[i * 128:(i + 1) * 128, :], hTt(u), "hT_u%d" % u, 1, 0, rows, tb=(0, 1))

    def load_wblk(col0):
        bi = cnt["wb"] % 3
        cnt["wb"] += 1
        P.dma("pool", wblk[bi], w_in[:, col0:col0 + 512].rearrange("(k p) n -> p k n", p=128), writes=["wblk%d" % bi])
        return wblk[bi], "wblk%d" % bi

    def next_bank():
        b = 2 + cnt["bank"] % 6
        cnt["bank"] += 1
        return b

    def ev_eng():
        cnt["ev"] += 1
        return "act" if cnt["ev"] % 2 == 0 else "dve"

    def evac(out, in_, keys_r, keys_w, scale=None, eng=None):
        eng = eng or ev_eng()
        if eng == "act":
            if scale is None:
                P.op("act", ACT(out, in_, AF.Copy), reads=keys_r, writes=keys_w)
            else:
                P.op("act", ACT(out, in_, AF.Copy, scale=scale), reads=keys_r, writes=keys_w)
        else:
            if scale is None:
                P.op("dve", CP(out, in_), reads=keys_r, writes=keys_w)
            else:
                P.op("dve", TS(out, in_, scale), reads=keys_r, writes=keys_w)

    fm_blocks = [(0, 128 ** -0.5, 4), (512, 128 ** -0.5, 4), (1024, None, 0), (1536, None, 0),
                 (3072, None, 4), (3584, None, 4), (4096, 0.0625, 4), (4608, 0.0625, 4)]
    for blk, (col0, scale, u_lo) in enumerate(fm_blocks):
        wb, wkey = load_wblk(col0)
        t_lo = u_lo * 128
        for sub in range(4):
            cb = blk * 4 + sub
            fi = cnt["fm"] % 2
            cnt["fm"] += 1
            stkeys = []
            for tbi, t0 in enumerate(range(t_lo, TK, 512)):
                t1 = min(t0 + 512, TK)
                b = next_bank()
                for k in range(KT):
                    rk = [wkey]
                    for u in range(t0 // 128, t1 // 128):
                        rk += hkeys(u, k)
                    P.op("pe", MM(psf(b)[:, 0:t1 - t0], wb[:, k, sub * 128:(sub + 1) * 128], hTr(k, t0, t1), k == 0, k == KT - 1),
                         reads=rk, writes=[PK[b]], signal=(k == KT - 1))
                key = "fmst%d_%d" % (fi, tbi)
                stkeys.append(key)
                evac(fmst[fi][:, t0:t1], psf(b)[:, 0:t1 - t0], [PK[b]], [key], scale)
            P.dma("sp", scr_fm[cb, :, t_lo:TK], fmst[fi][:, t_lo:TK], reads=stkeys, writes=["scr_fm"])

    def out_index(i):
        if NOWN - 3 <= i <= NOWN:
            return i - (NOWN - 3)
        if i == NM - 1:
            return 4
        return None

    def mcol_of(u):
        return NPRE - 4 + u if u < 4 else NPRE + (u - 4)

    for wbi in range(2):
        wb, wkey = load_wblk(2048 + wbi * 512)
        for u in range(NK):
            b = next_bank()
            for k in range(KT):
                P.op("pe", MM(psf(b), hTt(u)[:, k, :], wb[:, k, :], k == 0, k == KT - 1),
                     reads=[wkey] + hkeys(u, k), writes=[PK[b]], signal=(k == KT - 1))
            ti = cnt["tm"] % 4
            cnt["tm"] += 1
            st = tmst[ti].rearrange("p (h e) -> p h e", h=4)
            mc = mcol_of(u)
            P.op("dve", TS(st[:, :, 0:128], psf(b).rearrange("p (h e) -> p h e", h=4), msk[:, 0, mc:mc + 1]),
                 reads=[PK[b], "msk"], writes=["tmst%d" % ti])
            P.op("pool", CP(st[:, :, 128], msk[:, 0, mc:mc + 1].broadcast_to([128, 4])), reads=["msk"], writes=["tmst%dx" % ti])
            P.dma("sp", scr_va[u, :, wbi * 516:(wbi + 1) * 516], tmst[ti], reads=["tmst%d" % ti, "tmst%dx" % ti], writes=["scr_va"])
            oi = out_index(u - 4) if u >= 4 else None
            if oi is not None:
                fi = cnt["f32"] % 3
                cnt["f32"] += 1
                P.op("act", ACT(f32st[fi], psf(b), AF.Copy), reads=[PK[b]], writes=["f32st%d" % fi])
                P.dma("sp", v_out[oi * 128:(oi + 1) * 128, wbi * 512:(wbi + 1) * 512], f32st[fi], reads=["f32st%d" % fi], writes=["v_out"])
    for (col0, off, scale) in ((4096, 0, 0.0625), (5120, 1024, None), (6144, 2048, None)):
        for wbi in range(2):
            wb, wkey = load_wblk(col0 + wbi * 512)
            for i in range(NM):
                u = 4 + i
                b = next_bank()
                for k in range(KT):
                    P.op("pe", MM(psf(b), hTt(u)[:, k, :], wb[:, k, :], k == 0, k == KT - 1),
                         reads=[wkey] + hkeys(u, k), writes=[PK[b]], signal=(k == KT - 1))
                ti = cnt["tm"] % 4
                cnt["tm"] += 1
                evac(tmst[ti][:, 0:512], psf(b), [PK[b]], ["tmst%d" % ti], scale)
                P.dma("sp", scr_tm[i, :, off + wbi * 512:off + (wbi + 1) * 512], tmst[ti][:, 0:512], reads=["tmst%d" % ti], writes=["scr_tm"])
    P.dma("pool", wg8, w_in[:, 7168:7176].rearrange("(k p) n -> p k n", p=128), writes=["wg8"])
    for i in range(NM):
        u = 4 + i
        b = next_bank()
        for k in range(KT):
            P.op("pe", MM(psf(b)[:, 0:8], hTt(u)[:, k, :], wg8[:, k, :], k == 0, k == KT - 1),
                 reads=["wg8"] + hkeys(u, k), writes=[PK[b]], signal=(k == KT - 1))
        gi = cnt["g"] % 2
        cnt["g"] += 1
        P.op("dve", CP(gst[gi], psf(b)[:, 0:8]), reads=[PK[b]], writes=["gst%d" % gi])
        P.dma("sp", scr_g[i], gst[gi], reads=["gst%d" % gi], writes=["scr_g"])
    for wbi in range(2):
        wb, wkey = load_wblk(1024 + wbi * 512)
        for i in range(NM):
            oi = out_index(i)
            if oi is None:
                continue
            u = 4 + i
            b = next_bank()
            for k in range(KT):
                P.op("pe", MM(psf(b), hTt(u)[:, k, :], wb[:, k, :], k == 0, k == KT - 1),
                     reads=[wkey] + hkeys(u, k), writes=[PK[b]], signal=(k == KT - 1))
            fi = cnt["f32"] % 3
            cnt["f32"] += 1
            P.op("act", ACT(f32st[fi], psf(b), AF.Copy), reads=[PK[b]], writes=["f32st%d" % fi])
            P.dma("sp", k_out[oi * 128:(oi + 1) * 128, wbi * 512:(wbi + 1) * 512], f32st[fi], reads=["f32st%d" % fi], writes=["k_out"])
    P.barrier()
    A.reset(persist_mark)
    if last_stage == 2:
        P.run()
        return nc

    gate_alloc()
    pay = A.alloc([2064], F32)
    Sl = pay[:, 0:2048].rearrange("p (h d e) -> p h d e", h=4, d=2)
    nl = pay[:, 2048:2056].rearrange("p (h d) -> p h d", h=4)
    Bt = pay[:, 2056:2060]
    ab = pay[:, 2060:2064]
    ml = A.alloc([2], F32)
    d4l = A.alloc([4], F32)
    fm_ = A.alloc([2, 8], F32)
    P.dma("sp", fm_, fmask, writes=["fm_"])
    P.op("pool", MSET(pay, 0.0), writes=["SL%d" % h for h in range(4)] + ["nL", "BtL", "abL"])
    P.op("pool", MSET(ml, -1e30), writes=["mL"])
    tbl = [A.alloc([2048], BF16) for _ in range(2)]
    gbl = [A.alloc([8], F32) for _ in range(2)]
    vpl = [A.alloc([4, 256], BF16) for _ in range(2)]
    payq = [A.alloc([2064], F32) for _ in range(2)]
    eq = A.alloc([8], F32)
    mB = A.alloc([4], F32)
    tq = A.alloc([4], F32)
    P.op("pool", MSET(mB, 0.0), writes=["mB"])

    def loadL(i):
        P.dma("sp", tbl[i % 2], scr_tm[i, :, 0:2048], writes=["tbl%d" % (i % 2)])
        P.dma("sp", gbl[i % 2], scr_g[i], writes=["gbl%d" % (i % 2)])

    loadL(0)
    for i in range(NOWN):
        i2 = i % 2
        if i + 1 < NOWN:
            loadL(i + 1)
        tk = "tbl%d" % i2
        km_tm = tbl[i2][:, 0:1024]
        vm_tm = tbl[i2][:, 1024:2048].rearrange("p (h e) -> p h e", h=4)
        G = gate_math(gbl[i2], ["gbl%d" % i2], NPRE + i, 6, 7)
        K = G["K"]
        P.op("dve", TT(vpl[i2], vm_tm, G["ex"][:, 0:4].unsqueeze(2).broadcast_to([128, 4, 256]), ALU.mult),
             reads=[tk, K + "ex"], writes=["vpl%d" % i2])
        P.op("dve", TT(Bt, Bt, G["pre"][:, 8:12], ALU.add), reads=["BtL", K + "pre"], writes=["BtL"])
        P.op("dve", TT(Bt, Bt, G["pre"][:, 12:16], ALU.add), reads=["BtL", K + "pre"], writes=["BtL"])
        for c in range(2):
            n_update(nl, "nL", 7, G, c, km_tm, tk)
            for h in range(4):
                state_update(Sl, "SL", None, None, h, G, c, h, km_tm, tk, vpl[i2], "vpl%d" % i2)
            m_update(ml[0:4, 0:1], "mL", G, c)
    P.op("dve", TS(d4l[0:4, 0:4], identf[0:4, 0:4], ml[0:4, 0:1]), reads=["mL", "cst"], writes=["d4l"])
    P.op("pe", MM(psf(6)[:, 0:4], ones4, d4l[0:4, 0:4]), reads=["cst", "d4l"], writes=[PK[6]])
    P.op("dve", CP(ab, psf(6)[:, 0:4]), reads=[PK[6]], writes=["abL"])
    P.dma("sp", cc_in, pay, reads=["SL%d" % h for h in range(4)] + ["nL", "BtL", "abL"], writes=["cc_in"])
    P.dma("pool", None, None, reads=["cc_in"], writes=["cc_out"], inc=1,
          custom=lambda e: e.collective_compute("AllGather", ALU.bypass, replica_groups=[list(range(8))], ins=[cc_in], outs=[cc_out]))
    for q in range(8):
        pq = payq[q % 2]
        pk = "payq%d" % (q % 2)
        P.dma("sp", pq, cc_out[q * 128:(q + 1) * 128, :], reads=["cc_out"], writes=[pk])
        act_q = fm_[:, 0, q:q + 1]
        P.op("dve", TS(eq[:, 0:4], pq[:, 2056:2060], act_q), reads=[pk, "fm_"], writes=["eq"])
        P.op("act", ACT(eq[:, 4:8], eq[:, 0:4], AF.Exp), reads=["eq"], writes=["eqe"])
        Sq = pq[:, 0:2048].rearrange("p (h de) -> p h de", h=4)
        for h in range(4):
            Sh = SX[:, h].rearrange("p d e -> p (d e)")
            P.op("act", ACT(Sh, Sh, AF.Copy, scale=eq[:, 4 + h:5 + h]), reads=["SX%d" % h, "eqe"], writes=["SX%d" % h])
            P.op("dve", STT(Sh, Sq[:, h, :], act_q, Sh, ALU.mult, ALU.add), reads=[pk, "fm_", "SX%d" % h], writes=["SX%d" % h])
        P.op("dve", TT(nX, nX, eq[:, 4:8].unsqueeze(2).broadcast_to([128, 4, 2]), ALU.mult), reads=["nX", "eqe"], writes=["nX"])
        P.op("dve", STT(nX, pq[:, 2048:2056].rearrange("p (h d) -> p h d", h=4), act_q, nX, ALU.mult, ALU.add), reads=[pk, "fm_", "nX"], writes=["nX"])
        P.op("dve", TT(mB, mB, eq[:, 0:4], ALU.add), reads=["mB", "eq"], writes=["mB"])
        P.op("dve", TS(tq, pq[:, 2060:2064], act_q, fm_[:, 1, q:q + 1], ALU.mult, ALU.add), reads=[pk, "fm_"], writes=["tq"])
        P.op("dve", TT(mB, mB, tq, ALU.max), reads=["mB", "tq"], writes=["mB"])
    P.op("pe", TR(psf(7)[0:4, 0:128], mB, identf), reads=["mB", "cst"], writes=[PK[7]])
    P.op("dve", CP(mP[0:4, 0:1], psf(7)[0:4, 0:1]), reads=[PK[7]], writes=["mP"])
    P.barrier()
    A.reset(persist_mark)

    gate_alloc()
    RS = 6
    mnb = A.alloc([1024], F32)
    btab = A.alloc([8, 5, 128], BF16)
    bs4 = A.alloc([8, 128], BF16)
    P.dma("sp", mnb, mnormB, writes=["mnb"])
    P.dma("pool", btab, bias_tab, writes=["btab"])
    P.dma("pool", bs4, bias_s4, writes=["bs4"])
    kring = [A.alloc([8, 128], BF16) for _ in range(RS)]
    vring = [A.alloc([8, 129], BF16) for _ in range(RS)]
    qbuf = [A.alloc([8, 128], BF16) for _ in range(2)]
    mbuf = [A.alloc([16, 128], BF16) for _ in range(2)]
    tbuf = [A.alloc([3072], BF16) for _ in range(2)]
    gbuf = [A.alloc([8], F32) for _ in range(2)]
    expT = [A.alloc([5, 128], BF16) for _ in range(2)]
    es0 = [A.alloc([4, 128], BF16) for _ in range(2)]
    es1 = [A.alloc([4, 128], BF16) for _ in range(2)]
    vp = [A.alloc([4, 256], BF16) for _ in range(2)]
    qz = [[A.alloc([8, 128], BF16) for _c in range(2)] for _ in range(2)]
    qkm = [A.alloc([4, 128], BF16) for _ in range(2)]
    mix = [A.alloc([2048], BF16) for _ in range(2)]
    mixTs = [A.alloc([KT, 128], BF16) for _ in range(2)]
    gsn = [A.alloc([1024], F32) for _ in range(2)]
    SY = A.alloc([4, 2, 256], F32)
    SXb = A.alloc([4, 2, 256], BF16)
    SYb = A.alloc([4, 2, 256], BF16)
    nY = A.alloc([4, 2], F32)
    nXb = A.alloc([4, 2], BF16)
    nYb = A.alloc([4, 2], BF16)
    rden = A.alloc([8], F32)
    dd = A.alloc([4], F32)
    ddn = A.alloc([4], F32)
    ssh = A.alloc([8], F32)
    scl = A.alloc([4], F32)
    hjunk = A.alloc([256], BF16)
    ms = A.alloc([2], F32)
    mcolt = A.alloc([4], F32)
    d4 = A.alloc([4], F32)
    emn = A.alloc([4], F32)
    em_in = A.alloc([2, 4], F32)
    Cst = A.alloc([4, 2, 256], F32)
    nst = A.alloc([4, 2], F32)
    nst8 = A.alloc([128], F32)
    nld = A.alloc([128], F32)
    kcT = [A.alloc([8, 512], BF16) for _ in range(2)]
    vc = [A.alloc([4, 8, 129], BF16) for _ in range(2)]
    kld = [A.alloc([1024], BF16) for _ in range(2)]
    SXK = ["SX%d" % h for h in range(4)]
    SYK = ["SY%d" % h for h in range(4)]
    for h in range(4):
        P.op("pool", CP(SXb[:, h], SX[:, h]), reads=["SX%d" % h], writes=["SXb%d" % h])
    P.op("pool", CP(nXb, nX), reads=["nX"], writes=["nXb"])
    for c in range(2):
        for i2 in range(2):
            P.op("pool", MSET(qz[i2][c], 0.0), writes=["qz%d_%d" % (i2, c)])
            P.op("pool", MSET(es0[i2], 0.0), writes=["es0_%d" % i2])
            P.op("pool", MSET(es1[i2], 0.0), writes=["es1_%d" % i2])

    def load_kv(u):
        s = u % RS
        P.dma("sp", kring[s], scr_fm[8:16, :, u * 128:(u + 1) * 128].rearrange("c p t -> p c t"), writes=["kring%d" % s])
        P.dma("sp", vring[s], scr_va[u].rearrange("p (h e) -> p h e", h=8), writes=["vring%d" % s])

    for u in range(4):
        load_kv(u)

    def emit_state(w, Sf, Skeys, nf, nkey, mt, mkey):
        P.op("dve", CP(mcolt[0:4, w:w + 1], mt), reads=[mkey], writes=["mcolt"])
        P.op("dve", TS(d4[0:4, 0:4], identf[0:4, 0:4], mt), reads=[mkey, "cst"], writes=["d4"])
        P.op("pe", MM(psf(1)[:, 0:4], ones4, d4[0:4, 0:4]), reads=["cst", "d4"], writes=[PK[1]])
        P.op("act", ACT(emn, psf(1)[:, 0:4], AF.Exp, scale=-1.0), reads=[PK[1]], writes=["emn"])
        for h in range(4):
            P.op("dve", TS(Cst[:, h].rearrange("p d e -> p (d e)"), Sf[:, h].rearrange("p d e -> p (d e)"), emn[:, h:h + 1]),
                 reads=[Skeys[h], "emn"], writes=["Cst"])
        P.dma("sp", C_out[w].rearrange("h (d p) e -> p h d e", p=128), Cst, reads=["Cst"], writes=["C_out"])
        P.op("dve", TT(nst, nf, emn.unsqueeze(2).broadcast_to([128, 4, 2]), ALU.mult), reads=[nkey, "emn"], writes=["nst"])
        P.op("pe", TR(psf(1)[0:8, 128:256], nst.rearrange("p h d -> p (h d)"), identf), reads=["nst", "cst"], writes=[PK[1]])
        P.op("dve", CP(nst8[0:8, :], psf(1)[0:8, 128:256]), reads=[PK[1]], writes=["nst8"])
        P.dma("sp", n_out[w].rearrange("h (d p) -> (h d) p", p=128), nst8[0:8, :], reads=["nst8"], writes=["n_out"])

    def load3(i):
        u = 4 + i
        i2 = i % 2
        load_kv(u)
        P.dma("sp", qbuf[i2], scr_fm[0:8, :, u * 128:(u + 1) * 128].rearrange("c p t -> p c t"), writes=["qbuf%d" % i2])
        P.dma("sp", mbuf[i2], scr_fm[16:32, :, u * 128:(u + 1) * 128].rearrange("c p t -> p c t"), writes=["mbuf%d" % i2])
        P.dma("sp", tbuf[i2], scr_tm[i], writes=["tbuf%d" % i2])
        P.dma("sp", gbuf[i2], scr_g[i], writes=["gbuf%d" % i2])

    GL = {}

    def views(i):
        i2 = i % 2
        return dict(i2=i2, u=4 + i, sample=(i == NM - 1), slot=(4 + i) % RS,
                    kmT=mbuf[i2][:, 8:16, :], qmT=mbuf[i2][:, 0:8, :], km_tm=tbuf[i2][:, 0:1024],
                    vm_tm=tbuf[i2][:, 1024:2048].rearrange("p (h e) -> p h e", h=4), om_tm=tbuf[i2][:, 2048:3072],
                    tk="tbuf%d" % i2, mk="mbuf%d" % i2, qk="qbuf%d" % i2)

    def sample_setup():
        for seq in range(2):
            for kt in range(4):
                li = (seq * 4 + kt) % 2
                P.dma("pool", kld[li], cache_k[seq, kt * 128:(kt + 1) * 128, :], writes=["kld%d" % li])
                for hh2 in range(2):
                    b = 4 + hh2
                    pt = psb(b).rearrange("p (k t) -> p k t", k=8)
                    for h4 in range(4):
                        h = hh2 * 4 + h4
                        P.op("pe", TR(pt[:, h4, :], kld[li][:, h * 128:(h + 1) * 128], identb), reads=["kld%d" % li, "identb"],
                             writes=[PK[b]], signal=(h4 == 3))
                    evac(kcT[seq][:, hh2 * 4:(hh2 + 1) * 4, kt * 128:(kt + 1) * 128], pt[:, 0:4, :], [PK[b]], ["kcT%d" % seq],
                         eng=("act" if hh2 == 0 else "dve"))
            for kt in range(4):
                P.dma("pool", vc[seq][:, kt, :, 0:128], cache_v[seq, kt * 128:(kt + 1) * 128, :].rearrange("p (h e) -> p h e", h=8),
                      writes=["vc%d_%d" % (seq, kt)])
            P.op("pool", MSET(vc[seq][:, :, :, 128], 1.0), writes=["vc%dx" % seq])
        P.dma("sp", em_in, st_mB, writes=["em_in"])
        P.op("act", ACT(em_in, em_in, AF.Exp), reads=["em_in"], writes=["em_in"])
        P.dma("sp", ms[0:4, 0:2], st_m4, writes=["ms0", "ms1"])
        for seq, (Sf, SK, Sb_, SbK, nf, nK, nb_, nbK) in enumerate(((SX, "SX", SXb, "SXb", nX, "nX", nXb, "nXb"),
                                                                     (SY, "SY", SYb, "SYb", nY, "nY", nYb, "nYb"))):
            P.dma("sp", Sf, st_C[seq].rearrange("h (d p) e -> p h d e", p=128), writes=[SK + str(h) for h in range(4)])
            P.dma("sp", nld[0:8, :], st_n[seq].rearrange("h (d p) -> (h d) p", p=128), writes=["nld"])
            P.op("pe", TR(psf(1)[:, 0:8], nld[0:8, :], identf[0:8, 0:8]), reads=["nld", "cst"], writes=[PK[1]])
            P.op("dve", TT(nf, psf(1)[:, 0:8].rearrange("p (h d) -> p h d", h=4), em_in[:, seq, :].unsqueeze(2).broadcast_to([128, 4, 2]), ALU.mult),
                 reads=[PK[1], "em_in"], writes=[nK])
            P.op("dve", CP(nb_, nf), reads=[nK], writes=[nbK])
            for h in range(4):
                P.op("dve", TS(Sf[:, h].rearrange("p d e -> p (d e)"), Sf[:, h].rearrange("p d e -> p (d e)"), em_in[:, seq, h:h + 1]),
                     reads=[SK + str(h), "em_in"], writes=[SK + str(h)])
                P.op("pool", CP(Sb_[:, h], Sf[:, h]), reads=[SK + str(h)], writes=[SbK + str(h)])

    def PA(i):
        V = views(i)
        i2 = V["i2"]
        G = gate_math(gbuf[i2], ["gbuf%d" % i2], NPRE + i, 1, 1)
        GL[i] = G
        K = G["K"]
        P.op("dve", TT(vp[i2], V["vm_tm"], G["ex"][:, 0:4].unsqueeze(2).broadcast_to([128, 4, 256]), ALU.mult),
             reads=[V["tk"], K + "ex"], writes=["vp%d" % i2])
        for c in range(2):
            P.op("pool", CP(qz[i2][c][:, :, 64 * c:64 * c + 64], V["qmT"][:, :, 64 * c:64 * c + 64]), reads=[V["mk"]], writes=["qz%d_%d" % (i2, c)])
        for h in range(4):
            for d in range(2):
                P.op("pe", MM(psf(0)[:, h * 128:(h + 1) * 128], V["kmT"][:, h * 2 + d, :], V["qmT"][:, h * 2 + d, :], d == 0, d == 1),
                     reads=[V["mk"]], writes=[PK[0]], signal=(h == 3 and d == 1))
        P.op("dve", TT(qkm[i2], psf(0).rearrange("p (h t) -> p h t", h=4), tri.unsqueeze(1).broadcast_to([128, 4, 128]), ALU.mult),
             reads=[PK[0], "cst"], writes=["qkm%d" % i2])
        P.op("act", ACT(gsn[i2], V["om_tm"], AF.Sigmoid), reads=[V["tk"]], writes=["gsn%d" % i2])
        P.op("pool", TT(gsn[i2], gsn[i2], mnb, ALU.mult), reads=["gsn%d" % i2, "mnb"], writes=["gsn%d" % i2])

    def PB(i):
        V = views(i)
        i2 = V["i2"]
        G = GL[i]
        n_update(nY, "nY", 1, G, 0, V["km_tm"], V["tk"], nb=nYb, nb_key="nYb", n_src=nX, n_src_key="nX")
        for h in range(4):
            state_update(SY, "SY", None, None, h % 2, G, 0, h, V["km_tm"], V["tk"], vp[i2], "vp%d" % i2, Sb=SYb, Sb_key="SYb",
                         S_src=SX, S_src_key="SX")

    def ATT(i, heads):
        V = views(i)
        i2, u, sample, slot, qk_ = V["i2"], V["u"], V["sample"], V["slot"], V["qk"]
        heads = list(heads)
        for h in heads:
            e = h % 2
            bx, by = (2, 3) if e == 0 else (4, 5)
            X, Y = psf(bx), psf(by)
            if not sample:
                for kt in range(5):
                    ks = (u - 4 + kt) % RS
                    o = X[:, kt * 128:(kt + 1) * 128] if kt < 4 else Y[:, 0:128]
                    bk = PK[bx] if kt < 4 else PK[by]
                    P.op("pe", MM(o, kring[ks][:, h, :], qbuf[i2][:, h, :], True, False), reads=["kring%d" % ks, qk_], writes=[bk], signal=False)
                    P.op("pe", MM(o, identb, btab[:, h, kt, :], False, True), reads=["identb", "btab"], writes=[bk], signal=(kt >= 3))
                P.op("act", ACT(expT[e][:, 0:4, :], X.rearrange("p (k t) -> p k t", k=4), AF.Exp), reads=[PK[bx]], writes=["expT%da" % e])
                P.op("act", ACT(expT[e][:, 4, :], Y[:, 0:128], AF.Exp), reads=[PK[by]], writes=["expT%db" % e])
            else:
                for kt in range(4):
                    for seq in range(2):
                        o = X[:, kt * 128 + seq * 64:kt * 128 + seq * 64 + 64]
                        P.op("pe", MM(o, kcT[seq][:, h, kt * 128:(kt + 1) * 128], qbuf[i2][:, h, seq * 64:seq * 64 + 64], True, False),
                             reads=["kcT%d" % seq, qk_], writes=[PK[bx]], signal=False)
                        P.op("pe", MM(o, identb, btab[:, h, kt, 0:64], False, True), reads=["identb", "btab"], writes=[PK[bx]],
                             signal=(kt == 3 and seq == 1))
                P.op("pe", MM(Y[:, 0:128], kring[slot][:, h, :], qbuf[i2][:, h, :], True, False), reads=["kring%d" % slot, qk_], writes=[PK[by]], signal=False)
                P.op("pe", MM(Y[:, 0:128], identb, bs4[:, h, :], False, True), reads=["identb", "bs4"], writes=[PK[by]])
                X3 = X.rearrange("p (k t) -> p k t", k=4)
                P.op("act", ACT(es0[e][:, :, 0:64], X3[:, :, 0:64], AF.Exp), reads=[PK[bx]], writes=["es0_%d" % e])
                P.op("act", ACT(es1[e][:, :, 64:128], X3[:, :, 64:128], AF.Exp), reads=[PK[bx]], writes=["es1_%d" % e])
                P.op("act", ACT(expT[e][:, 4, :], Y[:, 0:128], AF.Exp), reads=[PK[by]], writes=["expT%db" % e])
        for h in heads:
            e = h % 2
            g = h // 3
            pb = 6 + g % 2
            hh = h % 3
            PV = psf(pb)
            o = PV[:, hh * 129:(hh + 1) * 129]
            if not sample:
                for kt in range(5):
                    ks = (u - 4 + kt) % RS
                    P.op("pe", MM(o, expT[e][:, kt, :], vring[ks][:, h, :], kt == 0, kt == 4),
                         reads=["expT%da" % e if kt < 4 else "expT%db" % e, "vring%d" % ks], writes=[PK[pb]], signal=(kt == 4))
            else:
                for kt in range(4):
                    P.op("pe", MM(o, es0[e][:, kt, :], vc[0][:, kt, h, :], kt == 0, False), reads=["es0_%d" % e, "vc0_%d" % kt, "vc0x"], writes=[PK[pb]], signal=False)
                    P.op("pe", MM(o, es1[e][:, kt, :], vc[1][:, kt, h, :], False, False), reads=["es1_%d" % e, "vc1_%d" % kt, "vc1x"], writes=[PK[pb]], signal=False)
                P.op("pe", MM(o, expT[e][:, 4, :], vring[slot][:, h, :], False, True), reads=["expT%db" % e, "vring%d" % slot], writes=[PK[pb]])
            if h in (2, 5, 7):
                nh = hh + 1
                h0 = h - hh
                PV3 = PV[:, 0:nh * 129].rearrange("p (h e) -> p h e", h=nh)
                P.op("dve", TS(rden[:, 0:nh], PV3[:, :, 128], 1e-30, None, ALU.add), reads=[PK[pb]], writes=["rden"])
                P.op("dve", lambda e_, o_=rden[:, 0:nh]: e_.reciprocal(out=o_, in_=o_), reads=["rden"], writes=["rden"])
                P.op("dve", TT(mix[i2][:, h0 * 128:(h0 + nh) * 128].rearrange("p (h e) -> p h e", h=nh), PV3[:, :, 0:128],
                               rden[:, 0:nh].unsqueeze(2).broadcast_to([128, nh, 128]), ALU.mult),
                     reads=[PK[pb], "rden"], writes=["mix%d_a%d" % (i2, g)])

    def HD(i):
        V = views(i)
        i2, sample, tk, km_tm, om_tm = V["i2"], V["sample"], V["tk"], V["km_tm"], V["om_tm"]
        G = GL[i]
        K = G["K"]
        for h in range(4):
            hb = h // 2
            Hh = psf(hb)[:, (h % 2) * 256:(h % 2 + 1) * 256]
            P.op("pe", MM(Hh, qkm[i2][:, h, :], vp[i2][:, h, :], True, False), reads=["qkm%d" % i2, "vp%d" % i2], writes=[PK[hb]], signal=False)
            for c, (Sb_, SbK) in enumerate(((SXb, "SXb"), (SYb, "SYb"))):
                for d in range(2):
                    P.op("pe", MM(Hh, qz[i2][c][:, h * 2 + d, :], Sb_[:, h, d, :], False, c == 1 and d == 1),
                         reads=["qz%d_%d" % (i2, c), SbK + str(h)], writes=[PK[hb]], signal=(c == 1 and d == 1))
        DEN = psf(7)[:, 400:404]
        for h in range(4):
            o = DEN[:, h:h + 1]
            P.op("pe", MM(o, qkm[i2][:, h, :], G["gbf"][:, h:h + 1], True, False), reads=["qkm%d" % i2, K + "gbf"], writes=[PK[7]], signal=False)
            for c, (nb_, nbK) in enumerate(((nXb, "nXb"), (nYb, "nYb"))):
                for d in range(2):
                    P.op("pe", MM(o, qz[i2][c][:, h * 2 + d, :], nb_[:, h, d:d + 1], False, c == 1 and d == 1),
                         reads=["qz%d_%d" % (i2, c), nbK], writes=[PK[7]], signal=(h == 3 and c == 1 and d == 1))
        eb = G["ex"][:, 4:8]
        P.op("dve", TT(dd, DEN, eb, ALU.mult), reads=[PK[7], K + "ex"], writes=["dd"])
        P.op("dve", TS(ddn, dd, -1.0), reads=["dd"], writes=["ddn"])
        P.op("dve", TT(dd, dd, ddn, ALU.max), reads=["dd", "ddn"], writes=["dd"])
        P.op("dve", TS(dd, dd, 1.0, None, ALU.max), reads=["dd"], writes=["dd"])
        P.op("dve", lambda e_, o_=dd, i_=dd: e_.reciprocal(out=o_, in_=i_), reads=["dd"], writes=["dd"])
        P.op("dve", TT(dd, dd, eb, ALU.mult), reads=["dd", K + "ex"], writes=["dd"])
        for h in range(4):
            hb = h // 2
            Hh = psf(hb)[:, (h % 2) * 256:(h % 2 + 1) * 256]
            P.op("act", ACT(hjunk, Hh, AF.Square, scale=dd[:, h:h + 1], accum_out=ssh[:, h:h + 1]), reads=[PK[hb], "dd"], writes=["hjunk", "ssh%d" % h])
        rstd_from_ss(ssh[:, 0:4], ssh[:, 4:8], 256, ["ssh%d" % h for h in range(4)], "sshr")
        P.op("dve", TT(scl, dd, ssh[:, 4:8], ALU.mult), reads=["dd", "sshr"], writes=["scl"])
        for h in range(4):
            hb = h // 2
            Hh = psf(hb)[:, (h % 2) * 256:(h % 2 + 1) * 256]
            P.op("dve", STT(mix[i2][:, 1024 + h * 256:1024 + (h + 1) * 256], Hh, scl[:, h:h + 1], gsn[i2][:, h * 256:(h + 1) * 256], ALU.mult, ALU.mult),
                 reads=[PK[hb], "scl", "gsn%d" % i2], writes=["mix%d_m%d" % (i2, h)])

    def SC1(i):
        V = views(i)
        i2, sample, tk, km_tm = V["i2"], V["sample"], V["tk"], V["km_tm"]
        G = GL[i]
        if not sample:
            n_update(nX, "nX", 1, G, 1, km_tm, tk, nb=nXb, nb_key="nXb", n_src=nY, n_src_key="nY")
            for h in range(4):
                state_update(SX, "SX", None, None, h % 2, G, 1, h, km_tm, tk, vp[i2], "vp%d" % i2, Sb=SXb, Sb_key="SXb",
                             S_src=SY, S_src_key="SY")
            m_update(mP[0:4, 0:1], "mP", G, 0)
            m_update(mP[0:4, 0:1], "mP", G, 1)
            if i == NOWN:
                emit_state(0, SX, SXK, nX, "nX", mP[0:4, 0:1], "mP")
        else:
            n_update(nX, "nX", 1, G, 0, km_tm, tk)
            n_update(nY, "nY", 1, G, 1, km_tm, tk)
            for h in range(4):
                state_update(SX, "SX", None, None, h % 2, G, 0, h, km_tm, tk, vp[i2], "vp%d" % i2)
            for h in range(4):
                state_update(SY, "SY", None, None, h % 2, G, 1, h, km_tm, tk, vp[i2], "vp%d" % i2)
            m_update(ms[0:4, 0:1], "ms0", G, 0)
            m_update(ms[0:4, 1:2], "ms1", G, 1)
            emit_state(1, SX, SXK, nX, "nX", ms[0:4, 0:1], "ms0")
            emit_state(2, SY, SYK, nY, "nY", ms[0:4, 1:2], "ms1")

    def TRS(i):
        i2 = i % 2
        mixkeys = ["mix%d_a%d" % (i2, g) for g in range(3)] + ["mix%d_m%d" % (i2, h) for h in range(4)]
        for half in range(2):
            b = 4 + half
            pt = psb(b).rearrange("p (k t) -> p k t", k=8)
            for k8 in range(8):
                k = half * 8 + k8
                P.op("pe", TR(pt[:, k8, :], mix[i2][:, k * 128:(k + 1) * 128], identb), reads=mixkeys + ["identb"], writes=[PK[b]], signal=(k8 == 7))
            evac(mixTs[i2][:, half * 8:(half + 1) * 8, :], pt, [PK[b]], ["mixTs%d_%d" % (i2, half)], eng=("act" if half == 0 else "dve"))
        P.dma("act", scr_mixT[i], mixTs[i2].rearrange("p k t -> p (k t)"), reads=["mixTs%d_0" % i2, "mixTs%d_1" % i2], writes=["scr_mixT"])

    load3(0)
    PA(0)
    for i in range(NM):
        sample_i = (i == NM - 1)
        if i + 1 < NM:
            load3(i + 1)
        ATT(i, [0, 1])
        if not sample_i:
            PB(i)
        if i > 0:
            TRS(i - 1)
        ATT(i, [2, 3])
        if i + 1 < NM:
            PA(i + 1)
        ATT(i, [4, 5])
        HD(i)
        ATT(i, [6, 7])
        SC1(i)
        if i + 1 == NM - 1:
            sample_setup()
    TRS(NM - 1)
    P.dma("sp", m_out, mcolt[0:4, 0:3], reads=["mcolt"], writes=["m_out"])
    P.barrier()
    A.reset(base_mark)
    if last_stage == 3:
        P.run()
        return nc


    wo = A.alloc([KT, D], BF16)
    for q in range(4):
        P.dma("pool", wo[:, :, q * 512:(q + 1) * 512], w_out[:, q * 512:(q + 1) * 512].rearrange("(k p) n -> p k n", p=128), writes=["wo%d" % q])
    gB1 = [A.alloc([D], F32) for _ in range(2)]
    build_gtgB(gB1[0], gB1[1], 0, "gB1p", "gB1s")
    fe_alloc(with_xs=False)
    xs4 = [A.alloc([D], F32) for _ in range(2)]
    x1s = [A.alloc([D], F32) for _ in range(2)]
    mT = [A.alloc([KT, 128], BF16) for _ in range(2)]
    h2s = [A.alloc([KT, 128], BF16) for _ in range(2)]
    ssq = [A.alloc([8], F32) for _ in range(2)]
    junk4 = A.alloc([512], BF16)
    def load4(i):
        P.dma("sp", mT[i % 2], scr_mixT[i].rearrange("p (k t) -> p k t", k=KT), writes=["mT%d" % (i % 2)])
        P.dma("sp", xs4[i % 2], x_main[i * 128:(i + 1) * 128, :], writes=["xs4_%d" % (i % 2)])

    load4(0)
    for i in range(NM):
        i2 = i % 2
        sample = (i == NM - 1)
        if i + 1 < NM:
            load4(i + 1)
        bo = 4 * i2
        for cb in range(4):
            for k in range(KT):
                P.op("pe", MM(psf(bo + cb), mT[i2][:, k, :], wo[:, k, cb * 512:(cb + 1) * 512], k == 0, k == KT - 1),
                     reads=["mT%d" % i2, "wo%d" % cb], writes=[PK[bo + cb]], signal=(k == KT - 1))
        for cb in range(4):
            P.op("act", ACT(junk4, psf(bo + cb), AF.Square, accum_out=ssq[i2][:, cb:cb + 1]), reads=[PK[bo + cb]], writes=["junk4", "ssq%d_%d" % (i2, cb)])
        P.op("dve", lambda e_, o_=ssq[i2][:, 4:5], i_=ssq[i2][:, 0:4]: e_.tensor_reduce(out=o_, in_=i_, axis=AX.X, op=ALU.add),
             reads=["ssq%d_%d" % (i2, cb) for cb in range(4)], writes=["ssq%d_s" % i2])
        rstd_from_ss(ssq[i2][:, 4:5], ssq[i2][:, 5:6], D, ["ssq%d_s" % i2], "ssq%d_r" % i2)
        gB, gBk = (gB1[1], "gB1s") if sample else (gB1[0], "gB1p")
        x1keys = []
        for cb in range(4):
            sl = slice(cb * 512, (cb + 1) * 512)
            key = "x1s%d_%d" % (i2, cb)
            x1keys.append(key)
            P.op("dve", STT(x1s[i2][:, sl], psf(bo + cb), ssq[i2][:, 5:6], gB[:, sl], ALU.mult, ALU.mult), reads=[PK[bo + cb], "ssq%d_r" % i2, gBk], writes=[key])
            P.op("pool", TT(x1s[i2][:, sl], x1s[i2][:, sl], xs4[i2][:, sl], ALU.add), reads=[key, "xs4_%d" % i2], writes=[key])
        if i > 0:
            P.dma("act", scr_x1[i], x1s[i2], reads=x1keys, writes=["scr_x1"])
        rows = (1, 2) if sample else (0, 0)
        front_end(None, h2s[i2], "h2s%d" % i2, 3, 2, rows, tb=(bo, bo + 1), xs_in=x1s[i2], xs_keys=x1keys)
        hk = []
        for k in range(KT):
            hk.append("h2s%d_k%d_0" % (i2, k))
            if sample:
                hk.append("h2s%d_k%d_64" % (i2, k))
        P.dma("act", scr_h2T[:, :, i * 128:(i + 1) * 128], h2s[i2], reads=hk, writes=["scr_h2T"])
    P.barrier()
    A.reset(base_mark)
    if last_stage == 4:
        P.run()
        return nc

    TMp = 128 + OWN
    base_s0 = 2 + TMp + 2
    base_s1 = base_s0 + 64 + 2
    ROW = base_s1 + 64
    h2T_all = A.alloc([KT, TM], BF16)
    for q in range(4):
        P.dma("sp", h2T_all[:, q * 4:(q + 1) * 4, :], scr_h2T[:, q * 4:(q + 1) * 4, :], writes=["h2T_%d" % q])
    H2K = ["h2T_%d" % q for q in range(4)]
    cw = A.alloc([FT, 4], F32)
    P.dma("sp", cw, cwT, writes=["cw"])
    cst_tm = A.alloc([DFF], F32)
    cstT = A.alloc([FT, 4], F32)
    P.dma("sp", cst_tm[0:4, :], conv_st, writes=["tmp22"])
    for ft in range(FT):
        P.op("pe", TR(psf(7)[:, ft * 4:(ft + 1) * 4], cst_tm[0:4, ft * 128:(ft + 1) * 128], identf[0:4, 0:4]), reads=["tmp22", "cst"],
             writes=[PK[7]], signal=(ft == FT - 1))
    P.op("dve", CP(cstT.rearrange("p f c -> p (f c)"), psf(7)[:, 0:FT * 4]), reads=[PK[7]], writes=["cstT"])
    ugrow = [A.alloc([ROW], F32) for _ in range(2)]
    acc = [A.alloc([512], F32) for _ in range(2)]
    gl = [A.alloc([512], F32) for _ in range(2)]
    arow = [A.alloc([TM], BF16) for _ in range(2)]
    convsave = A.alloc([FT, 6], F32)
    cso = cst_tm
    wgb = [A.alloc([KT, 512], BF16) for _ in range(2)]
    wub = [A.alloc([KT, 512], BF16) for _ in range(2)]
    for i2 in range(2):
        P.op("pool", MSET(ugrow[i2][:, 0:2], 0.0), writes=["ug%d_pad" % i2])
    blocks = []
    for m0 in range(0, TMp, 512):
        m1 = min(m0 + 512, TMp)
        blocks.append((m0, m1, [(m0, m1, 2 + m0)]))
    blocks.append((TMp, TMp + 128, [(TMp, TMp + 64, base_s0), (TMp + 64, TMp + 128, base_s1)]))
    nG = 0
    nU = 0
    npc = 0
    for ftg in range(FT // 4):
        wi = ftg % 2
        P.dma("pool", wgb[wi], w_g[:, ftg * 512:(ftg + 1) * 512].rearrange("(k p) n -> p k n", p=128), writes=["wgb%d" % wi])
        P.dma("pool", wub[wi], w_u[:, ftg * 512:(ftg + 1) * 512].rearrange("(k p) n -> p k n", p=128), writes=["wub%d" % wi])
        for sub in range(4):
            ft = ftg * 4 + sub
            u2 = ft % 2
            ug = ugrow[u2]
            P.op("pool", CP(ug[:, base_s0 - 2:base_s0], cstT[:, ft, 0:2]), reads=["cstT"], writes=["ug%d_s0" % u2])
            P.op("pool", CP(ug[:, base_s1 - 2:base_s1], cstT[:, ft, 2:4]), reads=["cstT"], writes=["ug%d_s1" % u2])
            akeys = []
            for bi, (m0, m1, pieces) in enumerate(blocks):
                n = m1 - m0
                gb = nG % 3
                nG += 1
                ub = 3 + nU % 4
                nU += 1
                for k in range(KT):
                    P.op("pe", MM(psf(gb)[:, 0:n], wgb[wi][:, k, sub * 128:(sub + 1) * 128], h2T_all[:, k, m0:m1], k == 0, k == KT - 1),
                         reads=["wgb%d" % wi, H2K[k // 4]], writes=[PK[gb]], signal=(k == KT - 1))
                for k in range(KT):
                    P.op("pe", MM(psf(ub)[:, 0:n], wub[wi][:, k, sub * 128:(sub + 1) * 128], h2T_all[:, k, m0:m1], k == 0, k == KT - 1),
                         reads=["wub%d" % wi, H2K[k // 4]], writes=[PK[ub]], signal=(k == KT - 1))
                for (p0, p1, c0) in pieces:
                    pn = p1 - p0
                    ukey = "ug%d_b%d_%d" % (u2, bi, p0)
                    P.op("act", ACT(ug[:, c0:c0 + pn], psf(gb)[:, p0 - m0:p1 - m0], AF.Copy), reads=[PK[gb]], writes=[ukey])
                    prev = ["ug%d_pad" % u2, "ug%d_s0" % u2, "ug%d_s1" % u2]
                    if bi > 0:
                        prev += ["ug%d_b%d_%d" % (u2, bi - 1, blocks[bi - 1][2][-1][0])]
                    if bi == 0:
                        P.op("pool", TS(ug[:, 2:130], ug[:, 2:130], msk[:, 0, NPRE:NPRE + 1]), reads=[ukey, "msk"], writes=[ukey])
                    a_ = acc[npc % 2]
                    g_ = gl[npc % 2]
                    ak, gk = "acc%d" % (npc % 2), "gl%d" % (npc % 2)
                    npc += 1
                    P.op("act", ACT(a_[:, 0:pn], ug[:, c0 - 2:c0 - 2 + pn], AF.Identity, scale=cw[:, ft, 0:1], bias=cw[:, ft, 3:4]),
                         reads=[ukey, "cw"] + prev, writes=[ak])
                    P.op("dve", STT(a_[:, 0:pn], ug[:, c0 - 1:c0 - 1 + pn], cw[:, ft, 1:2], a_[:, 0:pn], ALU.mult, ALU.add),
                         reads=[ukey, "cw", ak] + prev, writes=[ak])
                    P.op("dve", STT(a_[:, 0:pn], ug[:, c0:c0 + pn], cw[:, ft, 2:3], a_[:, 0:pn], ALU.mult, ALU.add), reads=[ukey, "cw", ak], writes=[ak])
                    P.op("act", ACT(g_[:, 0:pn], a_[:, 0:pn], AF.Gelu), reads=[ak], writes=[gk])
                    akey = "arow%d_%d" % (u2, p0)
                    akeys.append(akey)
                    P.op("dve", TT(arow[u2][:, p0:p1], g_[:, 0:pn], psf(ub)[:, p0 - m0:p1 - m0], ALU.mult), reads=[gk, PK[ub]], writes=[akey])
            allug = ["ug%d_b%d_%d" % (u2, bi, pc[0]) for bi, (_, _, pcs) in enumerate(blocks) for pc in pcs]
            P.op("pool", CP(convsave[:, ft, 0:2], ug[:, 2 + TMp - 2:2 + TMp]), reads=allug, writes=["convsave"])
            P.op("pool", CP(convsave[:, ft, 2:4], ug[:, base_s0 + 62:base_s0 + 64]), reads=allug, writes=["convsave"])
            P.op("pool", CP(convsave[:, ft, 4:6], ug[:, base_s1 + 62:base_s1 + 64]), reads=allug, writes=["convsave"])
            P.dma("sp", scr_aT[ft], arow[u2], reads=akeys, writes=["scr_aT"])
    for g4 in range(FT // 4):
        b = g4 % 2
        for s4 in range(4):
            ft = g4 * 4 + s4
            P.op("pe", TR(psf(b)[0:6, s4 * 128:(s4 + 1) * 128], convsave[:, ft, :], identf), reads=["convsave", "cst"], writes=[PK[b]], signal=(s4 == 3))
        P.op("dve", CP(cso[0:6, g4 * 512:(g4 + 1) * 512], psf(b)[0:6, :]), reads=[PK[b]], writes=["tmp22"])
    P.dma("sp", conv_out, cso[0:6, :], reads=["tmp22"], writes=["conv_out"])
    P.barrier()
    A.reset(base_mark)
    if last_stage == 5:
        P.run()
        return nc

    GS = 6
    gB2 = [A.alloc([D], F32) for _ in range(2)]
    build_gtgB(gB2[0], gB2[1], 1, "gB2p", "gB2s")
    aTg = A.alloc([FT, GS * 128], BF16)
    ystage = [A.alloc([D], F32) for _ in range(GS)]
    x1t = [A.alloc([D], F32) for _ in range(2)]
    NWD = 5
    wd = [A.alloc([4, 512], BF16) for _ in range(NWD)]
    ssq6 = A.alloc([GS, 8], F32)
    junk6 = A.alloc([512], BF16)
    out_tiles = list(range(1, NM))
    nwd = 0
    nx1 = 0
    for g0 in range(0, len(out_tiles), GS):
        tiles = out_tiles[g0:g0 + GS]
        nt = len(tiles)
        tok0 = tiles[0] * 128
        ntok = nt * 128
        for q in range(4):
            P.dma("sp", aTg[:, q * 11:(q + 1) * 11, 0:ntok], scr_aT[q * 11:(q + 1) * 11, :, tok0:tok0 + ntok].rearrange("f p t -> p f t"),
                  writes=["aTg_%d" % q])
        for cb in range(4):
            for ftq in range(FT // 4):
                wi = nwd % NWD
                nwd += 1
                P.dma("pool", wd[wi], w_d[ftq * 512:(ftq + 1) * 512, cb * 512:(cb + 1) * 512].rearrange("(f p) n -> p f n", p=128), writes=["wd%d" % wi])
                for s4 in range(4):
                    ft = ftq * 4 + s4
                    for ti in range(nt):
                        P.op("pe", MM(psf(ti), aTg[:, ft, ti * 128:(ti + 1) * 128], wd[wi][:, s4, :], ft == 0, ft == FT - 1),
                             reads=["aTg_%d" % (ft // 11), "wd%d" % wi], writes=[PK[ti]], signal=(ft == FT - 1 or (s4 == 3 and ti == nt - 1)))
            for ti in range(nt):
                P.op("act", ACT(junk6, psf(ti), AF.Square, accum_out=ssq6[:, ti, cb:cb + 1]), reads=[PK[ti]], writes=["junk6", "ssq6_%d_%d" % (ti, cb)])
                P.op("dve", CP(ystage[ti][:, cb * 512:(cb + 1) * 512], psf(ti)), reads=[PK[ti]], writes=["ys%d_%d" % (ti, cb)])
        for ti, tile in enumerate(tiles):
            sample = (tile == NM - 1)
            xi = nx1 % 2
            nx1 += 1
            P.dma("sp", x1t[xi], scr_x1[tile], writes=["x1t%d" % xi])
            P.op("dve", lambda e_, o_=ssq6[:, ti, 4:5], i_=ssq6[:, ti, 0:4]: e_.tensor_reduce(out=o_, in_=i_, axis=AX.X, op=ALU.add),
                 reads=["ssq6_%d_%d" % (ti, cb) for cb in range(4)], writes=["ssq6s_%d" % ti])
            rstd_from_ss(ssq6[:, ti, 4:5], ssq6[:, ti, 5:6], D, ["ssq6s_%d" % ti], "ssq6r_%d" % ti)
            gB, gBk = (gB2[1], "gB2s") if sample else (gB2[0], "gB2p")
            yk = ["ys%d_%d" % (ti, cb) for cb in range(4)]
            P.op("dve", STT(ystage[ti], ystage[ti], ssq6[:, ti, 5:6], gB, ALU.mult, ALU.mult), reads=yk + ["ssq6r_%d" % ti, gBk], writes=yk)
            P.op("pool", TT(ystage[ti], ystage[ti], x1t[xi], ALU.add), reads=yk + ["x1t%d" % xi], writes=yk)
            P.dma("act", y_main[(tile - 1) * 128:tile * 128, :], ystage[ti], reads=yk, writes=["y_main"])
    P.barrier()
    P.run()
    return nc


def make_consts():
    c = np.zeros((128, 8, 128), np.float32)
    c[:, 0, :] = np.eye(128, dtype=np.float32)
    s = np.arange(128)[:, None]
    t = np.arange(128)[None, :]
    c[:, 1, :] = ((s // 64 == t // 64) & (s <= t)).astype(np.float32)
    c[0:64, 2, :] = 1.0
    c[64:128, 3, :] = 1.0
    c[0, 4, :] = 1.0
    c[1, 5, 0:64] = 1.0
    c[2, 5, 64:128] = 1.0
    c[0:64, 6, 0] = 1.0
    c[64:128, 6, 1] = 1.0
    c[:, 7, :] = 1.0
    return c


def make_consts2():
    return None


def prep_inputs(inp, SEQ):
    OWN = SEQ // 4
    NOWN = OWN // 128
    NPRE = 4
    NM = NOWN + 2
    f32 = np.float32
    xp = np.asarray(inp["x_prompt"], f32)
    xsamp = np.asarray(inp["x_sample"], f32)
    relb = np.asarray(inp["att_rel_bias"], f32)[0]
    row = np.arange(128)[:, None, None]
    kk = np.arange(5)[None, :, None]
    qc = np.arange(128)[None, None, :]
    p = 128 * kk + row - 64 * (qc // 64)
    dist = 512 + (qc % 64) - p
    idx = np.clip(dist, -256, 256) + 256
    valid = (p >= 0) & (p < 576)
    bias_tab = np.empty((128, 8, 5, 128), f32)
    for h in range(8):
        bias_tab[:, h] = np.where(valid, relb[h][idx], NEG)
    rr = np.arange(128)[:, None]
    qq = np.arange(128)[None, :]
    same = (rr // 64) == (qq // 64)
    idx4 = np.clip((qq % 64) - (rr % 64), -256, 256) + 256
    bias_s4 = np.empty((128, 8, 128), f32)
    for h in range(8):
        bias_s4[:, h] = np.where(same, relb[h][idx4], NEG)
    consts = make_consts()
    shared = {
        "adabT": np.ascontiguousarray(np.asarray(inp["ada_b"], f32)[0].reshape(96, 128).T),
        "adab3": np.ascontiguousarray(np.broadcast_to(np.asarray(inp["ada_b"], f32)[0][None], (3, 12288))),
        "gpreT": np.ascontiguousarray(np.stack([np.asarray(inp["norm_pre_mix"], f32)[0].reshape(16, 128).T,
                                                 np.asarray(inp["norm_pre_ffn"], f32)[0].reshape(16, 128).T], axis=1)),
        "gpost3": np.ascontiguousarray(np.broadcast_to(np.stack([np.asarray(inp["norm_post_mix"], f32)[0],
                                                                  np.asarray(inp["norm_post_ffn"], f32)[0]])[None], (3, 2, D))),
        "mnormB": np.ascontiguousarray(np.broadcast_to(np.asarray(inp["mlstm_norm"], f32)[0][None], (128, 1024))),
        "bgate": np.ascontiguousarray(np.broadcast_to(np.concatenate([np.asarray(inp["b_igate"], f32)[0],
                                                                       np.asarray(inp["b_fgate"], f32)[0]])[None], (128, 8))),
        "cwT": np.ascontiguousarray(np.concatenate([np.asarray(inp["ffn_conv_w"], f32)[0].reshape(3, FT, 128),
                                                    np.asarray(inp["ffn_conv_b"], f32)[0].reshape(1, FT, 128)], 0).transpose(2, 1, 0)),
        "bias_tab": bias_tab,
        "bias_s4": bias_s4,
        "ada_w": np.asarray(inp["ada_w"], f32)[0],
        "w_in": np.asarray(inp["w_in"], f32)[0],
        "w_out": np.asarray(inp["w_out"], f32)[0],
        "w_g": np.asarray(inp["w_ffn_gate"], f32)[0],
        "w_u": np.asarray(inp["w_ffn_up"], f32)[0],
        "w_d": np.asarray(inp["w_ffn_down"], f32)[0],
    }
    maps = []
    for r in range(8):
        b, j = r // 4, r % 4
        s0 = j * OWN
        m = dict(shared)
        xm = np.zeros((NM * 128, D), f32)
        if j > 0:
            xm[0:128] = xp[b, s0 - 128:s0]
        xm[128:128 + OWN] = xp[b, s0:s0 + OWN]
        xm[128 + OWN:] = xsamp[2 * r:2 * r + 2].reshape(128, D)
        m["x_main"] = xm
        xpre = np.zeros((NPRE * 128, D), f32)
        lo = s0 - 128 - NPRE * 128
        mk = np.zeros((128, 3, NPRE + NM), f32)
        for t in range(NPRE):
            a = lo + t * 128
            if a >= 0:
                xpre[t * 128:(t + 1) * 128] = xp[b, a:a + 128]
                mk[:, 0, t] = 1.0
        mk[:, 0, NPRE] = 1.0 if j > 0 else 0.0
        mk[:, 0, NPRE + 1:] = 1.0
        mk[:, 1] = -mk[:, 0]
        mk[:, 2] = np.where(mk[:, 0] > 0, 0.0, -1e30)
        m["x_pre"] = xpre
        m["masks"] = mk
        fmk = np.zeros((128, 2, 8), f32)
        for q in range(8):
            if q // 4 == b and q % 4 < j:
                fmk[:, 0, q] = 1.0
            else:
                fmk[:, 1, q] = -1e30
        m["fmask"] = fmk
        c3 = np.stack([np.asarray(inp["c_prompt"], f32)[b], np.asarray(inp["c_sample"], f32)[2 * r],
                       np.asarray(inp["c_sample"], f32)[2 * r + 1]])
        m["c3T"] = np.ascontiguousarray(c3.reshape(3, 16, 128).transpose(2, 1, 0))
        cc = consts.copy()
        m["consts"] = cc
        m["cache_k"] = np.ascontiguousarray(np.asarray(inp["cache_att_k"], f32)[0, 2 * r:2 * r + 2].reshape(2, 512, 1024))
        m["cache_v"] = np.ascontiguousarray(np.asarray(inp["cache_att_v"], f32)[0, 2 * r:2 * r + 2].reshape(2, 512, 1024))
        m["st_C"] = np.ascontiguousarray(np.asarray(inp["state_mlstm_C"], f32)[0, 2 * r:2 * r + 2])
        m["st_n"] = np.ascontiguousarray(np.asarray(inp["state_mlstm_n"], f32)[0, 2 * r:2 * r + 2])
        sm = np.asarray(inp["state_mlstm_m"], f32)[0, 2 * r:2 * r + 2]
        m["st_mB"] = np.ascontiguousarray(np.broadcast_to(sm[None], (128, 2, 4)))
        m["st_m4"] = np.ascontiguousarray(sm.T)
        m["conv_st"] = np.ascontiguousarray(np.asarray(inp["state_ffn_conv"], f32)[0, 2 * r:2 * r + 2].reshape(4, 5632))
        maps.append(m)
    return maps


SEQ_FULL = 8192
_CACHE = {}


def kernel(**inputs):
    SEQ = int(np.asarray(inputs["x_prompt"]).shape[1])
    OWN = SEQ // 4
    if SEQ not in _CACHE:
        _CACHE[SEQ] = None
    nc = build(SEQ)
    maps = prep_inputs(inputs, SEQ)
    res = run_bass_kernel_spmd(nc, maps, core_ids=list(range(8)))
    R_ = res.results
    f32 = np.float32
    y_p = np.empty((2, SEQ, D), f32)
    y_s = np.empty((16, 64, D), f32)
    p_k = np.empty((1, 2, 512, 8, 128), f32)
    p_v = np.empty((1, 2, 512, 8, 128), f32)
    p_C = np.empty((1, 2, 4, 256, 256), f32)
    p_n = np.empty((1, 2, 4, 256), f32)
    p_m = np.empty((1, 2, 4), f32)
    p_conv = np.empty((1, 2, 2, DFF), f32)
    s_k = np.empty((1, 16, 64, 8, 128), f32)
    s_v = np.empty((1, 16, 64, 8, 128), f32)
    s_C = np.empty((1, 16, 4, 256, 256), f32)
    s_n = np.empty((1, 16, 4, 256), f32)
    s_m = np.empty((1, 16, 4), f32)
    s_conv = np.empty((1, 16, 2, DFF), f32)
    for r in range(8):
        b, j = r // 4, r % 4
        o = R_[r]
        ym = np.asarray(o["y_main"])
        y_p[b, j * OWN:(j + 1) * OWN] = ym[:OWN]
        y_s[2 * r:2 * r + 2] = ym[OWN:].reshape(2, 64, D)
        ko, vo = np.asarray(o["k_out"]), np.asarray(o["v_out"])
        s_k[0, 2 * r:2 * r + 2] = ko[512:640].reshape(2, 64, 8, 128)
        s_v[0, 2 * r:2 * r + 2] = vo[512:640].reshape(2, 64, 8, 128)
        Co, no, mo = np.asarray(o["C_out"]), np.asarray(o["n_out"]), np.asarray(o["m_out"])
        s_C[0, 2 * r:2 * r + 2] = Co[1:3]
        s_n[0, 2 * r:2 * r + 2] = no[1:3]
        s_m[0, 2 * r:2 * r + 2] = mo[:, 1:3].T
        co = np.asarray(o["conv_out"]).reshape(3, 2, DFF)
        s_conv[0, 2 * r:2 * r + 2] = co[1:3]
        if j == 3:
            p_k[0, b] = ko[0:512].reshape(512, 8, 128)
            p_v[0, b] = vo[0:512].reshape(512, 8, 128)
            p_C[0, b] = Co[0]
            p_n[0, b] = no[0]
            p_m[0, b] = mo[:, 0]
            p_conv[0, b] = co[0]
    return (y_p, y_s, p_k, p_v, p_C, p_n, p_m, p_conv, s_k, s_v, s_C, s_n, s_m, s_conv)
```

```python
import contextlib
import numpy as np
import concourse.bass as bass
import concourse.mybir as mybir
from concourse.bass_utils import run_bass_kernel_spmd

F32 = mybir.dt.float32
BF16 = mybir.dt.bfloat16
AF = mybir.ActivationFunctionType
ALU = mybir.AluOpType
AX = mybir.AxisListType

D = 2048
KT = 16
DFF = 5632
FT = 44
INC = 7176
EPS = 1e-6
NEG = -30000.0


class Prog:
    ENG = ("pe", "act", "dve", "pool", "sp")

    def __init__(self, nc, n_dma=24):
        self.nc = nc
        self.q = {e: [] for e in self.ENG}
        self.cnt = {e: 0 for e in self.ENG}
        self.waited = {}
        self.lastw = {}
        self.readers = {}
        self.n_dma = n_dma
        self.dma_val = {}
        self.dma_rr = {e: 0 for e in self.ENG}
        self.semh = {}

    def _deps(self, eng, reads, writes):
        deps = []
        for k in reads:
            deps += self.lastw.get(k, [])
        for k in writes:
            deps += self.lastw.get(k, [])
            deps += self.readers.get(k, [])
        waits = []
        for semkey, val in deps:
            if eng == "pe" and semkey == ("e", "pe"):
                continue
            if self.waited.get((eng, semkey), 0) >= val:
                continue
            self.waited[(eng, semkey)] = val
            waits.append((semkey, val))
        return waits

    def _record(self, tok, reads, writes):
        for k in writes:
            self.lastw[k] = [tok]
            self.readers[k] = []
        for k in reads:
            self.readers.setdefault(k, []).append(tok)

    def op(self, eng, fn, reads=(), writes=(), signal=True):
        ex = [k for k in reads if k.startswith("ps")]
        if ex:
            reads = [k for k in reads if not k.startswith("ps")]
            writes = list(writes) + ex
        waits = self._deps(eng, reads, writes)
        if signal:
            self.cnt[eng] += 1
            tok = (("e", eng), self.cnt[eng])
        else:
            tok = (("e", eng), self.cnt[eng] + 1)
        self.q[eng].append((waits, fn, signal))
        self._record(tok, reads, writes)
        return tok

    def dma(self, qeng, out, in_, reads=(), writes=(), custom=None, inc=16):
        waits = self._deps(qeng, reads, writes)
        idx = self.dma_rr[qeng] % self.n_dma
        self.dma_rr[qeng] += 1
        semkey = ("d", qeng, idx)
        v = self.dma_val.get(semkey, 0)
        if v > 0 and self.waited.get((qeng, semkey), 0) < v:
            self.waited[(qeng, semkey)] = v
            waits.append((semkey, v))
        self.dma_val[semkey] = v + inc
        tok = (semkey, v + inc)

        def fn(e, out=out, in_=in_, semkey=semkey, custom=custom):
            if custom is not None:
                return custom(e).then_inc(self.semh[semkey], inc)
            return e.dma_start(out=out, in_=in_).then_inc(self.semh[semkey], 16)

        self.q[qeng].append((waits, fn, False))
        self._record(tok, reads, writes)
        return tok

    def barrier(self):
        for eng in self.ENG:
            waits = []
            for semkey, v in self.dma_val.items():
                if self.waited.get((eng, semkey), 0) < v:
                    self.waited[(eng, semkey)] = v
                    waits.append((semkey, v))
            for e in self.ENG:
                if e == eng or self.cnt[e] == 0:
                    continue
                semkey = ("e", e)
                if self.waited.get((eng, semkey), 0) < self.cnt[e]:
                    self.waited[(eng, semkey)] = self.cnt[e]
                    waits.append((semkey, self.cnt[e]))
            if waits:
                self.q[eng].append((waits, None, False))
        self.lastw = {}
        self.readers = {}

    def run(self):
        nc = self.nc
        with contextlib.ExitStack() as st:
            for e in self.ENG:
                self.semh[("e", e)] = st.enter_context(nc.semaphore("s_" + e))
            for qe in self.ENG:
                for i in range(min(self.dma_rr[qe], self.n_dma)):
                    self.semh[("d", qe, i)] = st.enter_context(nc.semaphore("d_%s_%d" % (qe, i)))
            block = st.enter_context(nc.Block())

            def runq(ename, e):
                for waits, fn, signal in self.q[ename]:
                    for semkey, val in waits:
                        e.wait_ge(self.semh[semkey], val)
                    if fn is None:
                        continue
                    ins = fn(e)
                    if signal:
                        ins.then_inc(self.semh[("e", ename)], 1)

            @block.tensor
            def _(e):
                runq("pe", e)

            @block.scalar
            def _(e):
                runq("act", e)

            @block.vector
            def _(e):
                runq("dve", e)

            @block.gpsimd
            def _(e):
                runq("pool", e)

            @block.sync
            def _(e):
                runq("sp", e)


def MM(out, lhsT, rhs, start=True, stop=True):
    return lambda e: e.matmul(out, lhsT=lhsT, rhs=rhs, start=start, stop=stop)


def TR(out, in_, ident):
    return lambda e: e.transpose(out=out, in_=in_, identity=ident)


def ACT(out, in_, func, **kw):
    return lambda e: e.activation(out=out, in_=in_, func=func, **kw)


def TS(out, in0, s1, s2=None, op0=ALU.mult, op1=None):
    if op1 is None:
        return lambda e: e.tensor_scalar(out=out, in0=in0, scalar1=s1, scalar2=None, op0=op0)
    return lambda e: e.tensor_scalar(out=out, in0=in0, scalar1=s1, scalar2=s2, op0=op0, op1=op1)


def TT(out, in0, in1, op):
    return lambda e: e.tensor_tensor(out=out, in0=in0, in1=in1, op=op)


def STT(out, in0, scalar, in1, op0, op1):
    return lambda e: e.scalar_tensor_tensor(out=out, in0=in0, scalar=scalar, in1=in1, op0=op0, op1=op1)


def CP(out, in_):
    return lambda e: e.tensor_copy(out=out, in_=in_)


def MSET(ap, v):
    return lambda e: e.memset(ap, v)


class Arena:
    def __init__(self, nc, nbytes):
        self.t = nc.alloc_sbuf_tensor("arena", [128, nbytes // 4], F32)
        self.nbytes = nbytes
        self.off = 0

    def mark(self):
        return self.off

    def reset(self, m):
        self.off = m

    def alloc(self, free_shape, dtype):
        n = int(np.prod(free_shape))
        sz = 4 if dtype == F32 else 2
        nb = (n * sz + 63) // 64 * 64
        assert self.off + nb <= self.nbytes, ("SBUF arena overflow", self.off, nb, self.nbytes)
        ap = self.t[:, self.off // 4:(self.off + nb) // 4]
        self.off += nb
        if dtype != F32:
            ap = ap.bitcast(dtype)
        ap = ap[:, 0:n]
        if len(free_shape) == 2:
            ap = ap.rearrange("p (a b) -> p a b", a=free_shape[0])
        elif len(free_shape) == 3:
            ap = ap.rearrange("p (a b c) -> p a b c", a=free_shape[0], b=free_shape[1])
        elif len(free_shape) == 4:
            ap = ap.rearrange("p (a b c d) -> p a b c d", a=free_shape[0], b=free_shape[1], c=free_shape[2])
        return ap


def build(SEQ, last_stage=6, debug=False):
    OWN = SEQ // 4
    NOWN = OWN // 128
    NPRE = 4
    NM = NOWN + 2
    NK = NM + 4
    TM = NM * 128
    TK = NK * 128
    NOUT = NOWN + 1
    assert NOWN >= 4 or NOWN == 2

    nc = bass.Bass("TRN2", target_bir_lowering=False)
    P = Prog(nc)

    def din(name, shape, dt=F32):
        return nc.dram_tensor(name, list(shape), dt, kind="ExternalInput").ap()

    def dout(name, shape, dt=F32):
        return nc.dram_tensor(name, list(shape), dt, kind="ExternalOutput").ap()

    def dscr(name, shape, dt):
        if debug:
            return nc.dram_tensor(name, list(shape), dt, kind="ExternalOutput").ap()
        return nc.dram_tensor(name, list(shape), dt).ap()

    x_main = din("x_main", [TM, D])
    x_pre = din("x_pre", [NPRE * 128, D])
    masks = din("masks", [128, 3, NPRE + NM])
    fmask = din("fmask", [128, 2, 8])
    c3T = din("c3T", [128, KT, 3])
    gpreT = din("gpreT", [128, 2, KT])
    adabT = din("adabT", [128, 96])
    adab3 = din("adab3", [3, 12288])
    gpost3 = din("gpost3", [3, 2, D])
    mnormB = din("mnormB", [128, 1024])
    bgate = din("bgate", [128, 8])
    cwT = din("cwT", [128, FT, 4])
    bias_tab = din("bias_tab", [128, 8, 5, 128])
    bias_s4 = din("bias_s4", [128, 8, 128])
    consts = din("consts", [128, 8, 128])
    cache_k = din("cache_k", [2, 512, 1024])
    cache_v = din("cache_v", [2, 512, 1024])
    st_C = din("st_C", [2, 4, 256, 256])
    st_n = din("st_n", [2, 4, 256])
    st_mB = din("st_mB", [128, 2, 4])
    st_m4 = din("st_m4", [4, 2])
    conv_st = din("conv_st", [4, 5632])
    ada_w = din("ada_w", [D, 12288])
    w_in = din("w_in", [D, INC])
    w_out = din("w_out", [D, D])
    w_g = din("w_g", [D, DFF])
    w_u = din("w_u", [D, DFF])
    w_d = din("w_d", [DFF, D])

    y_main = dout("y_main", [NOUT * 128, D])
    k_out = dout("k_out", [5 * 128, 1024])
    v_out = dout("v_out", [5 * 128, 1024])
    C_out = dout("C_out", [3, 4, 256, 256])
    n_out = dout("n_out", [3, 4, 256])
    m_out = dout("m_out", [4, 3])
    conv_out = dout("conv_out", [6, 5632])

    scr_fm = dscr("scr_fm", [32, 128, TK], BF16)
    scr_va = dscr("scr_va", [NK, 128, 8 * 129], BF16)
    scr_tm = dscr("scr_tm", [NM, 128, 3072], BF16)
    scr_g = dscr("scr_g", [NM, 128, 8], F32)
    scr_gtg = nc.dram_tensor("scr_gtg", [3, 2, D], F32).ap()
    cc_in = nc.dram_tensor("cc_in", [128, 2064], F32).ap()
    cc_out = nc.dram_tensor("cc_out", [1024, 2064], F32).ap()
    scr_mixT = dscr("scr_mixT", [NM, 128, KT * 128], BF16)
    scr_x1 = dscr("scr_x1", [NM, 128, D], F32)
    scr_h2T = dscr("scr_h2T", [128, KT, TM], BF16)
    scr_aT = dscr("scr_aT", [FT, 128, TM], BF16)
    dbg = dscr("dbg", [128, 4096], F32) if debug else None

    A = Arena(nc, 206 * 1024)
    ps = [nc.alloc_psum_tensor("ps%d" % b, [128, 512], F32) for b in range(8)]

    def psf(b):
        return ps[b][:]

    def psb(b):
        return ps[b][:].bitcast(BF16)

    PK = ["ps%d" % b for b in range(8)]

    cst = A.alloc([8, 128], F32)
    identf = cst[:, 0, :]
    tri = cst[:, 1, :]
    ind = cst[:, 2:4, :]
    sel = cst[0:3, 4:6, :]
    ind2 = cst[:, 6, 0:2]
    ones4 = cst[0:4, 7, :]
    identb = A.alloc([128], BF16)
    msk = A.alloc([3, NPRE + NM], F32)
    modT = A.alloc([4, KT, 3], F32)
    bgt = A.alloc([8], F32)
    epsc = A.alloc([1], F32)
    onec = A.alloc([1], F32)
    P.dma("sp", cst, consts, writes=["cst"])
    P.dma("sp", msk, masks, writes=["msk"])
    P.dma("sp", bgt, bgate, writes=["bgt"])
    P.op("dve", MSET(epsc, EPS), writes=["epsc"])
    P.op("dve", MSET(onec, 1.0), writes=["onec"])
    P.op("dve", CP(identb, identf), reads=["cst"], writes=["identb"])
    persist_mark = A.mark()

    siluT = A.alloc([KT, 3], BF16)
    gtg = A.alloc([2, D], F32)
    c3s = A.alloc([KT, 3], F32)
    gpre = A.alloc([2, KT], F32)
    adab = A.alloc([96], F32)
    adab3s = A.alloc([2, D], F32)
    gpost = A.alloc([2, D], F32)
    ablk = [A.alloc([KT, 512], BF16) for _ in range(3)]
    P.dma("sp", c3s, c3T, writes=["c3s"])
    P.dma("sp", gpre, gpreT, writes=["gpre"])
    P.dma("sp", adab, adabT, writes=["adab"])
    P.dma("sp", adab3s[0:3, 0, :], adab3[:, 2 * D:3 * D], writes=["adab3s0"])
    P.dma("sp", adab3s[0:3, 1, :], adab3[:, 5 * D:6 * D], writes=["adab3s1"])
    P.dma("sp", gpost[0:3], gpost3, writes=["gpost"])
    P.op("act", ACT(siluT, c3s, AF.Silu), reads=["c3s"], writes=["siluT"])

    nblk = 0
    for v, slot in ((0, 0), (1, 1), (3, 2), (4, 3)):
        for cb4 in range(4):
            bi = nblk % 3
            nblk += 1
            c0 = v * D + cb4 * 512
            P.dma("pool", ablk[bi], ada_w[:, c0:c0 + 512].rearrange("(k p) n -> p k n", p=128), writes=["ablk%d" % bi])
            for sub in range(4):
                kc = cb4 * 4 + sub
                b = kc % 4
                for k in range(KT):
                    P.op("pe", MM(psf(b)[:, 0:3], ablk[bi][:, k, sub * 128:(sub + 1) * 128], siluT[:, k, :], k == 0, k == KT - 1),
                         reads=["ablk%d" % bi, "siluT"], writes=[PK[b]], signal=(k == KT - 1))
                P.op("dve", TS(modT[:, slot, kc, :], psf(b)[:, 0:3], adab[:, v * 16 + kc:v * 16 + kc + 1], None, ALU.add),
                     reads=[PK[b], "adab"], writes=["modT"])
    for slot, gi in ((1, 0), (3, 1)):
        P.op("dve", STT(modT[:, slot], modT[:, slot], 1.0, gpre[:, gi, :].unsqueeze(2).broadcast_to([128, KT, 3]), ALU.add, ALU.mult),
             reads=["modT", "gpre"], writes=["modT"])
    for gi, v in ((0, 2), (1, 5)):
        for cb4 in range(4):
            bi = nblk % 3
            nblk += 1
            c0 = v * D + cb4 * 512
            P.dma("pool", ablk[bi], ada_w[:, c0:c0 + 512].rearrange("(k p) n -> p k n", p=128), writes=["ablk%d" % bi])
            b = 4 + cb4 % 4
            for k in range(KT):
                P.op("pe", MM(psf(b)[0:3, :], siluT[:, k, :], ablk[bi][:, k, :], k == 0, k == KT - 1),
                     reads=["ablk%d" % bi, "siluT"], writes=[PK[b]], signal=(k == KT - 1))
            P.op("dve", TT(gtg[0:3, gi, cb4 * 512:(cb4 + 1) * 512], psf(b)[0:3, :], adab3s[0:3, gi, cb4 * 512:(cb4 + 1) * 512], ALU.add),
                 reads=[PK[b], "adab3s%d" % gi], writes=["gtg%d_%d" % (gi, cb4)])
            P.op("dve", TT(gtg[0:3, gi, cb4 * 512:(cb4 + 1) * 512], gtg[0:3, gi, cb4 * 512:(cb4 + 1) * 512],
                           gpost[0:3, gi, cb4 * 512:(cb4 + 1) * 512], ALU.mult),
                 reads=["gpost", "gtg%d_%d" % (gi, cb4)], writes=["gtg%d_%d" % (gi, cb4)])
    GTG_KEYS = [["gtg%d_%d" % (gi, c) for c in range(4)] for gi in range(2)]
    for gi in range(2):
        P.dma("sp", scr_gtg[:, gi, :], gtg[0:3, gi, :], reads=GTG_KEYS[gi], writes=["scr_gtg"])

    def build_gtgB(dst_p, dst_s, gi, keyp, keys_):
        tmp = A.alloc([D], F32)
        P.dma("sp", tmp[0:3, :], scr_gtg[:, gi, :], writes=["gtgtmp"])
        for vi, dst, key in ((0, dst_p, keyp), (1, dst_s, keys_)):
            for cb4 in range(4):
                b = cb4
                P.op("pe", MM(psf(b), sel[:, vi, :], tmp[0:3, cb4 * 512:(cb4 + 1) * 512]),
                     reads=["cst", "gtgtmp"], writes=[PK[b]])
                P.op("act", ACT(dst[:, cb4 * 512:(cb4 + 1) * 512], psf(b), AF.Copy), reads=[PK[b]], writes=[key])

    P.barrier()
    A.reset(persist_mark)
    if last_stage == 0:
        if debug:
            P.dma("sp", dbg[:, 0:192], modT.rearrange("p a b c -> p (a b c)"), reads=["modT"], writes=["dbg"])
            P.dma("sp", dbg[0:3, 192:192 + 4096 - 192], gtg[0:3].rearrange("p a b -> p (a b)")[:, 0:4096 - 192], reads=sum(GTG_KEYS, []), writes=["dbg2"])
        P.barrier()
        P.run()
        return nc

    def rstd_from_ss(ss_ap, out_ap, n, keys_in, key_out):
        P.op("act", ACT(out_ap, ss_ap, AF.Ln, scale=1.0 / n, bias=epsc[:, 0:1]), reads=keys_in + ["epsc"], writes=[key_out])
        P.op("act", ACT(out_ap, out_ap, AF.Exp, scale=-0.5), reads=[key_out], writes=[key_out])

    fe = {}

    def fe_alloc(with_xs=True, nxs=2):
        if with_xs:
            fe["xs"] = [A.alloc([D], F32) for _ in range(nxs)]
        fe["xn"] = [A.alloc([D], BF16) for _ in range(2)]
        fe["junk"] = A.alloc([D], BF16)
        fe["ss"] = [A.alloc([2], F32) for _ in range(2)]
        fe["n"] = 0

    def front_end(x_src, dst, dst_key, slot_a, slot_sh, rows, tb=(0, 1), xs_in=None, xs_keys=None):
        i = fe["n"] % 2
        fe["n"] += 1
        xn, ss = fe["xn"][i], fe["ss"][i]
        kx, kn, ks = "fe_xs%d" % i, "fe_xn%d" % i, "fe_ss%d" % i
        if xs_in is None:
            xs = fe["xs"][i % len(fe["xs"])]
            kx = "fe_xs%d" % (i % len(fe["xs"]))
            P.dma("sp", xs, x_src, writes=[kx])
            kxl = [kx]
        else:
            xs, kxl = xs_in, list(xs_keys)
        P.op("act", ACT(fe["junk"], xs, AF.Square, accum_out=ss[:, 0:1]), reads=kxl, writes=["fe_junk", ks])
        rstd_from_ss(ss[:, 0:1], ss[:, 1:2], D, [ks], ks)
        P.op("dve", TS(xn, xs, ss[:, 1:2]), reads=kxl + [ks], writes=[kn])
        for half in range(2):
            b = tb[half]
            pt = psb(b).rearrange("p (k t) -> p k t", k=8)
            for k8 in range(8):
                k = half * 8 + k8
                P.op("pe", TR(pt[:, k8, :], xn[:, k * 128:(k + 1) * 128], identb), reads=[kn, "identb"], writes=[PK[b]],
                     signal=(k8 == 7))
            for k8 in range(8):
                k = half * 8 + k8
                eng = "act" if half == 0 else "dve"
                if rows[0] == rows[1]:
                    segs = [(0, 128, rows[0])]
                else:
                    segs = [(0, 64, rows[0]), (64, 128, rows[1])]
                for (t0, t1, r) in segs:
                    a_ap = modT[:, slot_a, k, r:r + 1]
                    s_ap = modT[:, slot_sh, k, r:r + 1]
                    if eng == "act":
                        P.op("act", ACT(dst[:, k, t0:t1], pt[:, k8, t0:t1], AF.Identity, scale=a_ap, bias=s_ap),
                             reads=[PK[b], "modT"], writes=["%s_k%d_%d" % (dst_key, k, t0)])
                    else:
                        P.op("dve", TS(dst[:, k, t0:t1], pt[:, k8, t0:t1], a_ap, s_ap, ALU.mult, ALU.add),
                             reads=[PK[b], "modT"], writes=["%s_k%d_%d" % (dst_key, k, t0)])

    gm = {}

    def gate_alloc():
        gm["gsb"] = [A.alloc([8], F32) for _ in range(2)]
        gm["lfm"] = [A.alloc([4], F32) for _ in range(2)]
        gm["pre"] = [A.alloc([16], F32) for _ in range(2)]
        gm["ex"] = [A.alloc([16], F32) for _ in range(2)]
        gm["gbf"] = [A.alloc([4], BF16) for _ in range(2)]
        gm["amax"] = [A.alloc([2], F32) for _ in range(2)]
        gm["B4"] = [A.alloc([2], F32) for _ in range(2)]
        gm["t4"] = A.alloc([2], F32)
        gm["n"] = 0

    def gate_math(g_src, g_keys, mcol, gb, gb2):
        i = gm["n"] % 2
        gm["n"] += 1
        gsb, lfm, pre, ex, gbf, amax, B4 = (gm[k][i] for k in ("gsb", "lfm", "pre", "ex", "gbf", "amax", "B4"))
        K = "gm%d_" % i
        P.op("dve", TT(gsb, g_src, bgt, ALU.add), reads=g_keys + ["bgt"], writes=[K + "gsb"])
        P.op("act", ACT(lfm, gsb[:, 4:8], AF.Exp, scale=-1.0), reads=[K + "gsb"], writes=[K + "lfm"])
        P.op("act", ACT(lfm, lfm, AF.Ln, bias=onec[:, 0:1]), reads=[K + "lfm", "onec"], writes=[K + "lfm"])
        P.op("dve", TS(lfm, lfm, msk[:, 1, mcol:mcol + 1]), reads=[K + "lfm", "msk"], writes=[K + "lfm"])
        g0 = psf(gb)
        P.op("pe", MM(g0[:, 0:4], tri, lfm), reads=["cst", K + "lfm"], writes=[PK[gb]], signal=False)
        P.op("pe", MM(g0[:, 4:8], ind[:, 0, :], lfm), reads=["cst", K + "lfm"], writes=[PK[gb]], signal=False)
        P.op("pe", MM(g0[:, 8:12], ind[:, 1, :], lfm), reads=["cst", K + "lfm"], writes=[PK[gb]], signal=False)
        P.op("pe", MM(g0[0:4, 12:14], lfm, ind2), reads=["cst", K + "lfm"], writes=[PK[gb]])
        P.op("dve", CP(pre[:, 4:16], g0[:, 0:12]), reads=[PK[gb]], writes=[K + "pre"])
        P.op("dve", CP(B4[0:4, :], g0[0:4, 12:14]), reads=[PK[gb]], writes=[K + "B4"])
        P.op("dve", TT(pre[:, 0:4], gsb[:, 0:4], pre[:, 4:8], ALU.subtract), reads=[K + "gsb", K + "pre"], writes=[K + "pre"])
        P.op("dve", TS(pre[:, 0:4], pre[:, 0:4], msk[:, 0, mcol:mcol + 1], msk[:, 2, mcol:mcol + 1], ALU.mult, ALU.add),
             reads=[K + "pre", "msk"], writes=[K + "pre"])
        P.op("act", ACT(ex, pre, AF.Exp), reads=[K + "pre"], writes=[K + "ex"])
        P.op("dve", CP(gbf, ex[:, 0:4]), reads=[K + "ex"], writes=[K + "gbf"])
        g1 = psf(gb2)
        P.op("pe", TR(g1[0:4, 128:256], pre[:, 0:4], identf), reads=[K + "pre", "cst"], writes=[PK[gb2]])
        P.op("dve", lambda e, o=amax[0:4, :], s=g1[0:4, 128:256].rearrange("p (c t) -> p c t", c=2):
             e.tensor_reduce(out=o, in_=s, axis=AX.X, op=ALU.max), reads=[PK[gb2]], writes=[K + "amax"])
        return dict(ex=ex, gbf=gbf, amax=amax, B4=B4, K=K, pre=pre)

    def m_update(mt, mkey, G, c):
        K = G["K"]
        t4 = gm["t4"]
        P.op("dve", TT(t4[0:4, 0:1], G["amax"][0:4, c:c + 1], G["B4"][0:4, c:c + 1], ALU.add),
             reads=[K + "amax", K + "B4"], writes=["t4"])
        P.op("dve", TT(mt, mt, G["B4"][0:4, c:c + 1], ALU.add), reads=[mkey, K + "B4"], writes=[mkey])
        P.op("dve", TT(mt, mt, t4[0:4, 0:1], ALU.max), reads=[mkey, "t4"], writes=[mkey])

    def state_update(Sf, Sf_key, nf, nf_key, DSb, G, c, h, km_ap, km_key, vp_ap, vp_key, Sb=None, Sb_key=None, nb=None, nb_key=None,
                     S_src=None, S_src_key=None, n_src=None, n_src_key=None):
        K = G["K"]
        if S_src is None:
            S_src, S_src_key, n_src, n_src_key = Sf, Sf_key, nf, nf_key
        r0, r1 = 64 * c, 64 * c + 64
        eB = G["ex"][:, 8 + 4 * c + h:8 + 4 * c + h + 1]
        DS = psf(DSb)
        for d in range(2):
            P.op("pe", MM(DS[:, d * 256:(d + 1) * 256], km_ap[r0:r1, h * 256 + d * 128:h * 256 + (d + 1) * 128], vp_ap[r0:r1, h, :]),
                 reads=[km_key, vp_key], writes=[PK[DSb]], signal=(d == 1))
        Sh = Sf[:, h].rearrange("p d e -> p (d e)")
        Ssh = S_src[:, h].rearrange("p d e -> p (d e)")
        P.op("act", ACT(Sh, Ssh, AF.Copy, scale=eB), reads=[S_src_key + str(h), K + "ex"], writes=[Sf_key + str(h)])
        P.op("dve", STT(Sh, DS, eB, Sh, ALU.mult, ALU.add), reads=[PK[DSb], K + "ex", Sf_key + str(h)], writes=[Sf_key + str(h)])
        if Sb is not None:
            P.op("act", ACT(Sb[:, h].rearrange("p d e -> p (d e)"), Sh, AF.Copy), reads=[Sf_key + str(h)], writes=[Sb_key + str(h)])

    def n_update(nf, nf_key, NBb, G, c, km_ap, km_key, nb=None, nb_key=None, n_src=None, n_src_key=None):
        K = G["K"]
        if n_src is None:
            n_src, n_src_key = nf, nf_key
        r0, r1 = 64 * c, 64 * c + 64
        NB = psf(NBb)
        for h in range(4):
            for d in range(2):
                P.op("pe", MM(NB[:, 256 + h * 2 + d:256 + h * 2 + d + 1], km_ap[r0:r1, h * 256 + d * 128:h * 256 + (d + 1) * 128],
                              G["gbf"][r0:r1, h:h + 1]),
                     reads=[km_key, K + "gbf"], writes=[PK[NBb]], signal=(h == 3 and d == 1))
        eB4 = G["ex"][:, 8 + 4 * c:12 + 4 * c].unsqueeze(2).broadcast_to([128, 4, 2])
        P.op("dve", TT(nf, n_src, NB[:, 256:264].rearrange("p (h d) -> p h d", h=4), ALU.add),
             reads=[n_src_key, PK[NBb]], writes=[nf_key])
        P.op("dve", TT(nf, nf, eB4, ALU.mult), reads=[nf_key, K + "ex"], writes=[nf_key])
        if nb is not None:
            P.op("dve", CP(nb, nf), reads=[nf_key], writes=[nb_key])

    base_mark = A.mark()
    SX = A.alloc([4, 2, 256], F32)
    nX = A.alloc([4, 2], F32)
    mP = A.alloc([2], F32)
    P.op("pool", MSET(SX, 0.0), writes=["SX%d" % h for h in range(4)])
    P.op("pool", MSET(nX, 0.0), writes=["nX"])
    P.op("pool", MSET(mP, 0.0), writes=["mP"])
    persist_mark = A.mark()

    hT_halo = A.alloc([KT, 512], BF16)
    s1_mark = A.mark()

    hT_main = A.alloc([KT, TM], BF16)

    def hTt(u):
        return hT_halo[:, :, u * 128:(u + 1) * 128] if u < 4 else hT_main[:, :, (u - 4) * 128:(u - 3) * 128]

    def hTr(k, t0, t1):
        return hT_halo[:, k, t0:t1] if t1 <= 512 else hT_main[:, k, t0 - 512:t1 - 512]

    fe_alloc(nxs=2)
    wblk = [A.alloc([KT, 512], BF16) for _ in range(3)]
    wg8 = A.alloc([KT, 8], BF16)
    fmst = [A.alloc([TK], BF16) for _ in range(2)]
    tmst = [A.alloc([516], BF16) for _ in range(4)]
    f32st = [A.alloc([512], F32) for _ in range(3)]
    gst = [A.alloc([8], F32) for _ in range(2)]
    cnt = {"wb": 0, "bank": 0, "ev": 0, "fm": 0, "tm": 0, "f32": 0, "g": 0}

    def hkeys(u, k):
        if u == NK - 1:
            return ["hT_u%d_k%d_0" % (u, k), "hT_u%d_k%d_64" % (u, k)]
        return ["hT_u%d_k%d_0" % (u, k)]

    for u in range(4):
        front_end(x_pre[u * 128:(u + 1) * 128, :], hTt(u), "hT_u%d" % u, 1, 0, (0, 0), tb=(0, 1))
    for i in range(NM):
        u = 4 + i
        rows = (0, 0) if i < NM - 1 else (1, 2)
        front_end(x_main[i * 128:(i + 1) * 128, :], hTt(u), "hT_u%d" % u, 1, 0, rows, tb=(0, 1))

    def load_wblk(col0):
        bi = cnt["wb"] % 3
        cnt["wb"] += 1
        P.dma("pool", wblk[bi], w_in[:, col0:col0 + 512].rearrange("(k p) n -> p k n", p=128), writes=["wblk%d" % bi])
        return wblk[bi], "wblk%d" % bi

    def next_bank():
        b = 2 + cnt["bank"] % 6
        cnt["bank"] += 1
        return b

    def ev_eng():
        cnt["ev"] += 1
        return "act" if cnt["ev"] % 2 == 0 else "dve"

    def evac(out, in_, keys_r, keys_w, scale=None, eng=None):
        eng = eng or ev_eng()
        if eng == "act":
            if scale is None:
                P.op("act", ACT(out, in_, AF.Copy), reads=keys_r, writes=keys_w)
            else:
                P.op("act", ACT(out, in_, AF.Copy, scale=scale), reads=keys_r, writes=keys_w)
        else:
            if scale is None:
                P.op("dve", CP(out, in_), reads=keys_r, writes=keys_w)
            else:
                P.op("dve", TS(out, in_, scale), reads=keys_r, writes=keys_w)

    fm_blocks = [(0, 128 ** -0.5, 4), (512, 128 ** -0.5, 4), (1024, None, 0), (1536, None, 0),
                 (3072, None, 4), (3584, None, 4), (4096, 0.0625, 4), (4608, 0.0625, 4)]
    for blk, (col0, scale, u_lo) in enumerate(fm_blocks):
        wb, wkey = load_wblk(col0)
        t_lo = u_lo * 128
        for sub in range(4):
            cb = blk * 4 + sub
            fi = cnt["fm"] % 2
            cnt["fm"] += 1
            stkeys = []
            for tbi, t0 in enumerate(range(t_lo, TK, 512)):
                t1 = min(t0 + 512, TK)
                b = next_bank()
                for k in range(KT):
                    rk = [wkey]
                    for u in range(t0 // 128, t1 // 128):
                        rk += hkeys(u, k)
                    P.op("pe", MM(psf(b)[:, 0:t1 - t0], wb[:, k, sub * 128:(sub + 1) * 128], hTr(k, t0, t1), k == 0, k == KT - 1),
                         reads=rk, writes=[PK[b]], signal=(k == KT - 1))
                key = "fmst%d_%d" % (fi, tbi)
                stkeys.append(key)
                evac(fmst[fi][:, t0:t1], psf(b)[:, 0:t1 - t0], [PK[b]], [key], scale)
            P.dma("sp", scr_fm[cb, :, t_lo:TK], fmst[fi][:, t_lo:TK], reads=stkeys, writes=["scr_fm"])

    def out_index(i):
        if NOWN - 3 <= i <= NOWN:
            return i - (NOWN - 3)
        if i == NM - 1:
            return 4
        return None

    def mcol_of(u):
        return NPRE - 4 + u if u < 4 else NPRE + (u - 4)

    for wbi in range(2):
        wb, wkey = load_wblk(2048 + wbi * 512)
        for u in range(NK):
            b = next_bank()
            for k in range(KT):
                P.op("pe", MM(psf(b), hTt(u)[:, k, :], wb[:, k, :], k == 0, k == KT - 1),
                     reads=[wkey] + hkeys(u, k), writes=[PK[b]], signal=(k == KT - 1))
            ti = cnt["tm"] % 4
            cnt["tm"] += 1
            st = tmst[ti].rearrange("p (h e) -> p h e", h=4)
            mc = mcol_of(u)
            P.op("dve", TS(st[:, :, 0:128], psf(b).rearrange("p (h e) -> p h e", h=4), msk[:, 0, mc:mc + 1]),
                 reads=[PK[b], "msk"], writes=["tmst%d" % ti])
            P.op("pool", CP(st[:, :, 128], msk[:, 0, mc:mc + 1].broadcast_to([128, 4])), reads=["msk"], writes=["tmst%dx" % ti])
            P.dma("sp", scr_va[u, :, wbi * 516:(wbi + 1) * 516], tmst[ti], reads=["tmst%d" % ti, "tmst%dx" % ti], writes=["scr_va"])
            oi = out_index(u - 4) if u >= 4 else None
            if oi is not None:
                fi = cnt["f32"] % 3
                cnt["f32"] += 1
                P.op("act", ACT(f32st[fi], psf(b), AF.Copy), reads=[PK[b]], writes=["f32st%d" % fi])
                P.dma("sp", v_out[oi * 128:(oi + 1) * 128, wbi * 512:(wbi + 1) * 512], f32st[fi], reads=["f32st%d" % fi], writes=["v_out"])
    for (col0, off, scale) in ((4096, 0, 0.0625), (5120, 1024, None), (6144, 2048, None)):
        for wbi in range(2):
            wb, wkey = load_wblk(col0 + wbi * 512)
            for i in range(NM):
                u = 4 + i
                b = next_bank()
                for k in range(KT):
                    P.op("pe", MM(psf(b), hTt(u)[:, k, :], wb[:, k, :], k == 0, k == KT - 1),
                         reads=[wkey] + hkeys(u, k), writes=[PK[b]], signal=(k == KT - 1))
                ti = cnt["tm"] % 4
                cnt["tm"] += 1
                evac(tmst[ti][:, 0:512], psf(b), [PK[b]], ["tmst%d" % ti], scale)
                P.dma("sp", scr_tm[i, :, off + wbi * 512:off + (wbi + 1) * 512], tmst[ti][:, 0:512], reads=["tmst%d" % ti], writes=["scr_tm"])
    P.dma("pool", wg8, w_in[:, 7168:7176].rearrange("(k p) n -> p k n", p=128), writes=["wg8"])
    for i in range(NM):
        u = 4 + i
        b = next_bank()
        for k in range(KT):
            P.op("pe", MM(psf(b)[:, 0:8], hTt(u)[:, k, :], wg8[:, k, :], k == 0, k == KT - 1),
                 reads=["wg8"] + hkeys(u, k), writes=[PK[b]], signal=(k == KT - 1))
        gi = cnt["g"] % 2
        cnt["g"] += 1
        P.op("dve", CP(gst[gi], psf(b)[:, 0:8]), reads=[PK[b]], writes=["gst%d" % gi])
        P.dma("sp", scr_g[i], gst[gi], reads=["gst%d" % gi], writes=["scr_g"])
    for wbi in range(2):
        wb, wkey = load_wblk(1024 + wbi * 512)
        for i in range(NM):
            oi = out_index(i)
            if oi is None:
                continue
            u = 4 + i
            b = next_bank()
            for k in range(KT):
                P.op("pe", MM(psf(b), hTt(u)[:, k, :], wb[:, k, :], k == 0, k == KT - 1),
                     reads=[wkey] + hkeys(u, k), writes=[PK[b]], signal=(k == KT - 1))
            fi = cnt["f32"] % 3
            cnt["f32"] += 1
            P.op("act", ACT(f32st[fi], psf(b), AF.Copy), reads=[PK[b]], writes=["f32st%d" % fi])
            P.dma("sp", k_out[oi * 128:(oi + 1) * 128, wbi * 512:(wbi + 1) * 512], f32st[fi], reads=["f32st%d" % fi], writes=["k_out"])
    P.barrier()
    A.reset(persist_mark)
    if last_stage == 2:
        P.run()
        return nc

    gate_alloc()
    pay = A.alloc([2064], F32)
    Sl = pay[:, 0:2048].rearrange("p (h d e) -> p h d e", h=4, d=2)
    nl = pay[:, 2048:2056].rearrange("p (h d) -> p h d", h=4)
    Bt = pay[:, 2056:2060]
    ab = pay[:, 2060:2064]
    ml = A.alloc([2], F32)
    d4l = A.alloc([4], F32)
    fm_ = A.alloc([2, 8], F32)
    P.dma("sp", fm_, fmask, writes=["fm_"])
    P.op("pool", MSET(pay, 0.0), writes=["SL%d" % h for h in range(4)] + ["nL", "BtL", "abL"])
    P.op("pool", MSET(ml, -1e30), writes=["mL"])
    tbl = [A.alloc([2048], BF16) for _ in range(2)]
    gbl = [A.alloc([8], F32) for _ in range(2)]
    vpl = [A.alloc([4, 256], BF16) for _ in range(2)]
    payq = [A.alloc([2064], F32) for _ in range(2)]
    eq = A.alloc([8], F32)
    mB = A.alloc([4], F32)
    tq = A.alloc([4], F32)
    P.op("pool", MSET(mB, 0.0), writes=["mB"])

    def loadL(i):
        P.dma("sp", tbl[i % 2], scr_tm[i, :, 0:2048], writes=["tbl%d" % (i % 2)])
        P.dma("sp", gbl[i % 2], scr_g[i], writes=["gbl%d" % (i % 2)])

    loadL(0)
    for i in range(NOWN):
        i2 = i % 2
        if i + 1 < NOWN:
            loadL(i + 1)
        tk = "tbl%d" % i2
        km_tm = tbl[i2][:, 0:1024]
        vm_tm = tbl[i2][:, 1024:2048].rearrange("p (h e) -> p h e", h=4)
        G = gate_math(gbl[i2], ["gbl%d" % i2], NPRE + i, 6, 7)
        K = G["K"]
        P.op("dve", TT(vpl[i2], vm_tm, G["ex"][:, 0:4].unsqueeze(2).broadcast_to([128, 4, 256]), ALU.mult),
             reads=[tk, K + "ex"], writes=["vpl%d" % i2])
        P.op("dve", TT(Bt, Bt, G["pre"][:, 8:12], ALU.add), reads=["BtL", K + "pre"], writes=["BtL"])
        P.op("dve", TT(Bt, Bt, G["pre"][:, 12:16], ALU.add), reads=["BtL", K + "pre"], writes=["BtL"])
        for c in range(2):
            n_update(nl, "nL", 7, G, c, km_tm, tk)
            for h in range(4):
                state_update(Sl, "SL", None, None, h, G, c, h, km_tm, tk, vpl[i2], "vpl%d" % i2)
            m_update(ml[0:4, 0:1], "mL", G, c)
    P.op("dve", TS(d4l[0:4, 0:4], identf[0:4, 0:4], ml[0:4, 0:1]), reads=["mL", "cst"], writes=["d4l"])
    P.op("pe", MM(psf(6)[:, 0:4], ones4, d4l[0:4, 0:4]), reads=["cst", "d4l"], writes=[PK[6]])
    P.op("dve", CP(ab, psf(6)[:, 0:4]), reads=[PK[6]], writes=["abL"])
    P.dma("sp", cc_in, pay, reads=["SL%d" % h for h in range(4)] + ["nL", "BtL", "abL"], writes=["cc_in"])
    P.dma("pool", None, None, reads=["cc_in"], writes=["cc_out"], inc=1,
          custom=lambda e: e.collective_compute("AllGather", ALU.bypass, replica_groups=[list(range(8))], ins=[cc_in], outs=[cc_out]))
    for q in range(8):
        pq = payq[q % 2]
        pk = "payq%d" % (q % 2)
        P.dma("sp", pq, cc_out[q * 128:(q + 1) * 128, :], reads=["cc_out"], writes=[pk])
        act_q = fm_[:, 0, q:q + 1]
        P.op("dve", TS(eq[:, 0:4], pq[:, 2056:2060], act_q), reads=[pk, "fm_"], writes=["eq"])
        P.op("act", ACT(eq[:, 4:8], eq[:, 0:4], AF.Exp), reads=["eq"], writes=["eqe"])
        Sq = pq[:, 0:2048].rearrange("p (h de) -> p h de", h=4)
        for h in range(4):
            Sh = SX[:, h].rearrange("p d e -> p (d e)")
            P.op("act", ACT(Sh, Sh, AF.Copy, scale=eq[:, 4 + h:5 + h]), reads=["SX%d" % h, "eqe"], writes=["SX%d" % h])
            P.op("dve", STT(Sh, Sq[:, h, :], act_q, Sh, ALU.mult, ALU.add), reads=[pk, "fm_", "SX%d" % h], writes=["SX%d" % h])
        P.op("dve", TT(nX, nX, eq[:, 4:8].unsqueeze(2).broadcast_to([128, 4, 2]), ALU.mult), reads=["nX", "eqe"], writes=["nX"])
        P.op("dve", STT(nX, pq[:, 2048:2056].rearrange("p (h d) -> p h d", h=4), act_q, nX, ALU.mult, ALU.add), reads=[pk, "fm_", "nX"], writes=["nX"])
        P.op("dve", TT(mB, mB, eq[:, 0:4], ALU.add), reads=["mB", "eq"], writes=["mB"])
        P.op("dve", TS(tq, pq[:, 2060:2064], act_q, fm_[:, 1, q:q + 1], ALU.mult, ALU.add), reads=[pk, "fm_"], writes=["tq"])
        P.op("dve", TT(mB, mB, tq, ALU.max), reads=["mB", "tq"], writes=["mB"])
    P.op("pe", TR(psf(7)[0:4, 0:128], mB, identf), reads=["mB", "cst"], writes=[PK[7]])
    P.op("dve", CP(mP[0:4, 0:1], psf(7)[0:4, 0:1]), reads=[PK[7]], writes=["mP"])
    P.barrier()
    A.reset(persist_mark)

    gate_alloc()
    RS = 6
    mnb = A.alloc([1024], F32)
    btab = A.alloc([8, 5, 128], BF16)
    bs4 = A.alloc([8, 128], BF16)
    P.dma("sp", mnb, mnormB, writes=["mnb"])
    P.dma("pool", btab, bias_tab, writes=["btab"])
    P.dma("pool", bs4, bias_s4, writes=["bs4"])
    kring = [A.alloc([8, 128], BF16) for _ in range(RS)]
    vring = [A.alloc([8, 129], BF16) for _ in range(RS)]
    qbuf = [A.alloc([8, 128], BF16) for _ in range(2)]
    mbuf = [A.alloc([16, 128], BF16) for _ in range(2)]
    tbuf = [A.alloc([3072], BF16) for _ in range(2)]
    gbuf = [A.alloc([8], F32) for _ in range(2)]
    expT = [A.alloc([5, 128], BF16) for _ in range(2)]
    es0 = [A.alloc([4, 128], BF16) for _ in range(2)]
    es1 = [A.alloc([4, 128], BF16) for _ in range(2)]
    vp = [A.alloc([4, 256], BF16) for _ in range(2)]
    qz = [[A.alloc([8, 128], BF16) for _c in range(2)] for _ in range(2)]
    qkm = [A.alloc([4, 128], BF16) for _ in range(2)]
    mix = [A.alloc([2048], BF16) for _ in range(2)]
    mixTs = [A.alloc([KT, 128], BF16) for _ in range(2)]
    gsn = [A.alloc([1024], F32) for _ in range(2)]
    SY = A.alloc([4, 2, 256], F32)
    SXb = A.alloc([4, 2, 256], BF16)
    SYb = A.alloc([4, 2, 256], BF16)
    nY = A.alloc([4, 2], F32)
    nXb = A.alloc([4, 2], BF16)
    nYb = A.alloc([4, 2], BF16)
    rden = A.alloc([8], F32)
    dd = A.alloc([4], F32)
    ddn = A.alloc([4], F32)
    ssh = A.alloc([8], F32)
    scl = A.alloc([4], F32)
    hjunk = A.alloc([256], BF16)
    ms = A.alloc([2], F32)
    mcolt = A.alloc([4], F32)
    d4 = A.alloc([4], F32)
    emn = A.alloc([4], F32)
    em_in = A.alloc([2, 4], F32)
    Cst = A.alloc([4, 2, 256], F32)
    nst = A.alloc([4, 2], F32)
    nst8 = A.alloc([128], F32)
    nld = A.alloc([128], F32)
    kcT = [A.alloc([8, 512], BF16) for _ in range(2)]
    vc = [A.alloc([4, 8, 129], BF16) for _ in range(2)]
    kld = [A.alloc([1024], BF16) for _ in range(2)]
    SXK = ["SX%d" % h for h in range(4)]
    SYK = ["SY%d" % h for h in range(4)]
    for h in range(4):
        P.op("pool", CP(SXb[:, h], SX[:, h]), reads=["SX%d" % h], writes=["SXb%d" % h])
    P.op("pool", CP(nXb, nX), reads=["nX"], writes=["nXb"])
    for c in range(2):
        for i2 in range(2):
            P.op("pool", MSET(qz[i2][c], 0.0), writes=["qz%d_%d" % (i2, c)])
            P.op("pool", MSET(es0[i2], 0.0), writes=["es0_%d" % i2])
            P.op("pool", MSET(es1[i2], 0.0), writes=["es1_%d" % i2])

    def load_kv(u):
        s = u % RS
        P.dma("sp", kring[s], scr_fm[8:16, :, u * 128:(u + 1) * 128].rearrange("c p t -> p c t"), writes=["kring%d" % s])
        P.dma("sp", vring[s], scr_va[u].rearrange("p (h e) -> p h e", h=8), writes=["vring%d" % s])

    for u in range(4):
        load_kv(u)

    def emit_state(w, Sf, Skeys, nf, nkey, mt, mkey):
        P.op("dve", CP(mcolt[0:4, w:w + 1], mt), reads=[mkey], writes=["mcolt"])
        P.op("dve", TS(d4[0:4, 0:4], identf[0:4, 0:4], mt), reads=[mkey, "cst"], writes=["d4"])
        P.op("pe", MM(psf(1)[:, 0:4], ones4, d4[0:4, 0:4]), reads=["cst", "d4"], writes=[PK[1]])
        P.op("act", ACT(emn, psf(1)[:, 0:4], AF.Exp, scale=-1.0), reads=[PK[1]], writes=["emn"])
        for h in range(4):
            P.op("dve", TS(Cst[:, h].rearrange("p d e -> p (d e)"), Sf[:, h].rearrange("p d e -> p (d e)"), emn[:, h:h + 1]),
                 reads=[Skeys[h], "emn"], writes=["Cst"])
        P.dma("sp", C_out[w].rearrange("h (d p) e -> p h d e", p=128), Cst, reads=["Cst"], writes=["C_out"])
        P.op("dve", TT(nst, nf, emn.unsqueeze(2).broadcast_to([128, 4, 2]), ALU.mult), reads=[nkey, "emn"], writes=["nst"])
        P.op("pe", TR(psf(1)[0:8, 128:256], nst.rearrange("p h d -> p (h d)"), identf), reads=["nst", "cst"], writes=[PK[1]])
        P.op("dve", CP(nst8[0:8, :], psf(1)[0:8, 128:256]), reads=[PK[1]], writes=["nst8"])
        P.dma("sp", n_out[w].rearrange("h (d p) -> (h d) p", p=128), nst8[0:8, :], reads=["nst8"], writes=["n_out"])

    def load3(i):
        u = 4 + i
        i2 = i % 2
        load_kv(u)
        P.dma("sp", qbuf[i2], scr_fm[0:8, :, u * 128:(u + 1) * 128].rearrange("c p t -> p c t"), writes=["qbuf%d" % i2])
        P.dma("sp", mbuf[i2], scr_fm[16:32, :, u * 128:(u + 1) * 128].rearrange("c p t -> p c t"), writes=["mbuf%d" % i2])
        P.dma("sp", tbuf[i2], scr_tm[i], writes=["tbuf%d" % i2])
        P.dma("sp", gbuf[i2], scr_g[i], writes=["gbuf%d" % i2])

    GL = {}

    def views(i):
        i2 = i % 2
        return dict(i2=i2, u=4 + i, sample=(i == NM - 1), slot=(4 + i) % RS,
                    kmT=mbuf[i2][:, 8:16, :], qmT=mbuf[i2][:, 0:8, :], km_tm=tbuf[i2][:, 0:1024],
                    vm_tm=tbuf[i2][:, 1024:2048].rearrange("p (h e) -> p h e", h=4), om_tm=tbuf[i2][:, 2048:3072],
                    tk="tbuf%d" % i2, mk="mbuf%d" % i2, qk="qbuf%d" % i2)

    def sample_setup():
        for seq in range(2):
            for kt in range(4):
                li = (seq * 4 + kt) % 2
                P.dma("pool", kld[li], cache_k[seq, kt * 128:(kt + 1) * 128, :], writes=["kld%d" % li])
                for hh2 in range(2):
                    b = 4 + hh2
                    pt = psb(b).rearrange("p (k t) -> p k t", k=8)
                    for h4 in range(4):
                        h = hh2 * 4 + h4
                        P.op("pe", TR(pt[:, h4, :], kld[li][:, h * 128:(h + 1) * 128], identb), reads=["kld%d" % li, "identb"],
                             writes=[PK[b]], signal=(h4 == 3))
                    evac(kcT[seq][:, hh2 * 4:(hh2 + 1) * 4, kt * 128:(kt + 1) * 128], pt[:, 0:4, :], [PK[b]], ["kcT%d" % seq],
                         eng=("act" if hh2 == 0 else "dve"))
            for kt in range(4):
                P.dma("pool", vc[seq][:, kt, :, 0:128], cache_v[seq, kt * 128:(kt + 1) * 128, :].rearrange("p (h e) -> p h e", h=8),
                      writes=["vc%d_%d" % (seq, kt)])
            P.op("pool", MSET(vc[seq][:, :, :, 128], 1.0), writes=["vc%dx" % seq])
        P.dma("sp", em_in, st_mB, writes=["em_in"])
        P.op("act", ACT(em_in, em_in, AF.Exp), reads=["em_in"], writes=["em_in"])
        P.dma("sp", ms[0:4, 0:2], st_m4, writes=["ms0", "ms1"])
        for seq, (Sf, SK, Sb_, SbK, nf, nK, nb_, nbK) in enumerate(((SX, "SX", SXb, "SXb", nX, "nX", nXb, "nXb"),
                                                                     (SY, "SY", SYb, "SYb", nY, "nY", nYb, "nYb"))):
            P.dma("sp", Sf, st_C[seq].rearrange("h (d p) e -> p h d e", p=128), writes=[SK + str(h) for h in range(4)])
            P.dma("sp", nld[0:8, :], st_n[seq].rearrange("h (d p) -> (h d) p", p=128), writes=["nld"])
            P.op("pe", TR(psf(1)[:, 0:8], nld[0:8, :], identf[0:8, 0:8]), reads=["nld", "cst"], writes=[PK[1]])
            P.op("dve", TT(nf, psf(1)[:, 0:8].rearrange("p (h d) -> p h d", h=4), em_in[:, seq, :].unsqueeze(2).broadcast_to([128, 4, 2]), ALU.mult),
                 reads=[PK[1], "em_in"], writes=[nK])
            P.op("dve", CP(nb_, nf), reads=[nK], writes=[nbK])
            for h in range(4):
                P.op("dve", TS(Sf[:, h].rearrange("p d e -> p (d e)"), Sf[:, h].rearrange("p d e -> p (d e)"), em_in[:, seq, h:h + 1]),
                     reads=[SK + str(h), "em_in"], writes=[SK + str(h)])
                P.op("pool", CP(Sb_[:, h], Sf[:, h]), reads=[SK + str(h)], writes=[SbK + str(h)])

    def PA(i):
        V = views(i)
        i2 = V["i2"]
        G = gate_math(gbuf[i2], ["gbuf%d" % i2], NPRE + i, 1, 1)
        GL[i] = G
        K = G["K"]
        P.op("dve", TT(vp[i2], V["vm_tm"], G["ex"][:, 0:4].unsqueeze(2).broadcast_to([128, 4, 256]), ALU.mult),
             reads=[V["tk"], K + "ex"], writes=["vp%d" % i2])
        for c in range(2):
            P.op("pool", CP(qz[i2][c][:, :, 64 * c:64 * c + 64], V["qmT"][:, :, 64 * c:64 * c + 64]), reads=[V["mk"]], writes=["qz%d_%d" % (i2, c)])
        for h in range(4):
            for d in range(2):
                P.op("pe", MM(psf(0)[:, h * 128:(h + 1) * 128], V["kmT"][:, h * 2 + d, :], V["qmT"][:, h * 2 + d, :], d == 0, d == 1),
                     reads=[V["mk"]], writes=[PK[0]], signal=(h == 3 and d == 1))
        P.op("dve", TT(qkm[i2], psf(0).rearrange("p (h t) -> p h t", h=4), tri.unsqueeze(1).broadcast_to([128, 4, 128]), ALU.mult),
             reads=[PK[0], "cst"], writes=["qkm%d" % i2])
        P.op("act", ACT(gsn[i2], V["om_tm"], AF.Sigmoid), reads=[V["tk"]], writes=["gsn%d" % i2])
        P.op("pool", TT(gsn[i2], gsn[i2], mnb, ALU.mult), reads=["gsn%d" % i2, "mnb"], writes=["gsn%d" % i2])

    def PB(i):
        V = views(i)
        i2 = V["i2"]
        G = GL[i]
        n_update(nY, "nY", 1, G, 0, V["km_tm"], V["tk"], nb=nYb, nb_key="nYb", n_src=nX, n_src_key="nX")
        for h in range(4):
            state_update(SY, "SY", None, None, h % 2, G, 0, h, V["km_tm"], V["tk"], vp[i2], "vp%d" % i2, Sb=SYb, Sb_key="SYb",
                         S_src=SX, S_src_key="SX")

    def ATT(i, heads):
        V = views(i)
        i2, u, sample, slot, qk_ = V["i2"], V["u"], V["sample"], V["slot"], V["qk"]
        heads = list(heads)
        for h in heads:
            e = h % 2
            bx, by = (2, 3) if e == 0 else (4, 5)
            X, Y = psf(bx), psf(by)
            if not sample:
                for kt in range(5):
                    ks = (u - 4 + kt) % RS
                    o = X[:, kt * 128:(kt + 1) * 128] if kt < 4 else Y[:, 0:128]
                    bk = PK[bx] if kt < 4 else PK[by]
                    P.op("pe", MM(o, kring[ks][:, h, :], qbuf[i2][:, h, :], True, False), reads=["kring%d" % ks, qk_], writes=[bk], signal=False)
                    P.op("pe", MM(o, identb, btab[:, h, kt, :], False, True), reads=["identb", "btab"], writes=[bk], signal=(kt >= 3))
                P.op("act", ACT(expT[e][:, 0:4, :], X.rearrange("p (k t) -> p k t", k=4), AF.Exp), reads=[PK[bx]], writes=["expT%da" % e])
                P.op("act", ACT(expT[e][:, 4, :], Y[:, 0:128], AF.Exp), reads=[PK[by]], writes=["expT%db" % e])
            else:
                for kt in range(4):
                    for seq in range(2):
                        o = X[:, kt * 128 + seq * 64:kt * 128 + seq * 64 + 64]
                        P.op("pe", MM(o, kcT[seq][:, h, kt * 128:(kt + 1) * 128], qbuf[i2][:, h, seq * 64:seq * 64 + 64], True, False),
                             reads=["kcT%d" % seq, qk_], writes=[PK[bx]], signal=False)
                        P.op("pe", MM(o, identb, btab[:, h, kt, 0:64], False, True), reads=["identb", "btab"], writes=[PK[bx]],
                             signal=(kt == 3 and seq == 1))
                P.op("pe", MM(Y[:, 0:128], kring[slot][:, h, :], qbuf[i2][:, h, :], True, False), reads=["kring%d" % slot, qk_], writes=[PK[by]], signal=False)
                P.op("pe", MM(Y[:, 0:128], identb, bs4[:, h, :], False, True), reads=["identb", "bs4"], writes=[PK[by]])
                X3 = X.rearrange("p (k t) -> p k t", k=4)
                P.op("act", ACT(es0[e][:, :, 0:64], X3[:, :, 0:64], AF.Exp), reads=[PK[bx]], writes=["es0_%d" % e])
                P.op("act", ACT(es1[e][:, :, 64:128], X3[:, :, 64:128], AF.Exp), reads=[PK[bx]], writes=["es1_%d" % e])
                P.op("act", ACT(expT[e][:, 4, :], Y[:, 0:128], AF.Exp), reads=[PK[by]], writes=["expT%db" % e])
        for h in heads:
            e = h % 2
            g = h // 3
            pb = 6 + g % 2
            hh = h % 3
            PV = psf(pb)
            o = PV[:, hh * 129:(hh + 1) * 129]
            if not sample:
                for kt in range(5):
                    ks = (u - 4 + kt) % RS
                    P.op("pe", MM(o, expT[e][:, kt, :], vring[ks][:, h, :], kt == 0, kt == 4),
                         reads=["expT%da" % e if kt < 4 else "expT%db" % e, "vring%d" % ks], writes=[PK[pb]], signal=(kt == 4))
            else:
                for kt in range(4):
                    P.op("pe", MM(o, es0[e][:, kt, :], vc[0][:, kt, h, :], kt == 0, False), reads=["es0_%d" % e, "vc0_%d" % kt, "vc0x"], writes=[PK[pb]], signal=False)
                    P.op("pe", MM(o, es1[e][:, kt, :], vc[1][:, kt, h, :], False, False), reads=["es1_%d" % e, "vc1_%d" % kt, "vc1x"], writes=[PK[pb]], signal=False)
                P.op("pe", MM(o, expT[e][:, 4, :], vring[slot][:, h, :], False, True), reads=["expT%db" % e, "vring%d" % slot], writes=[PK[pb]])
            if h in (2, 5, 7):
                nh = hh + 1
                h0 = h - hh
                PV3 = PV[:, 0:nh * 129].rearrange("p (h e) -> p h e", h=nh)
                P.op("dve", TS(rden[:, 0:nh], PV3[:, :, 128], 1e-30, None, ALU.add), reads=[PK[pb]], writes=["rden"])
                P.op("dve", lambda e_, o_=rden[:, 0:nh]: e_.reciprocal(out=o_, in_=o_), reads=["rden"], writes=["rden"])
                P.op("dve", TT(mix[i2][:, h0 * 128:(h0 + nh) * 128].rearrange("p (h e) -> p h e", h=nh), PV3[:, :, 0:128],
                               rden[:, 0:nh].unsqueeze(2).broadcast_to([128, nh, 128]), ALU.mult),
                     reads=[PK[pb], "rden"], writes=["mix%d_a%d" % (i2, g)])

    def HD(i):
        V = views(i)
        i2, sample, tk, km_tm, om_tm = V["i2"], V["sample"], V["tk"], V["km_tm"], V["om_tm"]
        G = GL[i]
        K = G["K"]
        for h in range(4):
            hb = h // 2
            Hh = psf(hb)[:, (h % 2) * 256:(h % 2 + 1) * 256]
            P.op("pe", MM(Hh, qkm[i2][:, h, :], vp[i2][:, h, :], True, False), reads=["qkm%d" % i2, "vp%d" % i2], writes=[PK[hb]], signal=False)
            for c, (Sb_, SbK) in enumerate(((SXb, "SXb"), (SYb, "SYb"))):
                for d in range(2):
                    P.op("pe", MM(Hh, qz[i2][c][:, h * 2 + d, :], Sb_[:, h, d, :], False, c == 1 and d == 1),
                         reads=["qz%d_%d" % (i2, c), SbK + str(h)], writes=[PK[hb]], signal=(c == 1 and d == 1))
        DEN = psf(7)[:, 400:404]
        for h in range(4):
            o = DEN[:, h:h + 1]
            P.op("pe", MM(o, qkm[i2][:, h, :], G["gbf"][:, h:h + 1], True, False), reads=["qkm%d" % i2, K + "gbf"], writes=[PK[7]], signal=False)
            for c, (nb_, nbK) in enumerate(((nXb, "nXb"), (nYb, "nYb"))):
                for d in range(2):
                    P.op("pe", MM(o, qz[i2][c][:, h * 2 + d, :], nb_[:, h, d:d + 1], False, c == 1 and d == 1),
                         reads=["qz%d_%d" % (i2, c), nbK], writes=[PK[7]], signal=(h == 3 and c == 1 and d == 1))
        eb = G["ex"][:, 4:8]
        P.op("dve", TT(dd, DEN, eb, ALU.mult), reads=[PK[7], K + "ex"], writes=["dd"])
        P.op("dve", TS(ddn, dd, -1.0), reads=["dd"], writes=["ddn"])
        P.op("dve", TT(dd, dd, ddn, ALU.max), reads=["dd", "ddn"], writes=["dd"])
        P.op("dve", TS(dd, dd, 1.0, None, ALU.max), reads=["dd"], writes=["dd"])
        P.op("dve", lambda e_, o_=dd, i_=dd: e_.reciprocal(out=o_, in_=i_), reads=["dd"], writes=["dd"])
        P.op("dve", TT(dd, dd, eb, ALU.mult), reads=["dd", K + "ex"], writes=["dd"])
        for h in range(4):
            hb = h // 2
            Hh = psf(hb)[:, (h % 2) * 256:(h % 2 + 1) * 256]
            P.op("act", ACT(hjunk, Hh, AF.Square, scale=dd[:, h:h + 1], accum_out=ssh[:, h:h + 1]), reads=[PK[hb], "dd"], writes=["hjunk", "ssh%d" % h])
        rstd_from_ss(ssh[:, 0:4], ssh[:, 4:8], 256, ["ssh%d" % h for h in range(4)], "sshr")
        P.op("dve", TT(scl, dd, ssh[:, 4:8], ALU.mult), reads=["dd", "sshr"], writes=["scl"])
        for h in range(4):
            hb = h // 2
            Hh = psf(hb)[:, (h % 2) * 256:(h % 2 + 1) * 256]
            P.op("dve", STT(mix[i2][:, 1024 + h * 256:1024 + (h + 1) * 256], Hh, scl[:, h:h + 1], gsn[i2][:, h * 256:(h + 1) * 256], ALU.mult, ALU.mult),
                 reads=[PK[hb], "scl", "gsn%d" % i2], writes=["mix%d_m%d" % (i2, h)])

    def SC1(i):
        V = views(i)
        i2, sample, tk, km_tm = V["i2"], V["sample"], V["tk"], V["km_tm"]
        G = GL[i]
        if not sample:
            n_update(nX, "nX", 1, G, 1, km_tm, tk, nb=nXb, nb_key="nXb", n_src=nY, n_src_key="nY")
            for h in range(4):
                state_update(SX, "SX", None, None, h % 2, G, 1, h, km_tm, tk, vp[i2], "vp%d" % i2, Sb=SXb, Sb_key="SXb",
                             S_src=SY, S_src_key="SY")
            m_update(mP[0:4, 0:1], "mP", G, 0)
            m_update(mP[0:4, 0:1], "mP", G, 1)
            if i == NOWN:
                emit_state(0, SX, SXK, nX, "nX", mP[0:4, 0:1], "mP")
        else:
            n_update(nX, "nX", 1, G, 0, km_tm, tk)
            n_update(nY, "nY", 1, G, 1, km_tm, tk)
            for h in range(4):
                state_update(SX, "SX", None, None, h % 2, G, 0, h, km_tm, tk, vp[i2], "vp%d" % i2)
            for h in range(4):
                state_update(SY, "SY", None, None, h % 2, G, 1, h, km_tm, tk, vp[i2], "vp%d" % i2)
            m_update(ms[0:4, 0:1], "ms0", G, 0)
            m_update(ms[0:4, 1:2], "ms1", G, 1)
            emit_state(1, SX, SXK, nX, "nX", ms[0:4, 0:1], "ms0")
            emit_state(2, SY, SYK, nY, "nY", ms[0:4, 1:2], "ms1")

    def TRS(i):
        i2 = i % 2
        mixkeys = ["mix%d_a%d" % (i2, g) for g in range(3)] + ["mix%d_m%d" % (i2, h) for h in range(4)]
        for half in range(2):
            b = 4 + half
            pt = psb(b).rearrange("p (k t) -> p k t", k=8)
            for k8 in range(8):
                k = half * 8 + k8
                P.op("pe", TR(pt[:, k8, :], mix[i2][:, k * 128:(k + 1) * 128], identb), reads=mixkeys + ["identb"], writes=[PK[b]], signal=(k8 == 7))
            evac(mixTs[i2][:, half * 8:(half + 1) * 8, :], pt, [PK[b]], ["mixTs%d_%d" % (i2, half)], eng=("act" if half == 0 else "dve"))
        P.dma("act", scr_mixT[i], mixTs[i2].rearrange("p k t -> p (k t)"), reads=["mixTs%d_0" % i2, "mixTs%d_1" % i2], writes=["scr_mixT"])

    load3(0)
    PA(0)
    for i in range(NM):
        sample_i = (i == NM - 1)
        if i + 1 < NM:
            load3(i + 1)
        ATT(i, [0, 1])
        if not sample_i:
            PB(i)
        if i > 0:
            TRS(i - 1)
        ATT(i, [2, 3])
        if i + 1 < NM:
            PA(i + 1)
        ATT(i, [4, 5])
        HD(i)
        ATT(i, [6, 7])
        SC1(i)
        if i + 1 == NM - 1:
            sample_setup()
    TRS(NM - 1)
    P.dma("sp", m_out, mcolt[0:4, 0:3], reads=["mcolt"], writes=["m_out"])
    P.barrier()
    A.reset(base_mark)
    if last_stage == 3:
        P.run()
        return nc


    wo = A.alloc([KT, D], BF16)
    for q in range(4):
        P.dma("pool", wo[:, :, q * 512:(q + 1) * 512], w_out[:, q * 512:(q + 1) * 512].rearrange("(k p) n -> p k n", p=128), writes=["wo%d" % q])
    gB1 = [A.alloc([D], F32) for _ in range(2)]
    build_gtgB(gB1[0], gB1[1], 0, "gB1p", "gB1s")
    fe_alloc(with_xs=False)
    xs4 = [A.alloc([D], F32) for _ in range(2)]
    x1s = [A.alloc([D], F32) for _ in range(2)]
    mT = [A.alloc([KT, 128], BF16) for _ in range(2)]
    h2s = [A.alloc([KT, 128], BF16) for _ in range(2)]
    ssq = [A.alloc([8], F32) for _ in range(2)]
    junk4 = A.alloc([512], BF16)
    def load4(i):
        P.dma("sp", mT[i % 2], scr_mixT[i].rearrange("p (k t) -> p k t", k=KT), writes=["mT%d" % (i % 2)])
        P.dma("sp", xs4[i % 2], x_main[i * 128:(i + 1) * 128, :], writes=["xs4_%d" % (i % 2)])

    def mm4(i):
        i2 = i % 2
        bo = 4 * i2
        for cb in range(4):
            for k in range(KT):
                P.op("pe", MM(psf(bo + cb), mT[i2][:, k, :], wo[:, k, cb * 512:(cb + 1) * 512], k == 0, k == KT - 1),
                     reads=["mT%d" % i2, "wo%d" % cb], writes=[PK[bo + cb]], signal=(k == KT - 1))

    def post4(i):
        i2 = i % 2
        bo = 4 * i2
        sample = (i == NM - 1)
        for cb in range(4):
            P.op("act", ACT(junk4, psf(bo + cb), AF.Square, accum_out=ssq[i2][:, cb:cb + 1]), reads=[PK[bo + cb]], writes=["junk4", "ssq%d_%d" % (i2, cb)])
        P.op("dve", lambda e_, o_=ssq[i2][:, 4:5], i_=ssq[i2][:, 0:4]: e_.tensor_reduce(out=o_, in_=i_, axis=AX.X, op=ALU.add),
             reads=["ssq%d_%d" % (i2, cb) for cb in range(4)], writes=["ssq%d_s" % i2])
        rstd_from_ss(ssq[i2][:, 4:5], ssq[i2][:, 5:6], D, ["ssq%d_s" % i2], "ssq%d_r" % i2)
        gB, gBk = (gB1[1], "gB1s") if sample else (gB1[0], "gB1p")
        x1keys = []
        for cb in range(4):
            sl = slice(cb * 512, (cb + 1) * 512)
            key = "x1s%d_%d" % (i2, cb)
            x1keys.append(key)
            P.op("dve", STT(x1s[i2][:, sl], psf(bo + cb), ssq[i2][:, 5:6], gB[:, sl], ALU.mult, ALU.mult), reads=[PK[bo + cb], "ssq%d_r" % i2, gBk], writes=[key])
            P.op("dve", TT(x1s[i2][:, sl], x1s[i2][:, sl], xs4[i2][:, sl], ALU.add), reads=[key, "xs4_%d" % i2], writes=[key])
        if i > 0:
            P.dma("act", scr_x1[i], x1s[i2], reads=x1keys, writes=["scr_x1"])
        rows = (1, 2) if sample else (0, 0)
        front_end(None, h2s[i2], "h2s%d" % i2, 3, 2, rows, tb=(bo, bo + 1), xs_in=x1s[i2], xs_keys=x1keys)
        hk = []
        for k in range(KT):
            hk.append("h2s%d_k%d_0" % (i2, k))
            if sample:
                hk.append("h2s%d_k%d_64" % (i2, k))
        P.dma("act", scr_h2T[:, :, i * 128:(i + 1) * 128], h2s[i2], reads=hk, writes=["scr_h2T"])

    load4(0)
    mm4(0)
    for i in range(NM):
        if i + 1 < NM:
            load4(i + 1)
            mm4(i + 1)
        post4(i)
    P.barrier()
    A.reset(base_mark)
    if last_stage == 4:
        P.run()
        return nc

    TMp = 128 + OWN
    base_s0 = 2 + TMp + 2
    base_s1 = base_s0 + 64 + 2
    ROW = base_s1 + 64
    h2T_all = A.alloc([KT, TM], BF16)
    for q in range(4):
        P.dma("sp", h2T_all[:, q * 4:(q + 1) * 4, :], scr_h2T[:, q * 4:(q + 1) * 4, :], writes=["h2T_%d" % q])
    H2K = ["h2T_%d" % q for q in range(4)]
    cw = A.alloc([FT, 4], F32)
    P.dma("sp", cw, cwT, writes=["cw"])
    cst_tm = A.alloc([DFF], F32)
    cstT = A.alloc([FT, 4], F32)
    P.dma("sp", cst_tm[0:4, :], conv_st, writes=["tmp22"])
    for ft in range(FT):
        P.op("pe", TR(psf(7)[:, ft * 4:(ft + 1) * 4], cst_tm[0:4, ft * 128:(ft + 1) * 128], identf[0:4, 0:4]), reads=["tmp22", "cst"],
             writes=[PK[7]], signal=(ft == FT - 1))
    P.op("dve", CP(cstT.rearrange("p f c -> p (f c)"), psf(7)[:, 0:FT * 4]), reads=[PK[7]], writes=["cstT"])
    ugrow = [A.alloc([ROW], F32) for _ in range(2)]
    acc = [A.alloc([512], F32) for _ in range(2)]
    gl = [A.alloc([512], F32) for _ in range(2)]
    arow = [A.alloc([TM], BF16) for _ in range(2)]
    convsave = A.alloc([FT, 6], F32)
    cso = cst_tm
    wgb = [A.alloc([KT, 512], BF16) for _ in range(2)]
    wub = [A.alloc([KT, 512], BF16) for _ in range(2)]
    for i2 in range(2):
        P.op("pool", MSET(ugrow[i2][:, 0:2], 0.0), writes=["ug%d_pad" % i2])
    blocks = []
    for m0 in range(0, TMp, 512):
        m1 = min(m0 + 512, TMp)
        blocks.append((m0, m1, [(m0, m1, 2 + m0)]))
    blocks.append((TMp, TMp + 128, [(TMp, TMp + 64, base_s0), (TMp + 64, TMp + 128, base_s1)]))
    nG = 0
    nU = 0
    npc = 0
    for ftg in range(FT // 4):
        wi = ftg % 2
        P.dma("pool", wgb[wi], w_g[:, ftg * 512:(ftg + 1) * 512].rearrange("(k p) n -> p k n", p=128), writes=["wgb%d" % wi])
        P.dma("pool", wub[wi], w_u[:, ftg * 512:(ftg + 1) * 512].rearrange("(k p) n -> p k n", p=128), writes=["wub%d" % wi])
        for sub in range(4):
            ft = ftg * 4 + sub
            u2 = ft % 2
            ug = ugrow[u2]
            P.op("pool", CP(ug[:, base_s0 - 2:base_s0], cstT[:, ft, 0:2]), reads=["cstT"], writes=["ug%d_s0" % u2])
            P.op("pool", CP(ug[:, base_s1 - 2:base_s1], cstT[:, ft, 2:4]), reads=["cstT"], writes=["ug%d_s1" % u2])
            akeys = []
            for bi, (m0, m1, pieces) in enumerate(blocks):
                n = m1 - m0
                gb = nG % 3
                nG += 1
                ub = 3 + nU % 4
                nU += 1
                for k in range(KT):
                    P.op("pe", MM(psf(gb)[:, 0:n], wgb[wi][:, k, sub * 128:(sub + 1) * 128], h2T_all[:, k, m0:m1], k == 0, k == KT - 1),
                         reads=["wgb%d" % wi, H2K[k // 4]], writes=[PK[gb]], signal=(k == KT - 1))
                for k in range(KT):
                    P.op("pe", MM(psf(ub)[:, 0:n], wub[wi][:, k, sub * 128:(sub + 1) * 128], h2T_all[:, k, m0:m1], k == 0, k == KT - 1),
                         reads=["wub%d" % wi, H2K[k // 4]], writes=[PK[ub]], signal=(k == KT - 1))
                for (p0, p1, c0) in pieces:
                    pn = p1 - p0
                    ukey = "ug%d_b%d_%d" % (u2, bi, p0)
                    P.op("act", ACT(ug[:, c0:c0 + pn], psf(gb)[:, p0 - m0:p1 - m0], AF.Copy), reads=[PK[gb]], writes=[ukey])
                    prev = ["ug%d_pad" % u2, "ug%d_s0" % u2, "ug%d_s1" % u2]
                    if bi > 0:
                        prev += ["ug%d_b%d_%d" % (u2, bi - 1, blocks[bi - 1][2][-1][0])]
                    if bi == 0:
                        P.op("pool", TS(ug[:, 2:130], ug[:, 2:130], msk[:, 0, NPRE:NPRE + 1]), reads=[ukey, "msk"], writes=[ukey])
                    a_ = acc[npc % 2]
                    g_ = gl[npc % 2]
                    ak, gk = "acc%d" % (npc % 2), "gl%d" % (npc % 2)
                    npc += 1
                    P.op("act", ACT(a_[:, 0:pn], ug[:, c0 - 2:c0 - 2 + pn], AF.Identity, scale=cw[:, ft, 0:1], bias=cw[:, ft, 3:4]),
                         reads=[ukey, "cw"] + prev, writes=[ak])
                    P.op("dve", STT(a_[:, 0:pn], ug[:, c0 - 1:c0 - 1 + pn], cw[:, ft, 1:2], a_[:, 0:pn], ALU.mult, ALU.add),
                         reads=[ukey, "cw", ak] + prev, writes=[ak])
                    P.op("dve", STT(a_[:, 0:pn], ug[:, c0:c0 + pn], cw[:, ft, 2:3], a_[:, 0:pn], ALU.mult, ALU.add), reads=[ukey, "cw", ak], writes=[ak])
                    P.op("act", ACT(g_[:, 0:pn], a_[:, 0:pn], AF.Gelu), reads=[ak], writes=[gk])
                    akey = "arow%d_%d" % (u2, p0)
                    akeys.append(akey)
                    P.op("dve", TT(arow[u2][:, p0:p1], g_[:, 0:pn], psf(ub)[:, p0 - m0:p1 - m0], ALU.mult), reads=[gk, PK[ub]], writes=[akey])
            allug = ["ug%d_b%d_%d" % (u2, bi, pc[0]) for bi, (_, _, pcs) in enumerate(blocks) for pc in pcs]
            P.op("pool", CP(convsave[:, ft, 0:2], ug[:, 2 + TMp - 2:2 + TMp]), reads=allug, writes=["convsave"])
            P.op("pool", CP(convsave[:, ft, 2:4], ug[:, base_s0 + 62:base_s0 + 64]), reads=allug, writes=["convsave"])
            P.op("pool", CP(convsave[:, ft, 4:6], ug[:, base_s1 + 62:base_s1 + 64]), reads=allug, writes=["convsave"])
            P.dma("sp", scr_aT[ft], arow[u2], reads=akeys, writes=["scr_aT"])
    for g4 in range(FT // 4):
        b = g4 % 2
        for s4 in range(4):
            ft = g4 * 4 + s4
            P.op("pe", TR(psf(b)[0:6, s4 * 128:(s4 + 1) * 128], convsave[:, ft, :], identf), reads=["convsave", "cst"], writes=[PK[b]], signal=(s4 == 3))
        P.op("dve", CP(cso[0:6, g4 * 512:(g4 + 1) * 512], psf(b)[0:6, :]), reads=[PK[b]], writes=["tmp22"])
    P.dma("sp", conv_out, cso[0:6, :], reads=["tmp22"], writes=["conv_out"])
    P.barrier()
    A.reset(base_mark)
    if last_stage == 5:
        P.run()
        return nc

    GS = 6
    gB2 = [A.alloc([D], F32) for _ in range(2)]
    build_gtgB(gB2[0], gB2[1], 1, "gB2p", "gB2s")
    aTg = A.alloc([FT, GS * 128], BF16)
    ystage = [A.alloc([D], F32) for _ in range(GS)]
    x1t = [A.alloc([D], F32) for _ in range(2)]
    NWD = 5
    wd = [A.alloc([4, 512], BF16) for _ in range(NWD)]
    ssq6 = A.alloc([GS, 8], F32)
    junk6 = A.alloc([512], BF16)
    out_tiles = list(range(1, NM))
    nwd = 0
    nx1 = 0
    for g0 in range(0, len(out_tiles), GS):
        tiles = out_tiles[g0:g0 + GS]
        nt = len(tiles)
        tok0 = tiles[0] * 128
        ntok = nt * 128
        for q in range(4):
            P.dma("sp", aTg[:, q * 11:(q + 1) * 11, 0:ntok], scr_aT[q * 11:(q + 1) * 11, :, tok0:tok0 + ntok].rearrange("f p t -> p f t"),
                  writes=["aTg_%d" % q])
        for cb in range(4):
            for ftq in range(FT // 4):
                wi = nwd % NWD
                nwd += 1
                P.dma("pool", wd[wi], w_d[ftq * 512:(ftq + 1) * 512, cb * 512:(cb + 1) * 512].rearrange("(f p) n -> p f n", p=128), writes=["wd%d" % wi])
                for s4 in range(4):
                    ft = ftq * 4 + s4
                    for ti in range(nt):
                        P.op("pe", MM(psf(ti), aTg[:, ft, ti * 128:(ti + 1) * 128], wd[wi][:, s4, :], ft == 0, ft == FT - 1),
                             reads=["aTg_%d" % (ft // 11), "wd%d" % wi], writes=[PK[ti]], signal=(ft == FT - 1 or (s4 == 3 and ti == nt - 1)))
            for ti in range(nt):
                P.op("act", ACT(junk6, psf(ti), AF.Square, accum_out=ssq6[:, ti, cb:cb + 1]), reads=[PK[ti]], writes=["junk6", "ssq6_%d_%d" % (ti, cb)])
                P.op("dve", CP(ystage[ti][:, cb * 512:(cb + 1) * 512], psf(ti)), reads=[PK[ti]], writes=["ys%d_%d" % (ti, cb)])
        for ti, tile in enumerate(tiles):
            sample = (tile == NM - 1)
            xi = nx1 % 2
            nx1 += 1
            P.dma("sp", x1t[xi], scr_x1[tile], writes=["x1t%d" % xi])
            P.op("dve", lambda e_, o_=ssq6[:, ti, 4:5], i_=ssq6[:, ti, 0:4]: e_.tensor_reduce(out=o_, in_=i_, axis=AX.X, op=ALU.add),
                 reads=["ssq6_%d_%d" % (ti, cb) for cb in range(4)], writes=["ssq6s_%d" % ti])
            rstd_from_ss(ssq6[:, ti, 4:5], ssq6[:, ti, 5:6], D, ["ssq6s_%d" % ti], "ssq6r_%d" % ti)
            gB, gBk = (gB2[1], "gB2s") if sample else (gB2[0], "gB2p")
            yk = ["ys%d_%d" % (ti, cb) for cb in range(4)]
            P.op("dve", STT(ystage[ti], ystage[ti], ssq6[:, ti, 5:6], gB, ALU.mult, ALU.mult), reads=yk + ["ssq6r_%d" % ti, gBk], writes=yk)
            P.op("dve", TT(ystage[ti], ystage[ti], x1t[xi], ALU.add), reads=yk + ["x1t%d" % xi], writes=yk)
            P.dma("act", y_main[(tile - 1) * 128:tile * 128, :], ystage[ti], reads=yk, writes=["y_main"])
    P.barrier()
    P.run()
    return nc


def make_consts():
    c = np.zeros((128, 8, 128), np.float32)
    c[:, 0, :] = np.eye(128, dtype=np.float32)
    s = np.arange(128)[:, None]
    t = np.arange(128)[None, :]
    c[:, 1, :] = ((s // 64 == t // 64) & (s <= t)).astype(np.float32)
    c[0:64, 2, :] = 1.0
    c[64:128, 3, :] = 1.0
    c[0, 4, :] = 1.0
    c[1, 5, 0:64] = 1.0
    c[2, 5, 64:128] = 1.0
    c[0:64, 6, 0] = 1.0
    c[64:128, 6, 1] = 1.0
    c[:, 7, :] = 1.0
    return c


def make_consts2():
    return None


def prep_inputs(inp, SEQ):
    OWN = SEQ // 4
    NOWN = OWN // 128
    NPRE = 4
    NM = NOWN + 2
    f32 = np.float32
    xp = np.asarray(inp["x_prompt"], f32)
    xsamp = np.asarray(inp["x_sample"], f32)
    relb = np.asarray(inp["att_rel_bias"], f32)[0]
    row = np.arange(128)[:, None, None]
    kk = np.arange(5)[None, :, None]
    qc = np.arange(128)[None, None, :]
    p = 128 * kk + row - 64 * (qc // 64)
    dist = 512 + (qc % 64) - p
    idx = np.clip(dist, -256, 256) + 256
    valid = (p >= 0) & (p < 576)
    bias_tab = np.empty((128, 8, 5, 128), f32)
    for h in range(8):
        bias_tab[:, h] = np.where(valid, relb[h][idx], NEG)
    rr = np.arange(128)[:, None]
    qq = np.arange(128)[None, :]
    same = (rr // 64) == (qq // 64)
    idx4 = np.clip((qq % 64) - (rr % 64), -256, 256) + 256
    bias_s4 = np.empty((128, 8, 128), f32)
    for h in range(8):
        bias_s4[:, h] = np.where(same, relb[h][idx4], NEG)
    consts = make_consts()
    shared = {
        "adabT": np.ascontiguousarray(np.asarray(inp["ada_b"], f32)[0].reshape(96, 128).T),
        "adab3": np.ascontiguousarray(np.broadcast_to(np.asarray(inp["ada_b"], f32)[0][None], (3, 12288))),
        "gpreT": np.ascontiguousarray(np.stack([np.asarray(inp["norm_pre_mix"], f32)[0].reshape(16, 128).T,
                                                 np.asarray(inp["norm_pre_ffn"], f32)[0].reshape(16, 128).T], axis=1)),
        "gpost3": np.ascontiguousarray(np.broadcast_to(np.stack([np.asarray(inp["norm_post_mix"], f32)[0],
                                                                  np.asarray(inp["norm_post_ffn"], f32)[0]])[None], (3, 2, D))),
        "mnormB": np.ascontiguousarray(np.broadcast_to(np.asarray(inp["mlstm_norm"], f32)[0][None], (128, 1024))),
        "bgate": np.ascontiguousarray(np.broadcast_to(np.concatenate([np.asarray(inp["b_igate"], f32)[0],
                                                                       np.asarray(inp["b_fgate"], f32)[0]])[None], (128, 8))),
        "cwT": np.ascontiguousarray(np.concatenate([np.asarray(inp["ffn_conv_w"], f32)[0].reshape(3, FT, 128),
                                                    np.asarray(inp["ffn_conv_b"], f32)[0].reshape(1, FT, 128)], 0).transpose(2, 1, 0)),
        "bias_tab": bias_tab,
        "bias_s4": bias_s4,
        "ada_w": np.asarray(inp["ada_w"], f32)[0],
        "w_in": np.asarray(inp["w_in"], f32)[0],
        "w_out": np.asarray(inp["w_out"], f32)[0],
        "w_g": np.asarray(inp["w_ffn_gate"], f32)[0],
        "w_u": np.asarray(inp["w_ffn_up"], f32)[0],
        "w_d": np.asarray(inp["w_ffn_down"], f32)[0],
    }
    maps = []
    for r in range(8):
        b, j = r // 4, r % 4
        s0 = j * OWN
        m = dict(shared)
        xm = np.zeros((NM * 128, D), f32)
        if j > 0:
            xm[0:128] = xp[b, s0 - 128:s0]
        xm[128:128 + OWN] = xp[b, s0:s0 + OWN]
        xm[128 + OWN:] = xsamp[2 * r:2 * r + 2].reshape(128, D)
        m["x_main"] = xm
        xpre = np.zeros((NPRE * 128, D), f32)
        lo = s0 - 128 - NPRE * 128
        mk = np.zeros((128, 3, NPRE + NM), f32)
        for t in range(NPRE):
            a = lo + t * 128
            if a >= 0:
                xpre[t * 128:(t + 1) * 128] = xp[b, a:a + 128]
                mk[:, 0, t] = 1.0
        mk[:, 0, NPRE] = 1.0 if j > 0 else 0.0
        mk[:, 0, NPRE + 1:] = 1.0
        mk[:, 1] = -mk[:, 0]
        mk[:, 2] = np.where(mk[:, 0] > 0, 0.0, -1e30)
        m["x_pre"] = xpre
        m["masks"] = mk
        fmk = np.zeros((128, 2, 8), f32)
        for q in range(8):
            if q // 4 == b and q % 4 < j:
                fmk[:, 0, q] = 1.0
            else:
                fmk[:, 1, q] = -1e30
        m["fmask"] = fmk
        c3 = np.stack([np.asarray(inp["c_prompt"], f32)[b], np.asarray(inp["c_sample"], f32)[2 * r],
                       np.asarray(inp["c_sample"], f32)[2 * r + 1]])
        m["c3T"] = np.ascontiguousarray(c3.reshape(3, 16, 128).transpose(2, 1, 0))
        cc = consts.copy()
        m["consts"] = cc
        m["cache_k"] = np.ascontiguousarray(np.asarray(inp["cache_att_k"], f32)[0, 2 * r:2 * r + 2].reshape(2, 512, 1024))
        m["cache_v"] = np.ascontiguousarray(np.asarray(inp["cache_att_v"], f32)[0, 2 * r:2 * r + 2].reshape(2, 512, 1024))
        m["st_C"] = np.ascontiguousarray(np.asarray(inp["state_mlstm_C"], f32)[0, 2 * r:2 * r + 2])
        m["st_n"] = np.ascontiguousarray(np.asarray(inp["state_mlstm_n"], f32)[0, 2 * r:2 * r + 2])
        sm = np.asarray(inp["state_mlstm_m"], f32)[0, 2 * r:2 * r + 2]
        m["st_mB"] = np.ascontiguousarray(np.broadcast_to(sm[None], (128, 2, 4)))
        m["st_m4"] = np.ascontiguousarray(sm.T)
        m["conv_st"] = np.ascontiguousarray(np.asarray(inp["state_ffn_conv"], f32)[0, 2 * r:2 * r + 2].reshape(4, 5632))
        maps.append(m)
    return maps


SEQ_FULL = 8192
_CACHE = {}


def kernel(**inputs):
    SEQ = int(np.asarray(inputs["x_prompt"]).shape[1])
    OWN = SEQ // 4
    if SEQ not in _CACHE:
        _CACHE[SEQ] = None
    nc = build(SEQ)
    maps = prep_inputs(inputs, SEQ)
    res = run_bass_kernel_spmd(nc, maps, core_ids=list(range(8)))
    R_ = res.results
    f32 = np.float32
    y_p = np.empty((2, SEQ, D), f32)
    y_s = np.empty((16, 64, D), f32)
    p_k = np.empty((1, 2, 512, 8, 128), f32)
    p_v = np.empty((1, 2, 512, 8, 128), f32)
    p_C = np.empty((1, 2, 4, 256, 256), f32)
    p_n = np.empty((1, 2, 4, 256), f32)
    p_m = np.empty((1, 2, 4), f32)
    p_conv = np.empty((1, 2, 2, DFF), f32)
    s_k = np.empty((1, 16, 64, 8, 128), f32)
    s_v = np.empty((1, 16, 64, 8, 128), f32)
    s_C = np.empty((1, 16, 4, 256, 256), f32)
    s_n = np.empty((1, 16, 4, 256), f32)
    s_m = np.empty((1, 16, 4), f32)
    s_conv = np.empty((1, 16, 2, DFF), f32)
    for r in range(8):
        b, j = r // 4, r % 4
        o = R_[r]
        ym = np.asarray(o["y_main"])
        y_p[b, j * OWN:(j + 1) * OWN] = ym[:OWN]
        y_s[2 * r:2 * r + 2] = ym[OWN:].reshape(2, 64, D)
        ko, vo = np.asarray(o["k_out"]), np.asarray(o["v_out"])
        s_k[0, 2 * r:2 * r + 2] = ko[512:640].reshape(2, 64, 8, 128)
        s_v[0, 2 * r:2 * r + 2] = vo[512:640].reshape(2, 64, 8, 128)
        Co, no, mo = np.asarray(o["C_out"]), np.asarray(o["n_out"]), np.asarray(o["m_out"])
        s_C[0, 2 * r:2 * r + 2] = Co[1:3]
        s_n[0, 2 * r:2 * r + 2] = no[1:3]
        s_m[0, 2 * r:2 * r + 2] = mo[:, 1:3].T
        co = np.asarray(o["conv_out"]).reshape(3, 2, DFF)
        s_conv[0, 2 * r:2 * r + 2] = co[1:3]
        if j == 3:
            p_k[0, b] = ko[0:512].reshape(512, 8, 128)
            p_v[0, b] = vo[0:512].reshape(512, 8, 128)
            p_C[0, b] = Co[0]
            p_n[0, b] = no[0]
            p_m[0, b] = mo[:, 0]
            p_conv[0, b] = co[0]
    return (y_p, y_s, p_k, p_v, p_C, p_n, p_m, p_conv, s_k, s_v, s_C, s_n, s_m, s_conv)
```

```python
import contextlib
import numpy as np
import concourse.bass as bass
import concourse.mybir as mybir
from concourse.bass_utils import run_bass_kernel_spmd

F32 = mybir.dt.float32
BF16 = mybir.dt.bfloat16
AF = mybir.ActivationFunctionType
ALU = mybir.AluOpType
AX = mybir.AxisListType

D = 2048
KT = 16
DFF = 5632
FT = 44
INC = 7176
EPS = 1e-6
NEG = -30000.0


class Prog:
    ENG = ("pe", "act", "dve", "pool", "sp")

    def __init__(self, nc, n_dma=24):
        self.nc = nc
        self.q = {e: [] for e in self.ENG}
        self.cnt = {e: 0 for e in self.ENG}
        self.waited = {}
        self.lastw = {}
        self.readers = {}
        self.n_dma = n_dma
        self.dma_val = {}
        self.dma_rr = {e: 0 for e in self.ENG}
        self.semh = {}

    def _deps(self, eng, reads, writes):
        deps = []
        for k in reads:
            deps += self.lastw.get(k, [])
        for k in writes:
            deps += self.lastw.get(k, [])
            deps += self.readers.get(k, [])
        waits = []
        for semkey, val in deps:
            if eng == "pe" and semkey == ("e", "pe"):
                continue
            if self.waited.get((eng, semkey), 0) >= val:
                continue
            self.waited[(eng, semkey)] = val
            waits.append((semkey, val))
        return waits

    def _record(self, tok, reads, writes):
        for k in writes:
            self.lastw[k] = [tok]
            self.readers[k] = []
        for k in reads:
            self.readers.setdefault(k, []).append(tok)

    def op(self, eng, fn, reads=(), writes=(), signal=True):
        ex = [k for k in reads if k.startswith("ps")]
        if ex:
            reads = [k for k in reads if not k.startswith("ps")]
            writes = list(writes) + ex
        waits = self._deps(eng, reads, writes)
        if signal:
            self.cnt[eng] += 1
            tok = (("e", eng), self.cnt[eng])
        else:
            tok = (("e", eng), self.cnt[eng] + 1)
        self.q[eng].append((waits, fn, signal))
        self._record(tok, reads, writes)
        return tok

    def dma(self, qeng, out, in_, reads=(), writes=(), custom=None, inc=16):
        waits = self._deps(qeng, reads, writes)
        if custom is not None:
            semkey = ("d", qeng, "cc")
        else:
            idx = self.dma_rr[qeng] % self.n_dma
            self.dma_rr[qeng] += 1
            semkey = ("d", qeng, idx)
        v = self.dma_val.get(semkey, 0)
        if v > 0 and self.waited.get((qeng, semkey), 0) < v:
            self.waited[(qeng, semkey)] = v
            waits.append((semkey, v))
        self.dma_val[semkey] = v + inc
        tok = (semkey, v + inc)

        def fn(e, out=out, in_=in_, semkey=semkey, custom=custom):
            if custom is not None:
                return custom(e).then_inc(self.semh[semkey], inc)
            return e.dma_start(out=out, in_=in_).then_inc(self.semh[semkey], 16)

        self.q[qeng].append((waits, fn, False))
        self._record(tok, reads, writes)
        return tok

    def barrier(self):
        for eng in self.ENG:
            waits = []
            for semkey, v in self.dma_val.items():
                if self.waited.get((eng, semkey), 0) < v:
                    self.waited[(eng, semkey)] = v
                    waits.append((semkey, v))
            for e in self.ENG:
                if e == eng or self.cnt[e] == 0:
                    continue
                semkey = ("e", e)
                if self.waited.get((eng, semkey), 0) < self.cnt[e]:
                    self.waited[(eng, semkey)] = self.cnt[e]
                    waits.append((semkey, self.cnt[e]))
            if waits:
                self.q[eng].append((waits, None, False))
        self.lastw = {}
        self.readers = {}

    def run(self):
        nc = self.nc
        with contextlib.ExitStack() as st:
            for e in self.ENG:
                self.semh[("e", e)] = st.enter_context(nc.semaphore("s_" + e))
            for qe in self.ENG:
                for i in range(min(self.dma_rr[qe], self.n_dma)):
                    self.semh[("d", qe, i)] = st.enter_context(nc.semaphore("d_%s_%d" % (qe, i)))
            for semkey in self.dma_val:
                if semkey not in self.semh:
                    self.semh[semkey] = st.enter_context(nc.semaphore("d_%s_%s" % (semkey[1], semkey[2])))
            block = st.enter_context(nc.Block())

            def runq(ename, e):
                for waits, fn, signal in self.q[ename]:
                    for semkey, val in waits:
                        e.wait_ge(self.semh[semkey], val)
                    if fn is None:
                        continue
                    ins = fn(e)
                    if signal:
                        ins.then_inc(self.semh[("e", ename)], 1)

            @block.tensor
            def _(e):
                runq("pe", e)

            @block.scalar
            def _(e):
                runq("act", e)

            @block.vector
            def _(e):
                runq("dve", e)

            @block.gpsimd
            def _(e):
                runq("pool", e)

            @block.sync
            def _(e):
                runq("sp", e)


def MM(out, lhsT, rhs, start=True, stop=True):
    return lambda e: e.matmul(out, lhsT=lhsT, rhs=rhs, start=start, stop=stop)


def TR(out, in_, ident):
    return lambda e: e.transpose(out=out, in_=in_, identity=ident)


def ACT(out, in_, func, **kw):
    return lambda e: e.activation(out=out, in_=in_, func=func, **kw)


def TS(out, in0, s1, s2=None, op0=ALU.mult, op1=None):
    if op1 is None:
        return lambda e: e.tensor_scalar(out=out, in0=in0, scalar1=s1, scalar2=None, op0=op0)
    return lambda e: e.tensor_scalar(out=out, in0=in0, scalar1=s1, scalar2=s2, op0=op0, op1=op1)


def TT(out, in0, in1, op):
    return lambda e: e.tensor_tensor(out=out, in0=in0, in1=in1, op=op)


def STT(out, in0, scalar, in1, op0, op1):
    return lambda e: e.scalar_tensor_tensor(out=out, in0=in0, scalar=scalar, in1=in1, op0=op0, op1=op1)


def CP(out, in_):
    return lambda e: e.tensor_copy(out=out, in_=in_)


def MSET(ap, v):
    return lambda e: e.memset(ap, v)


class Arena:
    def __init__(self, nc, nbytes):
        self.t = nc.alloc_sbuf_tensor("arena", [128, nbytes // 4], F32)
        self.nbytes = nbytes
        self.off = 0

    def mark(self):
        return self.off

    def reset(self, m):
        self.off = m

    def alloc(self, free_shape, dtype):
        n = int(np.prod(free_shape))
        sz = 4 if dtype == F32 else 2
        nb = (n * sz + 63) // 64 * 64
        assert self.off + nb <= self.nbytes, ("SBUF arena overflow", self.off, nb, self.nbytes)
        ap = self.t[:, self.off // 4:(self.off + nb) // 4]
        self.off += nb
        if dtype != F32:
            ap = ap.bitcast(dtype)
        ap = ap[:, 0:n]
        if len(free_shape) == 2:
            ap = ap.rearrange("p (a b) -> p a b", a=free_shape[0])
        elif len(free_shape) == 3:
            ap = ap.rearrange("p (a b c) -> p a b c", a=free_shape[0], b=free_shape[1])
        elif len(free_shape) == 4:
            ap = ap.rearrange("p (a b c d) -> p a b c d", a=free_shape[0], b=free_shape[1], c=free_shape[2])
        return ap


def build(SEQ, last_stage=6, debug=False):
    OWN = SEQ // 4
    NOWN = OWN // 128
    NPRE = 4
    NM = NOWN + 2
    NK = NM + 4
    TM = NM * 128
    TK = NK * 128
    NOUT = NOWN + 1
    assert NOWN >= 4 or NOWN == 2

    nc = bass.Bass("TRN2", target_bir_lowering=False)
    P = Prog(nc)

    def din(name, shape, dt=F32):
        return nc.dram_tensor(name, list(shape), dt, kind="ExternalInput").ap()

    def dout(name, shape, dt=F32):
        return nc.dram_tensor(name, list(shape), dt, kind="ExternalOutput").ap()

    def dscr(name, shape, dt):
        if debug:
            return nc.dram_tensor(name, list(shape), dt, kind="ExternalOutput").ap()
        return nc.dram_tensor(name, list(shape), dt).ap()

    x_main = din("x_main", [TM, D])
    x_pre = din("x_pre", [NPRE * 128, D])
    masks = din("masks", [128, 3, NPRE + NM])
    fmask = din("fmask", [128, 2, 8])
    c3T = din("c3T", [128, KT, 3])
    gpreT = din("gpreT", [128, 2, KT])
    adabT = din("adabT", [128, 96])
    adab3 = din("adab3", [3, 12288])
    gpost3 = din("gpost3", [3, 2, D])
    mnormB = din("mnormB", [128, 1024])
    bgate = din("bgate", [128, 8])
    cwT = din("cwT", [128, FT, 4])
    bias_tab = din("bias_tab", [128, 8, 5, 128])
    bias_s4 = din("bias_s4", [128, 8, 128])
    consts = din("consts", [128, 8, 128])
    cache_k = din("cache_k", [2, 512, 1024])
    cache_v = din("cache_v", [2, 512, 1024])
    st_C = din("st_C", [2, 4, 256, 256])
    st_n = din("st_n", [2, 4, 256])
    st_mB = din("st_mB", [128, 2, 4])
    st_m4 = din("st_m4", [4, 2])
    conv_st = din("conv_st", [4, 5632])
    ada_w = din("ada_w", [D, 12288])
    w_in = din("w_in", [D, INC])
    w_out = din("w_out", [D, D])
    w_g = din("w_g", [D, DFF])
    w_u = din("w_u", [D, DFF])
    w_d = din("w_d", [DFF, D])

    y_main = dout("y_main", [NOUT * 128, D])
    k_out = dout("k_out", [5 * 128, 1024])
    v_out = dout("v_out", [5 * 128, 1024])
    C_out = dout("C_out", [3, 4, 256, 256])
    n_out = dout("n_out", [3, 4, 256])
    m_out = dout("m_out", [4, 3])
    conv_out = dout("conv_out", [6, 5632])

    scr_fm = dscr("scr_fm", [32, 128, TK], BF16)
    scr_va = dscr("scr_va", [NK, 128, 8 * 129], BF16)
    scr_tm = dscr("scr_tm", [NM, 128, 3072], BF16)
    scr_g = dscr("scr_g", [NM, 128, 8], F32)
    scr_gtg = nc.dram_tensor("scr_gtg", [3, 2, D], F32).ap()
    cc_in = nc.dram_tensor("cc_in", [128, 2064], F32).ap()
    cc_out = nc.dram_tensor("cc_out", [1024, 2064], F32).ap()
    scr_mixT = dscr("scr_mixT", [NM, 128, KT * 128], BF16)
    scr_x1 = dscr("scr_x1", [NM, 128, D], F32)
    scr_h2T = dscr("scr_h2T", [128, KT, TM], BF16)
    scr_aT = dscr("scr_aT", [FT, 128, TM], BF16)
    dbg = dscr("dbg", [128, 4096], F32) if debug else None

    A = Arena(nc, 206 * 1024)
    ps = [nc.alloc_psum_tensor("ps%d" % b, [128, 512], F32) for b in range(8)]

    def psf(b):
        return ps[b][:]

    def psb(b):
        return ps[b][:].bitcast(BF16)

    PK = ["ps%d" % b for b in range(8)]

    cst = A.alloc([8, 128], F32)
    identf = cst[:, 0, :]
    tri = cst[:, 1, :]
    ind = cst[:, 2:4, :]
    sel = cst[0:3, 4:6, :]
    ind2 = cst[:, 6, 0:2]
    ones4 = cst[0:4, 7, :]
    identb = A.alloc([128], BF16)
    msk = A.alloc([3, NPRE + NM], F32)
    modT = A.alloc([4, KT, 3], F32)
    bgt = A.alloc([8], F32)
    epsc = A.alloc([1], F32)
    onec = A.alloc([1], F32)
    P.dma("sp", cst, consts, writes=["cst"])
    P.dma("sp", msk, masks, writes=["msk"])
    P.dma("sp", bgt, bgate, writes=["bgt"])
    P.op("dve", MSET(epsc, EPS), writes=["epsc"])
    P.op("dve", MSET(onec, 1.0), writes=["onec"])
    P.op("dve", CP(identb, identf), reads=["cst"], writes=["identb"])
    siluT = A.alloc([KT, 3], BF16)
    gpre = A.alloc([2, KT], F32)
    adab = A.alloc([96], F32)
    persist_mark = A.mark()

    c3s = A.alloc([KT, 3], F32)
    ablk = [A.alloc([KT, 512], BF16) for _ in range(3)]
    P.dma("sp", c3s, c3T, writes=["c3s"])
    P.dma("sp", gpre, gpreT, writes=["gpre"])
    P.dma("sp", adab, adabT, writes=["adab"])
    P.op("act", ACT(siluT, c3s, AF.Silu), reads=["c3s"], writes=["siluT"])
    nblk = 0
    for v, slot in ((0, 0), (1, 1)):
        for cb4 in range(4):
            bi = nblk % 3
            nblk += 1
            c0 = v * D + cb4 * 512
            P.dma("pool", ablk[bi], ada_w[:, c0:c0 + 512].rearrange("(k p) n -> p k n", p=128), writes=["ablk%d" % bi])
            for sub in range(4):
                kc = cb4 * 4 + sub
                b = kc % 4
                for k in range(KT):
                    P.op("pe", MM(psf(b)[:, 0:3], ablk[bi][:, k, sub * 128:(sub + 1) * 128], siluT[:, k, :], k == 0, k == KT - 1),
                         reads=["ablk%d" % bi, "siluT"], writes=[PK[b]], signal=(k == KT - 1))
                P.op("dve", TS(modT[:, slot, kc, :], psf(b)[:, 0:3], adab[:, v * 16 + kc:v * 16 + kc + 1], None, ALU.add),
                     reads=[PK[b], "adab"], writes=["modT"])
    P.op("dve", STT(modT[:, 1], modT[:, 1], 1.0, gpre[:, 0, :].unsqueeze(2).broadcast_to([128, KT, 3]), ALU.add, ALU.mult),
         reads=["modT", "gpre"], writes=["modT"])

    def ada_rest():
        ab = [A.alloc([KT, 256], BF16) for _ in range(2)]
        t3 = [A.alloc([256], F32) for _ in range(3)]
        n = 0
        for v, slot in ((3, 2), (4, 3)):
            for cb8 in range(8):
                bi = n % 2
                n += 1
                c0 = v * D + cb8 * 256
                P.dma("pool", ab[bi], ada_w[:, c0:c0 + 256].rearrange("(k p) n -> p k n", p=128), writes=["adab_%d" % bi])
                for sub in range(2):
                    kc = cb8 * 2 + sub
                    b = kc % 2
                    for k in range(KT):
                        P.op("pe", MM(psf(b)[:, 0:3], ab[bi][:, k, sub * 128:(sub + 1) * 128], siluT[:, k, :], k == 0, k == KT - 1),
                             reads=["adab_%d" % bi, "siluT"], writes=[PK[b]], signal=(k == KT - 1))
                    P.op("dve", TS(modT[:, slot, kc, :], psf(b)[:, 0:3], adab[:, v * 16 + kc:v * 16 + kc + 1], None, ALU.add),
                         reads=[PK[b], "adab"], writes=["modT"])
                yield
        P.op("dve", STT(modT[:, 3], modT[:, 3], 1.0, gpre[:, 1, :].unsqueeze(2).broadcast_to([128, KT, 3]), ALU.add, ALU.mult),
             reads=["modT", "gpre"], writes=["modT"])
        for gi, v in ((0, 2), (1, 5)):
            for cb8 in range(8):
                bi = n % 2
                n += 1
                c0 = v * D + cb8 * 256
                P.dma("pool", ab[bi], ada_w[:, c0:c0 + 256].rearrange("(k p) n -> p k n", p=128), writes=["adab_%d" % bi])
                P.dma("sp", t3[0][0:3, :], adab3[:, c0:c0 + 256], writes=["t3a"])
                P.dma("sp", t3[1][0:3, :], gpost3[:, gi, cb8 * 256:(cb8 + 1) * 256], writes=["t3b"])
                b = n % 2
                for k in range(KT):
                    P.op("pe", MM(psf(b)[0:3, 0:256], siluT[:, k, :], ab[bi][:, k, :], k == 0, k == KT - 1),
                         reads=["adab_%d" % bi, "siluT"], writes=[PK[b]], signal=(k == KT - 1))
                P.op("dve", TT(t3[2][0:3, :], psf(b)[0:3, 0:256], t3[0][0:3, :], ALU.add), reads=[PK[b], "t3a"], writes=["t3c"])
                P.op("dve", TT(t3[2][0:3, :], t3[2][0:3, :], t3[1][0:3, :], ALU.mult), reads=["t3c", "t3b"], writes=["t3c"])
                P.dma("sp", scr_gtg[:, gi, cb8 * 256:(cb8 + 1) * 256], t3[2][0:3, :], reads=["t3c"], writes=["scr_gtg"])
                yield

    def build_gtgB(dst_p, dst_s, gi, keyp, keys_):
        tmp = A.alloc([D], F32)
        P.dma("sp", tmp[0:3, :], scr_gtg[:, gi, :], writes=["gtgtmp"])
        for vi, dst, key in ((0, dst_p, keyp), (1, dst_s, keys_)):
            for cb4 in range(4):
                b = cb4
                P.op("pe", MM(psf(b), sel[:, vi, :], tmp[0:3, cb4 * 512:(cb4 + 1) * 512]),
                     reads=["cst", "gtgtmp"], writes=[PK[b]])
                P.op("act", ACT(dst[:, cb4 * 512:(cb4 + 1) * 512], psf(b), AF.Copy), reads=[PK[b]], writes=[key])

    P.barrier()
    A.reset(persist_mark)

    def rstd_from_ss(ss_ap, out_ap, n, keys_in, key_out):
        P.op("act", ACT(out_ap, ss_ap, AF.Ln, scale=1.0 / n, bias=epsc[:, 0:1]), reads=keys_in + ["epsc"], writes=[key_out])
        P.op("act", ACT(out_ap, out_ap, AF.Exp, scale=-0.5), reads=[key_out], writes=[key_out])

    fe = {}

    def fe_alloc(with_xs=True, nxs=2):
        if with_xs:
            fe["xs"] = [A.alloc([D], F32) for _ in range(nxs)]
        fe["xn"] = [A.alloc([D], BF16) for _ in range(2)]
        fe["junk"] = A.alloc([D], BF16)
        fe["ss"] = [A.alloc([2], F32) for _ in range(2)]
        fe["n"] = 0

    def front_end(x_src, dst, dst_key, slot_a, slot_sh, rows, tb=(0, 1), xs_in=None, xs_keys=None):
        i = fe["n"] % 2
        fe["n"] += 1
        xn, ss = fe["xn"][i], fe["ss"][i]
        kx, kn, ks = "fe_xs%d" % i, "fe_xn%d" % i, "fe_ss%d" % i
        if xs_in is None:
            xs = fe["xs"][i % len(fe["xs"])]
            kx = "fe_xs%d" % (i % len(fe["xs"]))
            P.dma("sp", xs, x_src, writes=[kx])
            kxl = [kx]
        else:
            xs, kxl = xs_in, list(xs_keys)
        P.op("act", ACT(fe["junk"], xs, AF.Square, accum_out=ss[:, 0:1]), reads=kxl, writes=["fe_junk", ks])
        rstd_from_ss(ss[:, 0:1], ss[:, 1:2], D, [ks], ks)
        P.op("dve", TS(xn, xs, ss[:, 1:2]), reads=kxl + [ks], writes=[kn])
        for half in range(2):
            b = tb[half]
            pt = psb(b).rearrange("p (k t) -> p k t", k=8)
            for k8 in range(8):
                k = half * 8 + k8
                P.op("pe", TR(pt[:, k8, :], xn[:, k * 128:(k + 1) * 128], identb), reads=[kn, "identb"], writes=[PK[b]],
                     signal=(k8 == 7))
            for k8 in range(8):
                k = half * 8 + k8
                eng = "act" if half == 0 else "dve"
                if rows[0] == rows[1]:
                    segs = [(0, 128, rows[0])]
                else:
                    segs = [(0, 64, rows[0]), (64, 128, rows[1])]
                for (t0, t1, r) in segs:
                    a_ap = modT[:, slot_a, k, r:r + 1]
                    s_ap = modT[:, slot_sh, k, r:r + 1]
                    if eng == "act":
                        P.op("act", ACT(dst[:, k, t0:t1], pt[:, k8, t0:t1], AF.Identity, scale=a_ap, bias=s_ap),
                             reads=[PK[b], "modT"], writes=["%s_k%d_%d" % (dst_key, k, t0)])
                    else:
                        P.op("dve", TS(dst[:, k, t0:t1], pt[:, k8, t0:t1], a_ap, s_ap, ALU.mult, ALU.add),
                             reads=[PK[b], "modT"], writes=["%s_k%d_%d" % (dst_key, k, t0)])

    gm = {}

    def gate_alloc():
        gm["gsb"] = [A.alloc([8], F32) for _ in range(2)]
        gm["lfm"] = [A.alloc([4], F32) for _ in range(2)]
        gm["pre"] = [A.alloc([16], F32) for _ in range(2)]
        gm["ex"] = [A.alloc([16], F32) for _ in range(2)]
        gm["gbf"] = [A.alloc([4], BF16) for _ in range(2)]
        gm["amax"] = [A.alloc([2], F32) for _ in range(2)]
        gm["B4"] = [A.alloc([2], F32) for _ in range(2)]
        gm["t4"] = A.alloc([2], F32)
        gm["n"] = 0

    def gate_math(g_src, g_keys, mcol, gb, gb2):
        i = gm["n"] % 2
        gm["n"] += 1
        gsb, lfm, pre, ex, gbf, amax, B4 = (gm[k][i] for k in ("gsb", "lfm", "pre", "ex", "gbf", "amax", "B4"))
        K = "gm%d_" % i
        P.op("dve", TT(gsb, g_src, bgt, ALU.add), reads=g_keys + ["bgt"], writes=[K + "gsb"])
        P.op("act", ACT(lfm, gsb[:, 4:8], AF.Exp, scale=-1.0), reads=[K + "gsb"], writes=[K + "lfm"])
        P.op("act", ACT(lfm, lfm, AF.Ln, bias=onec[:, 0:1]), reads=[K + "lfm", "onec"], writes=[K + "lfm"])
        P.op("dve", TS(lfm, lfm, msk[:, 1, mcol:mcol + 1]), reads=[K + "lfm", "msk"], writes=[K + "lfm"])
        g0 = psf(gb)
        P.op("pe", MM(g0[:, 0:4], tri, lfm), reads=["cst", K + "lfm"], writes=[PK[gb]], signal=False)
        P.op("pe", MM(g0[:, 4:8], ind[:, 0, :], lfm), reads=["cst", K + "lfm"], writes=[PK[gb]], signal=False)
        P.op("pe", MM(g0[:, 8:12], ind[:, 1, :], lfm), reads=["cst", K + "lfm"], writes=[PK[gb]], signal=False)
        P.op("pe", MM(g0[0:4, 12:14], lfm, ind2), reads=["cst", K + "lfm"], writes=[PK[gb]])
        P.op("dve", CP(pre[:, 4:16], g0[:, 0:12]), reads=[PK[gb]], writes=[K + "pre"])
        P.op("dve", CP(B4[0:4, :], g0[0:4, 12:14]), reads=[PK[gb]], writes=[K + "B4"])
        P.op("dve", TT(pre[:, 0:4], gsb[:, 0:4], pre[:, 4:8], ALU.subtract), reads=[K + "gsb", K + "pre"], writes=[K + "pre"])
        P.op("dve", TS(pre[:, 0:4], pre[:, 0:4], msk[:, 0, mcol:mcol + 1], msk[:, 2, mcol:mcol + 1], ALU.mult, ALU.add),
             reads=[K + "pre", "msk"], writes=[K + "pre"])
        P.op("act", ACT(ex, pre, AF.Exp), reads=[K + "pre"], writes=[K + "ex"])
        P.op("dve", CP(gbf, ex[:, 0:4]), reads=[K + "ex"], writes=[K + "gbf"])
        g1 = psf(gb2)
        P.op("pe", TR(g1[0:4, 128:256], pre[:, 0:4], identf), reads=[K + "pre", "cst"], writes=[PK[gb2]])
        P.op("dve", lambda e, o=amax[0:4, :], s=g1[0:4, 128:256].rearrange("p (c t) -> p c t", c=2):
             e.tensor_reduce(out=o, in_=s, axis=AX.X, op=ALU.max), reads=[PK[gb2]], writes=[K + "amax"])
        return dict(ex=ex, gbf=gbf, amax=amax, B4=B4, K=K, pre=pre)

    def m_update(mt, mkey, G, c):
        K = G["K"]
        t4 = gm["t4"]
        P.op("dve", TT(t4[0:4, 0:1], G["amax"][0:4, c:c + 1], G["B4"][0:4, c:c + 1], ALU.add),
             reads=[K + "amax", K + "B4"], writes=["t4"])
        P.op("dve", TT(mt, mt, G["B4"][0:4, c:c + 1], ALU.add), reads=[mkey, K + "B4"], writes=[mkey])
        P.op("dve", TT(mt, mt, t4[0:4, 0:1], ALU.max), reads=[mkey, "t4"], writes=[mkey])

    def state_update(Sf, Sf_key, nf, nf_key, DSb, G, c, h, km_ap, km_key, vp_ap, vp_key, Sb=None, Sb_key=None, nb=None, nb_key=None,
                     S_src=None, S_src_key=None, n_src=None, n_src_key=None):
        K = G["K"]
        if S_src is None:
            S_src, S_src_key, n_src, n_src_key = Sf, Sf_key, nf, nf_key
        r0, r1 = 64 * c, 64 * c + 64
        eB = G["ex"][:, 8 + 4 * c + h:8 + 4 * c + h + 1]
        DS = psf(DSb)
        for d in range(2):
            P.op("pe", MM(DS[:, d * 256:(d + 1) * 256], km_ap[r0:r1, h * 256 + d * 128:h * 256 + (d + 1) * 128], vp_ap[r0:r1, h, :]),
                 reads=[km_key, vp_key], writes=[PK[DSb]], signal=(d == 1))
        Sh = Sf[:, h].rearrange("p d e -> p (d e)")
        Ssh = S_src[:, h].rearrange("p d e -> p (d e)")
        P.op("act", ACT(Sh, Ssh, AF.Copy, scale=eB), reads=[S_src_key + str(h), K + "ex"], writes=[Sf_key + str(h)])
        P.op("dve", STT(Sh, DS, eB, Sh, ALU.mult, ALU.add), reads=[PK[DSb], K + "ex", Sf_key + str(h)], writes=[Sf_key + str(h)])
        if Sb is not None:
            P.op("act", ACT(Sb[:, h].rearrange("p d e -> p (d e)"), Sh, AF.Copy), reads=[Sf_key + str(h)], writes=[Sb_key + str(h)])

    def n_update(nf, nf_key, NBb, G, c, km_ap, km_key, nb=None, nb_key=None, n_src=None, n_src_key=None):
        K = G["K"]
        if n_src is None:
            n_src, n_src_key = nf, nf_key
        r0, r1 = 64 * c, 64 * c + 64
        NB = psf(NBb)
        for h in range(4):
            for d in range(2):
                P.op("pe", MM(NB[:, 256 + h * 2 + d:256 + h * 2 + d + 1], km_ap[r0:r1, h * 256 + d * 128:h * 256 + (d + 1) * 128],
                              G["gbf"][r0:r1, h:h + 1]),
                     reads=[km_key, K + "gbf"], writes=[PK[NBb]], signal=(h == 3 and d == 1))
        eB4 = G["ex"][:, 8 + 4 * c:12 + 4 * c].unsqueeze(2).broadcast_to([128, 4, 2])
        P.op("dve", TT(nf, n_src, NB[:, 256:264].rearrange("p (h d) -> p h d", h=4), ALU.add),
             reads=[n_src_key, PK[NBb]], writes=[nf_key])
        P.op("dve", TT(nf, nf, eB4, ALU.mult), reads=[nf_key, K + "ex"], writes=[nf_key])
        if nb is not None:
            P.op("dve", CP(nb, nf), reads=[nf_key], writes=[nb_key])

    base_mark = A.mark()
    SX = A.alloc([4, 2, 256], F32)
    nX = A.alloc([4, 2], F32)
    mP = A.alloc([2], F32)
    P.op("pool", MSET(SX, 0.0), writes=["SX%d" % h for h in range(4)])
    P.op("pool", MSET(nX, 0.0), writes=["nX"])
    P.op("pool", MSET(mP, 0.0), writes=["mP"])
    persist_mark = A.mark()

    hT_halo = A.alloc([KT, 512], BF16)
    s1_mark = A.mark()

    hT_main = A.alloc([KT, TM], BF16)

    def hTt(u):
        return hT_halo[:, :, u * 128:(u + 1) * 128] if u < 4 else hT_main[:, :, (u - 4) * 128:(u - 3) * 128]

    def hTr(k, t0, t1):
        return hT_halo[:, k, t0:t1] if t1 <= 512 else hT_main[:, k, t0 - 512:t1 - 512]

    wblk = [A.alloc([KT, 512], BF16) for _ in range(3)]
    wg8 = A.alloc([KT, 8], BF16)
    fmst = [A.alloc([TK], BF16) for _ in range(2)]
    tmst = [A.alloc([516], BF16) for _ in range(4)]
    f32st = [A.alloc([512], F32) for _ in range(3)]
    gst = [A.alloc([8], F32) for _ in range(2)]
    fe_mark = A.mark()
    fe_alloc(nxs=2)
    cnt = {"wb": 0, "bank": 0, "ev": 0, "fm": 0, "tm": 0, "f32": 0, "g": 0}

    def hkeys(u, k):
        if u == NK - 1:
            return ["hT_u%d_k%d_0" % (u, k), "hT_u%d_k%d_64" % (u, k)]
        return ["hT_u%d_k%d_0" % (u, k)]

    for u in range(4):
        front_end(x_pre[u * 128:(u + 1) * 128, :], hTt(u), "hT_u%d" % u, 1, 0, (0, 0), tb=(0, 1))
    for i in range(NM):
        u = 4 + i
        rows = (0, 0) if i < NM - 1 else (1, 2)
        front_end(x_main[i * 128:(i + 1) * 128, :], hTt(u), "hT_u%d" % u, 1, 0, rows, tb=(0, 1))

    P.barrier()
    A.reset(fe_mark)
    ada_gen = ada_rest()

    def ada_step(n=1):
        for _ in range(n):
            next(ada_gen, None)

    def load_wblk(col0):
        bi = cnt["wb"] % 3
        cnt["wb"] += 1
        P.dma("pool", wblk[bi], w_in[:, col0:col0 + 512].rearrange("(k p) n -> p k n", p=128), writes=["wblk%d" % bi])
        return wblk[bi], "wblk%d" % bi

    def next_bank():
        b = 2 + cnt["bank"] % 6
        cnt["bank"] += 1
        return b

    def ev_eng():
        cnt["ev"] += 1
        return "act" if cnt["ev"] % 2 == 0 else "dve"

    def evac(out, in_, keys_r, keys_w, scale=None, eng=None):
        eng = eng or ev_eng()
        if eng == "act":
            if scale is None:
                P.op("act", ACT(out, in_, AF.Copy), reads=keys_r, writes=keys_w)
            else:
                P.op("act", ACT(out, in_, AF.Copy, scale=scale), reads=keys_r, writes=keys_w)
        else:
            if scale is None:
                P.op("dve", CP(out, in_), reads=keys_r, writes=keys_w)
            else:
                P.op("dve", TS(out, in_, scale), reads=keys_r, writes=keys_w)

    fm_blocks = [(0, 128 ** -0.5, 4), (512, 128 ** -0.5, 4), (1024, None, 0), (1536, None, 0),
                 (3072, None, 4), (3584, None, 4), (4096, 0.0625, 4), (4608, 0.0625, 4)]
    for blk, (col0, scale, u_lo) in enumerate(fm_blocks):
        wb, wkey = load_wblk(col0)
        t_lo = u_lo * 128
        for sub in range(4):
            cb = blk * 4 + sub
            fi = cnt["fm"] % 2
            cnt["fm"] += 1
            stkeys = []
            for tbi, t0 in enumerate(range(t_lo, TK, 512)):
                t1 = min(t0 + 512, TK)
                b = next_bank()
                for k in range(KT):
                    rk = [wkey]
                    for u in range(t0 // 128, t1 // 128):
                        rk += hkeys(u, k)
                    P.op("pe", MM(psf(b)[:, 0:t1 - t0], wb[:, k, sub * 128:(sub + 1) * 128], hTr(k, t0, t1), k == 0, k == KT - 1),
                         reads=rk, writes=[PK[b]], signal=(k == KT - 1))
                key = "fmst%d_%d" % (fi, tbi)
                stkeys.append(key)
                evac(fmst[fi][:, t0:t1], psf(b)[:, 0:t1 - t0], [PK[b]], [key], scale)
            P.dma("sp", scr_fm[cb, :, t_lo:TK], fmst[fi][:, t_lo:TK], reads=stkeys, writes=["scr_fm"])
        ada_step(2)

    def out_index(i):
        if NOWN - 3 <= i <= NOWN:
            return i - (NOWN - 3)
        if i == NM - 1:
            return 4
        return None

    def mcol_of(u):
        return NPRE - 4 + u if u < 4 else NPRE + (u - 4)

    for wbi in range(2):
        ada_step(2)
        wb, wkey = load_wblk(2048 + wbi * 512)
        for u in range(NK):
            b = next_bank()
            for k in range(KT):
                P.op("pe", MM(psf(b), hTt(u)[:, k, :], wb[:, k, :], k == 0, k == KT - 1),
                     reads=[wkey] + hkeys(u, k), writes=[PK[b]], signal=(k == KT - 1))
            ti = cnt["tm"] % 4
            cnt["tm"] += 1
            st = tmst[ti].rearrange("p (h e) -> p h e", h=4)
            mc = mcol_of(u)
            P.op("dve", TS(st[:, :, 0:128], psf(b).rearrange("p (h e) -> p h e", h=4), msk[:, 0, mc:mc + 1]),
                 reads=[PK[b], "msk"], writes=["tmst%d" % ti])
            P.op("pool", CP(st[:, :, 128], msk[:, 0, mc:mc + 1].broadcast_to([128, 4])), reads=["msk"], writes=["tmst%dx" % ti])
            P.dma("sp", scr_va[u, :, wbi * 516:(wbi + 1) * 516], tmst[ti], reads=["tmst%d" % ti, "tmst%dx" % ti], writes=["scr_va"])
            oi = out_index(u - 4) if u >= 4 else None
            if oi is not None:
                fi = cnt["f32"] % 3
                cnt["f32"] += 1
                P.op("act", ACT(f32st[fi], psf(b), AF.Copy), reads=[PK[b]], writes=["f32st%d" % fi])
                P.dma("sp", v_out[oi * 128:(oi + 1) * 128, wbi * 512:(wbi + 1) * 512], f32st[fi], reads=["f32st%d" % fi], writes=["v_out"])
    for (col0, off, scale) in ((4096, 0, 0.0625), (5120, 1024, None), (6144, 2048, None)):
        for wbi in range(2):
            ada_step(2)
            wb, wkey = load_wblk(col0 + wbi * 512)
            for i in range(NM):
                u = 4 + i
                b = next_bank()
                for k in range(KT):
                    P.op("pe", MM(psf(b), hTt(u)[:, k, :], wb[:, k, :], k == 0, k == KT - 1),
                         reads=[wkey] + hkeys(u, k), writes=[PK[b]], signal=(k == KT - 1))
                ti = cnt["tm"] % 4
                cnt["tm"] += 1
                evac(tmst[ti][:, 0:512], psf(b), [PK[b]], ["tmst%d" % ti], scale)
                P.dma("sp", scr_tm[i, :, off + wbi * 512:off + (wbi + 1) * 512], tmst[ti][:, 0:512], reads=["tmst%d" % ti], writes=["scr_tm"])
    P.dma("pool", wg8, w_in[:, 7168:7176].rearrange("(k p) n -> p k n", p=128), writes=["wg8"])
    for i in range(NM):
        u = 4 + i
        b = next_bank()
        for k in range(KT):
            P.op("pe", MM(psf(b)[:, 0:8], hTt(u)[:, k, :], wg8[:, k, :], k == 0, k == KT - 1),
                 reads=["wg8"] + hkeys(u, k), writes=[PK[b]], signal=(k == KT - 1))
        gi = cnt["g"] % 2
        cnt["g"] += 1
        P.op("dve", CP(gst[gi], psf(b)[:, 0:8]), reads=[PK[b]], writes=["gst%d" % gi])
        P.dma("sp", scr_g[i], gst[gi], reads=["gst%d" % gi], writes=["scr_g"])
    for wbi in range(2):
        wb, wkey = load_wblk(1024 + wbi * 512)
        for i in range(NM):
            oi = out_index(i)
            if oi is None:
                continue
            u = 4 + i
            b = next_bank()
            for k in range(KT):
                P.op("pe", MM(psf(b), hTt(u)[:, k, :], wb[:, k, :], k == 0, k == KT - 1),
                     reads=[wkey] + hkeys(u, k), writes=[PK[b]], signal=(k == KT - 1))
            fi = cnt["f32"] % 3
            cnt["f32"] += 1
            P.op("act", ACT(f32st[fi], psf(b), AF.Copy), reads=[PK[b]], writes=["f32st%d" % fi])
            P.dma("sp", k_out[oi * 128:(oi + 1) * 128, wbi * 512:(wbi + 1) * 512], f32st[fi], reads=["f32st%d" % fi], writes=["k_out"])
    for _ in ada_gen:
        pass
    P.barrier()
    A.reset(persist_mark)
    if last_stage == 2:
        P.run()
        return nc

    gate_alloc()
    pay = A.alloc([2064], F32)
    Sl = pay[:, 0:2048].rearrange("p (h d e) -> p h d e", h=4, d=2)
    nl = pay[:, 2048:2056].rearrange("p (h d) -> p h d", h=4)
    Bt = pay[:, 2056:2060]
    ab = pay[:, 2060:2064]
    ml = A.alloc([2], F32)
    d4l = A.alloc([4], F32)
    fm_ = A.alloc([2, 8], F32)
    P.dma("sp", fm_, fmask, writes=["fm_"])
    P.op("pool", MSET(pay, 0.0), writes=["SL%d" % h for h in range(4)] + ["nL", "BtL", "abL"])
    P.op("pool", MSET(ml, -1e30), writes=["mL"])
    tbl = [A.alloc([2048], BF16) for _ in range(2)]
    gbl = [A.alloc([8], F32) for _ in range(2)]
    vpl = [A.alloc([4, 256], BF16) for _ in range(2)]
    payq = [A.alloc([2064], F32) for _ in range(2)]
    eq = A.alloc([8], F32)
    mB = A.alloc([4], F32)
    tq = A.alloc([4], F32)
    P.op("pool", MSET(mB, 0.0), writes=["mB"])

    def loadL(i):
        P.dma("sp", tbl[i % 2], scr_tm[i, :, 0:2048], writes=["tbl%d" % (i % 2)])
        P.dma("sp", gbl[i % 2], scr_g[i], writes=["gbl%d" % (i % 2)])

    loadL(0)
    for i in range(NOWN):
        i2 = i % 2
        if i + 1 < NOWN:
            loadL(i + 1)
        tk = "tbl%d" % i2
        km_tm = tbl[i2][:, 0:1024]
        vm_tm = tbl[i2][:, 1024:2048].rearrange("p (h e) -> p h e", h=4)
        G = gate_math(gbl[i2], ["gbl%d" % i2], NPRE + i, 6, 7)
        K = G["K"]
        P.op("dve", TT(vpl[i2], vm_tm, G["ex"][:, 0:4].unsqueeze(2).broadcast_to([128, 4, 256]), ALU.mult),
             reads=[tk, K + "ex"], writes=["vpl%d" % i2])
        P.op("dve", TT(Bt, Bt, G["pre"][:, 8:12], ALU.add), reads=["BtL", K + "pre"], writes=["BtL"])
        P.op("dve", TT(Bt, Bt, G["pre"][:, 12:16], ALU.add), reads=["BtL", K + "pre"], writes=["BtL"])
        for c in range(2):
            n_update(nl, "nL", 7, G, c, km_tm, tk)
            for h in range(4):
                state_update(Sl, "SL", None, None, h, G, c, h, km_tm, tk, vpl[i2], "vpl%d" % i2)
            m_update(ml[0:4, 0:1], "mL", G, c)
    P.op("dve", TS(d4l[0:4, 0:4], identf[0:4, 0:4], ml[0:4, 0:1]), reads=["mL", "cst"], writes=["d4l"])
    P.op("pe", MM(psf(6)[:, 0:4], ones4, d4l[0:4, 0:4]), reads=["cst", "d4l"], writes=[PK[6]])
    P.op("dve", CP(ab, psf(6)[:, 0:4]), reads=[PK[6]], writes=["abL"])
    P.dma("sp", cc_in, pay, reads=["SL%d" % h for h in range(4)] + ["nL", "BtL", "abL"], writes=["cc_in"])
    P.dma("pool", None, None, reads=["cc_in"], writes=["cc_out"], inc=1,
          custom=lambda e: e.collective_compute("AllGather", ALU.bypass, replica_groups=[list(range(8))], ins=[cc_in], outs=[cc_out]))
    for q in range(8):
        pq = payq[q % 2]
        pk = "payq%d" % (q % 2)
        P.dma("sp", pq, cc_out[q * 128:(q + 1) * 128, :], reads=["cc_out"], writes=[pk])
        act_q = fm_[:, 0, q:q + 1]
        P.op("dve", TS(eq[:, 0:4], pq[:, 2056:2060], act_q), reads=[pk, "fm_"], writes=["eq"])
        P.op("act", ACT(eq[:, 4:8], eq[:, 0:4], AF.Exp), reads=["eq"], writes=["eqe"])
        Sq = pq[:, 0:2048].rearrange("p (h de) -> p h de", h=4)
        for h in range(4):
            Sh = SX[:, h].rearrange("p d e -> p (d e)")
            P.op("act", ACT(Sh, Sh, AF.Copy, scale=eq[:, 4 + h:5 + h]), reads=["SX%d" % h, "eqe"], writes=["SX%d" % h])
            P.op("dve", STT(Sh, Sq[:, h, :], act_q, Sh, ALU.mult, ALU.add), reads=[pk, "fm_", "SX%d" % h], writes=["SX%d" % h])
        P.op("dve", TT(nX, nX, eq[:, 4:8].unsqueeze(2).broadcast_to([128, 4, 2]), ALU.mult), reads=["nX", "eqe"], writes=["nX"])
        P.op("dve", STT(nX, pq[:, 2048:2056].rearrange("p (h d) -> p h d", h=4), act_q, nX, ALU.mult, ALU.add), reads=[pk, "fm_", "nX"], writes=["nX"])
        P.op("dve", TT(mB, mB, eq[:, 0:4], ALU.add), reads=["mB", "eq"], writes=["mB"])
        P.op("dve", TS(tq, pq[:, 2060:2064], act_q, fm_[:, 1, q:q + 1], ALU.mult, ALU.add), reads=[pk, "fm_"], writes=["tq"])
        P.op("dve", TT(mB, mB, tq, ALU.max), reads=["mB", "tq"], writes=["mB"])
    P.op("pe", TR(psf(7)[0:4, 0:128], mB, identf), reads=["mB", "cst"], writes=[PK[7]])
    P.op("dve", CP(mP[0:4, 0:1], psf(7)[0:4, 0:1]), reads=[PK[7]], writes=["mP"])
    P.barrier()
    A.reset(persist_mark)

    gate_alloc()
    RS = 6
    mnb = A.alloc([1024], F32)
    btab = A.alloc([8, 5, 128], BF16)
    bs4 = A.alloc([8, 128], BF16)
    P.dma("sp", mnb, mnormB, writes=["mnb"])
    P.dma("pool", btab, bias_tab, writes=["btab"])
    P.dma("pool", bs4, bias_s4, writes=["bs4"])
    kring = [A.alloc([8, 128], BF16) for _ in range(RS)]
    vring = [A.alloc([8, 129], BF16) for _ in range(RS)]
    qbuf = [A.alloc([8, 128], BF16) for _ in range(2)]
    mbuf = [A.alloc([16, 128], BF16) for _ in range(2)]
    tbuf = [A.alloc([3072], BF16) for _ in range(2)]
    gbuf = [A.alloc([8], F32) for _ in range(2)]
    expT = [A.alloc([5, 128], BF16) for _ in range(2)]
    es0 = [A.alloc([4, 128], BF16) for _ in range(2)]
    es1 = [A.alloc([4, 128], BF16) for _ in range(2)]
    vp = [A.alloc([4, 256], BF16) for _ in range(2)]
    qz = [[A.alloc([8, 128], BF16) for _c in range(2)] for _ in range(2)]
    qkm = [A.alloc([4, 128], BF16) for _ in range(2)]
    mix = [A.alloc([2048], BF16) for _ in range(2)]
    mixTs = [A.alloc([KT, 128], BF16) for _ in range(2)]
    gsn = [A.alloc([1024], F32) for _ in range(2)]
    SY = A.alloc([4, 2, 256], F32)
    SXb = A.alloc([4, 2, 256], BF16)
    SYb = A.alloc([4, 2, 256], BF16)
    nY = A.alloc([4, 2], F32)
    nXb = A.alloc([4, 2], BF16)
    nYb = A.alloc([4, 2], BF16)
    rden = A.alloc([8], F32)
    dd = A.alloc([4], F32)
    ddn = A.alloc([4], F32)
    ssh = A.alloc([8], F32)
    scl = A.alloc([4], F32)
    hjunk = A.alloc([256], BF16)
    ms = A.alloc([2], F32)
    mcolt = A.alloc([4], F32)
    d4 = A.alloc([4], F32)
    emn = A.alloc([4], F32)
    em_in = A.alloc([2, 4], F32)
    Cst = A.alloc([4, 2, 256], F32)
    nst = A.alloc([4, 2], F32)
    nst8 = A.alloc([128], F32)
    nld = A.alloc([128], F32)
    kcT = [A.alloc([8, 512], BF16) for _ in range(2)]
    vc = [A.alloc([4, 8, 129], BF16) for _ in range(2)]
    kld = [A.alloc([1024], BF16) for _ in range(2)]
    SXK = ["SX%d" % h for h in range(4)]
    SYK = ["SY%d" % h for h in range(4)]
    for h in range(4):
        P.op("pool", CP(SXb[:, h], SX[:, h]), reads=["SX%d" % h], writes=["SXb%d" % h])
    P.op("pool", CP(nXb, nX), reads=["nX"], writes=["nXb"])
    for c in range(2):
        for i2 in range(2):
            P.op("pool", MSET(qz[i2][c], 0.0), writes=["qz%d_%d" % (i2, c)])
            P.op("pool", MSET(es0[i2], 0.0), writes=["es0_%d" % i2])
            P.op("pool", MSET(es1[i2], 0.0), writes=["es1_%d" % i2])

    def load_kv(u):
        s = u % RS
        P.dma("sp", kring[s], scr_fm[8:16, :, u * 128:(u + 1) * 128].rearrange("c p t -> p c t"), writes=["kring%d" % s])
        P.dma("sp", vring[s], scr_va[u].rearrange("p (h e) -> p h e", h=8), writes=["vring%d" % s])

    for u in range(4):
        load_kv(u)

    def emit_state(w, Sf, Skeys, nf, nkey, mt, mkey):
        P.op("dve", CP(mcolt[0:4, w:w + 1], mt), reads=[mkey], writes=["mcolt"])
        P.op("dve", TS(d4[0:4, 0:4], identf[0:4, 0:4], mt), reads=[mkey, "cst"], writes=["d4"])
        P.op("pe", MM(psf(1)[:, 0:4], ones4, d4[0:4, 0:4]), reads=["cst", "d4"], writes=[PK[1]])
        P.op("act", ACT(emn, psf(1)[:, 0:4], AF.Exp, scale=-1.0), reads=[PK[1]], writes=["emn"])
        for h in range(4):
            P.op("dve", TS(Cst[:, h].rearrange("p d e -> p (d e)"), Sf[:, h].rearrange("p d e -> p (d e)"), emn[:, h:h + 1]),
                 reads=[Skeys[h], "emn"], writes=["Cst"])
        P.dma("sp", C_out[w].rearrange("h (d p) e -> p h d e", p=128), Cst, reads=["Cst"], writes=["C_out"])
        P.op("dve", TT(nst, nf, emn.unsqueeze(2).broadcast_to([128, 4, 2]), ALU.mult), reads=[nkey, "emn"], writes=["nst"])
        P.op("pe", TR(psf(1)[0:8, 128:256], nst.rearrange("p h d -> p (h d)"), identf), reads=["nst", "cst"], writes=[PK[1]])
        P.op("dve", CP(nst8[0:8, :], psf(1)[0:8, 128:256]), reads=[PK[1]], writes=["nst8"])
        P.dma("sp", n_out[w].rearrange("h (d p) -> (h d) p", p=128), nst8[0:8, :], reads=["nst8"], writes=["n_out"])

    def load3(i):
        u = 4 + i
        i2 = i % 2
        load_kv(u)
        P.dma("sp", qbuf[i2], scr_fm[0:8, :, u * 128:(u + 1) * 128].rearrange("c p t -> p c t"), writes=["qbuf%d" % i2])
        P.dma("sp", mbuf[i2], scr_fm[16:32, :, u * 128:(u + 1) * 128].rearrange("c p t -> p c t"), writes=["mbuf%d" % i2])
        P.dma("sp", tbuf[i2], scr_tm[i], writes=["tbuf%d" % i2])
        P.dma("sp", gbuf[i2], scr_g[i], writes=["gbuf%d" % i2])

    GL = {}

    def views(i):
        i2 = i % 2
        return dict(i2=i2, u=4 + i, sample=(i == NM - 1), slot=(4 + i) % RS,
                    kmT=mbuf[i2][:, 8:16, :], qmT=mbuf[i2][:, 0:8, :], km_tm=tbuf[i2][:, 0:1024],
                    vm_tm=tbuf[i2][:, 1024:2048].rearrange("p (h e) -> p h e", h=4), om_tm=tbuf[i2][:, 2048:3072],
                    tk="tbuf%d" % i2, mk="mbuf%d" % i2, qk="qbuf%d" % i2)

    def sample_setup():
        for seq in range(2):
            for kt in range(4):
                li = (seq * 4 + kt) % 2
                P.dma("pool", kld[li], cache_k[seq, kt * 128:(kt + 1) * 128, :], writes=["kld%d" % li])
                for hh2 in range(2):
                    b = 4 + hh2
                    pt = psb(b).rearrange("p (k t) -> p k t", k=8)
                    for h4 in range(4):
                        h = hh2 * 4 + h4
                        P.op("pe", TR(pt[:, h4, :], kld[li][:, h * 128:(h + 1) * 128], identb), reads=["kld%d" % li, "identb"],
                             writes=[PK[b]], signal=(h4 == 3))
                    evac(kcT[seq][:, hh2 * 4:(hh2 + 1) * 4, kt * 128:(kt + 1) * 128], pt[:, 0:4, :], [PK[b]], ["kcT%d" % seq],
                         eng=("act" if hh2 == 0 else "dve"))
            for kt in range(4):
                P.dma("pool", vc[seq][:, kt, :, 0:128], cache_v[seq, kt * 128:(kt + 1) * 128, :].rearrange("p (h e) -> p h e", h=8),
                      writes=["vc%d_%d" % (seq, kt)])
            P.op("pool", MSET(vc[seq][:, :, :, 128], 1.0), writes=["vc%dx" % seq])
        P.dma("sp", em_in, st_mB, writes=["em_in"])
        P.op("act", ACT(em_in, em_in, AF.Exp), reads=["em_in"], writes=["em_in"])
        P.dma("sp", ms[0:4, 0:2], st_m4, writes=["ms0", "ms1"])
        for seq, (Sf, SK, Sb_, SbK, nf, nK, nb_, nbK) in enumerate(((SX, "SX", SXb, "SXb", nX, "nX", nXb, "nXb"),
                                                                     (SY, "SY", SYb, "SYb", nY, "nY", nYb, "nYb"))):
            P.dma("sp", Sf, st_C[seq].rearrange("h (d p) e -> p h d e", p=128), writes=[SK + str(h) for h in range(4)])
            P.dma("sp", nld[0:8, :], st_n[seq].rearrange("h (d p) -> (h d) p", p=128), writes=["nld"])
            P.op("pe", TR(psf(1)[:, 0:8], nld[0:8, :], identf[0:8, 0:8]), reads=["nld", "cst"], writes=[PK[1]])
            P.op("dve", TT(nf, psf(1)[:, 0:8].rearrange("p (h d) -> p h d", h=4), em_in[:, seq, :].unsqueeze(2).broadcast_to([128, 4, 2]), ALU.mult),
                 reads=[PK[1], "em_in"], writes=[nK])
            P.op("dve", CP(nb_, nf), reads=[nK], writes=[nbK])
            for h in range(4):
                P.op("dve", TS(Sf[:, h].rearrange("p d e -> p (d e)"), Sf[:, h].rearrange("p d e -> p (d e)"), em_in[:, seq, h:h + 1]),
                     reads=[SK + str(h), "em_in"], writes=[SK + str(h)])
                P.op("pool", CP(Sb_[:, h], Sf[:, h]), reads=[SK + str(h)], writes=[SbK + str(h)])

    def PA(i):
        V = views(i)
        i2 = V["i2"]
        G = gate_math(gbuf[i2], ["gbuf%d" % i2], NPRE + i, 1, 1)
        GL[i] = G
        K = G["K"]
        P.op("dve", TT(vp[i2], V["vm_tm"], G["ex"][:, 0:4].unsqueeze(2).broadcast_to([128, 4, 256]), ALU.mult),
             reads=[V["tk"], K + "ex"], writes=["vp%d" % i2])
        for c in range(2):
            P.op("pool", CP(qz[i2][c][:, :, 64 * c:64 * c + 64], V["qmT"][:, :, 64 * c:64 * c + 64]), reads=[V["mk"]], writes=["qz%d_%d" % (i2, c)])
        for h in range(4):
            for d in range(2):
                P.op("pe", MM(psf(0)[:, h * 128:(h + 1) * 128], V["kmT"][:, h * 2 + d, :], V["qmT"][:, h * 2 + d, :], d == 0, d == 1),
                     reads=[V["mk"]], writes=[PK[0]], signal=(h == 3 and d == 1))
        P.op("dve", TT(qkm[i2], psf(0).rearrange("p (h t) -> p h t", h=4), tri.unsqueeze(1).broadcast_to([128, 4, 128]), ALU.mult),
             reads=[PK[0], "cst"], writes=["qkm%d" % i2])
        P.op("act", ACT(gsn[i2], V["om_tm"], AF.Sigmoid), reads=[V["tk"]], writes=["gsn%d" % i2])
        P.op("pool", TT(gsn[i2], gsn[i2], mnb, ALU.mult), reads=["gsn%d" % i2, "mnb"], writes=["gsn%d" % i2])

    def PB(i):
        V = views(i)
        i2 = V["i2"]
        G = GL[i]
        n_update(nY, "nY", 1, G, 0, V["km_tm"], V["tk"], nb=nYb, nb_key="nYb", n_src=nX, n_src_key="nX")
        for h in range(4):
            state_update(SY, "SY", None, None, h % 2, G, 0, h, V["km_tm"], V["tk"], vp[i2], "vp%d" % i2, Sb=SYb, Sb_key="SYb",
                         S_src=SX, S_src_key="SX")

    def ATT(i, heads):
        V = views(i)
        i2, u, sample, slot, qk_ = V["i2"], V["u"], V["sample"], V["slot"], V["qk"]
        heads = list(heads)
        for h in heads:
            e = h % 2
            bx, by = (2, 3) if e == 0 else (4, 5)
            X, Y = psf(bx), psf(by)
            if not sample:
                for kt in range(5):
                    ks = (u - 4 + kt) % RS
                    o = X[:, kt * 128:(kt + 1) * 128] if kt < 4 else Y[:, 0:128]
                    bk = PK[bx] if kt < 4 else PK[by]
                    P.op("pe", MM(o, kring[ks][:, h, :], qbuf[i2][:, h, :], True, False), reads=["kring%d" % ks, qk_], writes=[bk], signal=False)
                    P.op("pe", MM(o, identb, btab[:, h, kt, :], False, True), reads=["identb", "btab"], writes=[bk], signal=(kt >= 3))
                P.op("act", ACT(expT[e][:, 0:4, :], X.rearrange("p (k t) -> p k t", k=4), AF.Exp), reads=[PK[bx]], writes=["expT%da" % e])
                P.op("act", ACT(expT[e][:, 4, :], Y[:, 0:128], AF.Exp), reads=[PK[by]], writes=["expT%db" % e])
            else:
                for kt in range(4):
                    for seq in range(2):
                        o = X[:, kt * 128 + seq * 64:kt * 128 + seq * 64 + 64]
                        P.op("pe", MM(o, kcT[seq][:, h, kt * 128:(kt + 1) * 128], qbuf[i2][:, h, seq * 64:seq * 64 + 64], True, False),
                             reads=["kcT%d" % seq, qk_], writes=[PK[bx]], signal=False)
                        P.op("pe", MM(o, identb, btab[:, h, kt, 0:64], False, True), reads=["identb", "btab"], writes=[PK[bx]],
                             signal=(kt == 3 and seq == 1))
                P.op("pe", MM(Y[:, 0:128], kring[slot][:, h, :], qbuf[i2][:, h, :], True, False), reads=["kring%d" % slot, qk_], writes=[PK[by]], signal=False)
                P.op("pe", MM(Y[:, 0:128], identb, bs4[:, h, :], False, True), reads=["identb", "bs4"], writes=[PK[by]])
                X3 = X.rearrange("p (k t) -> p k t", k=4)
                P.op("act", ACT(es0[e][:, :, 0:64], X3[:, :, 0:64], AF.Exp), reads=[PK[bx]], writes=["es0_%d" % e])
                P.op("act", ACT(es1[e][:, :, 64:128], X3[:, :, 64:128], AF.Exp), reads=[PK[bx]], writes=["es1_%d" % e])
                P.op("act", ACT(expT[e][:, 4, :], Y[:, 0:128], AF.Exp), reads=[PK[by]], writes=["expT%db" % e])
        for h in heads:
            e = h % 2
            g = h // 3
            pb = 6 + g % 2
            hh = h % 3
            PV = psf(pb)
            o = PV[:, hh * 129:(hh + 1) * 129]
            if not sample:
                for kt in range(5):
                    ks = (u - 4 + kt) % RS
                    P.op("pe", MM(o, expT[e][:, kt, :], vring[ks][:, h, :], kt == 0, kt == 4),
                         reads=["expT%da" % e if kt < 4 else "expT%db" % e, "vring%d" % ks], writes=[PK[pb]], signal=(kt == 4))
            else:
                for kt in range(4):
                    P.op("pe", MM(o, es0[e][:, kt, :], vc[0][:, kt, h, :], kt == 0, False), reads=["es0_%d" % e, "vc0_%d" % kt, "vc0x"], writes=[PK[pb]], signal=False)
                    P.op("pe", MM(o, es1[e][:, kt, :], vc[1][:, kt, h, :], False, False), reads=["es1_%d" % e, "vc1_%d" % kt, "vc1x"], writes=[PK[pb]], signal=False)
                P.op("pe", MM(o, expT[e][:, 4, :], vring[slot][:, h, :], False, True), reads=["expT%db" % e, "vring%d" % slot], writes=[PK[pb]])
            if h in (2, 5, 7):
                nh = hh + 1
                h0 = h - hh
                PV3 = PV[:, 0:nh * 129].rearrange("p (h e) -> p h e", h=nh)
                P.op("dve", TS(rden[:, 0:nh], PV3[:, :, 128], 1e-30, None, ALU.add), reads=[PK[pb]], writes=["rden"])
                P.op("dve", lambda e_, o_=rden[:, 0:nh]: e_.reciprocal(out=o_, in_=o_), reads=["rden"], writes=["rden"])
                P.op("dve", TT(mix[i2][:, h0 * 128:(h0 + nh) * 128].rearrange("p (h e) -> p h e", h=nh), PV3[:, :, 0:128],
                               rden[:, 0:nh].unsqueeze(2).broadcast_to([128, nh, 128]), ALU.mult),
                     reads=[PK[pb], "rden"], writes=["mix%d_a%d" % (i2, g)])

    def HD(i):
        V = views(i)
        i2, sample, tk, km_tm, om_tm = V["i2"], V["sample"], V["tk"], V["km_tm"], V["om_tm"]
        G = GL[i]
        K = G["K"]
        for h in range(4):
            hb = h // 2
            Hh = psf(hb)[:, (h % 2) * 256:(h % 2 + 1) * 256]
            P.op("pe", MM(Hh, qkm[i2][:, h, :], vp[i2][:, h, :], True, False), reads=["qkm%d" % i2, "vp%d" % i2], writes=[PK[hb]], signal=False)
            for c, (Sb_, SbK) in enumerate(((SXb, "SXb"), (SYb, "SYb"))):
                for d in range(2):
                    P.op("pe", MM(Hh, qz[i2][c][:, h * 2 + d, :], Sb_[:, h, d, :], False, c == 1 and d == 1),
                         reads=["qz%d_%d" % (i2, c), SbK + str(h)], writes=[PK[hb]], signal=(c == 1 and d == 1))
        DEN = psf(7)[:, 400:404]
        for h in range(4):
            o = DEN[:, h:h + 1]
            P.op("pe", MM(o, qkm[i2][:, h, :], G["gbf"][:, h:h + 1], True, False), reads=["qkm%d" % i2, K + "gbf"], writes=[PK[7]], signal=False)
            for c, (nb_, nbK) in enumerate(((nXb, "nXb"), (nYb, "nYb"))):
                for d in range(2):
                    P.op("pe", MM(o, qz[i2][c][:, h * 2 + d, :], nb_[:, h, d:d + 1], False, c == 1 and d == 1),
                         reads=["qz%d_%d" % (i2, c), nbK], writes=[PK[7]], signal=(h == 3 and c == 1 and d == 1))
        eb = G["ex"][:, 4:8]
        P.op("dve", TT(dd, DEN, eb, ALU.mult), reads=[PK[7], K + "ex"], writes=["dd"])
        P.op("dve", TS(ddn, dd, -1.0), reads=["dd"], writes=["ddn"])
        P.op("dve", TT(dd, dd, ddn, ALU.max), reads=["dd", "ddn"], writes=["dd"])
        P.op("dve", TS(dd, dd, 1.0, None, ALU.max), reads=["dd"], writes=["dd"])
        P.op("dve", lambda e_, o_=dd, i_=dd: e_.reciprocal(out=o_, in_=i_), reads=["dd"], writes=["dd"])
        P.op("dve", TT(dd, dd, eb, ALU.mult), reads=["dd", K + "ex"], writes=["dd"])
        for h in range(4):
            hb = h // 2
            Hh = psf(hb)[:, (h % 2) * 256:(h % 2 + 1) * 256]
            P.op("act", ACT(hjunk, Hh, AF.Square, scale=dd[:, h:h + 1], accum_out=ssh[:, h:h + 1]), reads=[PK[hb], "dd"], writes=["hjunk", "ssh%d" % h])
        rstd_from_ss(ssh[:, 0:4], ssh[:, 4:8], 256, ["ssh%d" % h for h in range(4)], "sshr")
        P.op("dve", TT(scl, dd, ssh[:, 4:8], ALU.mult), reads=["dd", "sshr"], writes=["scl"])
        for h in range(4):
            hb = h // 2
            Hh = psf(hb)[:, (h % 2) * 256:(h % 2 + 1) * 256]
            P.op("dve", STT(mix[i2][:, 1024 + h * 256:1024 + (h + 1) * 256], Hh, scl[:, h:h + 1], gsn[i2][:, h * 256:(h + 1) * 256], ALU.mult, ALU.mult),
                 reads=[PK[hb], "scl", "gsn%d" % i2], writes=["mix%d_m%d" % (i2, h)])

    def SC1(i):
        V = views(i)
        i2, sample, tk, km_tm = V["i2"], V["sample"], V["tk"], V["km_tm"]
        G = GL[i]
        if not sample:
            n_update(nX, "nX", 1, G, 1, km_tm, tk, nb=nXb, nb_key="nXb", n_src=nY, n_src_key="nY")
            for h in range(4):
                state_update(SX, "SX", None, None, h % 2, G, 1, h, km_tm, tk, vp[i2], "vp%d" % i2, Sb=SXb, Sb_key="SXb",
                             S_src=SY, S_src_key="SY")
            m_update(mP[0:4, 0:1], "mP", G, 0)
            m_update(mP[0:4, 0:1], "mP", G, 1)
            if i == NOWN:
                emit_state(0, SX, SXK, nX, "nX", mP[0:4, 0:1], "mP")
        else:
            n_update(nX, "nX", 1, G, 0, km_tm, tk)
            n_update(nY, "nY", 1, G, 1, km_tm, tk)
            for h in range(4):
                state_update(SX, "SX", None, None, h % 2, G, 0, h, km_tm, tk, vp[i2], "vp%d" % i2)
            for h in range(4):
                state_update(SY, "SY", None, None, h % 2, G, 1, h, km_tm, tk, vp[i2], "vp%d" % i2)
            m_update(ms[0:4, 0:1], "ms0", G, 0)
            m_update(ms[0:4, 1:2], "ms1", G, 1)
            emit_state(1, SX, SXK, nX, "nX", ms[0:4, 0:1], "ms0")
            emit_state(2, SY, SYK, nY, "nY", ms[0:4, 1:2], "ms1")

    def TRS(i):
        i2 = i % 2
        mixkeys = ["mix%d_a%d" % (i2, g) for g in range(3)] + ["mix%d_m%d" % (i2, h) for h in range(4)]
        for half in range(2):
            b = 4 + half
            pt = psb(b).rearrange("p (k t) -> p k t", k=8)
            for k8 in range(8):
                k = half * 8 + k8
                P.op("pe", TR(pt[:, k8, :], mix[i2][:, k * 128:(k + 1) * 128], identb), reads=mixkeys + ["identb"], writes=[PK[b]], signal=(k8 == 7))
            evac(mixTs[i2][:, half * 8:(half + 1) * 8, :], pt, [PK[b]], ["mixTs%d_%d" % (i2, half)], eng=("act" if half == 0 else "dve"))
        P.dma("act", scr_mixT[i], mixTs[i2].rearrange("p k t -> p (k t)"), reads=["mixTs%d_0" % i2, "mixTs%d_1" % i2], writes=["scr_mixT"])

    load3(0)
    PA(0)
    for i in range(NM):
        sample_i = (i == NM - 1)
        if i + 1 < NM:
            load3(i + 1)
        ATT(i, [0, 1])
        if not sample_i:
            PB(i)
        if i > 0:
            TRS(i - 1)
        ATT(i, [2, 3])
        if i + 1 < NM:
            PA(i + 1)
        ATT(i, [4, 5])
        HD(i)
        ATT(i, [6, 7])
        SC1(i)
        if i + 1 == NM - 1:
            sample_setup()
    TRS(NM - 1)
    P.dma("sp", m_out, mcolt[0:4, 0:3], reads=["mcolt"], writes=["m_out"])
    P.barrier()
    A.reset(base_mark)
    if last_stage == 3:
        P.run()
        return nc


    wo = A.alloc([KT, D], BF16)
    for q in range(4):
        P.dma("pool", wo[:, :, q * 512:(q + 1) * 512], w_out[:, q * 512:(q + 1) * 512].rearrange("(k p) n -> p k n", p=128), writes=["wo%d" % q])
    gB1 = [A.alloc([D], F32) for _ in range(2)]
    build_gtgB(gB1[0], gB1[1], 0, "gB1p", "gB1s")
    fe_alloc(with_xs=False)
    xs4 = [A.alloc([D], F32) for _ in range(2)]
    x1s = [A.alloc([D], F32) for _ in range(2)]
    mT = [A.alloc([KT, 128], BF16) for _ in range(2)]
    h2s = [A.alloc([KT, 128], BF16) for _ in range(2)]
    ssq = [A.alloc([8], F32) for _ in range(2)]
    junk4 = A.alloc([512], BF16)
    def load4(i):
        P.dma("sp", mT[i % 2], scr_mixT[i].rearrange("p (k t) -> p k t", k=KT), writes=["mT%d" % (i % 2)])
        P.dma("sp", xs4[i % 2], x_main[i * 128:(i + 1) * 128, :], writes=["xs4_%d" % (i % 2)])

    def mm4(i):
        i2 = i % 2
        bo = 4 * i2
        for cb in range(4):
            for k in range(KT):
                P.op("pe", MM(psf(bo + cb), mT[i2][:, k, :], wo[:, k, cb * 512:(cb + 1) * 512], k == 0, k == KT - 1),
                     reads=["mT%d" % i2, "wo%d" % cb], writes=[PK[bo + cb]], signal=(k == KT - 1))

    def post4(i):
        i2 = i % 2
        bo = 4 * i2
        sample = (i == NM - 1)
        for cb in range(4):
            P.op("act", ACT(junk4, psf(bo + cb), AF.Square, accum_out=ssq[i2][:, cb:cb + 1]), reads=[PK[bo + cb]], writes=["junk4", "ssq%d_%d" % (i2, cb)])
        P.op("dve", lambda e_, o_=ssq[i2][:, 4:5], i_=ssq[i2][:, 0:4]: e_.tensor_reduce(out=o_, in_=i_, axis=AX.X, op=ALU.add),
             reads=["ssq%d_%d" % (i2, cb) for cb in range(4)], writes=["ssq%d_s" % i2])
        rstd_from_ss(ssq[i2][:, 4:5], ssq[i2][:, 5:6], D, ["ssq%d_s" % i2], "ssq%d_r" % i2)
        gB, gBk = (gB1[1], "gB1s") if sample else (gB1[0], "gB1p")
        x1keys = []
        for cb in range(4):
            sl = slice(cb * 512, (cb + 1) * 512)
            key = "x1s%d_%d" % (i2, cb)
            x1keys.append(key)
            P.op("dve", STT(x1s[i2][:, sl], psf(bo + cb), ssq[i2][:, 5:6], gB[:, sl], ALU.mult, ALU.mult), reads=[PK[bo + cb], "ssq%d_r" % i2, gBk], writes=[key])
            P.op("dve", TT(x1s[i2][:, sl], x1s[i2][:, sl], xs4[i2][:, sl], ALU.add), reads=[key, "xs4_%d" % i2], writes=[key])
        if i > 0:
            P.dma("act", scr_x1[i], x1s[i2], reads=x1keys, writes=["scr_x1"])
        rows = (1, 2) if sample else (0, 0)
        front_end(None, h2s[i2], "h2s%d" % i2, 3, 2, rows, tb=(bo, bo + 1), xs_in=x1s[i2], xs_keys=x1keys)
        hk = []
        for k in range(KT):
            hk.append("h2s%d_k%d_0" % (i2, k))
            if sample:
                hk.append("h2s%d_k%d_64" % (i2, k))
        P.dma("act", scr_h2T[:, :, i * 128:(i + 1) * 128], h2s[i2], reads=hk, writes=["scr_h2T"])

    load4(0)
    mm4(0)
    for i in range(NM):
        if i + 1 < NM:
            load4(i + 1)
            mm4(i + 1)
        post4(i)
    P.barrier()
    A.reset(base_mark)
    if last_stage == 4:
        P.run()
        return nc

    TMp = 128 + OWN
    base_s0 = 2 + TMp + 2
    base_s1 = base_s0 + 64 + 2
    ROW = base_s1 + 64
    h2T_all = A.alloc([KT, TM], BF16)
    for q in range(4):
        P.dma("sp", h2T_all[:, q * 4:(q + 1) * 4, :], scr_h2T[:, q * 4:(q + 1) * 4, :], writes=["h2T_%d" % q])
    H2K = ["h2T_%d" % q for q in range(4)]
    cw = A.alloc([FT, 4], F32)
    P.dma("sp", cw, cwT, writes=["cw"])
    cst_tm = A.alloc([DFF], F32)
    cstT = A.alloc([FT, 4], F32)
    P.dma("sp", cst_tm[0:4, :], conv_st, writes=["tmp22"])
    for ft in range(FT):
        P.op("pe", TR(psf(7)[:, ft * 4:(ft + 1) * 4], cst_tm[0:4, ft * 128:(ft + 1) * 128], identf[0:4, 0:4]), reads=["tmp22", "cst"],
             writes=[PK[7]], signal=(ft == FT - 1))
    P.op("dve", CP(cstT.rearrange("p f c -> p (f c)"), psf(7)[:, 0:FT * 4]), reads=[PK[7]], writes=["cstT"])
    ugrow = [A.alloc([ROW], F32) for _ in range(2)]
    acc = [A.alloc([512], F32) for _ in range(2)]
    gl = [A.alloc([512], F32) for _ in range(2)]
    arow = [A.alloc([TM], BF16) for _ in range(2)]
    convsave = A.alloc([FT, 6], F32)
    cso = cst_tm
    wgb = [A.alloc([KT, 512], BF16) for _ in range(2)]
    wub = [A.alloc([KT, 512], BF16) for _ in range(2)]
    for i2 in range(2):
        P.op("pool", MSET(ugrow[i2][:, 0:2], 0.0), writes=["ug%d_pad" % i2])
    blocks = []
    for m0 in range(0, TMp, 512):
        m1 = min(m0 + 512, TMp)
        blocks.append((m0, m1, [(m0, m1, 2 + m0)]))
    blocks.append((TMp, TMp + 128, [(TMp, TMp + 64, base_s0), (TMp + 64, TMp + 128, base_s1)]))
    nG = 0
    nU = 0
    npc = 0
    for ftg in range(FT // 4):
        wi = ftg % 2
        P.dma("pool", wgb[wi], w_g[:, ftg * 512:(ftg + 1) * 512].rearrange("(k p) n -> p k n", p=128), writes=["wgb%d" % wi])
        P.dma("pool", wub[wi], w_u[:, ftg * 512:(ftg + 1) * 512].rearrange("(k p) n -> p k n", p=128), writes=["wub%d" % wi])
        for sub in range(4):
            ft = ftg * 4 + sub
            u2 = ft % 2
            ug = ugrow[u2]
            P.op("pool", CP(ug[:, base_s0 - 2:base_s0], cstT[:, ft, 0:2]), reads=["cstT"], writes=["ug%d_s0" % u2])
            P.op("pool", CP(ug[:, base_s1 - 2:base_s1], cstT[:, ft, 2:4]), reads=["cstT"], writes=["ug%d_s1" % u2])
            akeys = []
            for bi, (m0, m1, pieces) in enumerate(blocks):
                n = m1 - m0
                gb = nG % 3
                nG += 1
                ub = 3 + nU % 4
                nU += 1
                for k in range(KT):
                    P.op("pe", MM(psf(gb)[:, 0:n], wgb[wi][:, k, sub * 128:(sub + 1) * 128], h2T_all[:, k, m0:m1], k == 0, k == KT - 1),
                         reads=["wgb%d" % wi, H2K[k // 4]], writes=[PK[gb]], signal=(k == KT - 1))
                for k in range(KT):
                    P.op("pe", MM(psf(ub)[:, 0:n], wub[wi][:, k, sub * 128:(sub + 1) * 128], h2T_all[:, k, m0:m1], k == 0, k == KT - 1),
                         reads=["wub%d" % wi, H2K[k // 4]], writes=[PK[ub]], signal=(k == KT - 1))
                for (p0, p1, c0) in pieces:
                    pn = p1 - p0
                    ukey = "ug%d_b%d_%d" % (u2, bi, p0)
                    P.op("act", ACT(ug[:, c0:c0 + pn], psf(gb)[:, p0 - m0:p1 - m0], AF.Copy), reads=[PK[gb]], writes=[ukey])
                    prev = ["ug%d_pad" % u2, "ug%d_s0" % u2, "ug%d_s1" % u2]
                    if bi > 0:
                        prev += ["ug%d_b%d_%d" % (u2, bi - 1, blocks[bi - 1][2][-1][0])]
                    if bi == 0:
                        P.op("pool", TS(ug[:, 2:130], ug[:, 2:130], msk[:, 0, NPRE:NPRE + 1]), reads=[ukey, "msk"], writes=[ukey])
                    a_ = acc[npc % 2]
                    g_ = gl[npc % 2]
                    ak, gk = "acc%d" % (npc % 2), "gl%d" % (npc % 2)
                    npc += 1
                    P.op("act", ACT(a_[:, 0:pn], ug[:, c0 - 2:c0 - 2 + pn], AF.Identity, scale=cw[:, ft, 0:1], bias=cw[:, ft, 3:4]),
                         reads=[ukey, "cw"] + prev, writes=[ak])
                    P.op("dve", STT(a_[:, 0:pn], ug[:, c0 - 1:c0 - 1 + pn], cw[:, ft, 1:2], a_[:, 0:pn], ALU.mult, ALU.add),
                         reads=[ukey, "cw", ak] + prev, writes=[ak])
                    P.op("dve", STT(a_[:, 0:pn], ug[:, c0:c0 + pn], cw[:, ft, 2:3], a_[:, 0:pn], ALU.mult, ALU.add), reads=[ukey, "cw", ak], writes=[ak])
                    P.op("act", ACT(g_[:, 0:pn], a_[:, 0:pn], AF.Gelu), reads=[ak], writes=[gk])
                    akey = "arow%d_%d" % (u2, p0)
                    akeys.append(akey)
                    P.op("dve", TT(arow[u2][:, p0:p1], g_[:, 0:pn], psf(ub)[:, p0 - m0:p1 - m0], ALU.mult), reads=[gk, PK[ub]], writes=[akey])
            allug = ["ug%d_b%d_%d" % (u2, bi, pc[0]) for bi, (_, _, pcs) in enumerate(blocks) for pc in pcs]
            P.op("pool", CP(convsave[:, ft, 0:2], ug[:, 2 + TMp - 2:2 + TMp]), reads=allug, writes=["convsave"])
            P.op("pool", CP(convsave[:, ft, 2:4], ug[:, base_s0 + 62:base_s0 + 64]), reads=allug, writes=["convsave"])
            P.op("pool", CP(convsave[:, ft, 4:6], ug[:, base_s1 + 62:base_s1 + 64]), reads=allug, writes=["convsave"])
            P.dma("sp", scr_aT[ft], arow[u2], reads=akeys, writes=["scr_aT"])
    for g4 in range(FT // 4):
        b = g4 % 2
        for s4 in range(4):
            ft = g4 * 4 + s4
            P.op("pe", TR(psf(b)[0:6, s4 * 128:(s4 + 1) * 128], convsave[:, ft, :], identf), reads=["convsave", "cst"], writes=[PK[b]], signal=(s4 == 3))
        P.op("dve", CP(cso[0:6, g4 * 512:(g4 + 1) * 512], psf(b)[0:6, :]), reads=[PK[b]], writes=["tmp22"])
    P.dma("sp", conv_out, cso[0:6, :], reads=["tmp22"], writes=["conv_out"])
    P.barrier()
    A.reset(base_mark)
    if last_stage == 5:
        P.run()
        return nc

    GS = 6
    gB2 = [A.alloc([D], F32) for _ in range(2)]
    build_gtgB(gB2[0], gB2[1], 1, "gB2p", "gB2s")
    aTg = A.alloc([FT, GS * 128], BF16)
    ystage = [A.alloc([D], F32) for _ in range(GS)]
    x1t = [A.alloc([D], F32) for _ in range(2)]
    NWD = 5
    wd = [A.alloc([4, 512], BF16) for _ in range(NWD)]
    ssq6 = A.alloc([GS, 8], F32)
    junk6 = A.alloc([512], BF16)
    out_tiles = list(range(1, NM))
    nwd = 0
    nx1 = 0
    for g0 in range(0, len(out_tiles), GS):
        tiles = out_tiles[g0:g0 + GS]
        nt = len(tiles)
        tok0 = tiles[0] * 128
        ntok = nt * 128
        for q in range(4):
            P.dma("sp", aTg[:, q * 11:(q + 1) * 11, 0:ntok], scr_aT[q * 11:(q + 1) * 11, :, tok0:tok0 + ntok].rearrange("f p t -> p f t"),
                  writes=["aTg_%d" % q])
        for cb in range(4):
            for ftq in range(FT // 4):
                wi = nwd % NWD
                nwd += 1
                P.dma("pool", wd[wi], w_d[ftq * 512:(ftq + 1) * 512, cb * 512:(cb + 1) * 512].rearrange("(f p) n -> p f n", p=128), writes=["wd%d" % wi])
                for s4 in range(4):
                    ft = ftq * 4 + s4
                    for ti in range(nt):
                        P.op("pe", MM(psf(ti), aTg[:, ft, ti * 128:(ti + 1) * 128], wd[wi][:, s4, :], ft == 0, ft == FT - 1),
                             reads=["aTg_%d" % (ft // 11), "wd%d" % wi], writes=[PK[ti]], signal=(ft == FT - 1 or (s4 == 3 and ti == nt - 1)))
            for ti in range(nt):
                P.op("act", ACT(junk6, psf(ti), AF.Square, accum_out=ssq6[:, ti, cb:cb + 1]), reads=[PK[ti]], writes=["junk6", "ssq6_%d_%d" % (ti, cb)])
                P.op("dve", CP(ystage[ti][:, cb * 512:(cb + 1) * 512], psf(ti)), reads=[PK[ti]], writes=["ys%d_%d" % (ti, cb)])
        for ti, tile in enumerate(tiles):
            sample = (tile == NM - 1)
            xi = nx1 % 2
            nx1 += 1
            P.dma("sp", x1t[xi], scr_x1[tile], writes=["x1t%d" % xi])
            P.op("dve", lambda e_, o_=ssq6[:, ti, 4:5], i_=ssq6[:, ti, 0:4]: e_.tensor_reduce(out=o_, in_=i_, axis=AX.X, op=ALU.add),
                 reads=["ssq6_%d_%d" % (ti, cb) for cb in range(4)], writes=["ssq6s_%d" % ti])
            rstd_from_ss(ssq6[:, ti, 4:5], ssq6[:, ti, 5:6], D, ["ssq6s_%d" % ti], "ssq6r_%d" % ti)
            gB, gBk = (gB2[1], "gB2s") if sample else (gB2[0], "gB2p")
            yk = ["ys%d_%d" % (ti, cb) for cb in range(4)]
            P.op("dve", STT(ystage[ti], ystage[ti], ssq6[:, ti, 5:6], gB, ALU.mult, ALU.mult), reads=yk + ["ssq6r_%d" % ti, gBk], writes=yk)
            P.op("dve", TT(ystage[ti], ystage[ti], x1t[xi], ALU.add), reads=yk + ["x1t%d" % xi], writes=yk)
            P.dma("act", y_main[(tile - 1) * 128:tile * 128, :], ystage[ti], reads=yk, writes=["y_main"])
    P.barrier()
    P.run()
    return nc


def make_consts():
    c = np.zeros((128, 8, 128), np.float32)
    c[:, 0, :] = np.eye(128, dtype=np.float32)
    s = np.arange(128)[:, None]
    t = np.arange(128)[None, :]
    c[:, 1, :] = ((s // 64 == t // 64) & (s <= t)).astype(np.float32)
    c[0:64, 2, :] = 1.0
    c[64:128, 3, :] = 1.0
    c[0, 4, :] = 1.0
    c[1, 5, 0:64] = 1.0
    c[2, 5, 64:128] = 1.0
    c[0:64, 6, 0] = 1.0
    c[64:128, 6, 1] = 1.0
    c[:, 7, :] = 1.0
    return c


def make_consts2():
    return None


def prep_inputs(inp, SEQ):
    OWN = SEQ // 4
    NOWN = OWN // 128
    NPRE = 4
    NM = NOWN + 2
    f32 = np.float32
    xp = np.asarray(inp["x_prompt"], f32)
    xsamp = np.asarray(inp["x_sample"], f32)
    relb = np.asarray(inp["att_rel_bias"], f32)[0]
    row = np.arange(128)[:, None, None]
    kk = np.arange(5)[None, :, None]
    qc = np.arange(128)[None, None, :]
    p = 128 * kk + row - 64 * (qc // 64)
    dist = 512 + (qc % 64) - p
    idx = np.clip(dist, -256, 256) + 256
    valid = (p >= 0) & (p < 576)
    bias_tab = np.empty((128, 8, 5, 128), f32)
    for h in range(8):
        bias_tab[:, h] = np.where(valid, relb[h][idx], NEG)
    rr = np.arange(128)[:, None]
    qq = np.arange(128)[None, :]
    same = (rr // 64) == (qq // 64)
    idx4 = np.clip((qq % 64) - (rr % 64), -256, 256) + 256
    bias_s4 = np.empty((128, 8, 128), f32)
    for h in range(8):
        bias_s4[:, h] = np.where(same, relb[h][idx4], NEG)
    consts = make_consts()
    shared = {
        "adabT": np.ascontiguousarray(np.asarray(inp["ada_b"], f32)[0].reshape(96, 128).T),
        "adab3": np.ascontiguousarray(np.broadcast_to(np.asarray(inp["ada_b"], f32)[0][None], (3, 12288))),
        "gpreT": np.ascontiguousarray(np.stack([np.asarray(inp["norm_pre_mix"], f32)[0].reshape(16, 128).T,
                                                 np.asarray(inp["norm_pre_ffn"], f32)[0].reshape(16, 128).T], axis=1)),
        "gpost3": np.ascontiguousarray(np.broadcast_to(np.stack([np.asarray(inp["norm_post_mix"], f32)[0],
                                                                  np.asarray(inp["norm_post_ffn"], f32)[0]])[None], (3, 2, D))),
        "mnormB": np.ascontiguousarray(np.broadcast_to(np.asarray(inp["mlstm_norm"], f32)[0][None], (128, 1024))),
        "bgate": np.ascontiguousarray(np.broadcast_to(np.concatenate([np.asarray(inp["b_igate"], f32)[0],
                                                                       np.asarray(inp["b_fgate"], f32)[0]])[None], (128, 8))),
        "cwT": np.ascontiguousarray(np.concatenate([np.asarray(inp["ffn_conv_w"], f32)[0].reshape(3, FT, 128),
                                                    np.asarray(inp["ffn_conv_b"], f32)[0].reshape(1, FT, 128)], 0).transpose(2, 1, 0)),
        "bias_tab": bias_tab,
        "bias_s4": bias_s4,
        "ada_w": np.asarray(inp["ada_w"], f32)[0],
        "w_in": np.asarray(inp["w_in"], f32)[0],
        "w_out": np.asarray(inp["w_out"], f32)[0],
        "w_g": np.asarray(inp["w_ffn_gate"], f32)[0],
        "w_u": np.asarray(inp["w_ffn_up"], f32)[0],
        "w_d": np.asarray(inp["w_ffn_down"], f32)[0],
    }
    maps = []
    for r in range(8):
        b, j = r // 4, r % 4
        s0 = j * OWN
        m = dict(shared)
        xm = np.zeros((NM * 128, D), f32)
        if j > 0:
            xm[0:128] = xp[b, s0 - 128:s0]
        xm[128:128 + OWN] = xp[b, s0:s0 + OWN]
        xm[128 + OWN:] = xsamp[2 * r:2 * r + 2].reshape(128, D)
        m["x_main"] = xm
        xpre = np.zeros((NPRE * 128, D), f32)
        lo = s0 - 128 - NPRE * 128
        mk = np.zeros((128, 3, NPRE + NM), f32)
        for t in range(NPRE):
            a = lo + t * 128
            if a >= 0:
                xpre[t * 128:(t + 1) * 128] = xp[b, a:a + 128]
                mk[:, 0, t] = 1.0
        mk[:, 0, NPRE] = 1.0 if j > 0 else 0.0
        mk[:, 0, NPRE + 1:] = 1.0
        mk[:, 1] = -mk[:, 0]
        mk[:, 2] = np.where(mk[:, 0] > 0, 0.0, -1e30)
        m["x_pre"] = xpre
        m["masks"] = mk
        fmk = np.zeros((128, 2, 8), f32)
        for q in range(8):
            if q // 4 == b and q % 4 < j:
                fmk[:, 0, q] = 1.0
            else:
                fmk[:, 1, q] = -1e30
        m["fmask"] = fmk
        c3 = np.stack([np.asarray(inp["c_prompt"], f32)[b], np.asarray(inp["c_sample"], f32)[2 * r],
                       np.asarray(inp["c_sample"], f32)[2 * r + 1]])
        m["c3T"] = np.ascontiguousarray(c3.reshape(3, 16, 128).transpose(2, 1, 0))
        cc = consts.copy()
        m["consts"] = cc
        m["cache_k"] = np.ascontiguousarray(np.asarray(inp["cache_att_k"], f32)[0, 2 * r:2 * r + 2].reshape(2, 512, 1024))
        m["cache_v"] = np.ascontiguousarray(np.asarray(inp["cache_att_v"], f32)[0, 2 * r:2 * r + 2].reshape(2, 512, 1024))
        m["st_C"] = np.ascontiguousarray(np.asarray(inp["state_mlstm_C"], f32)[0, 2 * r:2 * r + 2])
        m["st_n"] = np.ascontiguousarray(np.asarray(inp["state_mlstm_n"], f32)[0, 2 * r:2 * r + 2])
        sm = np.asarray(inp["state_mlstm_m"], f32)[0, 2 * r:2 * r + 2]
        m["st_mB"] = np.ascontiguousarray(np.broadcast_to(sm[None], (128, 2, 4)))
        m["st_m4"] = np.ascontiguousarray(sm.T)
        m["conv_st"] = np.ascontiguousarray(np.asarray(inp["state_ffn_conv"], f32)[0, 2 * r:2 * r + 2].reshape(4, 5632))
        maps.append(m)
    return maps


SEQ_FULL = 8192
_CACHE = {}


def kernel(**inputs):
    SEQ = int(np.asarray(inputs["x_prompt"]).shape[1])
    OWN = SEQ // 4
    if SEQ not in _CACHE:
        _CACHE[SEQ] = None
    nc = build(SEQ)
    maps = prep_inputs(inputs, SEQ)
    res = run_bass_kernel_spmd(nc, maps, core_ids=list(range(8)))
    R_ = res.results
    f32 = np.float32
    y_p = np.empty((2, SEQ, D), f32)
    y_s = np.empty((16, 64, D), f32)
    p_k = np.empty((1, 2, 512, 8, 128), f32)
    p_v = np.empty((1, 2, 512, 8, 128), f32)
    p_C = np.empty((1, 2, 4, 256, 256), f32)
    p_n = np.empty((1, 2, 4, 256), f32)
    p_m = np.empty((1, 2, 4), f32)
    p_conv = np.empty((1, 2, 2, DFF), f32)
    s_k = np.empty((1, 16, 64, 8, 128), f32)
    s_v = np.empty((1, 16, 64, 8, 128), f32)
    s_C = np.empty((1, 16, 4, 256, 256), f32)
    s_n = np.empty((1, 16, 4, 256), f32)
    s_m = np.empty((1, 16, 4), f32)
    s_conv = np.empty((1, 16, 2, DFF), f32)
    for r in range(8):
        b, j = r // 4, r % 4
        o = R_[r]
        ym = np.asarray(o["y_main"])
        y_p[b, j * OWN:(j + 1) * OWN] = ym[:OWN]
        y_s[2 * r:2 * r + 2] = ym[OWN:].reshape(2, 64, D)
        ko, vo = np.asarray(o["k_out"]), np.asarray(o["v_out"])
        s_k[0, 2 * r:2 * r + 2] = ko[512:640].reshape(2, 64, 8, 128)
        s_v[0, 2 * r:2 * r + 2] = vo[512:640].reshape(2, 64, 8, 128)
        Co, no, mo = np.asarray(o["C_out"]), np.asarray(o["n_out"]), np.asarray(o["m_out"])
        s_C[0, 2 * r:2 * r + 2] = Co[1:3]
        s_n[0, 2 * r:2 * r + 2] = no[1:3]
        s_m[0, 2 * r:2 * r + 2] = mo[:, 1:3].T
        co = np.asarray(o["conv_out"]).reshape(3, 2, DFF)
        s_conv[0, 2 * r:2 * r + 2] = co[1:3]
        if j == 3:
            p_k[0, b] = ko[0:512].reshape(512, 8, 128)
            p_v[0, b] = vo[0:512].reshape(512, 8, 128)
            p_C[0, b] = Co[0]
            p_n[0, b] = no[0]
            p_m[0, b] = mo[:, 0]
            p_conv[0, b] = co[0]
    return (y_p, y_s, p_k, p_v, p_C, p_n, p_m, p_conv, s_k, s_v, s_C, s_n, s_m, s_conv)
```
